# Optimizing a Trainium2 kernel written in Bass

```python
import jax, jax.numpy as jnp
from jax import lax
import numpy as np

D_MODEL = 2048
BATCH = 16
SEQ = 256
DEPTH = 2
DEC_BATCH = 4
DEC_SEQ = 2048
PAST_LEN = 512

GRID_W = 64
POOL_WIDTH = 512
POOL_GROUPS = 4
POOL_GROUP_DIM = POOL_WIDTH // POOL_GROUPS
POOL_WINDOWS = (2, 4, 8, 16)
ATTN_WIDTH = 512
N_HEADS = 4
V_DIM = ATTN_WIDTH // N_HEADS
QK_DIM = V_DIM // 2
ROPE_AXIS_DIM = QK_DIM // 2
ROPE_BASE = 10000.0
Q_BLOCK = 128
CONV_CH = 512
CONV_TAPS = 31
FOURIER_WIDTH = 512
FOURIER_HEADS = 4
FOURIER_HEAD_DIM = FOURIER_WIDTH // FOURIER_HEADS

MIX_WIDTH = POOL_WIDTH + ATTN_WIDTH + CONV_CH + FOURIER_WIDTH
IN_COLS = POOL_WIDTH + 3 * ATTN_WIDTH + 2 * CONV_CH + FOURIER_WIDTH
D_FF = ((8 * D_MODEL + 3 * 256 - 1) // (3 * 256)) * 256
EPS = 1e-6

kernel_name = "hybrid_diffusion_prefix_step"


def rmsnorm(x, g):
    xf = x.astype(jnp.float32)
    y = xf * lax.rsqrt(jnp.mean(xf * xf, axis=-1, keepdims=True) + EPS)
    return (y * g.astype(jnp.float32)).astype(x.dtype)


def layernorm(x, g, b):
    xf = x.astype(jnp.float32)
    mu = jnp.mean(xf, axis=-1, keepdims=True)
    xc = xf - mu
    y = xc * lax.rsqrt(jnp.mean(xc * xc, axis=-1, keepdims=True) + EPS)
    return (y * g.astype(jnp.float32) + b.astype(jnp.float32)).astype(x.dtype)


def axial_rope_tables(rows):
    t = jnp.arange(rows * GRID_W)
    row = (t // GRID_W).astype(jnp.float32)
    col = (t % GRID_W).astype(jnp.float32)
    inv = 1.0 / (ROPE_BASE ** (jnp.arange(0, ROPE_AXIS_DIM, 2, dtype=jnp.float32) / ROPE_AXIS_DIM))
    ar = row[:, None] * inv[None, :]
    ac = col[:, None] * inv[None, :]
    return (jnp.cos(ar), jnp.sin(ar), jnp.cos(ac), jnp.sin(ac))


def _rope_axis(x, cos, sin):
    half = ROPE_AXIS_DIM // 2
    cos = cos[None, :, None, None, :]
    sin = sin[None, :, None, None, :]
    x1, x2 = x[..., :half], x[..., half:]
    return jnp.concatenate([x1 * cos - x2 * sin, x2 * cos + x1 * sin], axis=-1)


def axial_rope(x, tabs):
    cr, sr, cc, sc = tabs
    xf = x.astype(jnp.float32)
    y = jnp.concatenate([_rope_axis(xf[..., :ROPE_AXIS_DIM], cr, sr),
                         _rope_axis(xf[..., ROPE_AXIS_DIM:], cc, sc)], axis=-1)
    return y.astype(x.dtype)


def multiscale_pool(u, pool_w, pool_scale):
    B, L, _ = u.shape
    uf = u.astype(jnp.float32)
    cs = jnp.concatenate([jnp.zeros((B, 1, POOL_WIDTH), jnp.float32), jnp.cumsum(uf, axis=1)], axis=1)
    t = jnp.arange(L)
    outs = []
    for g, w in enumerate(POOL_WINDOWS):
        sl = slice(g * POOL_GROUP_DIM, (g + 1) * POOL_GROUP_DIM)
        lo = jnp.maximum(t - w // 2, 0)
        hi = jnp.minimum(t + w // 2 - 1, L - 1)
        csg = cs[..., sl]
        s = jnp.take(csg, hi + 1, axis=1) - jnp.take(csg, lo, axis=1)
        cnt = (hi - lo + 1).astype(jnp.float32)[None, :, None]
        outs.append(s / cnt - uf[..., sl])
    p = jnp.stack(outs, axis=2).astype(u.dtype)
    y = jnp.einsum('blgc,gcd->blgd', p, pool_w).reshape(B, L, POOL_WIDTH)
    return y * pool_scale


def conformer_conv(u, dw, dw_b, ln_g, ln_b, pw, pw_b):
    a, b = jnp.split(u, 2, axis=-1)
    g = a * jax.nn.sigmoid(b)
    y = lax.conv_general_dilated(g, dw[:, None, :].astype(g.dtype), (1,),
                                 [(CONV_TAPS // 2, CONV_TAPS // 2)],
                                 dimension_numbers=('NWC', 'WIO', 'NWC'),
                                 feature_group_count=CONV_CH) + dw_b
    y = jax.nn.silu(layernorm(y, ln_g, ln_b))
    return y @ pw + pw_b


def fourier_mix(u, w):
    B, L, _ = u.shape
    uh = u.astype(jnp.float32).reshape(B, L, FOURIER_HEADS, FOURIER_HEAD_DIM)
    f = jnp.fft.fftn(uh, axes=(1, 3), norm='ortho').real
    return f.reshape(B, L, FOURIER_WIDTH).astype(u.dtype) @ w


def diff_attention(q, k, v, lam):
    B, Lq = q.shape[0], q.shape[1]
    nb = Lq // Q_BLOCK
    qb = jnp.moveaxis(q.reshape(B, nb, Q_BLOCK, N_HEADS, 2, QK_DIM), 1, 0)
    scale = QK_DIM ** -0.5

    def block(qi):
        s = jnp.einsum('bqhmd,bkhmd->bhmqk', qi, k, preferred_element_type=jnp.float32) * scale
        p = jax.nn.softmax(s, axis=-1)
        a = p[:, :, 0] - lam * p[:, :, 1]
        return jnp.einsum('bhqk,bkhd->bqhd', a.astype(v.dtype), v)

    o = lax.map(block, qb)
    return jnp.moveaxis(o, 0, 1).reshape(B, Lq, N_HEADS, V_DIM)


def token_mixer(h, lp, lam_init, rope, ctx_k, ctx_v):
    B, L, _ = h.shape
    z = h @ lp['w_in']
    o1 = POOL_WIDTH
    o2 = o1 + ATTN_WIDTH
    o3 = o2 + ATTN_WIDTH
    o4 = o3 + ATTN_WIDTH
    o5 = o4 + 2 * CONV_CH
    u_pool = z[..., :o1]
    q = rmsnorm(z[..., o1:o2].reshape(B, L, N_HEADS, 2, QK_DIM), lp['g_q'])
    k = rmsnorm(z[..., o2:o3].reshape(B, L, N_HEADS, 2, QK_DIM), lp['g_k'])
    v = z[..., o3:o4].reshape(B, L, N_HEADS, V_DIM)
    u_conv = z[..., o4:o5]
    u_four = z[..., o5:]

    if rope is None:
        keys, vals = k, v
    else:
        q = axial_rope(q, rope)
        keys = jnp.concatenate([axial_rope(k, rope), ctx_k.astype(k.dtype)], axis=1)
        vals = jnp.concatenate([v, ctx_v.astype(v.dtype)], axis=1)

    lv = lp['lam'].astype(jnp.float32)
    lam = jnp.exp(jnp.sum(lv[0] * lv[1])) - jnp.exp(jnp.sum(lv[2] * lv[3])) + lam_init
    att = diff_attention(q, keys, vals, lam)
    att = (rmsnorm(att, lp['g_subln']) * (1.0 - lam_init)).reshape(B, L, ATTN_WIDTH)

    y_pool = multiscale_pool(u_pool, lp['pool_w'], lp['pool_scale'])
    y_conv = conformer_conv(u_conv, lp['conv_dw'], lp['conv_dw_b'], lp['conv_ln_g'],
                            lp['conv_ln_b'], lp['conv_pw'], lp['conv_pw_b'])
    y_four = fourier_mix(u_four, lp['fourier_w'])
    y = jnp.concatenate([y_pool, att, y_conv, y_four], axis=-1) @ lp['w_out']
    return y, k, v


def trunk_layer(x, mod, lp, lam_init, rope, ctx_k, ctx_v):
    sh1, sc1, g1, sh2, sc2, g2 = jnp.split(mod, 6, axis=-1)
    h = rmsnorm(x, lp['g_norm1']) * (1.0 + sc1) + sh1
    a, k, v = token_mixer(h, lp, lam_init, rope, ctx_k, ctx_v)
    x = x + g1 * a
    h = rmsnorm(x, lp['g_norm2']) * (1.0 + sc2) + sh2
    f = (jax.nn.silu(h @ lp['w_gate']) * (h @ lp['w_up'])) @ lp['w_down']
    return x + g2 * f, k, v


def setup_inputs(seed: int = 0) -> dict:
    key = jax.random.key(seed)
    ks = jax.random.split(key, 32)
    f32 = jnp.float32

    def nrm(k, shape, scale=1.0):
        return jax.random.normal(k, shape, f32) * scale

    def gain(k, shape):
        return 1.0 + 0.05 * jax.random.normal(k, shape, f32)

    return {
        "x_prompt": nrm(ks[0], (BATCH, SEQ, D_MODEL)),
        "x_sample": nrm(ks[1], (DEC_BATCH, DEC_SEQ, D_MODEL)),
        "cache_k": nrm(ks[2], (DEC_BATCH, DEPTH, PAST_LEN, N_HEADS, 2, QK_DIM)),
        "cache_v": nrm(ks[3], (DEC_BATCH, DEPTH, PAST_LEN, N_HEADS, V_DIM)),
        "c": nrm(ks[4], (DEC_BATCH, D_MODEL)),
        "c_ctx": nrm(ks[5], (D_MODEL,)),
        "w_ada": nrm(ks[6], (DEPTH, D_MODEL, 6 * D_MODEL), 0.5 * D_MODEL ** -0.5),
        "b_ada": nrm(ks[7], (DEPTH, 6 * D_MODEL), 0.02),
        "g_norm1": gain(ks[8], (DEPTH, D_MODEL)),
        "w_in": nrm(ks[9], (DEPTH, D_MODEL, IN_COLS), D_MODEL ** -0.5),
        "pool_w": nrm(ks[10], (DEPTH, POOL_GROUPS, POOL_GROUP_DIM, POOL_GROUP_DIM), POOL_GROUP_DIM ** -0.5),
        "pool_scale": gain(ks[11], (DEPTH, POOL_WIDTH)),
        "g_q": gain(ks[12], (DEPTH, QK_DIM)),
        "g_k": gain(ks[13], (DEPTH, QK_DIM)),
        "lam": nrm(ks[14], (DEPTH, 4, QK_DIM), 0.1),
        "g_subln": gain(ks[15], (DEPTH, V_DIM)),
        "conv_dw": nrm(ks[16], (DEPTH, CONV_TAPS, CONV_CH), CONV_TAPS ** -0.5),
        "conv_dw_b": nrm(ks[17], (DEPTH, CONV_CH), 0.02),
        "conv_ln_g": gain(ks[18], (DEPTH, CONV_CH)),
        "conv_ln_b": nrm(ks[19], (DEPTH, CONV_CH), 0.02),
        "conv_pw": nrm(ks[20], (DEPTH, CONV_CH, CONV_CH), CONV_CH ** -0.5),
        "conv_pw_b": nrm(ks[21], (DEPTH, CONV_CH), 0.02),
        "fourier_w": nrm(ks[22], (DEPTH, FOURIER_WIDTH, FOURIER_WIDTH), FOURIER_WIDTH ** -0.5),
        "w_out": nrm(ks[23], (DEPTH, MIX_WIDTH, D_MODEL), MIX_WIDTH ** -0.5),
        "g_norm2": gain(ks[24], (DEPTH, D_MODEL)),
        "w_gate": nrm(ks[25], (DEPTH, D_MODEL, D_FF), D_MODEL ** -0.5),
        "w_up": nrm(ks[26], (DEPTH, D_MODEL, D_FF), D_MODEL ** -0.5),
        "w_down": nrm(ks[27], (DEPTH, D_FF, D_MODEL), D_FF ** -0.5),
    }


def reference(x_prompt, x_sample, cache_k, cache_v, c, c_ctx, w_ada, b_ada, g_norm1, w_in,
              pool_w, pool_scale, g_q, g_k, lam, g_subln, conv_dw, conv_dw_b, conv_ln_g,
              conv_ln_b, conv_pw, conv_pw_b, fourier_w, w_out, g_norm2, w_gate, w_up, w_down):
    rows = x_sample.shape[1] // GRID_W
    rope = axial_rope_tables(rows)
    yp, ys = x_prompt, x_sample
    new_k, new_v = [], []
    for l in range(DEPTH):
        lp = dict(w_in=w_in[l], pool_w=pool_w[l], pool_scale=pool_scale[l], g_q=g_q[l], g_k=g_k[l],
                  lam=lam[l], g_subln=g_subln[l], conv_dw=conv_dw[l], conv_dw_b=conv_dw_b[l],
                  conv_ln_g=conv_ln_g[l], conv_ln_b=conv_ln_b[l], conv_pw=conv_pw[l],
                  conv_pw_b=conv_pw_b[l], fourier_w=fourier_w[l], w_out=w_out[l],
                  g_norm1=g_norm1[l], g_norm2=g_norm2[l], w_gate=w_gate[l], w_up=w_up[l],
                  w_down=w_down[l])
        lam_init = 0.8 - 0.6 * float(np.exp(-0.3 * l))
        mod_ctx = (jax.nn.silu(c_ctx) @ w_ada[l] + b_ada[l])[None, None, :]
        mod_lat = (jax.nn.silu(c) @ w_ada[l] + b_ada[l])[:, None, :]
        yp, kc, vc = trunk_layer(yp, mod_ctx, lp, lam_init, None, None, None)
        new_k.append(kc)
        new_v.append(vc)
        ys, _, _ = trunk_layer(ys, mod_lat, lp, lam_init, rope, cache_k[:, l], cache_v[:, l])
    new_cache_k = jnp.stack(new_k, axis=1)
    new_cache_v = jnp.stack(new_v, axis=1)
    return (yp, ys, new_cache_k, new_cache_v)
```

```python
import contextlib
import os
import numpy as np
import ml_dtypes
import concourse.bass as bass
import concourse.mybir as mybir
from concourse.bass_utils import run_bass_kernel_spmd

F32 = mybir.dt.float32
BF16 = mybir.dt.bfloat16
AF = mybir.ActivationFunctionType
ALU = mybir.AluOpType

D = 2048
KC = 16
DFF = 5632
FC = 44
INC = 3584
EPS = 1e-6
NCORES = 8
TP = 512
TS = 1024
LP = 256
LS = 2048
PAST = 512
XR = 1568
C_G1, C_G2, C_BADA, C_PSC, C_DW, C_DWB, C_LNG, C_LNB, C_PWB = 0, 16, 32, 128, 132, 256, 260, 264, 268
C_GQ, C_GK, C_GSUB, C_LAM = 272, 273, 274, 275
NV = 280
LAM_INIT = [0.8 - 0.6 * float(np.exp(-0.3 * l)) for l in range(2)]
FFN_GROUPS = [(0, 12), (12, 12), (24, 12), (36, 8)]


class Buf:
    __slots__ = ("name", "w", "r", "dkey", "dcnt")

    def __init__(self, name):
        self.name = name
        self.w = {}
        self.r = {}
        self.dkey = {}
        self.dcnt = 0


class _Rec:
    def __init__(self):
        self.call = None

    def __getattr__(self, name):
        def f(*a, **k):
            self.call = (name, a, k)
            return None
        return f


class Eng:
    def __init__(self, name, key):
        self.name = name
        self.key = key
        self.ops = []
        self.cnt = 0
        self.waited = {}


class Prog:
    def __init__(self, nc, stack, n_dsem=100):
        self.nc = nc
        self.stack = stack
        self.sems = []
        self.pe = Eng("tensor", self._sem("s_pe"))
        self.act = Eng("scalar", self._sem("s_act"))
        self.dve = Eng("vector", self._sem("s_dve"))
        self.pool = Eng("gpsimd", self._sem("s_pool"))
        self.sp = Eng("sync", self._sem("s_sp"))
        self.engs = [self.pe, self.act, self.dve, self.pool, self.sp]
        self.free_dsems = {"sync": [], "gpsimd": [], "scalar": []}
        self.dsem_cnt = {}
        self.n_dsem = 0
        self.arena_deps = {}
        self.flip = 0

    def _sem(self, name):
        s = self.stack.enter_context(self.nc.semaphore(name))
        self.sems.append(s)
        return len(self.sems) - 1

    def buf(self, name):
        b = Buf(name)
        b.w = dict(self.arena_deps)
        return b

    def emit(self, eng, fn, reads=(), writes=(), pwrites=(), sig=True, dma=None, inc=None):
        waits = {}

        def need(d):
            for s, v in d.items():
                if waits.get(s, 0) < v:
                    waits[s] = v

        for b in reads:
            need(b.w)
        for b in writes:
            need(b.w)
            need(b.r)
        for b in pwrites:
            need(b.w)
            need(b.r)
        wl = []
        for s, v in waits.items():
            if s == eng.key and (v > eng.cnt or eng is self.pe):
                continue
            if eng.waited.get(s, 0) < v:
                eng.waited[s] = v
                wl.append((s, v))
        if dma is not None:
            if eng.name not in dma.dkey:
                fl = self.free_dsems[eng.name]
                if fl:
                    dma.dkey[eng.name] = fl.pop(0)
                else:
                    dma.dkey[eng.name] = self._sem("d%d" % self.n_dsem)
                    self.n_dsem += 1
                    self.dsem_cnt[dma.dkey[eng.name]] = 0
            dk = dma.dkey[eng.name]
            self.dsem_cnt[dk] += 16
            tok = (dk, self.dsem_cnt[dk])
            incr = (dk, 16)
        elif inc is not None:
            tok = inc
            incr = (inc[0], 1)
        else:
            tok = (eng.key, eng.cnt + 1)
            if sig:
                eng.cnt += 1
                incr = (eng.key, 1)
            else:
                incr = None
        rec = _Rec()
        fn(rec)
        assert rec.call is not None
        eng.ops.append((wl, rec.call, incr))
        for b in reads:
            if b.r.get(tok[0], 0) < tok[1]:
                b.r[tok[0]] = tok[1]
        for b in writes:
            b.w = {tok[0]: tok[1]}
            b.r = {}
        for b in pwrites:
            if b.w.get(tok[0], 0) < tok[1]:
                b.w[tok[0]] = tok[1]
        return tok

    def retire(self, bufs):
        for b in bufs:
            for en_, dk_ in b.dkey.items():
                self.free_dsems[en_].append(dk_)
            b.dkey = {}
            for d in (b.w, b.r):
                for s, v in d.items():
                    if self.arena_deps.get(s, 0) < v:
                        self.arena_deps[s] = v

    def check(self):
        semv = {}
        pos = {e.name: 0 for e in self.engs}
        progress = True
        while progress:
            progress = False
            for e in self.engs:
                while pos[e.name] < len(e.ops):
                    wl, fn, incr = e.ops[pos[e.name]]
                    if all(semv.get(s_, 0) >= v for s_, v in wl):
                        if incr is not None:
                            semv[incr[0]] = semv.get(incr[0], 0) + incr[1]
                        pos[e.name] += 1
                        progress = True
                    else:
                        break
        stuck = {e.name: (pos[e.name], len(e.ops)) for e in self.engs if pos[e.name] < len(e.ops)}
        if stuck:
            for e in self.engs:
                if pos[e.name] < len(e.ops):
                    wl, fn, incr = e.ops[pos[e.name]]
                    print("STUCK", e.name, pos[e.name], "/", len(e.ops), "waits", [(s_, v, semv.get(s_, 0)) for s_, v in wl])
            raise RuntimeError("deadlock in emitted program: %s" % stuck)
        print("check ok: ops per engine", {e.name: len(e.ops) for e in self.engs}, "nsems", len(self.sems))

    def replay(self):
        self.check()
        nc = self.nc
        sems = self.sems
        with nc.Block() as block:
            def mk(eng):
                def body(e):
                    for wl, fn, incr in eng.ops:
                        for s, v in wl:
                            e.wait_ge(sems[s], v)
                        ins = getattr(e, fn[0])(*fn[1], **fn[2])
                        if incr is not None:
                            ins.then_inc(sems[incr[0]], incr[1])
                return body
            block.tensor(mk(self.pe))
            block.scalar(mk(self.act))
            block.vector(mk(self.dve))
            block.gpsimd(mk(self.pool))
            block.sync(mk(self.sp))


class Arena:
    def __init__(self, prog, tensor, nwords):
        self.p = prog
        self.t = tensor
        self.n = nwords
        self.top = 0
        self.live = []

    def mark(self):
        return (self.top, len(self.live))

    def release(self, m):
        top, nl = m
        self.p.retire(self.live[nl:])
        del self.live[nl:]
        self.top = top

    def alloc(self, name, shape, dt):
        n = 1
        for s in shape:
            n *= s
        words = n if dt == F32 else (n + 1) // 2
        words = (words + 7) // 8 * 8
        assert self.top + words <= self.n, ("arena overflow", name, self.top, words, self.n)
        ap = self.t[:, self.top:self.top + words]
        self.top += words
        if dt != F32:
            ap = ap.bitcast(dt)
        ap = ap[:, 0:n]
        if len(shape) == 2:
            ap = ap.rearrange("p (a b) -> p a b", a=shape[0])
        elif len(shape) == 3:
            ap = ap.rearrange("p (a b c) -> p a b c", a=shape[0], b=shape[1])
        return ap

    def newbuf(self, name):
        b = self.p.buf(name)
        self.live.append(b)
        return b


def build_program(stop_after=None):
    nc = bass.Bass("TRN2", target_bir_lowering=False)
    stack = contextlib.ExitStack()
    with stack:
        _build(nc, stack, stop_after)
    return nc


def _build(nc, stack, stop_after):
    P = Prog(nc, stack)
    emit = P.emit
    pe, act, dve, pool, sp = P.pe, P.act, P.dve, P.pool, P.sp

    def din(name, shape, dt=F32):
        return nc.dram_tensor(name, list(shape), dt, kind="ExternalInput").ap()

    def dout(name, shape, dt=F32):
        return nc.dram_tensor(name, list(shape), dt, kind="ExternalOutput").ap()

    xp_d = din("xp", [TP, D])
    xs_d = din("xs", [TS, D])
    ck_d = din("ck", [2, PAST, 512])
    cv_d = din("cv", [2, PAST, 512])
    cT_d = din("cT", [128, KC, 8])
    pvec_d = din("pvec", [2, 128, NV])
    ident_d = din("ident", [128, 128])
    rotm_d = din("rotm", [128, 128])
    bones_d = din("bones", [128, 128])
    rope_d = din("rope", [128, 2, TS])
    icntp_d = din("icnt_p", [4, LP])
    icnts_d = din("icnt_s", [4, TS])
    dft128_d = din("dft128", [128, 256])
    dftp_d = din("dftp", [LP, 2, LP], BF16)
    dfts_d = din("dfts", [LS, 2, TS], BF16)
    hmask_d = din("hmask", [128, 2])
    w_ada_d = din("w_ada", [2, D, 6 * D])
    w_in_d = din("w_in", [2, D, INC])
    pool_w_d = din("pool_w", [2, 4, 128, 128])
    conv_pw_d = din("conv_pw", [2, 512, 512])
    four_w_d = din("fourier_w", [2, 512, 512])
    w_out_d = din("w_out", [2, D, D])
    w_gate_d = din("w_gate", [2, D, DFF])
    w_up_d = din("w_up", [2, D, DFF])
    w_down_d = din("w_down", [2, DFF, D])
    yp_d = dout("yp", [TP, D])
    ys_d = dout("ys", [TS, D])
    nk_d = dout("nk", [2, 2, LP, 512])
    nv_d = dout("nv", [2, 2, LP, 512])
    dbg_d = dout("dbg", [128, KC * TS]) if stop_after else None
    XRA, XRB = 544, 1024

    class _V:
        def __init__(self, t):
            self.t = t

        def ap(self):
            return self.t.ap().rearrange("p c -> (p c)").rearrange("(r c) -> r c", c=1024)
    xsA_t = [nc.dram_tensor("xsA%d" % l, [128, XRA * 8], BF16) for l in range(2)]
    xrA_t = [nc.dram_tensor("xrA%d" % l, [256, XRA * 8], BF16) for l in range(2)]
    xsB_t = [nc.dram_tensor("xsB%d" % l, [128, XRB * 8], BF16) for l in range(2)]
    xrB_t = [nc.dram_tensor("xrB%d" % l, [256, XRB * 8], BF16) for l in range(2)]
    xsA = [_V(t) for t in xsA_t]
    xrA = [_V(t) for t in xrA_t]
    xsB = [_V(t) for t in xsB_t]
    xrB = [_V(t) for t in xrB_t]
    xsendQ = [[P.buf("xsend%d_%d" % (l, q)) for q in range(2)] for l in range(2)]
    xrecvQ = [[P.buf("xrecv%d_%d" % (l, q)) for q in range(2)] for l in range(2)]
    cc_keys = [[P._sem("cc%d_%d" % (l, q)) for q in range(2)] for l in range(2)]

    def sb(name, shape, dt):
        return stack.enter_context(nc.sbuf_tensor("sb_" + name, list(shape), dt))

    xT = sb("xT", [128, KC, TS], F32)
    xTb = [[P.buf("xT%d_%d" % (c, n)) for n in range(2)] for c in range(KC)]
    NSLOT = 4
    wring = [sb("wr%d" % i, [128, 4096], BF16) for i in range(NSLOT)]
    wringB = [P.buf("wr%d" % i) for i in range(NSLOT)]
    wnext = [0]
    pv = sb("pv", [128, 2, NV], F32)
    pvB = P.buf("pv")
    modt = sb("modt", [128, 2, 96, 2], F32)
    modB = P.buf("modt")
    gsc = sb("gsc", [128, 2, 2, KC, 2], F32)
    gscB = P.buf("gsc")
    ident = sb("ident", [128, 128], F32)
    rotm = sb("rotm", [128, 128], F32)
    bones = sb("bones", [128, 128], F32)
    ones_f = sb("ones_f", [128, 128], F32)
    ones_b = sb("ones_b", [128, 128], BF16)
    constB = P.buf("const")
    rope = sb("rope", [128, 2, TS], F32)
    dft128 = sb("dft128", [128, 256], BF16)
    dftp = sb("dftp", [128, 2, 2, LP], BF16)
    hmask = sb("hmask", [128, 2], F32)
    sT = sb("sT", [128, KC, 8], BF16)
    small = sb("small", [128, 16], F32)
    smallB = P.buf("small")
    ARENA_WORDS = 23600
    arena_t = sb("arena", [128, ARENA_WORDS], F32)
    A = Arena(P, arena_t, ARENA_WORDS)
    banks = [stack.enter_context(nc.psum_tensor("ps%d" % i, [128, 512], F32)) for i in range(8)]
    bankB = [P.buf("ps%d" % i) for i in range(8)]
    dn = [0]
    mn = [0]

    def dbank():
        i = dn[0] % 6
        dn[0] += 1
        return i

    def mbank():
        i = 6 + mn[0] % 2
        mn[0] += 1
        return i

    def ev_eng():
        P.flip ^= 1
        return act if P.flip else dve

    def copy_on(eng, out, in_, reads, writes, pwrites=()):
        if eng is act:
            return emit(act, lambda e: e.activation(out=out, in_=in_, func=AF.Identity), reads, writes, pwrites)
        return emit(dve, lambda e: e.tensor_copy(out=out, in_=in_), reads, writes, pwrites)

    def wload(src, kc, cw):
        i = wnext[0] % NSLOT
        wnext[0] += 1
        view = wring[i][:, 0:kc * cw].rearrange("p (k c) -> p k c", k=kc)
        emit(pool, lambda e: e.dma_start(out=view, in_=src), [], [wringB[i]], dma=wringB[i])
        return wringB[i], view

    def wsrc(wd, l, r0, nrows, c0, cw):
        return wd[l, r0:r0 + nrows, c0:c0 + cw].rearrange("(k p) n -> p k n", p=128)

    eps_ap = small[:, 0:1]

    emit(sp, lambda e: e.dma_start(out=pv[:], in_=pvec_d.rearrange("l p v -> p l v")), [], [pvB], dma=pvB)
    cB = P.buf("cld")
    for (t, d) in ((ident, ident_d), (rotm, rotm_d), (bones, bones_d)):
        emit(sp, lambda e, t=t, d=d: e.dma_start(out=t[:], in_=d), [], [], pwrites=[constB], dma=cB)
    emit(sp, lambda e: e.dma_start(out=rope[:], in_=rope_d), [], [], pwrites=[constB], dma=cB)
    emit(sp, lambda e: e.dma_start(out=hmask[:], in_=hmask_d), [], [], pwrites=[constB], dma=cB)
    emit(sp, lambda e: e.dma_start(out=dftp[:], in_=dftp_d.rearrange("(i p) a k -> p i a k", p=128)), [], [],
         pwrites=[constB], dma=cB)
    emit(pool, lambda e: e.dma_start(out=dft128[:], in_=dft128_d), [], [], pwrites=[constB], dma=cB)
    emit(dve, lambda e: e.memset(ones_f[:], 1.0), [], [], pwrites=[constB])
    emit(dve, lambda e: e.memset(ones_b[:], 1.0), [], [], pwrites=[constB])
    emit(dve, lambda e: e.memset(small[:, 0:1], EPS), [], [smallB])

    m0 = A.mark()
    ctmp = A.alloc("ctmp", [KC, 8], F32)
    ctB = A.newbuf("ctmp")
    sTB = P.buf("sT")
    emit(sp, lambda e: e.dma_start(out=ctmp, in_=cT_d), [], [ctB], dma=ctB)
    emit(act, lambda e: e.activation(out=sT[:], in_=ctmp, func=AF.Silu), [ctB], [sTB])
    lp = A.alloc("lamp", [4], F32)
    lpB = A.newbuf("lamp")
    for l in range(2):
        for q in range(2):
            emit(dve, lambda e, l=l, q=q: e.tensor_tensor(
                out=lp[:, 2 * l + q:2 * l + q + 1], in0=pv[:, l, C_LAM + 2 * q:C_LAM + 2 * q + 1],
                in1=pv[:, l, C_LAM + 2 * q + 1:C_LAM + 2 * q + 2], op=ALU.mult), [pvB], [], pwrites=[lpB])
    bi = mbank()
    emit(pe, lambda e: e.matmul(banks[bi][:, 0:4], lhsT=ones_f[:], rhs=lp, start=True, stop=True),
         [lpB, constB], [bankB[bi]])
    le = A.alloc("lame", [4], F32)
    leB = A.newbuf("lame")
    emit(act, lambda e: e.activation(out=le, in_=banks[bi][:, 0:4], func=AF.Exp), [bankB[bi]], [leB])
    for l in range(2):
        emit(dve, lambda e, l=l: e.tensor_tensor(out=small[:, 5 + l:6 + l], in0=le[:, 2 * l + 1:2 * l + 2],
                                                  in1=le[:, 2 * l:2 * l + 1], op=ALU.subtract),
             [leB], [], pwrites=[smallB])
        emit(dve, lambda e, l=l: e.tensor_scalar(out=small[:, 1 + l:2 + l], in0=small[:, 5 + l:6 + l],
                                                  scalar1=-LAM_INIT[l], scalar2=None, op0=ALU.add),
             [smallB], [], pwrites=[smallB])
        emit(dve, lambda e, l=l: e.tensor_scalar(out=small[:, 3 + l:4 + l], in0=pv[:, l, C_GSUB:C_GSUB + 1],
                                                  scalar1=1.0 - LAM_INIT[l], scalar2=None, op0=ALU.mult),
             [pvB], [], pwrites=[smallB])

    def gsc_emit(l, s_):
        for j in range(2):
            emit(dve, lambda e, j=j: e.scalar_tensor_tensor(
                out=gsc[:, l, s_, :, j], in0=modt[:, l, 48 * s_ + 16:48 * s_ + 32, j], scalar=1.0,
                in1=pv[:, l, (C_G1 if s_ == 0 else C_G2):(C_G1 if s_ == 0 else C_G2) + KC],
                op0=ALU.add, op1=ALU.mult), [modB, pvB], [], pwrites=[gscB])

    def mods_gen(l, mlo, mhi, gsc_after=None):
        for blk in range(mlo // 2, mhi // 2):
            wb, wv = wload(wsrc(w_ada_d, l, 0, D, blk * 256, 256), KC, 256)
            mb = mbank()
            mps = banks[mb][:, 0:16].rearrange("p (m j) -> p m j", j=8)
            for sub in range(2):
                for k in range(KC):
                    first = (sub == 0 and k == 0)
                    emit(pe, lambda e, k=k, sub=sub: e.matmul(
                        mps[:, sub, :], lhsT=wv[:, k, sub * 128:(sub + 1) * 128], rhs=sT[:, k, :],
                        start=(k == 0), stop=(k == KC - 1)),
                        [wb, sTB], [bankB[mb]] if first else [], sig=(k == KC - 1 and sub == 1))
            for j in range(2):
                emit(dve, lambda e, j=j, blk=blk: e.tensor_tensor(
                    out=modt[:, l, 2 * blk:2 * blk + 2, j], in0=mps[:, :, j],
                    in1=pv[:, l, C_BADA + 2 * blk:C_BADA + 2 * blk + 2], op=ALU.add),
                    [bankB[mb], pvB], [], pwrites=[modB])
            yield 1
        if gsc_after is not None:
            gsc_emit(l, gsc_after)

    for _ in mods_gen(0, 0, 32, 0):
        pass

    def _bg_chain():
        yield from mods_gen(0, 32, 48)
        yield from mods_gen(0, 48, 80, 1)
        yield from mods_gen(0, 80, 96)
        yield from mods_gen(1, 0, 32, 0)
        yield from mods_gen(1, 32, 80, 1)
        yield from mods_gen(1, 80, 96)
    bg = [_bg_chain(), 0, 0]

    def bg_poll():
        if bg[0] is None:
            return False
        try:
            next(bg[0])
            bg[1] += 1
            return True
        except StopIteration:
            bg[0] = None
            return False

    def bg_tick(light=False):
        bg[2] += 1
        if light:
            bg_poll()
            bg_poll()
        elif bg[1] < 32:
            bg_poll()
        elif bg[2] % 2 == 0:
            bg_poll()

    def bg_drain(nblocks=None):
        while bg[0] is not None and (nblocks is None or bg[1] < nblocks):
            if not bg_poll():
                break
    A.release(m0)
    early = stop_after in ("mods", "loadx")

    def load_x(src_d, T):
        m = A.mark()
        stg = [A.alloc("xstg%d" % i, [D], F32) for i in range(2)]
        stgB = [A.newbuf("xstg%d" % i) for i in range(2)]
        for i in range(T // 128):
            s_ = i % 2
            emit(sp, lambda e, i=i, s_=s_: e.dma_start(out=stg[s_], in_=src_d[i * 128:(i + 1) * 128, :]),
                 [], [stgB[s_]], dma=stgB[s_])
            for c4 in range(4):
                b = mbank()
                for q in range(4):
                    c = c4 * 4 + q
                    emit(pe, lambda e, b=b, q=q, c=c, s_=s_: e.transpose(
                        out=banks[b][:, q * 128:(q + 1) * 128], in_=stg[s_][:, c * 128:(c + 1) * 128],
                        identity=ident[:]), [stgB[s_], constB], [bankB[b]] if q == 0 else [], sig=(q == 3))
                n = i // 4
                o = xT[:, c4 * 4:(c4 + 1) * 4, i * 128:(i + 1) * 128]
                copy_on(ev_eng(), o, banks[b][:].rearrange("p (q t) -> p q t", q=4), [bankB[b]], [],
                        pwrites=[xTb[c4 * 4 + q][n] for q in range(4)])
        A.release(m)

    def store_y(dst_d, T):
        m = A.mark()
        stg = [A.alloc("ystg%d" % i, [D], F32) for i in range(2)]
        stgB = [A.newbuf("ystg%d" % i) for i in range(2)]
        last = []
        for i in range(T // 128):
            s_ = i % 2
            n = i // 4
            for c4 in range(4):
                b = mbank()
                for q in range(4):
                    c = c4 * 4 + q
                    emit(pe, lambda e, b=b, q=q, c=c, i=i: e.transpose(
                        out=banks[b][:, q * 128:(q + 1) * 128], in_=xT[:, c, i * 128:(i + 1) * 128],
                        identity=ident[:]), [xTb[c][n], constB], [bankB[b]] if q == 0 else [], sig=(q == 3))
                if c4 == 0:
                    copy_on(ev_eng(), stg[s_][:, 0:512], banks[b][:], [bankB[b]], [stgB[s_]])
                else:
                    copy_on(ev_eng(), stg[s_][:, c4 * 512:(c4 + 1) * 512], banks[b][:], [bankB[b]], [],
                            pwrites=[stgB[s_]])
            tok = emit(sp, lambda e, i=i, s_=s_: e.dma_start(out=dst_d[i * 128:(i + 1) * 128, :], in_=stg[s_]),
                       [stgB[s_]], [], dma=stgB[s_])
            last.append(stgB[s_])
        A.release(m)
        return last

    def rstd_from(ps_ap, scale, out_ap, rB, wB, tmp_ap, tmpB):
        emit(act, lambda e: e.activation(out=tmp_ap, in_=ps_ap, func=AF.Sqrt, bias=eps_ap, scale=scale),
             rB + [smallB], [tmpB])
        emit(dve, lambda e: e.reciprocal(out=out_ap, in_=tmp_ap), [tmpB], wB)

    def norm(l, s, T, j, hT, hTb):
        m = A.mark()
        sq = [A.alloc("sq%d" % i, [512], BF16) for i in range(2)]
        sqB = [A.newbuf("sq%d" % i) for i in range(2)]
        tm = [A.alloc("ntm%d" % i, [512], F32) for i in range(2)]
        tmB = [A.newbuf("ntm%d" % i) for i in range(2)]
        rs = A.alloc("nrs", [512], F32)
        rsB = A.newbuf("nrs")
        rt = A.alloc("nrt", [512], F32)
        rtB = A.newbuf("nrt")
        for n in range(T // 512):
            b = mbank()
            for c in range(KC):
                i = c % 2
                emit(act, lambda e, c=c, n=n, i=i: e.activation(out=sq[i], in_=xT[:, c, n * 512:(n + 1) * 512],
                                                               func=AF.Square), [xTb[c][n]], [sqB[i]])
                emit(pe, lambda e, c=c, i=i, b=b: e.matmul(banks[b][:], lhsT=ones_b[:], rhs=sq[i],
                                                          start=(c == 0), stop=(c == KC - 1)),
                     [sqB[i], constB], [bankB[b]] if c == 0 else [], pwrites=[] if c == 0 else [bankB[b]], sig=True)
            import os
            NP_ = int(os.environ.get("NORM_PARTS", "3"))
            if NP_ < 2:
                continue
            rstd_from(banks[b][:], 1.0 / D, rs, [bankB[b]], [rsB], rt, rtB)
            if NP_ < 3:
                continue
            for c in range(KC):
                i = c % 2
                emit(dve, lambda e, c=c, n=n, i=i: e.tensor_tensor(out=tm[i], in0=xT[:, c, n * 512:(n + 1) * 512],
                                                                  in1=rs, op=ALU.mult), [xTb[c][n], rsB], [tmB[i]])
                NV_ = int(os.environ.get("NORM_VAR", "0"))
                if NV_ == 1:
                    emit(act, lambda e, c=c, n=n, i=i: e.activation(
                        out=hT[:, c, n * 512:(n + 1) * 512], in_=tm[i], func=AF.Identity),
                        [tmB[i], modB, gscB], [hTb[c][n]])
                elif NV_ == 2:
                    pass
                elif NV_ == 3:
                    emit(act, lambda e, c=c, n=n, i=i: e.activation(
                        out=hT[:, c, n * 512:(n + 1) * 512], in_=tm[i], func=AF.Identity,
                        bias=small[:, 0:1], scale=small[:, 0:1]),
                        [tmB[i], modB, gscB], [hTb[c][n]])
                else:
                    emit(act, lambda e, c=c, n=n, i=i: e.activation(
                        out=hT[:, c, n * 512:(n + 1) * 512], in_=tm[i], func=AF.Identity,
                        bias=modt[:, l, 48 * s + c, j:j + 1], scale=gsc[:, l, s, c, j:j + 1]),
                        [tmB[i], modB, gscB], [hTb[c][n]])
        A.release(m)

    def job(K, NT, ncols, lhs_fn, rhs_fn):
        bl = [dbank() for _ in range(NT)]
        for k in range(K):
            wb, lhsT = lhs_fn(k)
            for n in range(NT):
                rb, rhs = rhs_fn(k, n)
                emit(pe, lambda e, b=bl[n], lhsT=lhsT, rhs=rhs, k=k: e.matmul(
                    banks[b][:, 0:ncols], lhsT=lhsT, rhs=rhs, start=(k == 0), stop=(k == K - 1)),
                    [wb, rb], [bankB[bl[n]]] if k == 0 else [], sig=(k == K - 1 and n == NT - 1))
        return bl

    def residual_partial(K, NT, l, gate_chunk0, j, wd, r0, rhs_ap, rhsB_fn, overwrite=False):
        cw = 512 if K <= 8 else 256
        for ob in range(D // cw):
            bg_tick(light=(K == 4))
            wb, wv = wload(wsrc(wd, l, r0, K * 128, ob * cw, cw), K, cw)
            for sub in range(cw // 128):
                mo = ob * (cw // 128) + sub
                bl = job(K, NT, 512,
                         lambda k, wv=wv, sub=sub: (wb, wv[:, k, sub * 128:(sub + 1) * 128]),
                         lambda k, n: (rhsB_fn(k, n), rhs_ap[:, k, n * 512:(n + 1) * 512]))
                for n in range(NT):
                    if overwrite:
                        emit(dve, lambda e, b=bl[n], mo=mo, n=n: e.tensor_scalar(
                            out=xT[:, mo, n * 512:(n + 1) * 512], in0=banks[b][:],
                            scalar1=modt[:, l, gate_chunk0 + mo, j:j + 1], scalar2=None, op0=ALU.mult),
                            [bankB[bl[n]], modB], [xTb[mo][n]])
                        continue
                    emit(dve, lambda e, b=bl[n], mo=mo, n=n: e.scalar_tensor_tensor(
                        out=xT[:, mo, n * 512:(n + 1) * 512], in0=banks[b][:],
                        scalar=modt[:, l, gate_chunk0 + mo, j:j + 1], in1=xT[:, mo, n * 512:(n + 1) * 512],
                        op0=ALU.mult, op1=ALU.add), [bankB[bl[n]], modB, xTb[mo][n]], [xTb[mo][n]])

    def qknorm(b, n, gcol, l, use_rope, out_bf, outB, out_f32=None, out_f32B=None, tmps=None):
        sq, sqB, rs, rsB, rt, rtB, qn, qnB, t1, t1B = tmps
        emit(act, lambda e: e.activation(out=sq, in_=banks[b][:], func=AF.Square), [bankB[b]], [sqB])
        b2 = mbank()
        emit(pe, lambda e: e.matmul(banks[b2][:], lhsT=bones[:], rhs=sq, start=True, stop=True),
             [sqB, constB], [bankB[b2]])
        rstd_from(banks[b2][:], 1.0 / 64, rs, [bankB[b2]], [rsB], rt, rtB)
        gq = pv[:, l, gcol:gcol + 1]
        if not use_rope:
            emit(dve, lambda e: e.scalar_tensor_tensor(out=qn, in0=banks[b][:], scalar=gq, in1=rs,
                                                       op0=ALU.mult, op1=ALU.mult), [bankB[b], rsB, pvB], [qnB])
            emit(act, lambda e: e.activation(out=out_bf, in_=qn, func=AF.Identity), [qnB], [], pwrites=[outB])
            return
        emit(dve, lambda e: e.scalar_tensor_tensor(out=qn, in0=banks[b][:], scalar=gq, in1=rs,
                                                   op0=ALU.mult, op1=ALU.mult), [bankB[b], rsB, pvB], [qnB])
        b3 = mbank()
        emit(pe, lambda e: e.matmul(banks[b3][:], lhsT=rotm[:], rhs=qn, start=True, stop=True),
             [qnB, constB], [bankB[b3]])
        emit(dve, lambda e: e.tensor_tensor(out=t1, in0=banks[b3][:], in1=rope[:, 1, n * 512:(n + 1) * 512],
                                            op=ALU.mult), [bankB[b3], constB], [t1B])
        emit(dve, lambda e: e.tensor_tensor(out=qn, in0=qn, in1=rope[:, 0, n * 512:(n + 1) * 512],
                                            op=ALU.mult), [qnB, constB], [qnB])
        emit(dve, lambda e: e.tensor_tensor(out=out_bf, in0=qn, in1=t1, op=ALU.add), [qnB, t1B], [],
             pwrites=[outB])

    def qk_tmps():
        sq = A.alloc("qsq", [512], F32); sqB = A.newbuf("qsq")
        rs = A.alloc("qrs", [512], F32); rsB = A.newbuf("qrs")
        qn = A.alloc("qqn", [512], F32); qnB = A.newbuf("qqn")
        t1 = A.alloc("qt1", [512], F32); t1B = A.newbuf("qt1")
        return (sq, sqB, rs, rsB, t1, t1B, qn, qnB, t1, t1B)

    def dbg_to_x(ap3, bufs, nch, ncol):
        emit(dve, lambda e: e.tensor_copy(out=xT[:, 0:nch, 0:ncol], in_=ap3), bufs, [],
             pwrites=[xTb[c][n] for c in range(KC) for n in range(2)])

    def dbg_dump(ap2d, bufs, ncols):
        emit(sp, lambda e: e.dma_start(out=dbg_d[:, 0:ncols], in_=ap2d), bufs, [], dma=P.buf("dbg"))

    def layer(l, G):
        T, NT, j, sample = G["T"], G["T"] // 512, G["j"], G["sample"]
        if l == 1 or not G.get("first", False):
            bg_drain()
        segs = G["segs"]
        PP, PC = 8, 16
        base = A.mark()
        qT = A.alloc("qT", [4, T], BF16); qTB = A.newbuf("qT")
        nseg = len(segs)
        Lseg = segs[0][1]
        UW = Lseg + 2 * PP
        GW = Lseg + 2 * PC
        if not sample:
            kTp = A.alloc("kTp", [4, T], BF16); kTpB = A.newbuf("kTp")
            vTp = A.alloc("vTp", [T // 128, 512], BF16); vTpB = A.newbuf("vTp")
            fTp = A.alloc("fTp", [4, T], BF16); fTpB = A.newbuf("fTp")
        m_q = A.mark()
        gT = A.alloc("gT", [4, nseg * GW], F32); gTB = [A.newbuf("gT%d" % c) for c in range(4)]
        m_g = A.mark()
        uP = A.alloc("uP", [4, nseg * UW], F32); uPB = [A.newbuf("uP%d" % g) for g in range(4)]
        m_ug = A.mark()
        hT = A.alloc("hT", [KC, T], BF16)
        hTb = [[A.newbuf("hT%d_%d" % (c, n)) for n in range(NT)] for c in range(KC)]
        norm(l, 0, T, j, hT, hTb)

        def hrhs(k, n):
            return (hTb[k][n], hT[:, k, n * 512:(n + 1) * 512])
        stg_tag = "%d%s" % (l, "s" if sample else "p")
        if stop_after == "norm" + stg_tag:
            return False

        def proj_jobs(col0, nchunks, evac):
            for blk in range(nchunks // 2):
                bg_tick()
                wb, wv = wload(wsrc(w_in_d, l, 0, D, col0 + blk * 256, 256), KC, 256)
                for sub in range(2):
                    ch = blk * 2 + sub
                    bl = job(KC, NT, 512, lambda k, wv=wv, sub=sub, wb=wb: (wb, wv[:, k, sub * 128:(sub + 1) * 128]),
                             hrhs)
                    for n in range(NT):
                        evac(ch, n, bl[n])

        mfr = A.mark()
        tmps = qk_tmps()
        if sample:
            kst = [A.alloc("kst%d" % i, [512], BF16) for i in range(2)]
            kstB = [A.newbuf("kst%d" % i) for i in range(2)]
            kcnt = [0]

            def k_evac(ch, n, b):
                i = kcnt[0] % 2
                kcnt[0] += 1
                qknorm(b, n, C_GK, l, True, kst[i], kstB[i], tmps=tmps)
                emit(sp, lambda e, i=i, ch=ch, n=n: e.dma_start(
                    out=xsA[l].ap()[ch * 128:(ch + 1) * 128, n * 512:(n + 1) * 512], in_=kst[i]),
                    [kstB[i]], [], pwrites=[xsendQ[l][0]], dma=kstB[i])
        else:
            kst2 = [A.alloc("nkst%d" % i, [512], F32) for i in range(2)]
            kst2B = [A.newbuf("nkst%d" % i) for i in range(2)]
            kcnt = [0]

            def k_evac(ch, n, b):
                sq, sqB, rs, rsB, rt, rtB, qn, qnB, t1, t1B = tmps
                emit(act, lambda e: e.activation(out=sq, in_=banks[b][:], func=AF.Square), [bankB[b]], [sqB])
                b2 = mbank()
                emit(pe, lambda e: e.matmul(banks[b2][:], lhsT=bones[:], rhs=sq, start=True, stop=True),
                     [sqB, constB], [bankB[b2]])
                rstd_from(banks[b2][:], 1.0 / 64, rs, [bankB[b2]], [rsB], rt, rtB)
                emit(dve, lambda e: e.scalar_tensor_tensor(out=qn, in0=banks[b][:], scalar=pv[:, l, C_GK:C_GK + 1],
                                                           in1=rs, op0=ALU.mult, op1=ALU.mult),
                     [bankB[b], rsB, pvB], [qnB])
                emit(act, lambda e: e.activation(out=kTp[:, ch, :], in_=qn, func=AF.Identity), [qnB], [],
                     pwrites=[kTpB])
                b3 = mbank()
                for q in range(4):
                    emit(pe, lambda e, q=q: e.transpose(out=banks[b3][:, q * 128:(q + 1) * 128],
                                                        in_=qn[:, q * 128:(q + 1) * 128], identity=ident[:]),
                         [qnB, constB], [bankB[b3]] if q == 0 else [], sig=(q == 3))
                i = kcnt[0] % 2
                kcnt[0] += 1
                copy_on(ev_eng(), kst2[i], banks[b3][:], [bankB[b3]], [kst2B[i]])
                for sq_ in range(2):
                    emit(sp, lambda e, i=i, ch=ch, sq_=sq_: e.dma_start(
                        out=nk_d[sq_, l, :, ch * 128:(ch + 1) * 128].rearrange("(h p) f -> p h f", p=128),
                        in_=kst2[i].rearrange("p (q f) -> p q f", q=4)[:, 2 * sq_:2 * sq_ + 2, :]),
                        [kst2B[i]], [], pwrites=[outB_all], dma=kst2B[i])
        proj_jobs(1024, 4, k_evac)
        if stop_after == "kproj" + stg_tag:
            return False

        if sample:
            vst, vstB = kst, kstB
        else:
            vst = [A.alloc("vst%d" % i, [512], BF16) for i in range(2)]
            vstB = [A.newbuf("vst%d" % i) for i in range(2)]
        if not sample:
            vsf = [A.alloc("vsf%d" % i, [512], F32) for i in range(2)]
            vsfB = [A.newbuf("vsf%d" % i) for i in range(2)]
        wv_blocks = [wload(wsrc(w_in_d, l, 0, D, 1536 + hb * 256, 256), KC, 256) for hb in range(2)]
        for i in range(T // 128):
            b = dbank()
            n = i // 4
            for hb in range(2):
                wb, wv = wv_blocks[hb]
                for k in range(KC):
                    emit(pe, lambda e, b=b, hb=hb, k=k, wv=wv, i=i: e.matmul(
                        banks[b][:, hb * 256:(hb + 1) * 256], lhsT=hT[:, k, i * 128:(i + 1) * 128], rhs=wv[:, k, :],
                        start=(k == 0), stop=(k == KC - 1)),
                        [wb, hTb[k][n]], [bankB[b]] if (k == 0 and hb == 0) else [],
                        sig=(k == KC - 1 and hb == 1))
            s_ = i % 2
            VV_ = int(os.environ.get("VVAR", "0"))
            if VV_ == 1:
                continue
            if sample:
                copy_on(ev_eng(), vst[s_], banks[b][:], [bankB[b]], [vstB[s_]])
                emit(sp, lambda e, i=i, s_=s_: e.dma_start(
                    out=xsB[l].ap()[512:1024, :].rearrange("r (a c) -> (r a) c", a=2)[i * 128:(i + 1) * 128, :],
                    in_=vst[s_]), [vstB[s_]], [], pwrites=[xsendQ[l][1]], dma=vstB[s_])
            else:
                emit(dve, lambda e, b=b, s_=s_: e.tensor_copy(out=vsf[s_], in_=banks[b][:]), [bankB[b]], [vsfB[s_]])
                copy_on(act, vTp[:, i, :], vsf[s_], [vsfB[s_]], [], pwrites=[vTpB])
                if VV_ == 3:
                    continue
                emit(sp, lambda e, i=i, s_=s_: e.dma_start(
                    out=nv_d[i // 2, l, (i % 2) * 128:(i % 2 + 1) * 128, :], in_=vsf[s_]),
                    [vsfB[s_]], [], pwrites=[outB_all], dma=vsfB[s_])

        if stop_after == "vproj" + stg_tag:
            return False
        if sample:
            fst, fstB = kst, kstB
            fcnt = [0]

            def f_evac(ch, n, b):
                i = fcnt[0] % 2
                fcnt[0] += 1
                copy_on(ev_eng(), fst[i], banks[b][:], [bankB[b]], [fstB[i]])
                emit(sp, lambda e, i=i, ch=ch, n=n: e.dma_start(
                    out=xsB[l].ap()[ch * 128:(ch + 1) * 128, n * 512:(n + 1) * 512], in_=fst[i]),
                    [fstB[i]], [], pwrites=[xsendQ[l][1]], dma=fstB[i])
        else:
            def f_evac(ch, n, b):
                copy_on(ev_eng(), fTp[:, ch, n * 512:(n + 1) * 512], banks[b][:], [bankB[b]], [], pwrites=[fTpB])
        proj_jobs(3072, 4, f_evac)
        if sample:
            emit(pool, lambda e: e.collective_compute(
                "AllGather", ALU.bypass, replica_groups=[[0, 1], [2, 3], [4, 5], [6, 7]],
                ins=[xsB_t[l].ap().opt()], outs=[xrB_t[l].ap().opt()]),
                [xsendQ[l][1]], [xrecvQ[l][1]], inc=(cc_keys[l][1], 1))

        def seg_cols(n, pad, W):
            out = []
            for si, (s0, L) in enumerate(segs):
                lo = max(s0, n * 512)
                hi = min(s0 + L, (n + 1) * 512)
                if lo < hi:
                    out.append((lo - n * 512, hi - lo, si * W + pad + lo - s0))
            return out

        def p_evac(ch, n, b):
            for (o, nc_, dc) in seg_cols(n, PP, UW):
                copy_on(ev_eng(), uP[:, ch, dc:dc + nc_], banks[b][:, o:o + nc_], [bankB[b]], [], pwrites=[uPB[ch]])
        proj_jobs(0, 4, p_evac)

        sg = A.alloc("sg", [T], F32)
        sgB = A.newbuf("sg")
        for half in range(2):
            bg_tick()
            wbb, wvb = wload(wsrc(w_in_d, l, 0, D, 2560 + half * 256, 256), KC, 256)
            wba, wva = wload(wsrc(w_in_d, l, 0, D, 2048 + half * 256, 256), KC, 256)
            for sub in range(2):
                cch = half * 2 + sub
                bl = job(KC, NT, 512, lambda k, sub=sub: (wbb, wvb[:, k, sub * 128:(sub + 1) * 128]), hrhs)
                for n in range(NT):
                    emit(act, lambda e, n=n, b=bl[n]: e.activation(out=sg[:, n * 512:(n + 1) * 512], in_=banks[b][:],
                                                                    func=AF.Sigmoid), [bankB[bl[n]]], [], pwrites=[sgB])
                bl = job(KC, NT, 512, lambda k, sub=sub: (wba, wva[:, k, sub * 128:(sub + 1) * 128]), hrhs)
                for n in range(NT):
                    for (o, nc_, dc) in seg_cols(n, PC, GW):
                        emit(dve, lambda e, o=o, nc_=nc_, dc=dc, n=n, b=bl[n], cch=cch: e.tensor_tensor(
                            out=gT[:, cch, dc:dc + nc_], in0=banks[b][:, o:o + nc_],
                            in1=sg[:, n * 512 + o:n * 512 + o + nc_], op=ALU.mult),
                            [bankB[bl[n]], sgB], [], pwrites=[gTB[cch]])

        if sample:
            hal = xsA[l].ap()[512:544, :].rearrange("r (f e) -> (r f) e", e=64).rearrange("(g p) e -> p g e", p=128)
            hsd = A.alloc("hsd", [4, 64], BF16); hsdB = A.newbuf("hsd")
            emit(dve, lambda e: e.memset(hsd, 0.0), [], [hsdB])
            emit(dve, lambda e: e.tensor_copy(out=hsd[:, :, 0:8], in_=uP[:, :, PP:PP + 8]), uPB, [], pwrites=[hsdB])
            emit(dve, lambda e: e.tensor_copy(out=hsd[:, :, 8:16], in_=uP[:, :, PP + Lseg - 8:PP + Lseg]), uPB, [], pwrites=[hsdB])
            emit(dve, lambda e: e.tensor_copy(out=hsd[:, :, 16:32], in_=gT[:, :, PC:PC + 16]), gTB, [], pwrites=[hsdB])
            emit(dve, lambda e: e.tensor_copy(out=hsd[:, :, 32:48], in_=gT[:, :, PC + Lseg - 16:PC + Lseg]), gTB, [], pwrites=[hsdB])
            emit(sp, lambda e: e.dma_start(out=hal, in_=hsd), [hsdB], [], pwrites=[xsendQ[l][0]], dma=hsdB)
            emit(pool, lambda e: e.collective_compute(
                "AllGather", ALU.bypass, replica_groups=[[0, 1], [2, 3], [4, 5], [6, 7]],
                ins=[xsA_t[l].ap().opt()], outs=[xrA_t[l].ap().opt()]),
                [xsendQ[l][0]], [xrecvQ[l][0]], inc=(cc_keys[l][0], 1))
        else:
            for g in range(4):
                for si in range(nseg):
                    emit(dve, lambda e, g=g, si=si: e.memset(uP[:, g, si * UW:si * UW + PP], 0.0), [], [], pwrites=[uPB[g]])
                    emit(dve, lambda e, g=g, si=si: e.memset(uP[:, g, si * UW + PP + Lseg:(si + 1) * UW], 0.0), [], [], pwrites=[uPB[g]])
                    emit(dve, lambda e, g=g, si=si: e.memset(gT[:, g, si * GW:si * GW + PC], 0.0), [], [], pwrites=[gTB[g]])
                    emit(dve, lambda e, g=g, si=si: e.memset(gT[:, g, si * GW + PC + Lseg:(si + 1) * GW], 0.0), [], [], pwrites=[gTB[g]])

        def q_evac(ch, n, b):
            if sample:
                qknorm(b, n, C_GQ, l, True, qT[:, ch, n * 512:(n + 1) * 512], qTB, tmps=tmps)
            else:
                qknorm(b, n, C_GQ, l, False, qT[:, ch, n * 512:(n + 1) * 512], qTB, tmps=tmps)
        proj_jobs(512, 4, q_evac)
        A.release(m_ug)
        if stop_after == "proj" + stg_tag:
            return False

        g1c = 32

        def ymix_alloc():
            y = A.alloc("ymix", [4, T], BF16)
            yB = [[A.newbuf("ymix%d_%d" % (c, n)) for n in range(NT)] for c in range(4)]
            return y, yB

        bg_drain(9)
        mp = A.mark()
        if sample:
            hst = A.alloc("hst", [4, 64], BF16); hstB = A.newbuf("hst")
            hst2 = A.alloc("hst2", [4, 64], BF16); hst2B = A.newbuf("hst2")
            rv = xrA[l].ap()
            h0 = rv[512:544, :].rearrange("r (f e) -> (r f) e", e=64).rearrange("(g p) e -> p g e", p=128)
            h1 = rv[XRA + 512:XRA + 544, :].rearrange("r (f e) -> (r f) e", e=64).rearrange("(g p) e -> p g e", p=128)
            emit(sp, lambda e: e.dma_start(out=hst, in_=h0), [xrecvQ[l][0]], [hstB], dma=hstB)
            emit(sp, lambda e: e.dma_start(out=hst2, in_=h1), [xrecvQ[l][0]], [hst2B], dma=hst2B)
            emit(dve, lambda e: e.tensor_scalar(out=uP[:, :, 0:PP], in0=hst[:, :, 8:16], scalar1=hmask[:, 0:1],
                                                scalar2=None, op0=ALU.mult), [hstB, constB], [], pwrites=uPB)
            emit(dve, lambda e: e.tensor_scalar(out=uP[:, :, PP + Lseg:PP + Lseg + PP], in0=hst2[:, :, 0:8],
                                                scalar1=hmask[:, 1:2], scalar2=None, op0=ALU.mult),
                 [hst2B, constB], [], pwrites=uPB)
            emit(dve, lambda e: e.tensor_scalar(out=gT[:, :, 0:PC], in0=hst[:, :, 32:48], scalar1=hmask[:, 0:1],
                                                scalar2=None, op0=ALU.mult), [hstB, constB], [], pwrites=gTB)
            emit(dve, lambda e: e.tensor_scalar(out=gT[:, :, PC + Lseg:PC + Lseg + PC], in0=hst2[:, :, 16:32],
                                                scalar1=hmask[:, 1:2], scalar2=None, op0=ALU.mult),
                 [hst2B, constB], [], pwrites=gTB)
        icn = A.alloc("icn", [4, Lseg], F32); icnB = A.newbuf("icn")
        icd = icnts_d if sample else icntp_d
        emit(sp, lambda e: e.dma_start(out=icn, in_=icd.partition_broadcast(128)), [], [icnB], dma=icnB)
        sa = A.alloc("sa", [UW], F32); saB = A.newbuf("sa")
        sbb = A.alloc("sbb", [UW], F32); sbB = A.newbuf("sbb")
        pT = A.alloc("pT", [4, T], BF16); pTB = [A.newbuf("pT%d" % g) for g in range(4)]
        ym, ymB = ymix_alloc()
        for g in range(4):
            for si, (s0, L) in enumerate(segs):
                u = uP[:, g, si * UW:(si + 1) * UW]
                W = UW
                emit(dve, lambda e, u=u: e.tensor_tensor(out=sa[:, 1:W], in0=u[:, 0:W - 1], in1=u[:, 1:W], op=ALU.add),
                     [uPB[g]], [saB])
                cur, curB, oth, othB = sa, saB, sbb, sbB
                lo, hi, step = 1, W, 1
                for lev in range(g):
                    nlo, nhi = lo + step, hi - step
                    emit(dve, lambda e, cur=cur, oth=oth, nlo=nlo, nhi=nhi, step=step: e.tensor_tensor(
                        out=oth[:, nlo:nhi], in0=cur[:, nlo - step:nhi - step], in1=cur[:, nlo + step:nhi + step],
                        op=ALU.add), [curB], [othB])
                    cur, curB, oth, othB = oth, othB, cur, curB
                    lo, hi, step = nlo, nhi, step * 2
                emit(dve, lambda e, cur=cur, g=g: e.tensor_tensor(out=cur[:, PP:PP + L], in0=cur[:, PP:PP + L],
                                                                  in1=icn[:, g, :], op=ALU.mult), [curB, icnB], [curB])
                emit(dve, lambda e, cur=cur, u=u, g=g, s0=s0, L=L: e.tensor_tensor(
                    out=pT[:, g, s0:s0 + L], in0=cur[:, PP:PP + L], in1=u[:, PP:PP + L], op=ALU.subtract),
                    [curB, uPB[g]], [], pwrites=[pTB[g]])
        wb, wv = wload(pool_w_d[l].rearrange("g c d -> c g d"), 4, 128)
        for g in range(4):
            bl = job(1, NT, 512, lambda k, g=g: (wb, wv[:, g, :]), lambda k, n, g=g: (pTB[g], pT[:, g, n * 512:(n + 1) * 512]))
            for n in range(NT):
                emit(act, lambda e, g=g, n=n, b=bl[n]: e.activation(
                    out=ym[:, g, n * 512:(n + 1) * 512], in_=banks[b][:], func=AF.Identity,
                    scale=pv[:, l, C_PSC + g:C_PSC + g + 1]), [bankB[bl[n]], pvB], [ymB[g][n]])
        residual_partial(4, NT, l, g1c, j, w_out_d, 0, ym, lambda k, n: ymB[k][n])
        A.release(mp)
        A.release(m_g)
        if stop_after == "pool%d%s" % (l, "s" if sample else "p"):
            return False

        mc = A.mark()
        acc = A.alloc("acc", [4, T], F32); accB = [[A.newbuf("acc%d_%d" % (c, n)) for n in range(NT)] for c in range(4)]
        tap_groups = []
        for c in range(4):
            for n in range(NT):
                def _grp(c=c, n=n):
                    pieces = seg_cols(n, PC, GW)
                    for (o, nc_, dc) in pieces:
                        for jt in range(31):
                            src = gT[:, c, dc - 15 + jt:dc - 15 + jt + nc_]
                            dst = acc[:, c, n * 512 + o:n * 512 + o + nc_]
                            dwc = pv[:, l, C_DW + c * 31 + jt:C_DW + c * 31 + jt + 1]
                            if jt == 0:
                                emit(dve, lambda e, src=src, dst=dst, dwc=dwc, c=c: e.tensor_scalar(
                                    out=dst, in0=src, scalar1=dwc, scalar2=pv[:, l, C_DWB + c:C_DWB + c + 1],
                                    op0=ALU.mult, op1=ALU.add), [gTB[c], pvB], [], pwrites=[accB[c][n]])
                            else:
                                emit(dve, lambda e, src=src, dst=dst, dwc=dwc: e.scalar_tensor_tensor(
                                    out=dst, in0=src, scalar=dwc, in1=dst, op0=ALU.mult, op1=ALU.add),
                                    [gTB[c], pvB, accB[c][n]], [], pwrites=[accB[c][n]])

                tap_groups.append(_grp)

        def taps_pop(k=1):
            for _ in range(k):
                if tap_groups:
                    tap_groups.pop(0)()
        mf = A.mark()
        taps_pop(2)
        ym, ymB = ymix_alloc()
        FT = A.alloc("FT", [4, T], BF16); FTB = [[A.newbuf("FT%d_%d" % (c, n)) for n in range(NT)] for c in range(4)]
        if sample:
            NTC = LS // 128
            fall = A.alloc("fall", [2, LS], BF16)
            AB = A.alloc("AB", [NTC, 2, 256], BF16)
            tring = [A.alloc("tr%d" % i, [2, 512], BF16) for i in range(4)]
            tringB = [A.newbuf("tr%d" % i) for i in range(4)]
            tcnt = [0]
            rv = xrB[l].ap()
            fallB = A.newbuf("fall")
            ABB = A.newbuf("AB")
            for hp in range(2):
                for r in range(2):
                    emit(sp, lambda e, r=r, hp=hp: e.dma_start(
                        out=fall[:, :, r * 1024:(r + 1) * 1024],
                        in_=rv[r * XRB + hp * 256:r * XRB + (hp + 1) * 256, :].rearrange("(h p) t -> p h t", p=128)),
                        [xrecvQ[l][1]], [fallB] if r == 0 else [], pwrites=[fallB] if r == 1 else [], dma=fallB)
                for i in range(NTC):
                    b = mbank()
                    for hh in range(2):
                        emit(pe, lambda e, b=b, hh=hh, i=i: e.matmul(
                            banks[b][:, hh * 256:(hh + 1) * 256], lhsT=fall[:, hh, i * 128:(i + 1) * 128], rhs=dft128[:],
                            start=True, stop=True), [fallB, constB], [bankB[b]] if hh == 0 else [], sig=(hh == 1))
                    copy_on(act, AB[:, i, :, :], banks[b][:].rearrange("p (h c) -> p h c", h=2), [bankB[b]],
                            [ABB] if i == 0 else [], pwrites=[ABB] if i > 0 else [])
                taps_pop(2)
                for kt in range(NT):
                    taps_pop(1)
                    bl = [dbank(), dbank()]
                    for i in range(NTC):
                        ti = tcnt[0] % 4
                        tcnt[0] += 1
                        emit(sp, lambda e, ti=ti, i=i, kt=kt: e.dma_start(
                            out=tring[ti], in_=dfts_d[i * 128:(i + 1) * 128, :, kt * 512:(kt + 1) * 512]),
                            [], [tringB[ti]], dma=tringB[ti])
                        for hh in range(2):
                            for cs in range(2):
                                emit(pe, lambda e, hh=hh, cs=cs, i=i, ti=ti, b=bl[hh]: e.matmul(
                                    banks[b][:], lhsT=AB[:, i, hh, cs * 128:(cs + 1) * 128], rhs=tring[ti][:, cs, :],
                                    start=(i == 0 and cs == 0), stop=(i == NTC - 1 and cs == 1)),
                                    [ABB, tringB[ti]], [bankB[bl[hh]]] if (i == 0 and cs == 0) else [],
                                    pwrites=[] if (i == 0 and cs == 0) else [bankB[bl[hh]]],
                                    sig=(cs == 1 and hh == 1))
                    for hh in range(2):
                        copy_on(act, FT[:, hp * 2 + hh, kt * 512:(kt + 1) * 512], banks[bl[hh]][:],
                                [bankB[bl[hh]]], [FTB[hp * 2 + hh][kt]])
        else:
            AB = A.alloc("ABp", [T // 128, 4, 256], BF16); ABB = A.newbuf("ABp")
            for i in range(T // 128):
                for hp in range(2):
                    b = mbank()
                    for hh in range(2):
                        h = hp * 2 + hh
                        emit(pe, lambda e, b=b, hh=hh, h=h, i=i: e.matmul(
                            banks[b][:, hh * 256:(hh + 1) * 256], lhsT=fTp[:, h, i * 128:(i + 1) * 128], rhs=dft128[:],
                            start=True, stop=True), [fTpB, constB], [bankB[b]] if hh == 0 else [], sig=(hh == 1))
                    copy_on(act, AB[:, i, hp * 2:hp * 2 + 2, :], banks[b][:].rearrange("p (h c) -> p h c", h=2),
                            [bankB[b]], [], pwrites=[ABB])
            for h in range(4):
                b = dbank()
                for si, (s0, L) in enumerate(segs):
                    ntc = L // 128
                    for i in range(ntc):
                        for cs in range(2):
                            emit(pe, lambda e, b=b, h=h, i=i, cs=cs, s0=s0, L=L: e.matmul(
                                banks[b][:, s0:s0 + L], lhsT=AB[:, s0 // 128 + i, h, cs * 128:(cs + 1) * 128],
                                rhs=dftp[:, i, cs, :], start=(i == 0 and cs == 0), stop=(i == ntc - 1 and cs == 1)),
                                [ABB, constB], [bankB[b]] if (si == 0 and i == 0 and cs == 0) else [],
                                sig=(si == nseg - 1 and i == ntc - 1 and cs == 1))
                copy_on(act, FT[:, h, 0:T], banks[b][:, 0:T], [bankB[b]], [FTB[h][0]])
        wb, wv = wload(wsrc(four_w_d, l, 0, 512, 0, 512), 4, 512)
        for mo in range(4):
            bl = job(4, NT, 512, lambda k, mo=mo: (wb, wv[:, k, mo * 128:(mo + 1) * 128]),
                     lambda k, n: (FTB[k][n], FT[:, k, n * 512:(n + 1) * 512]))
            for n in range(NT):
                copy_on(act, ym[:, mo, n * 512:(n + 1) * 512], banks[bl[n]][:], [bankB[bl[n]]], [ymB[mo][n]])
        residual_partial(4, NT, l, g1c, j, w_out_d, 1536, ym, lambda k, n: ymB[k][n])
        A.release(mf)

        if not sample:
            ma = A.mark()
            ym, ymB = ymix_alloc()
            pt = [A.alloc("ptile%d" % i, [512], BF16) for i in range(3)]
            ptB = [A.newbuf("ptile%d" % i) for i in range(3)]
            r0 = A.alloc("ar0", [512], F32); r0B = A.newbuf("ar0")
            r1 = A.alloc("ar1", [512], F32); r1B = A.newbuf("ar1")
            o0 = A.alloc("ao0", [512], F32); o0B = A.newbuf("ao0")
            asq = A.alloc("asq", [512], F32); asqB = A.newbuf("asq")
            ars = A.alloc("ars", [512], F32); arsB = A.newbuf("ars")
            art = A.alloc("art", [512], F32); artB = A.newbuf("art")
            if sample:
                NKC = 20
                kall = A.alloc("kall", [4, NKC * 128], BF16); kallB = A.newbuf("kall")
                vall = A.alloc("vall", [NKC, 512], BF16); vallB = A.newbuf("vall")
                rv = xrA[l].ap()
                rvb = xrB[l].ap()
                for r in range(2):
                    emit(sp, lambda e, r=r: e.dma_start(
                        out=kall[:, :, r * 1024:(r + 1) * 1024],
                        in_=rv[r * XRA:r * XRA + 512, :].rearrange("(h p) t -> p h t", p=128)),
                        [xrecvQ[l][0]], [], pwrites=[kallB], dma=kallB)
                    emit(sp, lambda e, r=r: e.dma_start(
                        out=vall[:, r * 8:(r + 1) * 8, :],
                        in_=rvb[r * XRB + 512:r * XRB + 1024, :].rearrange("r (a c) -> (r a) c", a=2).rearrange(
                            "(i p) c -> p i c", p=128)), [xrecvQ[l][1]], [], pwrites=[vallB], dma=vallB)
                emit(pool, lambda e: e.dma_start(out=vall[:, 16:20, :],
                                                 in_=cv_d[l].rearrange("(i p) c -> p i c", p=128)),
                     [], [], pwrites=[vallB], dma=vallB)
                cks = [A.alloc("cks%d" % i, [512], F32) for i in range(2)]
                cksB = [A.newbuf("cks%d" % i) for i in range(2)]
                for i in range(4):
                    s_ = i % 2
                    emit(sp, lambda e, i=i, s_=s_: e.dma_start(out=cks[s_], in_=ck_d[l, i * 128:(i + 1) * 128, :]),
                         [], [cksB[s_]], dma=cksB[s_])
                    b = mbank()
                    for h in range(4):
                        emit(pe, lambda e, h=h, b=b, s_=s_: e.transpose(out=banks[b][:, h * 128:(h + 1) * 128],
                                                                       in_=cks[s_][:, h * 128:(h + 1) * 128], identity=ident[:]),
                             [cksB[s_], constB], [bankB[b]] if h == 0 else [], sig=(h == 3))
                    copy_on(ev_eng(), kall[:, :, 2048 + i * 128:2048 + (i + 1) * 128],
                            banks[b][:].rearrange("p (h t) -> p h t", h=4), [bankB[b]], [], pwrites=[kallB])
                att_units = [(0, 512, n * 512, [(kall, kallB, vall, vallB, jc) for jc in range(NKC)]) for n in range(NT)]
            else:
                att_units = []
                for si, (s0, L) in enumerate(segs):
                    att_units.append((1, L, s0, [(kTp, kTpB, vTp, vTpB, s0 // 128 + jc) for jc in range(L // 128)]))
            pcnt = [0]
            for h in range(4):
                for (_, NQ, q0, klist) in att_units:
                    taps_pop(1)
                    bO = [0, 1]
                    bL = [2, 3]
                    nk_ = len(klist)
                    its = [(ki, mmap) for ki in range(nk_) for mmap in range(2)]
                    base_i = pcnt[0]
                    pcnt[0] += len(its)

                    def e_qk(ix):
                        ki, mmap = its[ix]
                        kt_, ktB_, vt_, vtB_, jc = klist[ki]
                        bs = 4 + ((base_i + ix) % 2)
                        emit(pe, lambda e: e.matmul(
                            banks[bs][:, 0:NQ], lhsT=kt_[mmap * 64:(mmap + 1) * 64, h, jc * 128:(jc + 1) * 128],
                            rhs=qT[mmap * 64:(mmap + 1) * 64, h, q0:q0 + NQ], start=True, stop=True),
                            [ktB_, qTB], [bankB[bs]])

                    def e_exp(ix):
                        bs = 4 + ((base_i + ix) % 2)
                        pi = (base_i + ix) % 3
                        emit(act, lambda e: e.activation(out=pt[pi][:, 0:NQ], in_=banks[bs][:, 0:NQ],
                                                         func=AF.Exp, scale=0.125), [bankB[bs]], [ptB[pi]])

                    def e_pv(ix):
                        ki, mmap = its[ix]
                        kt_, ktB_, vt_, vtB_, jc = klist[ki]
                        pi = (base_i + ix) % 3
                        emit(pe, lambda e: e.matmul(
                            banks[bO[mmap]][:, 0:NQ], lhsT=vt_[:, jc, h * 128:(h + 1) * 128], rhs=pt[pi][:, 0:NQ],
                            start=(ki == 0), stop=(ki == nk_ - 1)),
                            [vtB_, ptB[pi]], [bankB[bO[mmap]]] if ki == 0 else [],
                            pwrites=[] if ki == 0 else [bankB[bO[mmap]]], sig=False)
                        emit(pe, lambda e: e.matmul(
                            banks[bL[mmap]][:, 0:NQ], lhsT=ones_b[:], rhs=pt[pi][:, 0:NQ],
                            start=(ki == 0), stop=(ki == nk_ - 1)),
                            [constB, ptB[pi]], [bankB[bL[mmap]]] if ki == 0 else [],
                            pwrites=[] if ki == 0 else [bankB[bL[mmap]]], sig=True)

                    e_qk(0)
                    for ix in range(len(its)):
                        if ix + 1 < len(its):
                            e_qk(ix + 1)
                        e_exp(ix)
                        e_pv(ix)
                    emit(dve, lambda e, NQ=NQ: e.reciprocal(out=r0[:, 0:NQ], in_=banks[2][:, 0:NQ]), [bankB[2]], [r0B])
                    emit(dve, lambda e, NQ=NQ: e.reciprocal(out=r1[:, 0:NQ], in_=banks[3][:, 0:NQ]), [bankB[3]], [r1B])
                    emit(dve, lambda e, NQ=NQ: e.tensor_tensor(out=o0[:, 0:NQ], in0=banks[0][:, 0:NQ], in1=r0[:, 0:NQ],
                                                               op=ALU.mult), [bankB[0], r0B], [o0B])
                    emit(dve, lambda e, NQ=NQ: e.tensor_tensor(out=r1[:, 0:NQ], in0=banks[1][:, 0:NQ], in1=r1[:, 0:NQ],
                                                               op=ALU.mult), [bankB[1], r1B], [r1B])
                    emit(dve, lambda e, NQ=NQ: e.scalar_tensor_tensor(out=o0[:, 0:NQ], in0=r1[:, 0:NQ],
                                                                      scalar=small[:, 1 + l:2 + l], in1=o0[:, 0:NQ],
                                                                      op0=ALU.mult, op1=ALU.add), [r1B, o0B, smallB], [o0B])
                    emit(act, lambda e, NQ=NQ: e.activation(out=asq[:, 0:NQ], in_=o0[:, 0:NQ], func=AF.Square), [o0B], [asqB])
                    b2 = mbank()
                    emit(pe, lambda e, NQ=NQ, b2=b2: e.matmul(banks[b2][:, 0:NQ], lhsT=ones_f[:], rhs=asq[:, 0:NQ],
                                                             start=True, stop=True), [asqB, constB], [bankB[b2]])
                    rstd_from(banks[b2][:, 0:NQ], 1.0 / 128, ars[:, 0:NQ], [bankB[b2]], [arsB], art[:, 0:NQ], artB)
                    nq_t = q0 // 512
                    emit(dve, lambda e, NQ=NQ, h=h, q0=q0: e.scalar_tensor_tensor(
                        out=ym[:, h, q0:q0 + NQ], in0=o0[:, 0:NQ], scalar=small[:, 3 + l:4 + l], in1=ars[:, 0:NQ],
                        op0=ALU.mult, op1=ALU.mult), [o0B, arsB, smallB], [], pwrites=[ymB[h][nq_t]])
            residual_partial(4, NT, l, g1c, j, w_out_d, 512, ym, lambda k, n: ymB[k][n])
            A.release(ma)
            if stop_after == "att%d%s" % (l, "s" if sample else "p"):
                return False


        taps_pop(100)
        csq = [A.alloc("csq%d" % i, [512], F32) for i in range(2)]
        csqB = [A.newbuf("csq%d" % i) for i in range(2)]
        mean = A.alloc("cmean", [512], F32); meanB = A.newbuf("cmean")
        msq = A.alloc("cmsq", [512], F32); msqB = A.newbuf("cmsq")
        crs = A.alloc("crs", [512], F32); crsB = A.newbuf("crs")
        crt = A.alloc("crt", [512], F32); crtB = A.newbuf("crt")
        ct1 = [A.alloc("ct1%d" % i, [512], F32) for i in range(2)]
        ct1B = [A.newbuf("ct1%d" % i) for i in range(2)]
        yb = A.alloc("cyb", [4, T], BF16); ybB = [[A.newbuf("cyb%d_%d" % (c, n)) for n in range(NT)] for c in range(4)]
        ym, ymB = ymix_alloc()
        for n in range(NT):
            b1 = mbank()
            for c in range(4):
                emit(pe, lambda e, c=c, n=n, b1=b1: e.matmul(banks[b1][:], lhsT=ones_f[:], rhs=acc[:, c, n * 512:(n + 1) * 512],
                                                            start=(c == 0), stop=(c == 3)),
                     [accB[c][n], constB], [bankB[b1]] if c == 0 else [], sig=(c == 3))
            b2 = mbank()
            for c in range(4):
                i = c % 2
                emit(act, lambda e, c=c, n=n, i=i: e.activation(out=csq[i], in_=acc[:, c, n * 512:(n + 1) * 512],
                                                               func=AF.Square), [accB[c][n]], [csqB[i]])
                emit(pe, lambda e, c=c, i=i, b2=b2: e.matmul(banks[b2][:], lhsT=ones_f[:], rhs=csq[i],
                                                            start=(c == 0), stop=(c == 3)),
                     [csqB[i], constB], [bankB[b2]] if c == 0 else [], pwrites=[] if c == 0 else [bankB[b2]], sig=True)
            emit(dve, lambda e, b1=b1: e.tensor_scalar(out=mean, in0=banks[b1][:], scalar1=1.0 / 512, scalar2=None,
                                                       op0=ALU.mult), [bankB[b1]], [meanB])
            emit(dve, lambda e: e.tensor_tensor(out=msq, in0=mean, in1=mean, op=ALU.mult), [meanB], [msqB])
            emit(dve, lambda e, b2=b2: e.scalar_tensor_tensor(out=msq, in0=banks[b2][:], scalar=1.0 / 512, in1=msq,
                                                              op0=ALU.mult, op1=ALU.subtract), [bankB[b2], msqB], [msqB])
            rstd_from(msq, 1.0, crs, [msqB], [crsB], crt, crtB)
            for c in range(4):
                i = c % 2
                emit(dve, lambda e, c=c, n=n, i=i: e.tensor_tensor(out=ct1[i], in0=acc[:, c, n * 512:(n + 1) * 512],
                                                                  in1=mean, op=ALU.subtract), [accB[c][n], meanB], [ct1B[i]])
                emit(dve, lambda e, i=i: e.tensor_tensor(out=ct1[i], in0=ct1[i], in1=crs, op=ALU.mult),
                     [ct1B[i], crsB], [ct1B[i]])
                emit(act, lambda e, c=c, n=n, i=i: e.activation(
                    out=yb[:, c, n * 512:(n + 1) * 512], in_=ct1[i], func=AF.Silu,
                    bias=pv[:, l, C_LNB + c:C_LNB + c + 1], scale=pv[:, l, C_LNG + c:C_LNG + c + 1]),
                    [ct1B[i], pvB], [ybB[c][n]])
        if stop_after == "dyb" + stg_tag:
            dbg_to_x(yb, [b_ for r_ in ybB for b_ in r_], 4, T)
            return False
        wb, wv = wload(wsrc(conv_pw_d, l, 0, 512, 0, 512), 4, 512)
        for mo in range(4):
            bl = job(4, NT, 512, lambda k, mo=mo: (wb, wv[:, k, mo * 128:(mo + 1) * 128]),
                     lambda k, n: (ybB[k][n], yb[:, k, n * 512:(n + 1) * 512]))
            for n in range(NT):
                emit(act, lambda e, mo=mo, n=n, b=bl[n]: e.activation(
                    out=ym[:, mo, n * 512:(n + 1) * 512], in_=banks[b][:], func=AF.Identity,
                    bias=pv[:, l, C_PWB + mo:C_PWB + mo + 1], scale=1.0), [bankB[bl[n]], pvB], [ymB[mo][n]])
        if stop_after == "dym" + stg_tag:
            dbg_to_x(ym, [b_ for r_ in ymB for b_ in r_], 4, T)
            return False
        if stop_after == "dnores" + stg_tag:
            return False
        residual_partial(4, NT, l, g1c, j, w_out_d, 1024, ym, lambda k, n: ymB[k][n], overwrite=(stop_after == "dres" + stg_tag))
        if stop_after == "dres" + stg_tag:
            return False
        A.release(mc)
        if stop_after == "conv%d%s" % (l, "s" if sample else "p"):
            return False

        if sample:
            ma = A.mark()
            ym, ymB = ymix_alloc()
            pt = [A.alloc("ptile%d" % i, [512], BF16) for i in range(3)]
            ptB = [A.newbuf("ptile%d" % i) for i in range(3)]
            r0 = A.alloc("ar0", [512], F32); r0B = A.newbuf("ar0")
            r1 = A.alloc("ar1", [512], F32); r1B = A.newbuf("ar1")
            o0 = A.alloc("ao0", [512], F32); o0B = A.newbuf("ao0")
            asq = A.alloc("asq", [512], F32); asqB = A.newbuf("asq")
            ars = A.alloc("ars", [512], F32); arsB = A.newbuf("ars")
            art = A.alloc("art", [512], F32); artB = A.newbuf("art")
            if sample:
                NKC = 20
                kall = A.alloc("kall", [4, NKC * 128], BF16); kallB = A.newbuf("kall")
                vall = A.alloc("vall", [NKC, 512], BF16); vallB = A.newbuf("vall")
                rv = xrA[l].ap()
                rvb = xrB[l].ap()
                for r in range(2):
                    emit(sp, lambda e, r=r: e.dma_start(
                        out=kall[:, :, r * 1024:(r + 1) * 1024],
                        in_=rv[r * XRA:r * XRA + 512, :].rearrange("(h p) t -> p h t", p=128)),
                        [xrecvQ[l][0]], [], pwrites=[kallB], dma=kallB)
                    emit(sp, lambda e, r=r: e.dma_start(
                        out=vall[:, r * 8:(r + 1) * 8, :],
                        in_=rvb[r * XRB + 512:r * XRB + 1024, :].rearrange("r (a c) -> (r a) c", a=2).rearrange(
                            "(i p) c -> p i c", p=128)), [xrecvQ[l][1]], [], pwrites=[vallB], dma=vallB)
                emit(pool, lambda e: e.dma_start(out=vall[:, 16:20, :],
                                                 in_=cv_d[l].rearrange("(i p) c -> p i c", p=128)),
                     [], [], pwrites=[vallB], dma=vallB)
                cks = [A.alloc("cks%d" % i, [512], F32) for i in range(2)]
                cksB = [A.newbuf("cks%d" % i) for i in range(2)]
                for i in range(4):
                    s_ = i % 2
                    emit(sp, lambda e, i=i, s_=s_: e.dma_start(out=cks[s_], in_=ck_d[l, i * 128:(i + 1) * 128, :]),
                         [], [cksB[s_]], dma=cksB[s_])
                    b = mbank()
                    for h in range(4):
                        emit(pe, lambda e, h=h, b=b, s_=s_: e.transpose(out=banks[b][:, h * 128:(h + 1) * 128],
                                                                       in_=cks[s_][:, h * 128:(h + 1) * 128], identity=ident[:]),
                             [cksB[s_], constB], [bankB[b]] if h == 0 else [], sig=(h == 3))
                    copy_on(ev_eng(), kall[:, :, 2048 + i * 128:2048 + (i + 1) * 128],
                            banks[b][:].rearrange("p (h t) -> p h t", h=4), [bankB[b]], [], pwrites=[kallB])
                att_units = [(0, 512, n * 512, [(kall, kallB, vall, vallB, jc) for jc in range(NKC)]) for n in range(NT)]
            else:
                att_units = []
                for si, (s0, L) in enumerate(segs):
                    att_units.append((1, L, s0, [(kTp, kTpB, vTp, vTpB, s0 // 128 + jc) for jc in range(L // 128)]))
            pcnt = [0]
            for h in range(4):
                for (_, NQ, q0, klist) in att_units:
                    taps_pop(1)
                    bO = [0, 1]
                    bL = [2, 3]
                    nk_ = len(klist)
                    its = [(ki, mmap) for ki in range(nk_) for mmap in range(2)]
                    base_i = pcnt[0]
                    pcnt[0] += len(its)

                    def e_qk(ix):
                        ki, mmap = its[ix]
                        kt_, ktB_, vt_, vtB_, jc = klist[ki]
                        bs = 4 + ((base_i + ix) % 2)
                        emit(pe, lambda e: e.matmul(
                            banks[bs][:, 0:NQ], lhsT=kt_[mmap * 64:(mmap + 1) * 64, h, jc * 128:(jc + 1) * 128],
                            rhs=qT[mmap * 64:(mmap + 1) * 64, h, q0:q0 + NQ], start=True, stop=True),
                            [ktB_, qTB], [bankB[bs]])

                    def e_exp(ix):
                        bs = 4 + ((base_i + ix) % 2)
                        pi = (base_i + ix) % 3
                        emit(act, lambda e: e.activation(out=pt[pi][:, 0:NQ], in_=banks[bs][:, 0:NQ],
                                                         func=AF.Exp, scale=0.125), [bankB[bs]], [ptB[pi]])

                    def e_pv(ix):
                        ki, mmap = its[ix]
                        kt_, ktB_, vt_, vtB_, jc = klist[ki]
                        pi = (base_i + ix) % 3
                        emit(pe, lambda e: e.matmul(
                            banks[bO[mmap]][:, 0:NQ], lhsT=vt_[:, jc, h * 128:(h + 1) * 128], rhs=pt[pi][:, 0:NQ],
                            start=(ki == 0), stop=(ki == nk_ - 1)),
                            [vtB_, ptB[pi]], [bankB[bO[mmap]]] if ki == 0 else [],
                            pwrites=[] if ki == 0 else [bankB[bO[mmap]]], sig=False)
                        emit(pe, lambda e: e.matmul(
                            banks[bL[mmap]][:, 0:NQ], lhsT=ones_b[:], rhs=pt[pi][:, 0:NQ],
                            start=(ki == 0), stop=(ki == nk_ - 1)),
                            [constB, ptB[pi]], [bankB[bL[mmap]]] if ki == 0 else [],
                            pwrites=[] if ki == 0 else [bankB[bL[mmap]]], sig=True)

                    e_qk(0)
                    for ix in range(len(its)):
                        if ix + 1 < len(its):
                            e_qk(ix + 1)
                        e_exp(ix)
                        e_pv(ix)
                    emit(dve, lambda e, NQ=NQ: e.reciprocal(out=r0[:, 0:NQ], in_=banks[2][:, 0:NQ]), [bankB[2]], [r0B])
                    emit(dve, lambda e, NQ=NQ: e.reciprocal(out=r1[:, 0:NQ], in_=banks[3][:, 0:NQ]), [bankB[3]], [r1B])
                    emit(dve, lambda e, NQ=NQ: e.tensor_tensor(out=o0[:, 0:NQ], in0=banks[0][:, 0:NQ], in1=r0[:, 0:NQ],
                                                               op=ALU.mult), [bankB[0], r0B], [o0B])
                    emit(dve, lambda e, NQ=NQ: e.tensor_tensor(out=r1[:, 0:NQ], in0=banks[1][:, 0:NQ], in1=r1[:, 0:NQ],
                                                               op=ALU.mult), [bankB[1], r1B], [r1B])
                    emit(dve, lambda e, NQ=NQ: e.scalar_tensor_tensor(out=o0[:, 0:NQ], in0=r1[:, 0:NQ],
                                                                      scalar=small[:, 1 + l:2 + l], in1=o0[:, 0:NQ],
                                                                      op0=ALU.mult, op1=ALU.add), [r1B, o0B, smallB], [o0B])
                    emit(act, lambda e, NQ=NQ: e.activation(out=asq[:, 0:NQ], in_=o0[:, 0:NQ], func=AF.Square), [o0B], [asqB])
                    b2 = mbank()
                    emit(pe, lambda e, NQ=NQ, b2=b2: e.matmul(banks[b2][:, 0:NQ], lhsT=ones_f[:], rhs=asq[:, 0:NQ],
                                                             start=True, stop=True), [asqB, constB], [bankB[b2]])
                    rstd_from(banks[b2][:, 0:NQ], 1.0 / 128, ars[:, 0:NQ], [bankB[b2]], [arsB], art[:, 0:NQ], artB)
                    nq_t = q0 // 512
                    emit(dve, lambda e, NQ=NQ, h=h, q0=q0: e.scalar_tensor_tensor(
                        out=ym[:, h, q0:q0 + NQ], in0=o0[:, 0:NQ], scalar=small[:, 3 + l:4 + l], in1=ars[:, 0:NQ],
                        op0=ALU.mult, op1=ALU.mult), [o0B, arsB, smallB], [], pwrites=[ymB[h][nq_t]])
            residual_partial(4, NT, l, g1c, j, w_out_d, 512, ym, lambda k, n: ymB[k][n])
            A.release(ma)
            if stop_after == "att%d%s" % (l, "s" if sample else "p"):
                return False


        A.release(m_q)
        A.release(base)
        if stop_after == "mix%d%s" % (l, "s" if sample else "p"):
            return False
        bg_drain(25)
        mffn = A.mark()
        hT = A.alloc("hT2", [KC, T], BF16)
        hTb = [[A.newbuf("hT2%d_%d" % (c, n)) for n in range(NT)] for c in range(KC)]
        norm(l, 1, T, j, hT, hTb)

        def hrhs2(k, n):
            return (hTb[k][n], hT[:, k, n * 512:(n + 1) * 512])
        aT = A.alloc("aT", [12, T], BF16)
        sl = [A.alloc("sl%d" % i, [T], F32) for i in range(2)]
        slB = [[A.newbuf("sl%d_%d" % (i, n)) for n in range(NT)] for i in range(2)]
        scnt = [0]
        aTB = [[A.newbuf("aT%d_%d" % (c, n)) for n in range(NT)] for c in range(12)]
        for (g0, gs) in FFN_GROUPS:
            for blk in range(gs // 2):
                col0 = (g0 + blk * 2) * 128
                bg_tick()
                wbg, wvg = wload(wsrc(w_gate_d, l, 0, D, col0, 256), KC, 256)
                wbu, wvu = wload(wsrc(w_up_d, l, 0, D, col0, 256), KC, 256)
                for sub in range(2):
                    cc = blk * 2 + sub
                    si_ = scnt[0] % 2
                    scnt[0] += 1
                    bl = job(KC, NT, 512, lambda k, sub=sub: (wbg, wvg[:, k, sub * 128:(sub + 1) * 128]), hrhs2)
                    for n in range(NT):
                        emit(act, lambda e, b=bl[n], si_=si_, n=n: e.activation(
                            out=sl[si_][:, n * 512:(n + 1) * 512], in_=banks[b][:], func=AF.Silu),
                            [bankB[bl[n]]], [slB[si_][n]])
                    bl = job(KC, NT, 512, lambda k, sub=sub: (wbu, wvu[:, k, sub * 128:(sub + 1) * 128]), hrhs2)
                    for n in range(NT):
                        emit(dve, lambda e, b=bl[n], si_=si_, n=n, cc=cc: e.tensor_tensor(
                            out=aT[:, cc, n * 512:(n + 1) * 512], in0=banks[b][:], in1=sl[si_][:, n * 512:(n + 1) * 512],
                            op=ALU.mult), [bankB[bl[n]], slB[si_][n]], [aTB[cc][n]])
            bg_drain(32)
            residual_partial(gs, NT, l, 80, j, w_down_d, g0 * 128, aT, lambda k, n: aTB[k][n])
        A.release(mffn)
        if stop_after == "end" + stg_tag:
            return False
        return True

    outB_all = P.buf("outs")
    GP = dict(T=TP, j=0, sample=False, segs=[(0, LP), (LP, LP)], first=False)
    GS = dict(T=TS, j=1, sample=True, segs=[(0, TS)], first=True)
    finals = []
    done = False
    for (G, src, dst) in ((GS, xs_d, ys_d), (GP, xp_d, yp_d)):
        if stop_after == "mods":
            done = True
            break
        load_x(src, G["T"])
        if stop_after == "loadx":
            done = True
            break
        ok = True
        for l in range(2):
            ok = layer(l, G)
            if not ok:
                break
        if not ok:
            done = True
            break
        if stop_after == ("endp" if not G["sample"] else "ends"):
            done = True
            break
        finals += store_y(dst, G["T"])
    if stop_after == "mods":
        db = P.buf("dbg")
        emit(sp, lambda e: e.dma_start(out=dbg_d[:, 0:384], in_=modt[:].rearrange("p l m j -> p (l m j)")), [modB, gscB], [db], dma=db)
        emit(sp, lambda e: e.dma_start(out=dbg_d[:, 384:384 + 128], in_=gsc[:].rearrange("p l s c j -> p (l s c j)")), [modB, gscB], [], pwrites=[db], dma=db)
        emit(sp, lambda e: e.dma_start(out=dbg_d[:, 512:528], in_=small[:]), [smallB], [], pwrites=[db], dma=db)
        finals.append(db)
    elif stop_after:
        allx = [xTb[c][n] for c in range(KC) for n in range(2)]
        db = P.buf("dbg")
        Tl = G["T"]
        emit(sp, lambda e: e.dma_start(out=dbg_d.rearrange("p (c t) -> p c t", c=KC)[:, :, 0:Tl], in_=xT[:, :, 0:Tl]), allx, [db], dma=db)
        finals.append(db)
    emit(sp, lambda e: e.nop(), [], finals + [outB_all], sig=True)
    P.replay()


def _consts():
    c = {}
    c["ident"] = np.eye(128, dtype=np.float32)
    rot = np.zeros((128, 128), np.float32)
    for p in range(128):
        partner = p + 16 if (p % 32) < 16 else p - 16
        rot[partner, p] = 1.0
    c["rotm"] = rot
    bo = np.zeros((128, 128), np.float32)
    bo[:64, :64] = 1.0
    bo[64:, 64:] = 1.0
    c["bones"] = bo
    k = np.arange(128)
    ang = 2 * np.pi * np.outer(k, k) / 128.0
    c["dft128"] = (np.concatenate([np.cos(ang), np.sin(ang)], axis=1) / np.sqrt(128.0)).astype(np.float32)
    t = np.arange(LP)
    ang = 2 * np.pi * np.outer(t, t) / LP
    c["dftp"] = (np.stack([np.cos(ang), -np.sin(ang)], axis=1) / np.sqrt(LP)).astype(ml_dtypes.bfloat16)

    def icnt(L, t):
        out = []
        for w in (2, 4, 8, 16):
            lo = np.maximum(t - w // 2, 0)
            hi = np.minimum(t + w // 2 - 1, L - 1)
            out.append(1.0 / (hi - lo + 1))
        return np.stack(out).astype(np.float32)
    c["icnt_p"] = icnt(LP, np.arange(LP))
    c["icnt_s"] = [icnt(LS, np.arange(hf * TS, (hf + 1) * TS)) for hf in range(2)]
    inv = 1.0 / (10000.0 ** (np.arange(0, 32, 2, dtype=np.float32) / 32.0))
    ropes = []
    for hf in range(2):
        tt = np.arange(hf * TS, (hf + 1) * TS)
        row = (tt // 64).astype(np.float32)
        col = (tt % 64).astype(np.float32)
        tab = np.zeros((128, 2, TS), np.float32)
        for p in range(128):
            d = p % 64
            pos = row if d < 32 else col
            a = pos * inv[d % 16]
            tab[p, 0] = np.cos(a)
            tab[p, 1] = (-np.sin(a)) if (d % 32) < 16 else np.sin(a)
        ropes.append(tab)
    c["rope"] = ropes
    t = np.arange(LS, dtype=np.float64)
    dfts = []
    for hf in range(2):
        kk = np.arange(hf * TS, (hf + 1) * TS, dtype=np.float64)
        ang = 2 * np.pi * (np.outer(t, kk) % LS) / LS
        dfts.append((np.stack([np.cos(ang), -np.sin(ang)], axis=1) / np.sqrt(LS)).astype(ml_dtypes.bfloat16))
    c["dfts"] = dfts
    c["hmask"] = [np.tile(np.array([[float(hf), 1.0 - hf]], np.float32), (128, 1)) for hf in range(2)]
    return c


def _pvec(inp):
    out = np.zeros((2, 128, NV), np.float32)
    for l in range(2):
        o = out[l]
        o[:, C_G1:C_G1 + 16] = inp["g_norm1"][l].reshape(16, 128).T
        o[:, C_G2:C_G2 + 16] = inp["g_norm2"][l].reshape(16, 128).T
        o[:, C_BADA:C_BADA + 96] = inp["b_ada"][l].reshape(96, 128).T
        o[:, C_PSC:C_PSC + 4] = inp["pool_scale"][l].reshape(4, 128).T
        dw = inp["conv_dw"][l]
        for c in range(4):
            o[:, C_DW + c * 31:C_DW + (c + 1) * 31] = dw[:, c * 128:(c + 1) * 128].T
        o[:, C_DWB:C_DWB + 4] = inp["conv_dw_b"][l].reshape(4, 128).T
        o[:, C_LNG:C_LNG + 4] = inp["conv_ln_g"][l].reshape(4, 128).T
        o[:, C_LNB:C_LNB + 4] = inp["conv_ln_b"][l].reshape(4, 128).T
        o[:, C_PWB:C_PWB + 4] = inp["conv_pw_b"][l].reshape(4, 128).T
        o[:, C_GQ] = np.tile(inp["g_q"][l], 2)
        o[:, C_GK] = np.tile(inp["g_k"][l], 2)
        o[:, C_GSUB] = inp["g_subln"][l]
        for q in range(4):
            o[:64, C_LAM + q] = inp["lam"][l][q]
    return out


_CACHE = {}


def make_in_maps(inp, cores):
    inp = {k: np.ascontiguousarray(np.asarray(v)) for k, v in inp.items()}
    cst = _consts()
    pvec = _pvec(inp)
    maps = []
    for c in cores:
        b, hf = c // 2, c % 2
        m = {
            "xp": inp["x_prompt"][2 * c:2 * c + 2].reshape(TP, D),
            "xs": inp["x_sample"][b, hf * TS:(hf + 1) * TS],
            "ck": inp["cache_k"][b].reshape(2, PAST, 512),
            "cv": inp["cache_v"][b].reshape(2, PAST, 512),
            "cT": np.ascontiguousarray(np.concatenate([np.stack([inp["c_ctx"], inp["c"][b]], axis=1), np.zeros((D, 6), np.float32)], axis=1).reshape(KC, 128, 8).transpose(1, 0, 2)),
            "pvec": pvec,
            "ident": cst["ident"], "rotm": cst["rotm"], "bones": cst["bones"],
            "rope": cst["rope"][hf], "icnt_p": cst["icnt_p"], "icnt_s": cst["icnt_s"][hf],
            "dft128": cst["dft128"], "dftp": cst["dftp"], "dfts": cst["dfts"][hf], "hmask": cst["hmask"][hf],
        }
        for k in ("w_ada", "w_in", "pool_w", "conv_pw", "fourier_w", "w_out", "w_gate", "w_up", "w_down"):
            m[k] = inp[k]
        maps.append({k: np.ascontiguousarray(v) for k, v in m.items()})
    return maps


def kernel(**inputs):
    if "nc" not in _CACHE:
        _CACHE["nc"] = build_program()
    nc = _CACHE["nc"]
    maps = make_in_maps(inputs, list(range(NCORES)))
    res = run_bass_kernel_spmd(nc, maps, core_ids=list(range(NCORES)))
    R = res.results
    yp = np.zeros((16, LP, D), np.float32)
    ys = np.zeros((4, LS, D), np.float32)
    nk = np.zeros((16, 2, LP, 4, 2, 64), np.float32)
    nv = np.zeros((16, 2, LP, 4, 128), np.float32)
    for c in range(NCORES):
        b, hf = c // 2, c % 2
        yp[2 * c:2 * c + 2] = np.asarray(R[c]["yp"]).reshape(2, LP, D)
        ys[b, hf * TS:(hf + 1) * TS] = np.asarray(R[c]["ys"])
        nk[2 * c:2 * c + 2] = np.asarray(R[c]["nk"]).reshape(2, 2, LP, 4, 2, 64)
        nv[2 * c:2 * c + 2] = np.asarray(R[c]["nv"]).reshape(2, 2, LP, 4, 128)
    return (yp, ys, nk, nv)
```

```python
import contextlib
import os
import numpy as np
import ml_dtypes
import concourse.bass as bass
import concourse.mybir as mybir
from concourse.bass_utils import run_bass_kernel_spmd

F32 = mybir.dt.float32
BF16 = mybir.dt.bfloat16
AF = mybir.ActivationFunctionType
ALU = mybir.AluOpType

D = 2048
KC = 16
DFF = 5632
FC = 44
INC = 3584
EPS = 1e-6
NCORES = 8
TP = 512
TS = 1024
LP = 256
LS = 2048
PAST = 512
XR = 1568
C_G1, C_G2, C_BADA, C_PSC, C_DW, C_DWB, C_LNG, C_LNB, C_PWB = 0, 16, 32, 128, 132, 256, 260, 264, 268
C_GQ, C_GK, C_GSUB, C_LAM = 272, 273, 274, 275
NV = 280
LAM_INIT = [0.8 - 0.6 * float(np.exp(-0.3 * l)) for l in range(2)]
FFN_GROUPS = [(0, 12), (12, 12), (24, 12), (36, 8)]


class Buf:
    __slots__ = ("name", "w", "r", "dkey", "dcnt")

    def __init__(self, name):
        self.name = name
        self.w = {}
        self.r = {}
        self.dkey = {}
        self.dcnt = 0


class _Rec:
    def __init__(self):
        self.call = None

    def __getattr__(self, name):
        def f(*a, **k):
            self.call = (name, a, k)
            return None
        return f


class Eng:
    def __init__(self, name, key):
        self.name = name
        self.key = key
        self.ops = []
        self.cnt = 0
        self.waited = {}


class Prog:
    def __init__(self, nc, stack, n_dsem=100):
        self.nc = nc
        self.stack = stack
        self.sems = []
        self.pe = Eng("tensor", self._sem("s_pe"))
        self.act = Eng("scalar", self._sem("s_act"))
        self.dve = Eng("vector", self._sem("s_dve"))
        self.pool = Eng("gpsimd", self._sem("s_pool"))
        self.sp = Eng("sync", self._sem("s_sp"))
        self.engs = [self.pe, self.act, self.dve, self.pool, self.sp]
        self.free_dsems = {"sync": [], "gpsimd": [], "scalar": []}
        self.dsem_cnt = {}
        self.n_dsem = 0
        self.arena_deps = {}
        self.flip = 0

    def _sem(self, name):
        s = self.stack.enter_context(self.nc.semaphore(name))
        self.sems.append(s)
        return len(self.sems) - 1

    def buf(self, name):
        b = Buf(name)
        b.w = dict(self.arena_deps)
        return b

    def emit(self, eng, fn, reads=(), writes=(), pwrites=(), sig=True, dma=None, inc=None):
        waits = {}

        def need(d):
            for s, v in d.items():
                if waits.get(s, 0) < v:
                    waits[s] = v

        for b in reads:
            need(b.w)
        for b in writes:
            need(b.w)
            need(b.r)
        for b in pwrites:
            need(b.w)
            need(b.r)
        wl = []
        for s, v in waits.items():
            if s == eng.key and (v > eng.cnt or eng is self.pe):
                continue
            if eng.waited.get(s, 0) < v:
                eng.waited[s] = v
                wl.append((s, v))
        if dma is not None:
            if eng.name not in dma.dkey:
                fl = self.free_dsems[eng.name]
                if fl:
                    dma.dkey[eng.name] = fl.pop(0)
                else:
                    dma.dkey[eng.name] = self._sem("d%d" % self.n_dsem)
                    self.n_dsem += 1
                    self.dsem_cnt[dma.dkey[eng.name]] = 0
            dk = dma.dkey[eng.name]
            self.dsem_cnt[dk] += 16
            tok = (dk, self.dsem_cnt[dk])
            incr = (dk, 16)
        elif inc is not None:
            tok = inc
            incr = (inc[0], 1)
        else:
            tok = (eng.key, eng.cnt + 1)
            if sig:
                eng.cnt += 1
                incr = (eng.key, 1)
            else:
                incr = None
        rec = _Rec()
        fn(rec)
        assert rec.call is not None
        eng.ops.append((wl, rec.call, incr))
        for b in reads:
            if b.r.get(tok[0], 0) < tok[1]:
                b.r[tok[0]] = tok[1]
        for b in writes:
            b.w = {tok[0]: tok[1]}
            b.r = {}
        for b in pwrites:
            if b.w.get(tok[0], 0) < tok[1]:
                b.w[tok[0]] = tok[1]
        return tok

    def retire(self, bufs):
        for b in bufs:
            for en_, dk_ in b.dkey.items():
                self.free_dsems[en_].append(dk_)
            b.dkey = {}
            for d in (b.w, b.r):
                for s, v in d.items():
                    if self.arena_deps.get(s, 0) < v:
                        self.arena_deps[s] = v

    def check(self):
        semv = {}
        pos = {e.name: 0 for e in self.engs}
        progress = True
        while progress:
            progress = False
            for e in self.engs:
                while pos[e.name] < len(e.ops):
                    wl, fn, incr = e.ops[pos[e.name]]
                    if all(semv.get(s_, 0) >= v for s_, v in wl):
                        if incr is not None:
                            semv[incr[0]] = semv.get(incr[0], 0) + incr[1]
                        pos[e.name] += 1
                        progress = True
                    else:
                        break
        stuck = {e.name: (pos[e.name], len(e.ops)) for e in self.engs if pos[e.name] < len(e.ops)}
        if stuck:
            for e in self.engs:
                if pos[e.name] < len(e.ops):
                    wl, fn, incr = e.ops[pos[e.name]]
                    print("STUCK", e.name, pos[e.name], "/", len(e.ops), "waits", [(s_, v, semv.get(s_, 0)) for s_, v in wl])
            raise RuntimeError("deadlock in emitted program: %s" % stuck)
        print("check ok: ops per engine", {e.name: len(e.ops) for e in self.engs}, "nsems", len(self.sems))

    def replay(self):
        self.check()
        nc = self.nc
        sems = self.sems
        with nc.Block() as block:
            def mk(eng):
                def body(e):
                    for wl, fn, incr in eng.ops:
                        for s, v in wl:
                            e.wait_ge(sems[s], v)
                        ins = getattr(e, fn[0])(*fn[1], **fn[2])
                        if incr is not None:
                            ins.then_inc(sems[incr[0]], incr[1])
                return body
            block.tensor(mk(self.pe))
            block.scalar(mk(self.act))
            block.vector(mk(self.dve))
            block.gpsimd(mk(self.pool))
            block.sync(mk(self.sp))


class Arena:
    def __init__(self, prog, tensor, nwords):
        self.p = prog
        self.t = tensor
        self.n = nwords
        self.top = 0
        self.live = []

    def mark(self):
        return (self.top, len(self.live))

    def release(self, m):
        top, nl = m
        self.p.retire(self.live[nl:])
        del self.live[nl:]
        self.top = top

    def alloc(self, name, shape, dt):
        n = 1
        for s in shape:
            n *= s
        words = n if dt == F32 else (n + 1) // 2
        words = (words + 7) // 8 * 8
        assert self.top + words <= self.n, ("arena overflow", name, self.top, words, self.n)
        ap = self.t[:, self.top:self.top + words]
        self.top += words
        if dt != F32:
            ap = ap.bitcast(dt)
        ap = ap[:, 0:n]
        if len(shape) == 2:
            ap = ap.rearrange("p (a b) -> p a b", a=shape[0])
        elif len(shape) == 3:
            ap = ap.rearrange("p (a b c) -> p a b c", a=shape[0], b=shape[1])
        return ap

    def newbuf(self, name):
        b = self.p.buf(name)
        self.live.append(b)
        return b


def build_program(stop_after=None):
    nc = bass.Bass("TRN2", target_bir_lowering=False)
    stack = contextlib.ExitStack()
    with stack:
        _build(nc, stack, stop_after)
    return nc


def _build(nc, stack, stop_after):
    P = Prog(nc, stack)
    emit = P.emit
    pe, act, dve, pool, sp = P.pe, P.act, P.dve, P.pool, P.sp

    def din(name, shape, dt=F32):
        return nc.dram_tensor(name, list(shape), dt, kind="ExternalInput").ap()

    def dout(name, shape, dt=F32):
        return nc.dram_tensor(name, list(shape), dt, kind="ExternalOutput").ap()

    xp_d = din("xp", [TP, D])
    xs_d = din("xs", [TS, D])
    ck_d = din("ck", [2, PAST, 512])
    cv_d = din("cv", [2, PAST, 512])
    cT_d = din("cT", [128, KC, 8])
    pvec_d = din("pvec", [2, 128, NV])
    ident_d = din("ident", [128, 128])
    rotm_d = din("rotm", [128, 128])
    bones_d = din("bones", [128, 128])
    rope_d = din("rope", [128, 2, TS])
    icntp_d = din("icnt_p", [4, LP])
    icnts_d = din("icnt_s", [4, TS])
    dft128_d = din("dft128", [128, 256])
    dftp_d = din("dftp", [LP, 2, LP], BF16)
    dfts_d = din("dfts", [LS, 2, TS], BF16)
    hmask_d = din("hmask", [128, 2])
    w_ada_d = din("w_ada", [2, D, 6 * D])
    w_in_d = din("w_in", [2, D, INC])
    pool_w_d = din("pool_w", [2, 4, 128, 128])
    conv_pw_d = din("conv_pw", [2, 512, 512])
    four_w_d = din("fourier_w", [2, 512, 512])
    w_out_d = din("w_out", [2, D, D])
    w_gate_d = din("w_gate", [2, D, DFF])
    w_up_d = din("w_up", [2, D, DFF])
    w_down_d = din("w_down", [2, DFF, D])
    yp_d = dout("yp", [TP, D])
    ys_d = dout("ys", [TS, D])
    nk_d = dout("nk", [2, 2, LP, 512])
    nv_d = dout("nv", [2, 2, LP, 512])
    dbg_d = dout("dbg", [128, KC * TS]) if stop_after else None
    XRA, XRB = 544, 1024

    class _V:
        def __init__(self, t):
            self.t = t

        def ap(self):
            return self.t.ap().rearrange("p c -> (p c)").rearrange("(r c) -> r c", c=1024)
    xsA_t = [nc.dram_tensor("xsA%d" % l, [128, XRA * 8], BF16) for l in range(2)]
    xrA_t = [nc.dram_tensor("xrA%d" % l, [256, XRA * 8], BF16) for l in range(2)]
    xsB_t = [nc.dram_tensor("xsB%d" % l, [128, XRB * 8], BF16) for l in range(2)]
    xrB_t = [nc.dram_tensor("xrB%d" % l, [256, XRB * 8], BF16) for l in range(2)]
    xsA = [_V(t) for t in xsA_t]
    xrA = [_V(t) for t in xrA_t]
    xsB = [_V(t) for t in xsB_t]
    xrB = [_V(t) for t in xrB_t]
    xsendQ = [[P.buf("xsend%d_%d" % (l, q)) for q in range(2)] for l in range(2)]
    xrecvQ = [[P.buf("xrecv%d_%d" % (l, q)) for q in range(2)] for l in range(2)]
    cc_keys = [[P._sem("cc%d_%d" % (l, q)) for q in range(2)] for l in range(2)]

    def sb(name, shape, dt):
        return stack.enter_context(nc.sbuf_tensor("sb_" + name, list(shape), dt))

    xT = sb("xT", [128, KC, TS], F32)
    xTb = [[P.buf("xT%d_%d" % (c, n)) for n in range(2)] for c in range(KC)]
    NSLOT = 4
    wring = [sb("wr%d" % i, [128, 4096], BF16) for i in range(NSLOT)]
    wringB = [P.buf("wr%d" % i) for i in range(NSLOT)]
    wnext = [0]
    pv = sb("pv", [128, 2, NV], F32)
    pvB = P.buf("pv")
    modt = sb("modt", [128, 2, 96, 2], F32)
    modB = P.buf("modt")
    gsc = sb("gsc", [128, 2, 2, KC, 2], F32)
    gscB = P.buf("gsc")
    ident = sb("ident", [128, 128], F32)
    rotm = sb("rotm", [128, 128], F32)
    bones = sb("bones", [128, 128], F32)
    ones_f = sb("ones_f", [128, 128], F32)
    ones_b = sb("ones_b", [128, 128], BF16)
    constB = P.buf("const")
    rope = sb("rope", [128, 2, TS], F32)
    dft128 = sb("dft128", [128, 256], BF16)
    dftp = sb("dftp", [128, 2, 2, LP], BF16)
    hmask = sb("hmask", [128, 2], F32)
    sT = sb("sT", [128, KC, 8], BF16)
    small = sb("small", [128, 16], F32)
    smallB = P.buf("small")
    ARENA_WORDS = 23600
    arena_t = sb("arena", [128, ARENA_WORDS], F32)
    A = Arena(P, arena_t, ARENA_WORDS)
    banks = [stack.enter_context(nc.psum_tensor("ps%d" % i, [128, 512], F32)) for i in range(8)]
    bankB = [P.buf("ps%d" % i) for i in range(8)]
    dn = [0]
    mn = [0]

    def dbank():
        i = dn[0] % 6
        dn[0] += 1
        return i

    def mbank():
        i = 6 + mn[0] % 2
        mn[0] += 1
        return i

    def ev_eng():
        P.flip ^= 1
        return act if P.flip else dve

    def copy_on(eng, out, in_, reads, writes, pwrites=()):
        if eng is act:
            return emit(act, lambda e: e.activation(out=out, in_=in_, func=AF.Identity), reads, writes, pwrites)
        return emit(dve, lambda e: e.tensor_copy(out=out, in_=in_), reads, writes, pwrites)

    def wload(src, kc, cw):
        i = wnext[0] % NSLOT
        wnext[0] += 1
        view = wring[i][:, 0:kc * cw].rearrange("p (k c) -> p k c", k=kc)
        emit(pool, lambda e: e.dma_start(out=view, in_=src), [], [wringB[i]], dma=wringB[i])
        return wringB[i], view

    def wsrc(wd, l, r0, nrows, c0, cw):
        return wd[l, r0:r0 + nrows, c0:c0 + cw].rearrange("(k p) n -> p k n", p=128)

    eps_ap = small[:, 0:1]

    emit(sp, lambda e: e.dma_start(out=pv[:], in_=pvec_d.rearrange("l p v -> p l v")), [], [pvB], dma=pvB)
    cB = P.buf("cld")
    for (t, d) in ((ident, ident_d), (rotm, rotm_d), (bones, bones_d)):
        emit(sp, lambda e, t=t, d=d: e.dma_start(out=t[:], in_=d), [], [], pwrites=[constB], dma=cB)
    emit(sp, lambda e: e.dma_start(out=rope[:], in_=rope_d), [], [], pwrites=[constB], dma=cB)
    emit(sp, lambda e: e.dma_start(out=hmask[:], in_=hmask_d), [], [], pwrites=[constB], dma=cB)
    emit(sp, lambda e: e.dma_start(out=dftp[:], in_=dftp_d.rearrange("(i p) a k -> p i a k", p=128)), [], [],
         pwrites=[constB], dma=cB)
    emit(pool, lambda e: e.dma_start(out=dft128[:], in_=dft128_d), [], [], pwrites=[constB], dma=cB)
    emit(dve, lambda e: e.memset(ones_f[:], 1.0), [], [], pwrites=[constB])
    emit(dve, lambda e: e.memset(ones_b[:], 1.0), [], [], pwrites=[constB])
    emit(dve, lambda e: e.memset(small[:, 0:1], EPS), [], [smallB])

    m0 = A.mark()
    ctmp = A.alloc("ctmp", [KC, 8], F32)
    ctB = A.newbuf("ctmp")
    sTB = P.buf("sT")
    emit(sp, lambda e: e.dma_start(out=ctmp, in_=cT_d), [], [ctB], dma=ctB)
    emit(act, lambda e: e.activation(out=sT[:], in_=ctmp, func=AF.Silu), [ctB], [sTB])
    lp = A.alloc("lamp", [4], F32)
    lpB = A.newbuf("lamp")
    for l in range(2):
        for q in range(2):
            emit(dve, lambda e, l=l, q=q: e.tensor_tensor(
                out=lp[:, 2 * l + q:2 * l + q + 1], in0=pv[:, l, C_LAM + 2 * q:C_LAM + 2 * q + 1],
                in1=pv[:, l, C_LAM + 2 * q + 1:C_LAM + 2 * q + 2], op=ALU.mult), [pvB], [], pwrites=[lpB])
    bi = mbank()
    emit(pe, lambda e: e.matmul(banks[bi][:, 0:4], lhsT=ones_f[:], rhs=lp, start=True, stop=True),
         [lpB, constB], [bankB[bi]])
    le = A.alloc("lame", [4], F32)
    leB = A.newbuf("lame")
    emit(act, lambda e: e.activation(out=le, in_=banks[bi][:, 0:4], func=AF.Exp), [bankB[bi]], [leB])
    for l in range(2):
        emit(dve, lambda e, l=l: e.tensor_tensor(out=small[:, 5 + l:6 + l], in0=le[:, 2 * l + 1:2 * l + 2],
                                                  in1=le[:, 2 * l:2 * l + 1], op=ALU.subtract),
             [leB], [], pwrites=[smallB])
        emit(dve, lambda e, l=l: e.tensor_scalar(out=small[:, 1 + l:2 + l], in0=small[:, 5 + l:6 + l],
                                                  scalar1=-LAM_INIT[l], scalar2=None, op0=ALU.add),
             [smallB], [], pwrites=[smallB])
        emit(dve, lambda e, l=l: e.tensor_scalar(out=small[:, 3 + l:4 + l], in0=pv[:, l, C_GSUB:C_GSUB + 1],
                                                  scalar1=1.0 - LAM_INIT[l], scalar2=None, op0=ALU.mult),
             [pvB], [], pwrites=[smallB])

    def gsc_emit(l, s_):
        for j in range(2):
            emit(dve, lambda e, j=j: e.scalar_tensor_tensor(
                out=gsc[:, l, s_, :, j], in0=modt[:, l, 48 * s_ + 16:48 * s_ + 32, j], scalar=1.0,
                in1=pv[:, l, (C_G1 if s_ == 0 else C_G2):(C_G1 if s_ == 0 else C_G2) + KC],
                op0=ALU.add, op1=ALU.mult), [modB, pvB], [], pwrites=[gscB])

    def mods_gen(l, mlo, mhi, gsc_after=None):
        for blk in range(mlo // 2, mhi // 2):
            wb, wv = wload(wsrc(w_ada_d, l, 0, D, blk * 256, 256), KC, 256)
            mb = mbank()
            mps = banks[mb][:, 0:16].rearrange("p (m j) -> p m j", j=8)
            for sub in range(2):
                for k in range(KC):
                    first = (sub == 0 and k == 0)
                    emit(pe, lambda e, k=k, sub=sub: e.matmul(
                        mps[:, sub, :], lhsT=wv[:, k, sub * 128:(sub + 1) * 128], rhs=sT[:, k, :],
                        start=(k == 0), stop=(k == KC - 1)),
                        [wb, sTB], [bankB[mb]] if first else [], sig=(k == KC - 1 and sub == 1))
            for j in range(2):
                emit(dve, lambda e, j=j, blk=blk: e.tensor_tensor(
                    out=modt[:, l, 2 * blk:2 * blk + 2, j], in0=mps[:, :, j],
                    in1=pv[:, l, C_BADA + 2 * blk:C_BADA + 2 * blk + 2], op=ALU.add),
                    [bankB[mb], pvB], [], pwrites=[modB])
            yield 1
        if gsc_after is not None:
            gsc_emit(l, gsc_after)

    for _ in mods_gen(0, 0, 32, 0):
        pass

    def _bg_chain():
        yield from mods_gen(0, 32, 48)
        yield from mods_gen(0, 48, 80, 1)
        yield from mods_gen(0, 80, 96)
        yield from mods_gen(1, 0, 32, 0)
        yield from mods_gen(1, 32, 80, 1)
        yield from mods_gen(1, 80, 96)
    bg = [_bg_chain(), 0, 0]

    def bg_poll():
        if bg[0] is None:
            return False
        try:
            next(bg[0])
            bg[1] += 1
            return True
        except StopIteration:
            bg[0] = None
            return False

    def bg_tick(light=False):
        bg[2] += 1
        if light:
            bg_poll()
            bg_poll()
        elif bg[1] < 32:
            bg_poll()
        elif bg[2] % 2 == 0:
            bg_poll()

    def bg_drain(nblocks=None):
        while bg[0] is not None and (nblocks is None or bg[1] < nblocks):
            if not bg_poll():
                break
    A.release(m0)
    early = stop_after in ("mods", "loadx")

    def load_x(src_d, T):
        m = A.mark()
        stg = [A.alloc("xstg%d" % i, [D], F32) for i in range(2)]
        stgB = [A.newbuf("xstg%d" % i) for i in range(2)]
        for i in range(T // 128):
            s_ = i % 2
            emit(sp, lambda e, i=i, s_=s_: e.dma_start(out=stg[s_], in_=src_d[i * 128:(i + 1) * 128, :]),
                 [], [stgB[s_]], dma=stgB[s_])
            for c4 in range(4):
                b = mbank()
                for q in range(4):
                    c = c4 * 4 + q
                    emit(pe, lambda e, b=b, q=q, c=c, s_=s_: e.transpose(
                        out=banks[b][:, q * 128:(q + 1) * 128], in_=stg[s_][:, c * 128:(c + 1) * 128],
                        identity=ident[:]), [stgB[s_], constB], [bankB[b]] if q == 0 else [], sig=(q == 3))
                n = i // 4
                o = xT[:, c4 * 4:(c4 + 1) * 4, i * 128:(i + 1) * 128]
                copy_on(ev_eng(), o, banks[b][:].rearrange("p (q t) -> p q t", q=4), [bankB[b]], [],
                        pwrites=[xTb[c4 * 4 + q][n] for q in range(4)])
        A.release(m)

    def store_y(dst_d, T):
        m = A.mark()
        stg = [A.alloc("ystg%d" % i, [D], F32) for i in range(2)]
        stgB = [A.newbuf("ystg%d" % i) for i in range(2)]
        last = []
        for i in range(T // 128):
            s_ = i % 2
            n = i // 4
            for c4 in range(4):
                b = mbank()
                for q in range(4):
                    c = c4 * 4 + q
                    emit(pe, lambda e, b=b, q=q, c=c, i=i: e.transpose(
                        out=banks[b][:, q * 128:(q + 1) * 128], in_=xT[:, c, i * 128:(i + 1) * 128],
                        identity=ident[:]), [xTb[c][n], constB], [bankB[b]] if q == 0 else [], sig=(q == 3))
                if c4 == 0:
                    copy_on(ev_eng(), stg[s_][:, 0:512], banks[b][:], [bankB[b]], [stgB[s_]])
                else:
                    copy_on(ev_eng(), stg[s_][:, c4 * 512:(c4 + 1) * 512], banks[b][:], [bankB[b]], [],
                            pwrites=[stgB[s_]])
            tok = emit(sp, lambda e, i=i, s_=s_: e.dma_start(out=dst_d[i * 128:(i + 1) * 128, :], in_=stg[s_]),
                       [stgB[s_]], [], dma=stgB[s_])
            last.append(stgB[s_])
        A.release(m)
        return last

    def rstd_from(ps_ap, scale, out_ap, rB, wB, tmp_ap, tmpB):
        emit(act, lambda e: e.activation(out=tmp_ap, in_=ps_ap, func=AF.Sqrt, bias=eps_ap, scale=scale),
             rB + [smallB], [tmpB])
        emit(dve, lambda e: e.reciprocal(out=out_ap, in_=tmp_ap), [tmpB], wB)

    def norm(l, s, T, j, hT, hTb):
        m = A.mark()
        sq = [A.alloc("sq%d" % i, [512], BF16) for i in range(2)]
        sqB = [A.newbuf("sq%d" % i) for i in range(2)]
        tm = [A.alloc("ntm%d" % i, [512], F32) for i in range(2)]
        tmB = [A.newbuf("ntm%d" % i) for i in range(2)]
        rs = A.alloc("nrs", [512], F32)
        rsB = A.newbuf("nrs")
        rt = A.alloc("nrt", [512], F32)
        rtB = A.newbuf("nrt")
        for n in range(T // 512):
            b = mbank()
            for c in range(KC):
                i = c % 2
                emit(act, lambda e, c=c, n=n, i=i: e.activation(out=sq[i], in_=xT[:, c, n * 512:(n + 1) * 512],
                                                               func=AF.Square), [xTb[c][n]], [sqB[i]])
                emit(pe, lambda e, c=c, i=i, b=b: e.matmul(banks[b][:], lhsT=ones_b[:], rhs=sq[i],
                                                          start=(c == 0), stop=(c == KC - 1)),
                     [sqB[i], constB], [bankB[b]] if c == 0 else [], pwrites=[] if c == 0 else [bankB[b]], sig=True)
            import os
            NP_ = int(os.environ.get("NORM_PARTS", "3"))
            if NP_ < 2:
                continue
            rstd_from(banks[b][:], 1.0 / D, rs, [bankB[b]], [rsB], rt, rtB)
            if NP_ < 3:
                continue
            for c in range(KC):
                i = c % 2
                emit(dve, lambda e, c=c, n=n, i=i: e.tensor_tensor(out=tm[i], in0=xT[:, c, n * 512:(n + 1) * 512],
                                                                  in1=rs, op=ALU.mult), [xTb[c][n], rsB], [tmB[i]])
                NV_ = int(os.environ.get("NORM_VAR", "0"))
                if NV_ == 1:
                    emit(act, lambda e, c=c, n=n, i=i: e.activation(
                        out=hT[:, c, n * 512:(n + 1) * 512], in_=tm[i], func=AF.Identity),
                        [tmB[i], modB, gscB], [hTb[c][n]])
                elif NV_ == 2:
                    pass
                elif NV_ == 3:
                    emit(act, lambda e, c=c, n=n, i=i: e.activation(
                        out=hT[:, c, n * 512:(n + 1) * 512], in_=tm[i], func=AF.Identity,
                        bias=small[:, 0:1], scale=small[:, 0:1]),
                        [tmB[i], modB, gscB], [hTb[c][n]])
                else:
                    emit(act, lambda e, c=c, n=n, i=i: e.activation(
                        out=hT[:, c, n * 512:(n + 1) * 512], in_=tm[i], func=AF.Identity,
                        bias=modt[:, l, 48 * s + c, j:j + 1], scale=gsc[:, l, s, c, j:j + 1]),
                        [tmB[i], modB, gscB], [hTb[c][n]])
        A.release(m)

    def job(K, NT, ncols, lhs_fn, rhs_fn):
        bl = [dbank() for _ in range(NT)]
        for k in range(K):
            wb, lhsT = lhs_fn(k)
            for n in range(NT):
                rb, rhs = rhs_fn(k, n)
                emit(pe, lambda e, b=bl[n], lhsT=lhsT, rhs=rhs, k=k: e.matmul(
                    banks[b][:, 0:ncols], lhsT=lhsT, rhs=rhs, start=(k == 0), stop=(k == K - 1)),
                    [wb, rb], [bankB[bl[n]]] if k == 0 else [], sig=(k == K - 1 and n == NT - 1))
        return bl

    def residual_partial(K, NT, l, gate_chunk0, j, wd, r0, rhs_ap, rhsB_fn, overwrite=False):
        cw = 512 if K <= 8 else 256
        for ob in range(D // cw):
            bg_tick(light=(K == 4))
            wb, wv = wload(wsrc(wd, l, r0, K * 128, ob * cw, cw), K, cw)
            for sub in range(cw // 128):
                mo = ob * (cw // 128) + sub
                bl = job(K, NT, 512,
                         lambda k, wv=wv, sub=sub: (wb, wv[:, k, sub * 128:(sub + 1) * 128]),
                         lambda k, n: (rhsB_fn(k, n), rhs_ap[:, k, n * 512:(n + 1) * 512]))
                for n in range(NT):
                    if overwrite:
                        emit(dve, lambda e, b=bl[n], mo=mo, n=n: e.tensor_scalar(
                            out=xT[:, mo, n * 512:(n + 1) * 512], in0=banks[b][:],
                            scalar1=modt[:, l, gate_chunk0 + mo, j:j + 1], scalar2=None, op0=ALU.mult),
                            [bankB[bl[n]], modB], [xTb[mo][n]])
                        continue
                    emit(dve, lambda e, b=bl[n], mo=mo, n=n: e.scalar_tensor_tensor(
                        out=xT[:, mo, n * 512:(n + 1) * 512], in0=banks[b][:],
                        scalar=modt[:, l, gate_chunk0 + mo, j:j + 1], in1=xT[:, mo, n * 512:(n + 1) * 512],
                        op0=ALU.mult, op1=ALU.add), [bankB[bl[n]], modB, xTb[mo][n]], [xTb[mo][n]])

    def qknorm(b, n, gcol, l, use_rope, out_bf, outB, out_f32=None, out_f32B=None, tmps=None):
        sq, sqB, rs, rsB, rt, rtB, qn, qnB, t1, t1B = tmps
        emit(act, lambda e: e.activation(out=sq, in_=banks[b][:], func=AF.Square), [bankB[b]], [sqB])
        b2 = mbank()
        emit(pe, lambda e: e.matmul(banks[b2][:], lhsT=bones[:], rhs=sq, start=True, stop=True),
             [sqB, constB], [bankB[b2]])
        rstd_from(banks[b2][:], 1.0 / 64, rs, [bankB[b2]], [rsB], rt, rtB)
        gq = pv[:, l, gcol:gcol + 1]
        if not use_rope:
            emit(dve, lambda e: e.scalar_tensor_tensor(out=qn, in0=banks[b][:], scalar=gq, in1=rs,
                                                       op0=ALU.mult, op1=ALU.mult), [bankB[b], rsB, pvB], [qnB])
            emit(act, lambda e: e.activation(out=out_bf, in_=qn, func=AF.Identity), [qnB], [], pwrites=[outB])
            return
        emit(dve, lambda e: e.scalar_tensor_tensor(out=qn, in0=banks[b][:], scalar=gq, in1=rs,
                                                   op0=ALU.mult, op1=ALU.mult), [bankB[b], rsB, pvB], [qnB])
        b3 = mbank()
        emit(pe, lambda e: e.matmul(banks[b3][:], lhsT=rotm[:], rhs=qn, start=True, stop=True),
             [qnB, constB], [bankB[b3]])
        emit(dve, lambda e: e.tensor_tensor(out=t1, in0=banks[b3][:], in1=rope[:, 1, n * 512:(n + 1) * 512],
                                            op=ALU.mult), [bankB[b3], constB], [t1B])
        emit(dve, lambda e: e.tensor_tensor(out=qn, in0=qn, in1=rope[:, 0, n * 512:(n + 1) * 512],
                                            op=ALU.mult), [qnB, constB], [qnB])
        emit(dve, lambda e: e.tensor_tensor(out=out_bf, in0=qn, in1=t1, op=ALU.add), [qnB, t1B], [],
             pwrites=[outB])

    def qk_tmps():
        sq = A.alloc("qsq", [512], F32); sqB = A.newbuf("qsq")
        rs = A.alloc("qrs", [512], F32); rsB = A.newbuf("qrs")
        qn = A.alloc("qqn", [512], F32); qnB = A.newbuf("qqn")
        t1 = A.alloc("qt1", [512], F32); t1B = A.newbuf("qt1")
        return (sq, sqB, rs, rsB, t1, t1B, qn, qnB, t1, t1B)

    def dbg_to_x(ap3, bufs, nch, ncol):
        emit(dve, lambda e: e.tensor_copy(out=xT[:, 0:nch, 0:ncol], in_=ap3), bufs, [],
             pwrites=[xTb[c][n] for c in range(KC) for n in range(2)])

    def dbg_dump(ap2d, bufs, ncols):
        emit(sp, lambda e: e.dma_start(out=dbg_d[:, 0:ncols], in_=ap2d), bufs, [], dma=P.buf("dbg"))

    def layer(l, G):
        T, NT, j, sample = G["T"], G["T"] // 512, G["j"], G["sample"]
        if l == 1 or not G.get("first", False):
            bg_drain()
        segs = G["segs"]
        PP, PC = 8, 16
        base = A.mark()
        qT = A.alloc("qT", [4, T], BF16); qTB = A.newbuf("qT")
        nseg = len(segs)
        Lseg = segs[0][1]
        UW = Lseg + 2 * PP
        GW = Lseg + 2 * PC
        if not sample:
            kTp = A.alloc("kTp", [4, T], BF16); kTpB = A.newbuf("kTp")
            vTp = A.alloc("vTp", [T // 128, 512], BF16); vTpB = A.newbuf("vTp")
            fTp = A.alloc("fTp", [4, T], BF16); fTpB = A.newbuf("fTp")
        m_q = A.mark()
        uP = A.alloc("uP", [4, nseg * UW], F32); uPB = [A.newbuf("uP%d" % g) for g in range(4)]
        gT = A.alloc("gT", [4, nseg * GW], F32); gTB = [A.newbuf("gT%d" % c) for c in range(4)]
        m_ug = A.mark()
        hT = A.alloc("hT", [KC, T], BF16)
        hTb = [[A.newbuf("hT%d_%d" % (c, n)) for n in range(NT)] for c in range(KC)]
        norm(l, 0, T, j, hT, hTb)

        def hrhs(k, n):
            return (hTb[k][n], hT[:, k, n * 512:(n + 1) * 512])
        stg_tag = "%d%s" % (l, "s" if sample else "p")
        if stop_after == "norm" + stg_tag:
            return False

        def proj_jobs(col0, nchunks, evac):
            for blk in range(nchunks // 2):
                bg_tick()
                wb, wv = wload(wsrc(w_in_d, l, 0, D, col0 + blk * 256, 256), KC, 256)
                for sub in range(2):
                    ch = blk * 2 + sub
                    bl = job(KC, NT, 512, lambda k, wv=wv, sub=sub, wb=wb: (wb, wv[:, k, sub * 128:(sub + 1) * 128]),
                             hrhs)
                    for n in range(NT):
                        evac(ch, n, bl[n])

        mfr = A.mark()
        tmps = qk_tmps()
        if sample:
            kst = [A.alloc("kst%d" % i, [512], BF16) for i in range(2)]
            kstB = [A.newbuf("kst%d" % i) for i in range(2)]
            kcnt = [0]

            def k_evac(ch, n, b):
                i = kcnt[0] % 2
                kcnt[0] += 1
                qknorm(b, n, C_GK, l, True, kst[i], kstB[i], tmps=tmps)
                emit(sp, lambda e, i=i, ch=ch, n=n: e.dma_start(
                    out=xsA[l].ap()[ch * 128:(ch + 1) * 128, n * 512:(n + 1) * 512], in_=kst[i]),
                    [kstB[i]], [], pwrites=[xsendQ[l][0]], dma=kstB[i])
        else:
            kst2 = [A.alloc("nkst%d" % i, [512], F32) for i in range(2)]
            kst2B = [A.newbuf("nkst%d" % i) for i in range(2)]
            kcnt = [0]

            def k_evac(ch, n, b):
                sq, sqB, rs, rsB, rt, rtB, qn, qnB, t1, t1B = tmps
                emit(act, lambda e: e.activation(out=sq, in_=banks[b][:], func=AF.Square), [bankB[b]], [sqB])
                b2 = mbank()
                emit(pe, lambda e: e.matmul(banks[b2][:], lhsT=bones[:], rhs=sq, start=True, stop=True),
                     [sqB, constB], [bankB[b2]])
                rstd_from(banks[b2][:], 1.0 / 64, rs, [bankB[b2]], [rsB], rt, rtB)
                emit(dve, lambda e: e.scalar_tensor_tensor(out=qn, in0=banks[b][:], scalar=pv[:, l, C_GK:C_GK + 1],
                                                           in1=rs, op0=ALU.mult, op1=ALU.mult),
                     [bankB[b], rsB, pvB], [qnB])
                emit(act, lambda e: e.activation(out=kTp[:, ch, :], in_=qn, func=AF.Identity), [qnB], [],
                     pwrites=[kTpB])
                b3 = mbank()
                for q in range(4):
                    emit(pe, lambda e, q=q: e.transpose(out=banks[b3][:, q * 128:(q + 1) * 128],
                                                        in_=qn[:, q * 128:(q + 1) * 128], identity=ident[:]),
                         [qnB, constB], [bankB[b3]] if q == 0 else [], sig=(q == 3))
                i = kcnt[0] % 2
                kcnt[0] += 1
                copy_on(ev_eng(), kst2[i], banks[b3][:], [bankB[b3]], [kst2B[i]])
                for sq_ in range(2):
                    emit(sp, lambda e, i=i, ch=ch, sq_=sq_: e.dma_start(
                        out=nk_d[sq_, l, :, ch * 128:(ch + 1) * 128].rearrange("(h p) f -> p h f", p=128),
                        in_=kst2[i].rearrange("p (q f) -> p q f", q=4)[:, 2 * sq_:2 * sq_ + 2, :]),
                        [kst2B[i]], [], pwrites=[outB_all], dma=kst2B[i])
        proj_jobs(1024, 4, k_evac)
        if stop_after == "kproj" + stg_tag:
            return False

        if sample:
            vst, vstB = kst, kstB
        else:
            vst = [A.alloc("vst%d" % i, [512], BF16) for i in range(2)]
            vstB = [A.newbuf("vst%d" % i) for i in range(2)]
        if not sample:
            vsf = [A.alloc("vsf%d" % i, [512], F32) for i in range(2)]
            vsfB = [A.newbuf("vsf%d" % i) for i in range(2)]
        wv_blocks = [wload(wsrc(w_in_d, l, 0, D, 1536 + hb * 256, 256), KC, 256) for hb in range(2)]
        for i in range(T // 128):
            b = dbank()
            n = i // 4
            for hb in range(2):
                wb, wv = wv_blocks[hb]
                for k in range(KC):
                    emit(pe, lambda e, b=b, hb=hb, k=k, wv=wv, i=i: e.matmul(
                        banks[b][:, hb * 256:(hb + 1) * 256], lhsT=hT[:, k, i * 128:(i + 1) * 128], rhs=wv[:, k, :],
                        start=(k == 0), stop=(k == KC - 1)),
                        [wb, hTb[k][n]], [bankB[b]] if (k == 0 and hb == 0) else [],
                        sig=(k == KC - 1 and hb == 1))
            s_ = i % 2
            VV_ = int(os.environ.get("VVAR", "0"))
            if VV_ == 1:
                continue
            if sample:
                copy_on(ev_eng(), vst[s_], banks[b][:], [bankB[b]], [vstB[s_]])
                emit(sp, lambda e, i=i, s_=s_: e.dma_start(
                    out=xsB[l].ap()[512:1024, :].rearrange("r (a c) -> (r a) c", a=2)[i * 128:(i + 1) * 128, :],
                    in_=vst[s_]), [vstB[s_]], [], pwrites=[xsendQ[l][1]], dma=vstB[s_])
            else:
                emit(dve, lambda e, b=b, s_=s_: e.tensor_copy(out=vsf[s_], in_=banks[b][:]), [bankB[b]], [vsfB[s_]])
                copy_on(act, vTp[:, i, :], vsf[s_], [vsfB[s_]], [], pwrites=[vTpB])
                if VV_ == 3:
                    continue
                emit(sp, lambda e, i=i, s_=s_: e.dma_start(
                    out=nv_d[i // 2, l, (i % 2) * 128:(i % 2 + 1) * 128, :], in_=vsf[s_]),
                    [vsfB[s_]], [], pwrites=[outB_all], dma=vsfB[s_])

        if stop_after == "vproj" + stg_tag:
            return False
        if sample:
            fst, fstB = kst, kstB
            fcnt = [0]

            def f_evac(ch, n, b):
                i = fcnt[0] % 2
                fcnt[0] += 1
                copy_on(ev_eng(), fst[i], banks[b][:], [bankB[b]], [fstB[i]])
                emit(sp, lambda e, i=i, ch=ch, n=n: e.dma_start(
                    out=xsB[l].ap()[ch * 128:(ch + 1) * 128, n * 512:(n + 1) * 512], in_=fst[i]),
                    [fstB[i]], [], pwrites=[xsendQ[l][1]], dma=fstB[i])
        else:
            def f_evac(ch, n, b):
                copy_on(ev_eng(), fTp[:, ch, n * 512:(n + 1) * 512], banks[b][:], [bankB[b]], [], pwrites=[fTpB])
        proj_jobs(3072, 4, f_evac)
        if sample:
            emit(pool, lambda e: e.collective_compute(
                "AllGather", ALU.bypass, replica_groups=[[0, 1], [2, 3], [4, 5], [6, 7]],
                ins=[xsB_t[l].ap().opt()], outs=[xrB_t[l].ap().opt()]),
                [xsendQ[l][1]], [xrecvQ[l][1]], inc=(cc_keys[l][1], 1))

        def seg_cols(n, pad, W):
            out = []
            for si, (s0, L) in enumerate(segs):
                lo = max(s0, n * 512)
                hi = min(s0 + L, (n + 1) * 512)
                if lo < hi:
                    out.append((lo - n * 512, hi - lo, si * W + pad + lo - s0))
            return out

        def p_evac(ch, n, b):
            for (o, nc_, dc) in seg_cols(n, PP, UW):
                copy_on(ev_eng(), uP[:, ch, dc:dc + nc_], banks[b][:, o:o + nc_], [bankB[b]], [], pwrites=[uPB[ch]])
        proj_jobs(0, 4, p_evac)

        sg = A.alloc("sg", [T], F32)
        sgB = A.newbuf("sg")
        for half in range(2):
            bg_tick()
            wbb, wvb = wload(wsrc(w_in_d, l, 0, D, 2560 + half * 256, 256), KC, 256)
            wba, wva = wload(wsrc(w_in_d, l, 0, D, 2048 + half * 256, 256), KC, 256)
            for sub in range(2):
                cch = half * 2 + sub
                bl = job(KC, NT, 512, lambda k, sub=sub: (wbb, wvb[:, k, sub * 128:(sub + 1) * 128]), hrhs)
                for n in range(NT):
                    emit(act, lambda e, n=n, b=bl[n]: e.activation(out=sg[:, n * 512:(n + 1) * 512], in_=banks[b][:],
                                                                    func=AF.Sigmoid), [bankB[bl[n]]], [], pwrites=[sgB])
                bl = job(KC, NT, 512, lambda k, sub=sub: (wba, wva[:, k, sub * 128:(sub + 1) * 128]), hrhs)
                for n in range(NT):
                    for (o, nc_, dc) in seg_cols(n, PC, GW):
                        emit(dve, lambda e, o=o, nc_=nc_, dc=dc, n=n, b=bl[n], cch=cch: e.tensor_tensor(
                            out=gT[:, cch, dc:dc + nc_], in0=banks[b][:, o:o + nc_],
                            in1=sg[:, n * 512 + o:n * 512 + o + nc_], op=ALU.mult),
                            [bankB[bl[n]], sgB], [], pwrites=[gTB[cch]])

        if sample:
            hal = xsA[l].ap()[512:544, :].rearrange("r (f e) -> (r f) e", e=64).rearrange("(g p) e -> p g e", p=128)
            hsd = A.alloc("hsd", [4, 64], BF16); hsdB = A.newbuf("hsd")
            emit(dve, lambda e: e.memset(hsd, 0.0), [], [hsdB])
            emit(dve, lambda e: e.tensor_copy(out=hsd[:, :, 0:8], in_=uP[:, :, PP:PP + 8]), uPB, [], pwrites=[hsdB])
            emit(dve, lambda e: e.tensor_copy(out=hsd[:, :, 8:16], in_=uP[:, :, PP + Lseg - 8:PP + Lseg]), uPB, [], pwrites=[hsdB])
            emit(dve, lambda e: e.tensor_copy(out=hsd[:, :, 16:32], in_=gT[:, :, PC:PC + 16]), gTB, [], pwrites=[hsdB])
            emit(dve, lambda e: e.tensor_copy(out=hsd[:, :, 32:48], in_=gT[:, :, PC + Lseg - 16:PC + Lseg]), gTB, [], pwrites=[hsdB])
            emit(sp, lambda e: e.dma_start(out=hal, in_=hsd), [hsdB], [], pwrites=[xsendQ[l][0]], dma=hsdB)
            emit(pool, lambda e: e.collective_compute(
                "AllGather", ALU.bypass, replica_groups=[[0, 1], [2, 3], [4, 5], [6, 7]],
                ins=[xsA_t[l].ap().opt()], outs=[xrA_t[l].ap().opt()]),
                [xsendQ[l][0]], [xrecvQ[l][0]], inc=(cc_keys[l][0], 1))
        else:
            for g in range(4):
                for si in range(nseg):
                    emit(dve, lambda e, g=g, si=si: e.memset(uP[:, g, si * UW:si * UW + PP], 0.0), [], [], pwrites=[uPB[g]])
                    emit(dve, lambda e, g=g, si=si: e.memset(uP[:, g, si * UW + PP + Lseg:(si + 1) * UW], 0.0), [], [], pwrites=[uPB[g]])
                    emit(dve, lambda e, g=g, si=si: e.memset(gT[:, g, si * GW:si * GW + PC], 0.0), [], [], pwrites=[gTB[g]])
                    emit(dve, lambda e, g=g, si=si: e.memset(gT[:, g, si * GW + PC + Lseg:(si + 1) * GW], 0.0), [], [], pwrites=[gTB[g]])

        def q_evac(ch, n, b):
            if sample:
                qknorm(b, n, C_GQ, l, True, qT[:, ch, n * 512:(n + 1) * 512], qTB, tmps=tmps)
            else:
                qknorm(b, n, C_GQ, l, False, qT[:, ch, n * 512:(n + 1) * 512], qTB, tmps=tmps)
        proj_jobs(512, 4, q_evac)
        A.release(m_ug)
        if stop_after == "proj" + stg_tag:
            return False

        g1c = 32

        def ymix_alloc():
            y = A.alloc("ymix", [4, T], BF16)
            yB = [[A.newbuf("ymix%d_%d" % (c, n)) for n in range(NT)] for c in range(4)]
            return y, yB

        bg_drain(9)
        mp = A.mark()
        if sample:
            hst = A.alloc("hst", [4, 64], BF16); hstB = A.newbuf("hst")
            hst2 = A.alloc("hst2", [4, 64], BF16); hst2B = A.newbuf("hst2")
            rv = xrA[l].ap()
            h0 = rv[512:544, :].rearrange("r (f e) -> (r f) e", e=64).rearrange("(g p) e -> p g e", p=128)
            h1 = rv[XRA + 512:XRA + 544, :].rearrange("r (f e) -> (r f) e", e=64).rearrange("(g p) e -> p g e", p=128)
            emit(sp, lambda e: e.dma_start(out=hst, in_=h0), [xrecvQ[l][0]], [hstB], dma=hstB)
            emit(sp, lambda e: e.dma_start(out=hst2, in_=h1), [xrecvQ[l][0]], [hst2B], dma=hst2B)
            emit(dve, lambda e: e.tensor_scalar(out=uP[:, :, 0:PP], in0=hst[:, :, 8:16], scalar1=hmask[:, 0:1],
                                                scalar2=None, op0=ALU.mult), [hstB, constB], [], pwrites=uPB)
            emit(dve, lambda e: e.tensor_scalar(out=uP[:, :, PP + Lseg:PP + Lseg + PP], in0=hst2[:, :, 0:8],
                                                scalar1=hmask[:, 1:2], scalar2=None, op0=ALU.mult),
                 [hst2B, constB], [], pwrites=uPB)
            emit(dve, lambda e: e.tensor_scalar(out=gT[:, :, 0:PC], in0=hst[:, :, 32:48], scalar1=hmask[:, 0:1],
                                                scalar2=None, op0=ALU.mult), [hstB, constB], [], pwrites=gTB)
            emit(dve, lambda e: e.tensor_scalar(out=gT[:, :, PC + Lseg:PC + Lseg + PC], in0=hst2[:, :, 16:32],
                                                scalar1=hmask[:, 1:2], scalar2=None, op0=ALU.mult),
                 [hst2B, constB], [], pwrites=gTB)
        icn = A.alloc("icn", [4, Lseg], F32); icnB = A.newbuf("icn")
        icd = icnts_d if sample else icntp_d
        emit(sp, lambda e: e.dma_start(out=icn, in_=icd.partition_broadcast(128)), [], [icnB], dma=icnB)
        sa = A.alloc("sa", [UW], F32); saB = A.newbuf("sa")
        sbb = A.alloc("sbb", [UW], F32); sbB = A.newbuf("sbb")
        pT = A.alloc("pT", [4, T], BF16); pTB = [A.newbuf("pT%d" % g) for g in range(4)]
        ym, ymB = ymix_alloc()
        for g in range(4):
            for si, (s0, L) in enumerate(segs):
                u = uP[:, g, si * UW:(si + 1) * UW]
                W = UW
                emit(dve, lambda e, u=u: e.tensor_tensor(out=sa[:, 1:W], in0=u[:, 0:W - 1], in1=u[:, 1:W], op=ALU.add),
                     [uPB[g]], [saB])
                cur, curB, oth, othB = sa, saB, sbb, sbB
                lo, hi, step = 1, W, 1
                for lev in range(g):
                    nlo, nhi = lo + step, hi - step
                    emit(dve, lambda e, cur=cur, oth=oth, nlo=nlo, nhi=nhi, step=step: e.tensor_tensor(
                        out=oth[:, nlo:nhi], in0=cur[:, nlo - step:nhi - step], in1=cur[:, nlo + step:nhi + step],
                        op=ALU.add), [curB], [othB])
                    cur, curB, oth, othB = oth, othB, cur, curB
                    lo, hi, step = nlo, nhi, step * 2
                emit(dve, lambda e, cur=cur, g=g: e.tensor_tensor(out=cur[:, PP:PP + L], in0=cur[:, PP:PP + L],
                                                                  in1=icn[:, g, :], op=ALU.mult), [curB, icnB], [curB])
                emit(dve, lambda e, cur=cur, u=u, g=g, s0=s0, L=L: e.tensor_tensor(
                    out=pT[:, g, s0:s0 + L], in0=cur[:, PP:PP + L], in1=u[:, PP:PP + L], op=ALU.subtract),
                    [curB, uPB[g]], [], pwrites=[pTB[g]])
        wb, wv = wload(pool_w_d[l].rearrange("g c d -> c g d"), 4, 128)
        for g in range(4):
            bl = job(1, NT, 512, lambda k, g=g: (wb, wv[:, g, :]), lambda k, n, g=g: (pTB[g], pT[:, g, n * 512:(n + 1) * 512]))
            for n in range(NT):
                emit(act, lambda e, g=g, n=n, b=bl[n]: e.activation(
                    out=ym[:, g, n * 512:(n + 1) * 512], in_=banks[b][:], func=AF.Identity,
                    scale=pv[:, l, C_PSC + g:C_PSC + g + 1]), [bankB[bl[n]], pvB], [ymB[g][n]])
        residual_partial(4, NT, l, g1c, j, w_out_d, 0, ym, lambda k, n: ymB[k][n])
        A.release(mp)
        if stop_after == "pool%d%s" % (l, "s" if sample else "p"):
            return False

        mc = A.mark()
        acc = A.alloc("acc", [4, T], F32); accB = [[A.newbuf("acc%d_%d" % (c, n)) for n in range(NT)] for c in range(4)]
        for c in range(4):
            for n in range(NT):
                pieces = seg_cols(n, PC, GW)
                for (o, nc_, dc) in pieces:
                    for jt in range(31):
                        src = gT[:, c, dc - 15 + jt:dc - 15 + jt + nc_]
                        dst = acc[:, c, n * 512 + o:n * 512 + o + nc_]
                        dwc = pv[:, l, C_DW + c * 31 + jt:C_DW + c * 31 + jt + 1]
                        if jt == 0:
                            emit(dve, lambda e, src=src, dst=dst, dwc=dwc, c=c: e.tensor_scalar(
                                out=dst, in0=src, scalar1=dwc, scalar2=pv[:, l, C_DWB + c:C_DWB + c + 1],
                                op0=ALU.mult, op1=ALU.add), [gTB[c], pvB], [], pwrites=[accB[c][n]])
                        else:
                            emit(dve, lambda e, src=src, dst=dst, dwc=dwc: e.scalar_tensor_tensor(
                                out=dst, in0=src, scalar=dwc, in1=dst, op0=ALU.mult, op1=ALU.add),
                                [gTB[c], pvB, accB[c][n]], [], pwrites=[accB[c][n]])
        if stop_after == "dacc" + stg_tag:
            dbg_to_x(acc, [b_ for r_ in accB for b_ in r_], 4, T)
            return False
        if stop_after == "dg" + stg_tag:
            dbg_to_x(gT[:, :, 0:nseg * GW], gTB, 4, nseg * GW)
            return False
        csq = [A.alloc("csq%d" % i, [512], F32) for i in range(2)]
        csqB = [A.newbuf("csq%d" % i) for i in range(2)]
        mean = A.alloc("cmean", [512], F32); meanB = A.newbuf("cmean")
        msq = A.alloc("cmsq", [512], F32); msqB = A.newbuf("cmsq")
        crs = A.alloc("crs", [512], F32); crsB = A.newbuf("crs")
        crt = A.alloc("crt", [512], F32); crtB = A.newbuf("crt")
        ct1 = [A.alloc("ct1%d" % i, [512], F32) for i in range(2)]
        ct1B = [A.newbuf("ct1%d" % i) for i in range(2)]
        yb = A.alloc("cyb", [4, T], BF16); ybB = [[A.newbuf("cyb%d_%d" % (c, n)) for n in range(NT)] for c in range(4)]
        ym, ymB = ymix_alloc()
        for n in range(NT):
            b1 = mbank()
            for c in range(4):
                emit(pe, lambda e, c=c, n=n, b1=b1: e.matmul(banks[b1][:], lhsT=ones_f[:], rhs=acc[:, c, n * 512:(n + 1) * 512],
                                                            start=(c == 0), stop=(c == 3)),
                     [accB[c][n], constB], [bankB[b1]] if c == 0 else [], sig=(c == 3))
            b2 = mbank()
            for c in range(4):
                i = c % 2
                emit(act, lambda e, c=c, n=n, i=i: e.activation(out=csq[i], in_=acc[:, c, n * 512:(n + 1) * 512],
                                                               func=AF.Square), [accB[c][n]], [csqB[i]])
                emit(pe, lambda e, c=c, i=i, b2=b2: e.matmul(banks[b2][:], lhsT=ones_f[:], rhs=csq[i],
                                                            start=(c == 0), stop=(c == 3)),
                     [csqB[i], constB], [bankB[b2]] if c == 0 else [], pwrites=[] if c == 0 else [bankB[b2]], sig=True)
            emit(dve, lambda e, b1=b1: e.tensor_scalar(out=mean, in0=banks[b1][:], scalar1=1.0 / 512, scalar2=None,
                                                       op0=ALU.mult), [bankB[b1]], [meanB])
            emit(dve, lambda e: e.tensor_tensor(out=msq, in0=mean, in1=mean, op=ALU.mult), [meanB], [msqB])
            emit(dve, lambda e, b2=b2: e.scalar_tensor_tensor(out=msq, in0=banks[b2][:], scalar=1.0 / 512, in1=msq,
                                                              op0=ALU.mult, op1=ALU.subtract), [bankB[b2], msqB], [msqB])
            rstd_from(msq, 1.0, crs, [msqB], [crsB], crt, crtB)
            for c in range(4):
                i = c % 2
                emit(dve, lambda e, c=c, n=n, i=i: e.tensor_tensor(out=ct1[i], in0=acc[:, c, n * 512:(n + 1) * 512],
                                                                  in1=mean, op=ALU.subtract), [accB[c][n], meanB], [ct1B[i]])
                emit(dve, lambda e, i=i: e.tensor_tensor(out=ct1[i], in0=ct1[i], in1=crs, op=ALU.mult),
                     [ct1B[i], crsB], [ct1B[i]])
                emit(act, lambda e, c=c, n=n, i=i: e.activation(
                    out=yb[:, c, n * 512:(n + 1) * 512], in_=ct1[i], func=AF.Silu,
                    bias=pv[:, l, C_LNB + c:C_LNB + c + 1], scale=pv[:, l, C_LNG + c:C_LNG + c + 1]),
                    [ct1B[i], pvB], [ybB[c][n]])
        if stop_after == "dyb" + stg_tag:
            dbg_to_x(yb, [b_ for r_ in ybB for b_ in r_], 4, T)
            return False
        wb, wv = wload(wsrc(conv_pw_d, l, 0, 512, 0, 512), 4, 512)
        for mo in range(4):
            bl = job(4, NT, 512, lambda k, mo=mo: (wb, wv[:, k, mo * 128:(mo + 1) * 128]),
                     lambda k, n: (ybB[k][n], yb[:, k, n * 512:(n + 1) * 512]))
            for n in range(NT):
                emit(act, lambda e, mo=mo, n=n, b=bl[n]: e.activation(
                    out=ym[:, mo, n * 512:(n + 1) * 512], in_=banks[b][:], func=AF.Identity,
                    bias=pv[:, l, C_PWB + mo:C_PWB + mo + 1], scale=1.0), [bankB[bl[n]], pvB], [ymB[mo][n]])
        if stop_after == "dym" + stg_tag:
            dbg_to_x(ym, [b_ for r_ in ymB for b_ in r_], 4, T)
            return False
        if stop_after == "dnores" + stg_tag:
            return False
        residual_partial(4, NT, l, g1c, j, w_out_d, 1024, ym, lambda k, n: ymB[k][n], overwrite=(stop_after == "dres" + stg_tag))
        if stop_after == "dres" + stg_tag:
            return False
        A.release(mc)
        A.release(m_q)
        if stop_after == "conv%d%s" % (l, "s" if sample else "p"):
            return False

        ma = A.mark()
        ym, ymB = ymix_alloc()
        pt = [A.alloc("ptile%d" % i, [512], BF16) for i in range(3)]
        ptB = [A.newbuf("ptile%d" % i) for i in range(3)]
        r0 = A.alloc("ar0", [512], F32); r0B = A.newbuf("ar0")
        r1 = A.alloc("ar1", [512], F32); r1B = A.newbuf("ar1")
        o0 = A.alloc("ao0", [512], F32); o0B = A.newbuf("ao0")
        asq = A.alloc("asq", [512], F32); asqB = A.newbuf("asq")
        ars = A.alloc("ars", [512], F32); arsB = A.newbuf("ars")
        art = A.alloc("art", [512], F32); artB = A.newbuf("art")
        if sample:
            NKC = 20
            kall = A.alloc("kall", [4, NKC * 128], BF16); kallB = A.newbuf("kall")
            vall = A.alloc("vall", [NKC, 512], BF16); vallB = A.newbuf("vall")
            rv = xrA[l].ap()
            rvb = xrB[l].ap()
            for r in range(2):
                emit(sp, lambda e, r=r: e.dma_start(
                    out=kall[:, :, r * 1024:(r + 1) * 1024],
                    in_=rv[r * XRA:r * XRA + 512, :].rearrange("(h p) t -> p h t", p=128)),
                    [xrecvQ[l][0]], [], pwrites=[kallB], dma=kallB)
                emit(sp, lambda e, r=r: e.dma_start(
                    out=vall[:, r * 8:(r + 1) * 8, :],
                    in_=rvb[r * XRB + 512:r * XRB + 1024, :].rearrange("r (a c) -> (r a) c", a=2).rearrange(
                        "(i p) c -> p i c", p=128)), [xrecvQ[l][1]], [], pwrites=[vallB], dma=vallB)
            emit(pool, lambda e: e.dma_start(out=vall[:, 16:20, :],
                                             in_=cv_d[l].rearrange("(i p) c -> p i c", p=128)),
                 [], [], pwrites=[vallB], dma=vallB)
            cks = [A.alloc("cks%d" % i, [512], F32) for i in range(2)]
            cksB = [A.newbuf("cks%d" % i) for i in range(2)]
            for i in range(4):
                s_ = i % 2
                emit(sp, lambda e, i=i, s_=s_: e.dma_start(out=cks[s_], in_=ck_d[l, i * 128:(i + 1) * 128, :]),
                     [], [cksB[s_]], dma=cksB[s_])
                b = mbank()
                for h in range(4):
                    emit(pe, lambda e, h=h, b=b, s_=s_: e.transpose(out=banks[b][:, h * 128:(h + 1) * 128],
                                                                   in_=cks[s_][:, h * 128:(h + 1) * 128], identity=ident[:]),
                         [cksB[s_], constB], [bankB[b]] if h == 0 else [], sig=(h == 3))
                copy_on(ev_eng(), kall[:, :, 2048 + i * 128:2048 + (i + 1) * 128],
                        banks[b][:].rearrange("p (h t) -> p h t", h=4), [bankB[b]], [], pwrites=[kallB])
            att_units = [(0, 512, n * 512, [(kall, kallB, vall, vallB, jc) for jc in range(NKC)]) for n in range(NT)]
        else:
            att_units = []
            for si, (s0, L) in enumerate(segs):
                att_units.append((1, L, s0, [(kTp, kTpB, vTp, vTpB, s0 // 128 + jc) for jc in range(L // 128)]))
        pcnt = [0]
        for h in range(4):
            for (_, NQ, q0, klist) in att_units:
                bO = [0, 1]
                bL = [2, 3]
                nk_ = len(klist)
                its = [(ki, mmap) for ki in range(nk_) for mmap in range(2)]
                base_i = pcnt[0]
                pcnt[0] += len(its)

                def e_qk(ix):
                    ki, mmap = its[ix]
                    kt_, ktB_, vt_, vtB_, jc = klist[ki]
                    bs = 4 + ((base_i + ix) % 2)
                    emit(pe, lambda e: e.matmul(
                        banks[bs][:, 0:NQ], lhsT=kt_[mmap * 64:(mmap + 1) * 64, h, jc * 128:(jc + 1) * 128],
                        rhs=qT[mmap * 64:(mmap + 1) * 64, h, q0:q0 + NQ], start=True, stop=True),
                        [ktB_, qTB], [bankB[bs]])

                def e_exp(ix):
                    bs = 4 + ((base_i + ix) % 2)
                    pi = (base_i + ix) % 3
                    emit(act, lambda e: e.activation(out=pt[pi][:, 0:NQ], in_=banks[bs][:, 0:NQ],
                                                     func=AF.Exp, scale=0.125), [bankB[bs]], [ptB[pi]])

                def e_pv(ix):
                    ki, mmap = its[ix]
                    kt_, ktB_, vt_, vtB_, jc = klist[ki]
                    pi = (base_i + ix) % 3
                    emit(pe, lambda e: e.matmul(
                        banks[bO[mmap]][:, 0:NQ], lhsT=vt_[:, jc, h * 128:(h + 1) * 128], rhs=pt[pi][:, 0:NQ],
                        start=(ki == 0), stop=(ki == nk_ - 1)),
                        [vtB_, ptB[pi]], [bankB[bO[mmap]]] if ki == 0 else [],
                        pwrites=[] if ki == 0 else [bankB[bO[mmap]]], sig=False)
                    emit(pe, lambda e: e.matmul(
                        banks[bL[mmap]][:, 0:NQ], lhsT=ones_b[:], rhs=pt[pi][:, 0:NQ],
                        start=(ki == 0), stop=(ki == nk_ - 1)),
                        [constB, ptB[pi]], [bankB[bL[mmap]]] if ki == 0 else [],
                        pwrites=[] if ki == 0 else [bankB[bL[mmap]]], sig=True)

                e_qk(0)
                for ix in range(len(its)):
                    if ix + 1 < len(its):
                        e_qk(ix + 1)
                    e_exp(ix)
                    e_pv(ix)
                emit(dve, lambda e, NQ=NQ: e.reciprocal(out=r0[:, 0:NQ], in_=banks[2][:, 0:NQ]), [bankB[2]], [r0B])
                emit(dve, lambda e, NQ=NQ: e.reciprocal(out=r1[:, 0:NQ], in_=banks[3][:, 0:NQ]), [bankB[3]], [r1B])
                emit(dve, lambda e, NQ=NQ: e.tensor_tensor(out=o0[:, 0:NQ], in0=banks[0][:, 0:NQ], in1=r0[:, 0:NQ],
                                                           op=ALU.mult), [bankB[0], r0B], [o0B])
                emit(dve, lambda e, NQ=NQ: e.tensor_tensor(out=r1[:, 0:NQ], in0=banks[1][:, 0:NQ], in1=r1[:, 0:NQ],
                                                           op=ALU.mult), [bankB[1], r1B], [r1B])
                emit(dve, lambda e, NQ=NQ: e.scalar_tensor_tensor(out=o0[:, 0:NQ], in0=r1[:, 0:NQ],
                                                                  scalar=small[:, 1 + l:2 + l], in1=o0[:, 0:NQ],
                                                                  op0=ALU.mult, op1=ALU.add), [r1B, o0B, smallB], [o0B])
                emit(act, lambda e, NQ=NQ: e.activation(out=asq[:, 0:NQ], in_=o0[:, 0:NQ], func=AF.Square), [o0B], [asqB])
                b2 = mbank()
                emit(pe, lambda e, NQ=NQ, b2=b2: e.matmul(banks[b2][:, 0:NQ], lhsT=ones_f[:], rhs=asq[:, 0:NQ],
                                                         start=True, stop=True), [asqB, constB], [bankB[b2]])
                rstd_from(banks[b2][:, 0:NQ], 1.0 / 128, ars[:, 0:NQ], [bankB[b2]], [arsB], art[:, 0:NQ], artB)
                nq_t = q0 // 512
                emit(dve, lambda e, NQ=NQ, h=h, q0=q0: e.scalar_tensor_tensor(
                    out=ym[:, h, q0:q0 + NQ], in0=o0[:, 0:NQ], scalar=small[:, 3 + l:4 + l], in1=ars[:, 0:NQ],
                    op0=ALU.mult, op1=ALU.mult), [o0B, arsB, smallB], [], pwrites=[ymB[h][nq_t]])
        residual_partial(4, NT, l, g1c, j, w_out_d, 512, ym, lambda k, n: ymB[k][n])
        A.release(ma)
        if stop_after == "att%d%s" % (l, "s" if sample else "p"):
            return False

        mf = A.mark()
        ym, ymB = ymix_alloc()
        FT = A.alloc("FT", [4, T], BF16); FTB = [[A.newbuf("FT%d_%d" % (c, n)) for n in range(NT)] for c in range(4)]
        if sample:
            NTC = LS // 128
            fall = A.alloc("fall", [2, LS], BF16)
            AB = A.alloc("AB", [NTC, 2, 256], BF16)
            tring = [A.alloc("tr%d" % i, [2, 512], BF16) for i in range(4)]
            tringB = [A.newbuf("tr%d" % i) for i in range(4)]
            tcnt = [0]
            rv = xrB[l].ap()
            fallB = A.newbuf("fall")
            ABB = A.newbuf("AB")
            for hp in range(2):
                for r in range(2):
                    emit(sp, lambda e, r=r, hp=hp: e.dma_start(
                        out=fall[:, :, r * 1024:(r + 1) * 1024],
                        in_=rv[r * XRB + hp * 256:r * XRB + (hp + 1) * 256, :].rearrange("(h p) t -> p h t", p=128)),
                        [xrecvQ[l][1]], [fallB] if r == 0 else [], pwrites=[fallB] if r == 1 else [], dma=fallB)
                for i in range(NTC):
                    b = mbank()
                    for hh in range(2):
                        emit(pe, lambda e, b=b, hh=hh, i=i: e.matmul(
                            banks[b][:, hh * 256:(hh + 1) * 256], lhsT=fall[:, hh, i * 128:(i + 1) * 128], rhs=dft128[:],
                            start=True, stop=True), [fallB, constB], [bankB[b]] if hh == 0 else [], sig=(hh == 1))
                    copy_on(ev_eng(), AB[:, i, :, :], banks[b][:].rearrange("p (h c) -> p h c", h=2), [bankB[b]],
                            [ABB] if i == 0 else [], pwrites=[ABB] if i > 0 else [])
                for kt in range(NT):
                    bl = [dbank(), dbank()]
                    for i in range(NTC):
                        ti = tcnt[0] % 4
                        tcnt[0] += 1
                        emit(sp, lambda e, ti=ti, i=i, kt=kt: e.dma_start(
                            out=tring[ti], in_=dfts_d[i * 128:(i + 1) * 128, :, kt * 512:(kt + 1) * 512]),
                            [], [tringB[ti]], dma=tringB[ti])
                        for hh in range(2):
                            for cs in range(2):
                                emit(pe, lambda e, hh=hh, cs=cs, i=i, ti=ti, b=bl[hh]: e.matmul(
                                    banks[b][:], lhsT=AB[:, i, hh, cs * 128:(cs + 1) * 128], rhs=tring[ti][:, cs, :],
                                    start=(i == 0 and cs == 0), stop=(i == NTC - 1 and cs == 1)),
                                    [ABB, tringB[ti]], [bankB[bl[hh]]] if (i == 0 and cs == 0) else [],
                                    pwrites=[] if (i == 0 and cs == 0) else [bankB[bl[hh]]],
                                    sig=(cs == 1 and hh == 1))
                    for hh in range(2):
                        copy_on(ev_eng(), FT[:, hp * 2 + hh, kt * 512:(kt + 1) * 512], banks[bl[hh]][:],
                                [bankB[bl[hh]]], [FTB[hp * 2 + hh][kt]])
        else:
            AB = A.alloc("ABp", [T // 128, 4, 256], BF16); ABB = A.newbuf("ABp")
            for i in range(T // 128):
                for hp in range(2):
                    b = mbank()
                    for hh in range(2):
                        h = hp * 2 + hh
                        emit(pe, lambda e, b=b, hh=hh, h=h, i=i: e.matmul(
                            banks[b][:, hh * 256:(hh + 1) * 256], lhsT=fTp[:, h, i * 128:(i + 1) * 128], rhs=dft128[:],
                            start=True, stop=True), [fTpB, constB], [bankB[b]] if hh == 0 else [], sig=(hh == 1))
                    copy_on(ev_eng(), AB[:, i, hp * 2:hp * 2 + 2, :], banks[b][:].rearrange("p (h c) -> p h c", h=2),
                            [bankB[b]], [], pwrites=[ABB])
            for h in range(4):
                b = dbank()
                for si, (s0, L) in enumerate(segs):
                    ntc = L // 128
                    for i in range(ntc):
                        for cs in range(2):
                            emit(pe, lambda e, b=b, h=h, i=i, cs=cs, s0=s0, L=L: e.matmul(
                                banks[b][:, s0:s0 + L], lhsT=AB[:, s0 // 128 + i, h, cs * 128:(cs + 1) * 128],
                                rhs=dftp[:, i, cs, :], start=(i == 0 and cs == 0), stop=(i == ntc - 1 and cs == 1)),
                                [ABB, constB], [bankB[b]] if (si == 0 and i == 0 and cs == 0) else [],
                                sig=(si == nseg - 1 and i == ntc - 1 and cs == 1))
                copy_on(ev_eng(), FT[:, h, 0:T], banks[b][:, 0:T], [bankB[b]], [FTB[h][0]])
        wb, wv = wload(wsrc(four_w_d, l, 0, 512, 0, 512), 4, 512)
        for mo in range(4):
            bl = job(4, NT, 512, lambda k, mo=mo: (wb, wv[:, k, mo * 128:(mo + 1) * 128]),
                     lambda k, n: (FTB[k][n], FT[:, k, n * 512:(n + 1) * 512]))
            for n in range(NT):
                copy_on(ev_eng(), ym[:, mo, n * 512:(n + 1) * 512], banks[bl[n]][:], [bankB[bl[n]]], [ymB[mo][n]])
        residual_partial(4, NT, l, g1c, j, w_out_d, 1536, ym, lambda k, n: ymB[k][n])
        A.release(mf)
        A.release(base)
        if stop_after == "mix%d%s" % (l, "s" if sample else "p"):
            return False

        bg_drain(25)
        mffn = A.mark()
        hT = A.alloc("hT2", [KC, T], BF16)
        hTb = [[A.newbuf("hT2%d_%d" % (c, n)) for n in range(NT)] for c in range(KC)]
        norm(l, 1, T, j, hT, hTb)

        def hrhs2(k, n):
            return (hTb[k][n], hT[:, k, n * 512:(n + 1) * 512])
        aT = A.alloc("aT", [12, T], BF16)
        sl = [A.alloc("sl%d" % i, [T], F32) for i in range(2)]
        slB = [[A.newbuf("sl%d_%d" % (i, n)) for n in range(NT)] for i in range(2)]
        scnt = [0]
        aTB = [[A.newbuf("aT%d_%d" % (c, n)) for n in range(NT)] for c in range(12)]
        for (g0, gs) in FFN_GROUPS:
            for blk in range(gs // 2):
                col0 = (g0 + blk * 2) * 128
                bg_tick()
                wbg, wvg = wload(wsrc(w_gate_d, l, 0, D, col0, 256), KC, 256)
                wbu, wvu = wload(wsrc(w_up_d, l, 0, D, col0, 256), KC, 256)
                for sub in range(2):
                    cc = blk * 2 + sub
                    si_ = scnt[0] % 2
                    scnt[0] += 1
                    bl = job(KC, NT, 512, lambda k, sub=sub: (wbg, wvg[:, k, sub * 128:(sub + 1) * 128]), hrhs2)
                    for n in range(NT):
                        emit(act, lambda e, b=bl[n], si_=si_, n=n: e.activation(
                            out=sl[si_][:, n * 512:(n + 1) * 512], in_=banks[b][:], func=AF.Silu),
                            [bankB[bl[n]]], [slB[si_][n]])
                    bl = job(KC, NT, 512, lambda k, sub=sub: (wbu, wvu[:, k, sub * 128:(sub + 1) * 128]), hrhs2)
                    for n in range(NT):
                        emit(dve, lambda e, b=bl[n], si_=si_, n=n, cc=cc: e.tensor_tensor(
                            out=aT[:, cc, n * 512:(n + 1) * 512], in0=banks[b][:], in1=sl[si_][:, n * 512:(n + 1) * 512],
                            op=ALU.mult), [bankB[bl[n]], slB[si_][n]], [aTB[cc][n]])
            bg_drain(32)
            residual_partial(gs, NT, l, 80, j, w_down_d, g0 * 128, aT, lambda k, n: aTB[k][n])
        A.release(mffn)
        if stop_after == "end" + stg_tag:
            return False
        return True

    outB_all = P.buf("outs")
    GP = dict(T=TP, j=0, sample=False, segs=[(0, LP), (LP, LP)], first=False)
    GS = dict(T=TS, j=1, sample=True, segs=[(0, TS)], first=True)
    finals = []
    done = False
    for (G, src, dst) in ((GS, xs_d, ys_d), (GP, xp_d, yp_d)):
        if stop_after == "mods":
            done = True
            break
        load_x(src, G["T"])
        if stop_after == "loadx":
            done = True
            break
        ok = True
        for l in range(2):
            ok = layer(l, G)
            if not ok:
                break
        if not ok:
            done = True
            break
        if stop_after == ("endp" if not G["sample"] else "ends"):
            done = True
            break
        finals += store_y(dst, G["T"])
    if stop_after == "mods":
        db = P.buf("dbg")
        emit(sp, lambda e: e.dma_start(out=dbg_d[:, 0:384], in_=modt[:].rearrange("p l m j -> p (l m j)")), [modB, gscB], [db], dma=db)
        emit(sp, lambda e: e.dma_start(out=dbg_d[:, 384:384 + 128], in_=gsc[:].rearrange("p l s c j -> p (l s c j)")), [modB, gscB], [], pwrites=[db], dma=db)
        emit(sp, lambda e: e.dma_start(out=dbg_d[:, 512:528], in_=small[:]), [smallB], [], pwrites=[db], dma=db)
        finals.append(db)
    elif stop_after:
        allx = [xTb[c][n] for c in range(KC) for n in range(2)]
        db = P.buf("dbg")
        Tl = G["T"]
        emit(sp, lambda e: e.dma_start(out=dbg_d.rearrange("p (c t) -> p c t", c=KC)[:, :, 0:Tl], in_=xT[:, :, 0:Tl]), allx, [db], dma=db)
        finals.append(db)
    emit(sp, lambda e: e.nop(), [], finals + [outB_all], sig=True)
    P.replay()


def _consts():
    c = {}
    c["ident"] = np.eye(128, dtype=np.float32)
    rot = np.zeros((128, 128), np.float32)
    for p in range(128):
        partner = p + 16 if (p % 32) < 16 else p - 16
        rot[partner, p] = 1.0
    c["rotm"] = rot
    bo = np.zeros((128, 128), np.float32)
    bo[:64, :64] = 1.0
    bo[64:, 64:] = 1.0
    c["bones"] = bo
    k = np.arange(128)
    ang = 2 * np.pi * np.outer(k, k) / 128.0
    c["dft128"] = (np.concatenate([np.cos(ang), np.sin(ang)], axis=1) / np.sqrt(128.0)).astype(np.float32)
    t = np.arange(LP)
    ang = 2 * np.pi * np.outer(t, t) / LP
    c["dftp"] = (np.stack([np.cos(ang), -np.sin(ang)], axis=1) / np.sqrt(LP)).astype(ml_dtypes.bfloat16)

    def icnt(L, t):
        out = []
        for w in (2, 4, 8, 16):
            lo = np.maximum(t - w // 2, 0)
            hi = np.minimum(t + w // 2 - 1, L - 1)
            out.append(1.0 / (hi - lo + 1))
        return np.stack(out).astype(np.float32)
    c["icnt_p"] = icnt(LP, np.arange(LP))
    c["icnt_s"] = [icnt(LS, np.arange(hf * TS, (hf + 1) * TS)) for hf in range(2)]
    inv = 1.0 / (10000.0 ** (np.arange(0, 32, 2, dtype=np.float32) / 32.0))
    ropes = []
    for hf in range(2):
        tt = np.arange(hf * TS, (hf + 1) * TS)
        row = (tt // 64).astype(np.float32)
        col = (tt % 64).astype(np.float32)
        tab = np.zeros((128, 2, TS), np.float32)
        for p in range(128):
            d = p % 64
            pos = row if d < 32 else col
            a = pos * inv[d % 16]
            tab[p, 0] = np.cos(a)
            tab[p, 1] = (-np.sin(a)) if (d % 32) < 16 else np.sin(a)
        ropes.append(tab)
    c["rope"] = ropes
    t = np.arange(LS, dtype=np.float64)
    dfts = []
    for hf in range(2):
        kk = np.arange(hf * TS, (hf + 1) * TS, dtype=np.float64)
        ang = 2 * np.pi * (np.outer(t, kk) % LS) / LS
        dfts.append((np.stack([np.cos(ang), -np.sin(ang)], axis=1) / np.sqrt(LS)).astype(ml_dtypes.bfloat16))
    c["dfts"] = dfts
    c["hmask"] = [np.tile(np.array([[float(hf), 1.0 - hf]], np.float32), (128, 1)) for hf in range(2)]
    return c


def _pvec(inp):
    out = np.zeros((2, 128, NV), np.float32)
    for l in range(2):
        o = out[l]
        o[:, C_G1:C_G1 + 16] = inp["g_norm1"][l].reshape(16, 128).T
        o[:, C_G2:C_G2 + 16] = inp["g_norm2"][l].reshape(16, 128).T
        o[:, C_BADA:C_BADA + 96] = inp["b_ada"][l].reshape(96, 128).T
        o[:, C_PSC:C_PSC + 4] = inp["pool_scale"][l].reshape(4, 128).T
        dw = inp["conv_dw"][l]
        for c in range(4):
            o[:, C_DW + c * 31:C_DW + (c + 1) * 31] = dw[:, c * 128:(c + 1) * 128].T
        o[:, C_DWB:C_DWB + 4] = inp["conv_dw_b"][l].reshape(4, 128).T
        o[:, C_LNG:C_LNG + 4] = inp["conv_ln_g"][l].reshape(4, 128).T
        o[:, C_LNB:C_LNB + 4] = inp["conv_ln_b"][l].reshape(4, 128).T
        o[:, C_PWB:C_PWB + 4] = inp["conv_pw_b"][l].reshape(4, 128).T
        o[:, C_GQ] = np.tile(inp["g_q"][l], 2)
        o[:, C_GK] = np.tile(inp["g_k"][l], 2)
        o[:, C_GSUB] = inp["g_subln"][l]
        for q in range(4):
            o[:64, C_LAM + q] = inp["lam"][l][q]
    return out


_CACHE = {}


def make_in_maps(inp, cores):
    inp = {k: np.ascontiguousarray(np.asarray(v)) for k, v in inp.items()}
    cst = _consts()
    pvec = _pvec(inp)
    maps = []
    for c in cores:
        b, hf = c // 2, c % 2
        m = {
            "xp": inp["x_prompt"][2 * c:2 * c + 2].reshape(TP, D),
            "xs": inp["x_sample"][b, hf * TS:(hf + 1) * TS],
            "ck": inp["cache_k"][b].reshape(2, PAST, 512),
            "cv": inp["cache_v"][b].reshape(2, PAST, 512),
            "cT": np.ascontiguousarray(np.concatenate([np.stack([inp["c_ctx"], inp["c"][b]], axis=1), np.zeros((D, 6), np.float32)], axis=1).reshape(KC, 128, 8).transpose(1, 0, 2)),
            "pvec": pvec,
            "ident": cst["ident"], "rotm": cst["rotm"], "bones": cst["bones"],
            "rope": cst["rope"][hf], "icnt_p": cst["icnt_p"], "icnt_s": cst["icnt_s"][hf],
            "dft128": cst["dft128"], "dftp": cst["dftp"], "dfts": cst["dfts"][hf], "hmask": cst["hmask"][hf],
        }
        for k in ("w_ada", "w_in", "pool_w", "conv_pw", "fourier_w", "w_out", "w_gate", "w_up", "w_down"):
            m[k] = inp[k]
        maps.append({k: np.ascontiguousarray(v) for k, v in m.items()})
    return maps


def kernel(**inputs):
    if "nc" not in _CACHE:
        _CACHE["nc"] = build_program()
    nc = _CACHE["nc"]
    maps = make_in_maps(inputs, list(range(NCORES)))
    res = run_bass_kernel_spmd(nc, maps, core_ids=list(range(NCORES)))
    R = res.results
    yp = np.zeros((16, LP, D), np.float32)
    ys = np.zeros((4, LS, D), np.float32)
    nk = np.zeros((16, 2, LP, 4, 2, 64), np.float32)
    nv = np.zeros((16, 2, LP, 4, 128), np.float32)
    for c in range(NCORES):
        b, hf = c // 2, c % 2
        yp[2 * c:2 * c + 2] = np.asarray(R[c]["yp"]).reshape(2, LP, D)
        ys[b, hf * TS:(hf + 1) * TS] = np.asarray(R[c]["ys"])
        nk[2 * c:2 * c + 2] = np.asarray(R[c]["nk"]).reshape(2, 2, LP, 4, 2, 64)
        nv[2 * c:2 * c + 2] = np.asarray(R[c]["nv"]).reshape(2, 2, LP, 4, 128)
    return (yp, ys, nk, nv)
```

```python
import contextlib
import os
import numpy as np
import ml_dtypes
import concourse.bass as bass
import concourse.mybir as mybir
from concourse.bass_utils import run_bass_kernel_spmd

F32 = mybir.dt.float32
BF16 = mybir.dt.bfloat16
AF = mybir.ActivationFunctionType
ALU = mybir.AluOpType

D = 2048
KC = 16
DFF = 5632
FC = 44
INC = 3584
EPS = 1e-6
NCORES = 8
TP = 512
TS = 1024
LP = 256
LS = 2048
PAST = 512
XR = 1568
C_G1, C_G2, C_BADA, C_PSC, C_DW, C_DWB, C_LNG, C_LNB, C_PWB = 0, 16, 32, 128, 132, 256, 260, 264, 268
C_GQ, C_GK, C_GSUB, C_LAM = 272, 273, 274, 275
NV = 280
LAM_INIT = [0.8 - 0.6 * float(np.exp(-0.3 * l)) for l in range(2)]
FFN_GROUPS = [(0, 12), (12, 12), (24, 12), (36, 8)]


class Buf:
    __slots__ = ("name", "w", "r", "dkey", "dcnt")

    def __init__(self, name):
        self.name = name
        self.w = {}
        self.r = {}
        self.dkey = {}
        self.dcnt = 0


class _Rec:
    def __init__(self):
        self.call = None

    def __getattr__(self, name):
        def f(*a, **k):
            self.call = (name, a, k)
            return None
        return f


class Eng:
    def __init__(self, name, key):
        self.name = name
        self.key = key
        self.ops = []
        self.cnt = 0
        self.waited = {}


class Prog:
    def __init__(self, nc, stack, n_dsem=100):
        self.nc = nc
        self.stack = stack
        self.sems = []
        self.pe = Eng("tensor", self._sem("s_pe"))
        self.act = Eng("scalar", self._sem("s_act"))
        self.dve = Eng("vector", self._sem("s_dve"))
        self.pool = Eng("gpsimd", self._sem("s_pool"))
        self.sp = Eng("sync", self._sem("s_sp"))
        self.engs = [self.pe, self.act, self.dve, self.pool, self.sp]
        self.free_dsems = {"sync": [], "gpsimd": [], "scalar": []}
        self.dsem_cnt = {}
        self.n_dsem = 0
        self.arena_deps = {}
        self.flip = 0

    def _sem(self, name):
        s = self.stack.enter_context(self.nc.semaphore(name))
        self.sems.append(s)
        return len(self.sems) - 1

    def buf(self, name):
        b = Buf(name)
        b.w = dict(self.arena_deps)
        return b

    def emit(self, eng, fn, reads=(), writes=(), pwrites=(), sig=True, dma=None, inc=None):
        waits = {}

        def need(d):
            for s, v in d.items():
                if waits.get(s, 0) < v:
                    waits[s] = v

        for b in reads:
            need(b.w)
        for b in writes:
            need(b.w)
            need(b.r)
        for b in pwrites:
            need(b.w)
            need(b.r)
        wl = []
        for s, v in waits.items():
            if s == eng.key and (v > eng.cnt or eng is self.pe):
                continue
            if eng.waited.get(s, 0) < v:
                eng.waited[s] = v
                wl.append((s, v))
        if dma is not None:
            if eng.name not in dma.dkey:
                fl = self.free_dsems[eng.name]
                if fl:
                    dma.dkey[eng.name] = fl.pop(0)
                else:
                    dma.dkey[eng.name] = self._sem("d%d" % self.n_dsem)
                    self.n_dsem += 1
                    self.dsem_cnt[dma.dkey[eng.name]] = 0
            dk = dma.dkey[eng.name]
            self.dsem_cnt[dk] += 16
            tok = (dk, self.dsem_cnt[dk])
            incr = (dk, 16)
        elif inc is not None:
            tok = inc
            incr = (inc[0], 1)
        else:
            tok = (eng.key, eng.cnt + 1)
            if sig:
                eng.cnt += 1
                incr = (eng.key, 1)
            else:
                incr = None
        rec = _Rec()
        fn(rec)
        assert rec.call is not None
        eng.ops.append((wl, rec.call, incr))
        for b in reads:
            if b.r.get(tok[0], 0) < tok[1]:
                b.r[tok[0]] = tok[1]
        for b in writes:
            b.w = {tok[0]: tok[1]}
            b.r = {}
        for b in pwrites:
            if b.w.get(tok[0], 0) < tok[1]:
                b.w[tok[0]] = tok[1]
        return tok

    def retire(self, bufs):
        for b in bufs:
            for en_, dk_ in b.dkey.items():
                self.free_dsems[en_].append(dk_)
            b.dkey = {}
            for d in (b.w, b.r):
                for s, v in d.items():
                    if self.arena_deps.get(s, 0) < v:
                        self.arena_deps[s] = v

    def check(self):
        semv = {}
        pos = {e.name: 0 for e in self.engs}
        progress = True
        while progress:
            progress = False
            for e in self.engs:
                while pos[e.name] < len(e.ops):
                    wl, fn, incr = e.ops[pos[e.name]]
                    if all(semv.get(s_, 0) >= v for s_, v in wl):
                        if incr is not None:
                            semv[incr[0]] = semv.get(incr[0], 0) + incr[1]
                        pos[e.name] += 1
                        progress = True
                    else:
                        break
        stuck = {e.name: (pos[e.name], len(e.ops)) for e in self.engs if pos[e.name] < len(e.ops)}
        if stuck:
            for e in self.engs:
                if pos[e.name] < len(e.ops):
                    wl, fn, incr = e.ops[pos[e.name]]
                    print("STUCK", e.name, pos[e.name], "/", len(e.ops), "waits", [(s_, v, semv.get(s_, 0)) for s_, v in wl])
            raise RuntimeError("deadlock in emitted program: %s" % stuck)
        print("check ok: ops per engine", {e.name: len(e.ops) for e in self.engs}, "nsems", len(self.sems))

    def replay(self):
        self.check()
        nc = self.nc
        sems = self.sems
        with nc.Block() as block:
            def mk(eng):
                def body(e):
                    for wl, fn, incr in eng.ops:
                        for s, v in wl:
                            e.wait_ge(sems[s], v)
                        ins = getattr(e, fn[0])(*fn[1], **fn[2])
                        if incr is not None:
                            ins.then_inc(sems[incr[0]], incr[1])
                return body
            block.tensor(mk(self.pe))
            block.scalar(mk(self.act))
            block.vector(mk(self.dve))
            block.gpsimd(mk(self.pool))
            block.sync(mk(self.sp))


class Arena:
    def __init__(self, prog, tensor, nwords):
        self.p = prog
        self.t = tensor
        self.n = nwords
        self.top = 0
        self.live = []

    def mark(self):
        return (self.top, len(self.live))

    def release(self, m):
        top, nl = m
        self.p.retire(self.live[nl:])
        del self.live[nl:]
        self.top = top

    def alloc(self, name, shape, dt):
        n = 1
        for s in shape:
            n *= s
        words = n if dt == F32 else (n + 1) // 2
        words = (words + 7) // 8 * 8
        assert self.top + words <= self.n, ("arena overflow", name, self.top, words, self.n)
        ap = self.t[:, self.top:self.top + words]
        self.top += words
        if dt != F32:
            ap = ap.bitcast(dt)
        ap = ap[:, 0:n]
        if len(shape) == 2:
            ap = ap.rearrange("p (a b) -> p a b", a=shape[0])
        elif len(shape) == 3:
            ap = ap.rearrange("p (a b c) -> p a b c", a=shape[0], b=shape[1])
        return ap

    def newbuf(self, name):
        b = self.p.buf(name)
        self.live.append(b)
        return b


def build_program(stop_after=None):
    nc = bass.Bass("TRN2", target_bir_lowering=False)
    stack = contextlib.ExitStack()
    with stack:
        _build(nc, stack, stop_after)
    return nc


def _build(nc, stack, stop_after):
    P = Prog(nc, stack)
    emit = P.emit
    pe, act, dve, pool, sp = P.pe, P.act, P.dve, P.pool, P.sp

    def din(name, shape, dt=F32):
        return nc.dram_tensor(name, list(shape), dt, kind="ExternalInput").ap()

    def dout(name, shape, dt=F32):
        return nc.dram_tensor(name, list(shape), dt, kind="ExternalOutput").ap()

    xp_d = din("xp", [TP, D])
    xs_d = din("xs", [TS, D])
    ck_d = din("ck", [2, PAST, 512])
    cv_d = din("cv", [2, PAST, 512])
    cT_d = din("cT", [128, KC, 8])
    pvec_d = din("pvec", [2, 128, NV])
    ident_d = din("ident", [128, 128])
    rotm_d = din("rotm", [128, 128])
    bones_d = din("bones", [128, 128])
    rope_d = din("rope", [128, 2, TS])
    icntp_d = din("icnt_p", [4, LP])
    icnts_d = din("icnt_s", [4, TS])
    dft128_d = din("dft128", [128, 256])
    dftp_d = din("dftp", [LP, 2, LP], BF16)
    dfts_d = din("dfts", [LS, 2, TS], BF16)
    hmask_d = din("hmask", [128, 2])
    w_ada_d = din("w_ada", [2, D, 6 * D])
    w_in_d = din("w_in", [2, D, INC])
    pool_w_d = din("pool_w", [2, 4, 128, 128])
    conv_pw_d = din("conv_pw", [2, 512, 512])
    four_w_d = din("fourier_w", [2, 512, 512])
    w_out_d = din("w_out", [2, D, D])
    w_gate_d = din("w_gate", [2, D, DFF])
    w_up_d = din("w_up", [2, D, DFF])
    w_down_d = din("w_down", [2, DFF, D])
    yp_d = dout("yp", [TP, D])
    ys_d = dout("ys", [TS, D])
    nk_d = dout("nk", [2, 2, LP, 512])
    nv_d = dout("nv", [2, 2, LP, 512])
    dbg_d = dout("dbg", [128, KC * TS]) if stop_after else None
    XRA, XRB = 544, 1024

    class _V:
        def __init__(self, t):
            self.t = t

        def ap(self):
            return self.t.ap().rearrange("p c -> (p c)").rearrange("(r c) -> r c", c=1024)
    xsA_t = [nc.dram_tensor("xsA%d" % l, [128, XRA * 8], BF16) for l in range(2)]
    xrA_t = [nc.dram_tensor("xrA%d" % l, [256, XRA * 8], BF16) for l in range(2)]
    xsB_t = [nc.dram_tensor("xsB%d" % l, [128, XRB * 8], BF16) for l in range(2)]
    xrB_t = [nc.dram_tensor("xrB%d" % l, [256, XRB * 8], BF16) for l in range(2)]
    xsA = [_V(t) for t in xsA_t]
    xrA = [_V(t) for t in xrA_t]
    xsB = [_V(t) for t in xsB_t]
    xrB = [_V(t) for t in xrB_t]
    xsendQ = [[P.buf("xsend%d_%d" % (l, q)) for q in range(2)] for l in range(2)]
    xrecvQ = [[P.buf("xrecv%d_%d" % (l, q)) for q in range(2)] for l in range(2)]
    cc_keys = [[P._sem("cc%d_%d" % (l, q)) for q in range(2)] for l in range(2)]

    def sb(name, shape, dt):
        return stack.enter_context(nc.sbuf_tensor("sb_" + name, list(shape), dt))

    xT = sb("xT", [128, KC, TS], F32)
    xTb = [[P.buf("xT%d_%d" % (c, n)) for n in range(2)] for c in range(KC)]
    NSLOT = 4
    wring = [sb("wr%d" % i, [128, 4096], BF16) for i in range(NSLOT)]
    wringB = [P.buf("wr%d" % i) for i in range(NSLOT)]
    wnext = [0]
    pv = sb("pv", [128, 2, NV], F32)
    pvB = P.buf("pv")
    modt = sb("modt", [128, 2, 96, 2], F32)
    modB = P.buf("modt")
    gsc = sb("gsc", [128, 2, 2, KC, 2], F32)
    gscB = P.buf("gsc")
    ident = sb("ident", [128, 128], F32)
    rotm = sb("rotm", [128, 128], F32)
    bones = sb("bones", [128, 128], F32)
    ones_f = sb("ones_f", [128, 128], F32)
    ones_b = sb("ones_b", [128, 128], BF16)
    constB = P.buf("const")
    rope = sb("rope", [128, 2, TS], F32)
    dft128 = sb("dft128", [128, 256], BF16)
    dftp = sb("dftp", [128, 2, 2, LP], BF16)
    hmask = sb("hmask", [128, 2], F32)
    sT = sb("sT", [128, KC, 8], BF16)
    small = sb("small", [128, 16], F32)
    smallB = P.buf("small")
    ARENA_WORDS = 23600
    arena_t = sb("arena", [128, ARENA_WORDS], F32)
    A = Arena(P, arena_t, ARENA_WORDS)
    banks = [stack.enter_context(nc.psum_tensor("ps%d" % i, [128, 512], F32)) for i in range(8)]
    bankB = [P.buf("ps%d" % i) for i in range(8)]
    dn = [0]
    mn = [0]

    def dbank():
        i = dn[0] % 6
        dn[0] += 1
        return i

    def mbank():
        i = 6 + mn[0] % 2
        mn[0] += 1
        return i

    def ev_eng():
        P.flip ^= 1
        return act if P.flip else dve

    def copy_on(eng, out, in_, reads, writes, pwrites=()):
        if eng is act:
            return emit(act, lambda e: e.activation(out=out, in_=in_, func=AF.Identity), reads, writes, pwrites)
        return emit(dve, lambda e: e.tensor_copy(out=out, in_=in_), reads, writes, pwrites)

    def wload(src, kc, cw):
        i = wnext[0] % NSLOT
        wnext[0] += 1
        view = wring[i][:, 0:kc * cw].rearrange("p (k c) -> p k c", k=kc)
        emit(pool, lambda e: e.dma_start(out=view, in_=src), [], [wringB[i]], dma=wringB[i])
        return wringB[i], view

    def wsrc(wd, l, r0, nrows, c0, cw):
        return wd[l, r0:r0 + nrows, c0:c0 + cw].rearrange("(k p) n -> p k n", p=128)

    eps_ap = small[:, 0:1]

    emit(sp, lambda e: e.dma_start(out=pv[:], in_=pvec_d.rearrange("l p v -> p l v")), [], [pvB], dma=pvB)
    cB = P.buf("cld")
    for (t, d) in ((ident, ident_d), (rotm, rotm_d), (bones, bones_d)):
        emit(sp, lambda e, t=t, d=d: e.dma_start(out=t[:], in_=d), [], [], pwrites=[constB], dma=cB)
    emit(sp, lambda e: e.dma_start(out=rope[:], in_=rope_d), [], [], pwrites=[constB], dma=cB)
    emit(sp, lambda e: e.dma_start(out=hmask[:], in_=hmask_d), [], [], pwrites=[constB], dma=cB)
    emit(sp, lambda e: e.dma_start(out=dftp[:], in_=dftp_d.rearrange("(i p) a k -> p i a k", p=128)), [], [],
         pwrites=[constB], dma=cB)
    emit(pool, lambda e: e.dma_start(out=dft128[:], in_=dft128_d), [], [], pwrites=[constB], dma=cB)
    emit(dve, lambda e: e.memset(ones_f[:], 1.0), [], [], pwrites=[constB])
    emit(dve, lambda e: e.memset(ones_b[:], 1.0), [], [], pwrites=[constB])
    emit(dve, lambda e: e.memset(small[:, 0:1], EPS), [], [smallB])

    m0 = A.mark()
    ctmp = A.alloc("ctmp", [KC, 8], F32)
    ctB = A.newbuf("ctmp")
    sTB = P.buf("sT")
    emit(sp, lambda e: e.dma_start(out=ctmp, in_=cT_d), [], [ctB], dma=ctB)
    emit(act, lambda e: e.activation(out=sT[:], in_=ctmp, func=AF.Silu), [ctB], [sTB])
    lp = A.alloc("lamp", [4], F32)
    lpB = A.newbuf("lamp")
    for l in range(2):
        for q in range(2):
            emit(dve, lambda e, l=l, q=q: e.tensor_tensor(
                out=lp[:, 2 * l + q:2 * l + q + 1], in0=pv[:, l, C_LAM + 2 * q:C_LAM + 2 * q + 1],
                in1=pv[:, l, C_LAM + 2 * q + 1:C_LAM + 2 * q + 2], op=ALU.mult), [pvB], [], pwrites=[lpB])
    bi = mbank()
    emit(pe, lambda e: e.matmul(banks[bi][:, 0:4], lhsT=ones_f[:], rhs=lp, start=True, stop=True),
         [lpB, constB], [bankB[bi]])
    le = A.alloc("lame", [4], F32)
    leB = A.newbuf("lame")
    emit(act, lambda e: e.activation(out=le, in_=banks[bi][:, 0:4], func=AF.Exp), [bankB[bi]], [leB])
    for l in range(2):
        emit(dve, lambda e, l=l: e.tensor_tensor(out=small[:, 5 + l:6 + l], in0=le[:, 2 * l + 1:2 * l + 2],
                                                  in1=le[:, 2 * l:2 * l + 1], op=ALU.subtract),
             [leB], [], pwrites=[smallB])
        emit(dve, lambda e, l=l: e.tensor_scalar(out=small[:, 1 + l:2 + l], in0=small[:, 5 + l:6 + l],
                                                  scalar1=-LAM_INIT[l], scalar2=None, op0=ALU.add),
             [smallB], [], pwrites=[smallB])
        emit(dve, lambda e, l=l: e.tensor_scalar(out=small[:, 3 + l:4 + l], in0=pv[:, l, C_GSUB:C_GSUB + 1],
                                                  scalar1=1.0 - LAM_INIT[l], scalar2=None, op0=ALU.mult),
             [pvB], [], pwrites=[smallB])

    def gsc_emit(l, s_):
        for j in range(2):
            emit(dve, lambda e, j=j: e.scalar_tensor_tensor(
                out=gsc[:, l, s_, :, j], in0=modt[:, l, 48 * s_ + 16:48 * s_ + 32, j], scalar=1.0,
                in1=pv[:, l, (C_G1 if s_ == 0 else C_G2):(C_G1 if s_ == 0 else C_G2) + KC],
                op0=ALU.add, op1=ALU.mult), [modB, pvB], [], pwrites=[gscB])

    def mods_gen(l, mlo, mhi, gsc_after=None):
        for blk in range(mlo // 2, mhi // 2):
            wb, wv = wload(wsrc(w_ada_d, l, 0, D, blk * 256, 256), KC, 256)
            mb = mbank()
            mps = banks[mb][:, 0:16].rearrange("p (m j) -> p m j", j=8)
            for sub in range(2):
                for k in range(KC):
                    first = (sub == 0 and k == 0)
                    emit(pe, lambda e, k=k, sub=sub: e.matmul(
                        mps[:, sub, :], lhsT=wv[:, k, sub * 128:(sub + 1) * 128], rhs=sT[:, k, :],
                        start=(k == 0), stop=(k == KC - 1)),
                        [wb, sTB], [bankB[mb]] if first else [], sig=(k == KC - 1 and sub == 1))
            for j in range(2):
                emit(dve, lambda e, j=j, blk=blk: e.tensor_tensor(
                    out=modt[:, l, 2 * blk:2 * blk + 2, j], in0=mps[:, :, j],
                    in1=pv[:, l, C_BADA + 2 * blk:C_BADA + 2 * blk + 2], op=ALU.add),
                    [bankB[mb], pvB], [], pwrites=[modB])
            yield 1
        if gsc_after is not None:
            gsc_emit(l, gsc_after)

    for _ in mods_gen(0, 0, 32, 0):
        pass

    def _bg_chain():
        yield from mods_gen(0, 32, 48)
        yield from mods_gen(0, 48, 80, 1)
        yield from mods_gen(0, 80, 96)
        yield from mods_gen(1, 0, 32, 0)
        yield from mods_gen(1, 32, 80, 1)
        yield from mods_gen(1, 80, 96)
    bg = [_bg_chain(), 0, 0]

    def bg_poll():
        if bg[0] is None:
            return False
        try:
            next(bg[0])
            bg[1] += 1
            return True
        except StopIteration:
            bg[0] = None
            return False

    def bg_tick(light=False):
        bg[2] += 1
        if light:
            bg_poll()
            bg_poll()
        elif bg[1] < 32:
            bg_poll()
        elif bg[2] % 2 == 0:
            bg_poll()

    def bg_drain(nblocks=None):
        while bg[0] is not None and (nblocks is None or bg[1] < nblocks):
            if not bg_poll():
                break
    A.release(m0)
    early = stop_after in ("mods", "loadx")

    def load_x(src_d, T):
        m = A.mark()
        stg = [A.alloc("xstg%d" % i, [D], F32) for i in range(2)]
        stgB = [A.newbuf("xstg%d" % i) for i in range(2)]
        for i in range(T // 128):
            s_ = i % 2
            emit(sp, lambda e, i=i, s_=s_: e.dma_start(out=stg[s_], in_=src_d[i * 128:(i + 1) * 128, :]),
                 [], [stgB[s_]], dma=stgB[s_])
            for c4 in range(4):
                b = mbank()
                for q in range(4):
                    c = c4 * 4 + q
                    emit(pe, lambda e, b=b, q=q, c=c, s_=s_: e.transpose(
                        out=banks[b][:, q * 128:(q + 1) * 128], in_=stg[s_][:, c * 128:(c + 1) * 128],
                        identity=ident[:]), [stgB[s_], constB], [bankB[b]] if q == 0 else [], sig=(q == 3))
                n = i // 4
                o = xT[:, c4 * 4:(c4 + 1) * 4, i * 128:(i + 1) * 128]
                copy_on(ev_eng(), o, banks[b][:].rearrange("p (q t) -> p q t", q=4), [bankB[b]], [],
                        pwrites=[xTb[c4 * 4 + q][n] for q in range(4)])
        A.release(m)

    def store_y(dst_d, T):
        m = A.mark()
        stg = [A.alloc("ystg%d" % i, [D], F32) for i in range(2)]
        stgB = [A.newbuf("ystg%d" % i) for i in range(2)]
        last = []
        for i in range(T // 128):
            s_ = i % 2
            n = i // 4
            for c4 in range(4):
                b = mbank()
                for q in range(4):
                    c = c4 * 4 + q
                    emit(pe, lambda e, b=b, q=q, c=c, i=i: e.transpose(
                        out=banks[b][:, q * 128:(q + 1) * 128], in_=xT[:, c, i * 128:(i + 1) * 128],
                        identity=ident[:]), [xTb[c][n], constB], [bankB[b]] if q == 0 else [], sig=(q == 3))
                if c4 == 0:
                    copy_on(ev_eng(), stg[s_][:, 0:512], banks[b][:], [bankB[b]], [stgB[s_]])
                else:
                    copy_on(ev_eng(), stg[s_][:, c4 * 512:(c4 + 1) * 512], banks[b][:], [bankB[b]], [],
                            pwrites=[stgB[s_]])
            tok = emit(sp, lambda e, i=i, s_=s_: e.dma_start(out=dst_d[i * 128:(i + 1) * 128, :], in_=stg[s_]),
                       [stgB[s_]], [], dma=stgB[s_])
            last.append(stgB[s_])
        A.release(m)
        return last

    def rstd_from(ps_ap, scale, out_ap, rB, wB, tmp_ap, tmpB):
        emit(act, lambda e: e.activation(out=tmp_ap, in_=ps_ap, func=AF.Sqrt, bias=eps_ap, scale=scale),
             rB + [smallB], [tmpB])
        emit(dve, lambda e: e.reciprocal(out=out_ap, in_=tmp_ap), [tmpB], wB)

    def norm(l, s, T, j, hT, hTb):
        m = A.mark()
        sq = [A.alloc("sq%d" % i, [512], BF16) for i in range(2)]
        sqB = [A.newbuf("sq%d" % i) for i in range(2)]
        tm = [A.alloc("ntm%d" % i, [512], F32) for i in range(2)]
        tmB = [A.newbuf("ntm%d" % i) for i in range(2)]
        rs = A.alloc("nrs", [512], F32)
        rsB = A.newbuf("nrs")
        rt = A.alloc("nrt", [512], F32)
        rtB = A.newbuf("nrt")
        for n in range(T // 512):
            b = mbank()
            for c in range(KC):
                i = c % 2
                if c % 2 == 0:
                    emit(act, lambda e, c=c, n=n, i=i: e.activation(out=sq[i], in_=xT[:, c, n * 512:(n + 1) * 512],
                                                                   func=AF.Square), [xTb[c][n]], [sqB[i]])
                else:
                    emit(dve, lambda e, c=c, n=n, i=i: e.tensor_tensor(out=sq[i], in0=xT[:, c, n * 512:(n + 1) * 512],
                                                                      in1=xT[:, c, n * 512:(n + 1) * 512], op=ALU.mult),
                         [xTb[c][n]], [sqB[i]])
                emit(pe, lambda e, c=c, i=i, b=b: e.matmul(banks[b][:], lhsT=ones_b[:], rhs=sq[i],
                                                          start=(c == 0), stop=(c == KC - 1)),
                     [sqB[i], constB], [bankB[b]] if c == 0 else [], pwrites=[] if c == 0 else [bankB[b]], sig=True)
            import os
            NP_ = int(os.environ.get("NORM_PARTS", "3"))
            if NP_ < 2:
                continue
            rstd_from(banks[b][:], 1.0 / D, rs, [bankB[b]], [rsB], rt, rtB)
            if NP_ < 3:
                continue
            for c in range(KC):
                i = c % 2
                emit(dve, lambda e, c=c, n=n, i=i: e.tensor_tensor(out=tm[i], in0=xT[:, c, n * 512:(n + 1) * 512],
                                                                  in1=rs, op=ALU.mult), [xTb[c][n], rsB], [tmB[i]])
                NV_ = int(os.environ.get("NORM_VAR", "0"))
                if NV_ == 1:
                    emit(act, lambda e, c=c, n=n, i=i: e.activation(
                        out=hT[:, c, n * 512:(n + 1) * 512], in_=tm[i], func=AF.Identity),
                        [tmB[i], modB, gscB], [hTb[c][n]])
                elif NV_ == 2:
                    pass
                elif NV_ == 3:
                    emit(act, lambda e, c=c, n=n, i=i: e.activation(
                        out=hT[:, c, n * 512:(n + 1) * 512], in_=tm[i], func=AF.Identity,
                        bias=small[:, 0:1], scale=small[:, 0:1]),
                        [tmB[i], modB, gscB], [hTb[c][n]])
                else:
                    emit(act, lambda e, c=c, n=n, i=i: e.activation(
                        out=hT[:, c, n * 512:(n + 1) * 512], in_=tm[i], func=AF.Identity,
                        bias=modt[:, l, 48 * s + c, j:j + 1], scale=gsc[:, l, s, c, j:j + 1]),
                        [tmB[i], modB, gscB], [hTb[c][n]])
        A.release(m)

    def job(K, NT, ncols, lhs_fn, rhs_fn):
        bl = [dbank() for _ in range(NT)]
        for k in range(K):
            wb, lhsT = lhs_fn(k)
            for n in range(NT):
                rb, rhs = rhs_fn(k, n)
                emit(pe, lambda e, b=bl[n], lhsT=lhsT, rhs=rhs, k=k: e.matmul(
                    banks[b][:, 0:ncols], lhsT=lhsT, rhs=rhs, start=(k == 0), stop=(k == K - 1)),
                    [wb, rb], [bankB[bl[n]]] if k == 0 else [], sig=(k == K - 1 and n == NT - 1))
        return bl

    def residual_partial(K, NT, l, gate_chunk0, j, wd, r0, rhs_ap, rhsB_fn, overwrite=False):
        cw = 512 if K <= 8 else 256
        for ob in range(D // cw):
            bg_tick(light=(K == 4))
            wb, wv = wload(wsrc(wd, l, r0, K * 128, ob * cw, cw), K, cw)
            for sub in range(cw // 128):
                mo = ob * (cw // 128) + sub
                bl = job(K, NT, 512,
                         lambda k, wv=wv, sub=sub: (wb, wv[:, k, sub * 128:(sub + 1) * 128]),
                         lambda k, n: (rhsB_fn(k, n), rhs_ap[:, k, n * 512:(n + 1) * 512]))
                for n in range(NT):
                    if overwrite:
                        emit(dve, lambda e, b=bl[n], mo=mo, n=n: e.tensor_scalar(
                            out=xT[:, mo, n * 512:(n + 1) * 512], in0=banks[b][:],
                            scalar1=modt[:, l, gate_chunk0 + mo, j:j + 1], scalar2=None, op0=ALU.mult),
                            [bankB[bl[n]], modB], [xTb[mo][n]])
                        continue
                    emit(dve, lambda e, b=bl[n], mo=mo, n=n: e.scalar_tensor_tensor(
                        out=xT[:, mo, n * 512:(n + 1) * 512], in0=banks[b][:],
                        scalar=modt[:, l, gate_chunk0 + mo, j:j + 1], in1=xT[:, mo, n * 512:(n + 1) * 512],
                        op0=ALU.mult, op1=ALU.add), [bankB[bl[n]], modB, xTb[mo][n]], [xTb[mo][n]])

    def qknorm(b, n, gcol, l, use_rope, out_bf, outB, out_f32=None, out_f32B=None, tmps=None):
        sq, sqB, rs, rsB, rt, rtB, qn, qnB, t1, t1B = tmps
        emit(act, lambda e: e.activation(out=sq, in_=banks[b][:], func=AF.Square), [bankB[b]], [sqB])
        b2 = mbank()
        emit(pe, lambda e: e.matmul(banks[b2][:], lhsT=bones[:], rhs=sq, start=True, stop=True),
             [sqB, constB], [bankB[b2]])
        rstd_from(banks[b2][:], 1.0 / 64, rs, [bankB[b2]], [rsB], rt, rtB)
        gq = pv[:, l, gcol:gcol + 1]
        if not use_rope:
            emit(dve, lambda e: e.scalar_tensor_tensor(out=qn, in0=banks[b][:], scalar=gq, in1=rs,
                                                       op0=ALU.mult, op1=ALU.mult), [bankB[b], rsB, pvB], [qnB])
            emit(act, lambda e: e.activation(out=out_bf, in_=qn, func=AF.Identity), [qnB], [], pwrites=[outB])
            return
        emit(dve, lambda e: e.scalar_tensor_tensor(out=qn, in0=banks[b][:], scalar=gq, in1=rs,
                                                   op0=ALU.mult, op1=ALU.mult), [bankB[b], rsB, pvB], [qnB])
        b3 = mbank()
        emit(pe, lambda e: e.matmul(banks[b3][:], lhsT=rotm[:], rhs=qn, start=True, stop=True),
             [qnB, constB], [bankB[b3]])
        emit(dve, lambda e: e.tensor_tensor(out=t1, in0=banks[b3][:], in1=rope[:, 1, n * 512:(n + 1) * 512],
                                            op=ALU.mult), [bankB[b3], constB], [t1B])
        emit(dve, lambda e: e.tensor_tensor(out=qn, in0=qn, in1=rope[:, 0, n * 512:(n + 1) * 512],
                                            op=ALU.mult), [qnB, constB], [qnB])
        emit(dve, lambda e: e.tensor_tensor(out=out_bf, in0=qn, in1=t1, op=ALU.add), [qnB, t1B], [],
             pwrites=[outB])

    def qk_tmps():
        sq = A.alloc("qsq", [512], F32); sqB = A.newbuf("qsq")
        rs = A.alloc("qrs", [512], F32); rsB = A.newbuf("qrs")
        qn = A.alloc("qqn", [512], F32); qnB = A.newbuf("qqn")
        t1 = A.alloc("qt1", [512], F32); t1B = A.newbuf("qt1")
        return (sq, sqB, rs, rsB, t1, t1B, qn, qnB, t1, t1B)

    def dbg_to_x(ap3, bufs, nch, ncol):
        emit(dve, lambda e: e.tensor_copy(out=xT[:, 0:nch, 0:ncol], in_=ap3), bufs, [],
             pwrites=[xTb[c][n] for c in range(KC) for n in range(2)])

    def dbg_dump(ap2d, bufs, ncols):
        emit(sp, lambda e: e.dma_start(out=dbg_d[:, 0:ncols], in_=ap2d), bufs, [], dma=P.buf("dbg"))

    def layer(l, G):
        T, NT, j, sample = G["T"], G["T"] // 512, G["j"], G["sample"]
        if l == 1 or not G.get("first", False):
            bg_drain()
        segs = G["segs"]
        PP, PC = 8, 16
        base = A.mark()
        qT = A.alloc("qT", [4, T], BF16); qTB = A.newbuf("qT")
        nseg = len(segs)
        Lseg = segs[0][1]
        UW = Lseg + 2 * PP
        GW = Lseg + 2 * PC
        if not sample:
            kTp = A.alloc("kTp", [4, T], BF16); kTpB = A.newbuf("kTp")
            vTp = A.alloc("vTp", [T // 128, 512], BF16); vTpB = A.newbuf("vTp")
            fTp = A.alloc("fTp", [4, T], BF16); fTpB = A.newbuf("fTp")
        m_q = A.mark()
        gT = A.alloc("gT", [4, nseg * GW], F32); gTB = [A.newbuf("gT%d" % c) for c in range(4)]
        m_g = A.mark()
        uP = A.alloc("uP", [4, nseg * UW], F32); uPB = [A.newbuf("uP%d" % g) for g in range(4)]
        m_ug = A.mark()
        hT = A.alloc("hT", [KC, T], BF16)
        hTb = [[A.newbuf("hT%d_%d" % (c, n)) for n in range(NT)] for c in range(KC)]
        norm(l, 0, T, j, hT, hTb)

        def hrhs(k, n):
            return (hTb[k][n], hT[:, k, n * 512:(n + 1) * 512])
        stg_tag = "%d%s" % (l, "s" if sample else "p")
        if stop_after == "norm" + stg_tag:
            return False

        def proj_jobs(col0, nchunks, evac):
            for blk in range(nchunks // 2):
                bg_tick()
                wb, wv = wload(wsrc(w_in_d, l, 0, D, col0 + blk * 256, 256), KC, 256)
                for sub in range(2):
                    ch = blk * 2 + sub
                    bl = job(KC, NT, 512, lambda k, wv=wv, sub=sub, wb=wb: (wb, wv[:, k, sub * 128:(sub + 1) * 128]),
                             hrhs)
                    for n in range(NT):
                        evac(ch, n, bl[n])

        mfr = A.mark()
        tmps = qk_tmps()
        if sample:
            kst = [A.alloc("kst%d" % i, [512], BF16) for i in range(2)]
            kstB = [A.newbuf("kst%d" % i) for i in range(2)]
            kcnt = [0]

            def k_evac(ch, n, b):
                i = kcnt[0] % 2
                kcnt[0] += 1
                qknorm(b, n, C_GK, l, True, kst[i], kstB[i], tmps=tmps)
                emit(sp, lambda e, i=i, ch=ch, n=n: e.dma_start(
                    out=xsA[l].ap()[ch * 128:(ch + 1) * 128, n * 512:(n + 1) * 512], in_=kst[i]),
                    [kstB[i]], [], pwrites=[xsendQ[l][0]], dma=kstB[i])
        else:
            kst2 = [A.alloc("nkst%d" % i, [512], F32) for i in range(2)]
            kst2B = [A.newbuf("nkst%d" % i) for i in range(2)]
            kcnt = [0]

            def k_evac(ch, n, b):
                sq, sqB, rs, rsB, rt, rtB, qn, qnB, t1, t1B = tmps
                emit(act, lambda e: e.activation(out=sq, in_=banks[b][:], func=AF.Square), [bankB[b]], [sqB])
                b2 = mbank()
                emit(pe, lambda e: e.matmul(banks[b2][:], lhsT=bones[:], rhs=sq, start=True, stop=True),
                     [sqB, constB], [bankB[b2]])
                rstd_from(banks[b2][:], 1.0 / 64, rs, [bankB[b2]], [rsB], rt, rtB)
                emit(dve, lambda e: e.scalar_tensor_tensor(out=qn, in0=banks[b][:], scalar=pv[:, l, C_GK:C_GK + 1],
                                                           in1=rs, op0=ALU.mult, op1=ALU.mult),
                     [bankB[b], rsB, pvB], [qnB])
                emit(act, lambda e: e.activation(out=kTp[:, ch, :], in_=qn, func=AF.Identity), [qnB], [],
                     pwrites=[kTpB])
                b3 = mbank()
                for q in range(4):
                    emit(pe, lambda e, q=q: e.transpose(out=banks[b3][:, q * 128:(q + 1) * 128],
                                                        in_=qn[:, q * 128:(q + 1) * 128], identity=ident[:]),
                         [qnB, constB], [bankB[b3]] if q == 0 else [], sig=(q == 3))
                i = kcnt[0] % 2
                kcnt[0] += 1
                copy_on(ev_eng(), kst2[i], banks[b3][:], [bankB[b3]], [kst2B[i]])
                for sq_ in range(2):
                    emit(sp, lambda e, i=i, ch=ch, sq_=sq_: e.dma_start(
                        out=nk_d[sq_, l, :, ch * 128:(ch + 1) * 128].rearrange("(h p) f -> p h f", p=128),
                        in_=kst2[i].rearrange("p (q f) -> p q f", q=4)[:, 2 * sq_:2 * sq_ + 2, :]),
                        [kst2B[i]], [], pwrites=[outB_all], dma=kst2B[i])
        proj_jobs(1024, 4, k_evac)
        if stop_after == "kproj" + stg_tag:
            return False

        if sample:
            vst, vstB = kst, kstB
        else:
            vst = [A.alloc("vst%d" % i, [512], BF16) for i in range(2)]
            vstB = [A.newbuf("vst%d" % i) for i in range(2)]
        if not sample:
            vsf = [A.alloc("vsf%d" % i, [512], F32) for i in range(2)]
            vsfB = [A.newbuf("vsf%d" % i) for i in range(2)]
        wv_blocks = [wload(wsrc(w_in_d, l, 0, D, 1536 + hb * 256, 256), KC, 256) for hb in range(2)]
        for i in range(T // 128):
            b = dbank()
            n = i // 4
            for hb in range(2):
                wb, wv = wv_blocks[hb]
                for k in range(KC):
                    emit(pe, lambda e, b=b, hb=hb, k=k, wv=wv, i=i: e.matmul(
                        banks[b][:, hb * 256:(hb + 1) * 256], lhsT=hT[:, k, i * 128:(i + 1) * 128], rhs=wv[:, k, :],
                        start=(k == 0), stop=(k == KC - 1)),
                        [wb, hTb[k][n]], [bankB[b]] if (k == 0 and hb == 0) else [],
                        sig=(k == KC - 1 and hb == 1))
            s_ = i % 2
            VV_ = int(os.environ.get("VVAR", "0"))
            if VV_ == 1:
                continue
            if sample:
                copy_on(ev_eng(), vst[s_], banks[b][:], [bankB[b]], [vstB[s_]])
                emit(sp, lambda e, i=i, s_=s_: e.dma_start(
                    out=xsB[l].ap()[512:1024, :].rearrange("r (a c) -> (r a) c", a=2)[i * 128:(i + 1) * 128, :],
                    in_=vst[s_]), [vstB[s_]], [], pwrites=[xsendQ[l][1]], dma=vstB[s_])
            else:
                emit(dve, lambda e, b=b, s_=s_: e.tensor_copy(out=vsf[s_], in_=banks[b][:]), [bankB[b]], [vsfB[s_]])
                copy_on(act, vTp[:, i, :], vsf[s_], [vsfB[s_]], [], pwrites=[vTpB])
                if VV_ == 3:
                    continue
                emit(sp, lambda e, i=i, s_=s_: e.dma_start(
                    out=nv_d[i // 2, l, (i % 2) * 128:(i % 2 + 1) * 128, :], in_=vsf[s_]),
                    [vsfB[s_]], [], pwrites=[outB_all], dma=vsfB[s_])

        if stop_after == "vproj" + stg_tag:
            return False
        if sample:
            fst, fstB = kst, kstB
            fcnt = [0]

            def f_evac(ch, n, b):
                i = fcnt[0] % 2
                fcnt[0] += 1
                copy_on(ev_eng(), fst[i], banks[b][:], [bankB[b]], [fstB[i]])
                emit(sp, lambda e, i=i, ch=ch, n=n: e.dma_start(
                    out=xsB[l].ap()[ch * 128:(ch + 1) * 128, n * 512:(n + 1) * 512], in_=fst[i]),
                    [fstB[i]], [], pwrites=[xsendQ[l][1]], dma=fstB[i])
        else:
            def f_evac(ch, n, b):
                copy_on(ev_eng(), fTp[:, ch, n * 512:(n + 1) * 512], banks[b][:], [bankB[b]], [], pwrites=[fTpB])
        proj_jobs(3072, 4, f_evac)
        if sample:
            emit(pool, lambda e: e.collective_compute(
                "AllGather", ALU.bypass, replica_groups=[[0, 1], [2, 3], [4, 5], [6, 7]],
                ins=[xsB_t[l].ap().opt()], outs=[xrB_t[l].ap().opt()]),
                [xsendQ[l][1]], [xrecvQ[l][1]], inc=(cc_keys[l][1], 1))

        def seg_cols(n, pad, W):
            out = []
            for si, (s0, L) in enumerate(segs):
                lo = max(s0, n * 512)
                hi = min(s0 + L, (n + 1) * 512)
                if lo < hi:
                    out.append((lo - n * 512, hi - lo, si * W + pad + lo - s0))
            return out

        def p_evac(ch, n, b):
            for (o, nc_, dc) in seg_cols(n, PP, UW):
                copy_on(ev_eng(), uP[:, ch, dc:dc + nc_], banks[b][:, o:o + nc_], [bankB[b]], [], pwrites=[uPB[ch]])
        proj_jobs(0, 4, p_evac)

        sg = A.alloc("sg", [T], F32)
        sgB = A.newbuf("sg")
        for half in range(2):
            bg_tick()
            wbb, wvb = wload(wsrc(w_in_d, l, 0, D, 2560 + half * 256, 256), KC, 256)
            wba, wva = wload(wsrc(w_in_d, l, 0, D, 2048 + half * 256, 256), KC, 256)
            for sub in range(2):
                cch = half * 2 + sub
                bl = job(KC, NT, 512, lambda k, sub=sub: (wbb, wvb[:, k, sub * 128:(sub + 1) * 128]), hrhs)
                for n in range(NT):
                    emit(act, lambda e, n=n, b=bl[n]: e.activation(out=sg[:, n * 512:(n + 1) * 512], in_=banks[b][:],
                                                                    func=AF.Sigmoid), [bankB[bl[n]]], [], pwrites=[sgB])
                bl = job(KC, NT, 512, lambda k, sub=sub: (wba, wva[:, k, sub * 128:(sub + 1) * 128]), hrhs)
                for n in range(NT):
                    for (o, nc_, dc) in seg_cols(n, PC, GW):
                        emit(dve, lambda e, o=o, nc_=nc_, dc=dc, n=n, b=bl[n], cch=cch: e.tensor_tensor(
                            out=gT[:, cch, dc:dc + nc_], in0=banks[b][:, o:o + nc_],
                            in1=sg[:, n * 512 + o:n * 512 + o + nc_], op=ALU.mult),
                            [bankB[bl[n]], sgB], [], pwrites=[gTB[cch]])

        if sample:
            hal = xsA[l].ap()[512:544, :].rearrange("r (f e) -> (r f) e", e=64).rearrange("(g p) e -> p g e", p=128)
            hsd = A.alloc("hsd", [4, 64], BF16); hsdB = A.newbuf("hsd")
            emit(dve, lambda e: e.memset(hsd, 0.0), [], [hsdB])
            emit(dve, lambda e: e.tensor_copy(out=hsd[:, :, 0:8], in_=uP[:, :, PP:PP + 8]), uPB, [], pwrites=[hsdB])
            emit(dve, lambda e: e.tensor_copy(out=hsd[:, :, 8:16], in_=uP[:, :, PP + Lseg - 8:PP + Lseg]), uPB, [], pwrites=[hsdB])
            emit(dve, lambda e: e.tensor_copy(out=hsd[:, :, 16:32], in_=gT[:, :, PC:PC + 16]), gTB, [], pwrites=[hsdB])
            emit(dve, lambda e: e.tensor_copy(out=hsd[:, :, 32:48], in_=gT[:, :, PC + Lseg - 16:PC + Lseg]), gTB, [], pwrites=[hsdB])
            emit(sp, lambda e: e.dma_start(out=hal, in_=hsd), [hsdB], [], pwrites=[xsendQ[l][0]], dma=hsdB)
            emit(pool, lambda e: e.collective_compute(
                "AllGather", ALU.bypass, replica_groups=[[0, 1], [2, 3], [4, 5], [6, 7]],
                ins=[xsA_t[l].ap().opt()], outs=[xrA_t[l].ap().opt()]),
                [xsendQ[l][0]], [xrecvQ[l][0]], inc=(cc_keys[l][0], 1))
        else:
            for g in range(4):
                for si in range(nseg):
                    emit(dve, lambda e, g=g, si=si: e.memset(uP[:, g, si * UW:si * UW + PP], 0.0), [], [], pwrites=[uPB[g]])
                    emit(dve, lambda e, g=g, si=si: e.memset(uP[:, g, si * UW + PP + Lseg:(si + 1) * UW], 0.0), [], [], pwrites=[uPB[g]])
                    emit(dve, lambda e, g=g, si=si: e.memset(gT[:, g, si * GW:si * GW + PC], 0.0), [], [], pwrites=[gTB[g]])
                    emit(dve, lambda e, g=g, si=si: e.memset(gT[:, g, si * GW + PC + Lseg:(si + 1) * GW], 0.0), [], [], pwrites=[gTB[g]])

        def q_evac(ch, n, b):
            if sample:
                qknorm(b, n, C_GQ, l, True, qT[:, ch, n * 512:(n + 1) * 512], qTB, tmps=tmps)
            else:
                qknorm(b, n, C_GQ, l, False, qT[:, ch, n * 512:(n + 1) * 512], qTB, tmps=tmps)
        proj_jobs(512, 4, q_evac)
        A.release(m_ug)
        if stop_after == "proj" + stg_tag:
            return False

        g1c = 32

        def ymix_alloc():
            y = A.alloc("ymix", [4, T], BF16)
            yB = [[A.newbuf("ymix%d_%d" % (c, n)) for n in range(NT)] for c in range(4)]
            return y, yB

        bg_drain(9)
        mp = A.mark()
        if sample:
            hst = A.alloc("hst", [4, 64], BF16); hstB = A.newbuf("hst")
            hst2 = A.alloc("hst2", [4, 64], BF16); hst2B = A.newbuf("hst2")
            rv = xrA[l].ap()
            h0 = rv[512:544, :].rearrange("r (f e) -> (r f) e", e=64).rearrange("(g p) e -> p g e", p=128)
            h1 = rv[XRA + 512:XRA + 544, :].rearrange("r (f e) -> (r f) e", e=64).rearrange("(g p) e -> p g e", p=128)
            emit(sp, lambda e: e.dma_start(out=hst, in_=h0), [xrecvQ[l][0]], [hstB], dma=hstB)
            emit(sp, lambda e: e.dma_start(out=hst2, in_=h1), [xrecvQ[l][0]], [hst2B], dma=hst2B)
            emit(dve, lambda e: e.tensor_scalar(out=uP[:, :, 0:PP], in0=hst[:, :, 8:16], scalar1=hmask[:, 0:1],
                                                scalar2=None, op0=ALU.mult), [hstB, constB], [], pwrites=uPB)
            emit(dve, lambda e: e.tensor_scalar(out=uP[:, :, PP + Lseg:PP + Lseg + PP], in0=hst2[:, :, 0:8],
                                                scalar1=hmask[:, 1:2], scalar2=None, op0=ALU.mult),
                 [hst2B, constB], [], pwrites=uPB)
            emit(dve, lambda e: e.tensor_scalar(out=gT[:, :, 0:PC], in0=hst[:, :, 32:48], scalar1=hmask[:, 0:1],
                                                scalar2=None, op0=ALU.mult), [hstB, constB], [], pwrites=gTB)
            emit(dve, lambda e: e.tensor_scalar(out=gT[:, :, PC + Lseg:PC + Lseg + PC], in0=hst2[:, :, 16:32],
                                                scalar1=hmask[:, 1:2], scalar2=None, op0=ALU.mult),
                 [hst2B, constB], [], pwrites=gTB)
        icn = A.alloc("icn", [4, Lseg], F32); icnB = A.newbuf("icn")
        icd = icnts_d if sample else icntp_d
        emit(sp, lambda e: e.dma_start(out=icn, in_=icd.partition_broadcast(128)), [], [icnB], dma=icnB)
        sa = A.alloc("sa", [UW], F32); saB = A.newbuf("sa")
        sbb = A.alloc("sbb", [UW], F32); sbB = A.newbuf("sbb")
        pT = A.alloc("pT", [4, T], BF16); pTB = [A.newbuf("pT%d" % g) for g in range(4)]
        ym, ymB = ymix_alloc()
        for g in range(4):
            for si, (s0, L) in enumerate(segs):
                u = uP[:, g, si * UW:(si + 1) * UW]
                W = UW
                emit(dve, lambda e, u=u: e.tensor_tensor(out=sa[:, 1:W], in0=u[:, 0:W - 1], in1=u[:, 1:W], op=ALU.add),
                     [uPB[g]], [saB])
                cur, curB, oth, othB = sa, saB, sbb, sbB
                lo, hi, step = 1, W, 1
                for lev in range(g):
                    nlo, nhi = lo + step, hi - step
                    emit(dve, lambda e, cur=cur, oth=oth, nlo=nlo, nhi=nhi, step=step: e.tensor_tensor(
                        out=oth[:, nlo:nhi], in0=cur[:, nlo - step:nhi - step], in1=cur[:, nlo + step:nhi + step],
                        op=ALU.add), [curB], [othB])
                    cur, curB, oth, othB = oth, othB, cur, curB
                    lo, hi, step = nlo, nhi, step * 2
                emit(dve, lambda e, cur=cur, g=g: e.tensor_tensor(out=cur[:, PP:PP + L], in0=cur[:, PP:PP + L],
                                                                  in1=icn[:, g, :], op=ALU.mult), [curB, icnB], [curB])
                emit(dve, lambda e, cur=cur, u=u, g=g, s0=s0, L=L: e.tensor_tensor(
                    out=pT[:, g, s0:s0 + L], in0=cur[:, PP:PP + L], in1=u[:, PP:PP + L], op=ALU.subtract),
                    [curB, uPB[g]], [], pwrites=[pTB[g]])
        wb, wv = wload(pool_w_d[l].rearrange("g c d -> c g d"), 4, 128)
        for g in range(4):
            bl = job(1, NT, 512, lambda k, g=g: (wb, wv[:, g, :]), lambda k, n, g=g: (pTB[g], pT[:, g, n * 512:(n + 1) * 512]))
            for n in range(NT):
                emit(act, lambda e, g=g, n=n, b=bl[n]: e.activation(
                    out=ym[:, g, n * 512:(n + 1) * 512], in_=banks[b][:], func=AF.Identity,
                    scale=pv[:, l, C_PSC + g:C_PSC + g + 1]), [bankB[bl[n]], pvB], [ymB[g][n]])
        residual_partial(4, NT, l, g1c, j, w_out_d, 0, ym, lambda k, n: ymB[k][n])
        A.release(mp)
        A.release(m_g)
        if stop_after == "pool%d%s" % (l, "s" if sample else "p"):
            return False

        mc = A.mark()
        acc = A.alloc("acc", [4, T], F32); accB = [[A.newbuf("acc%d_%d" % (c, n)) for n in range(NT)] for c in range(4)]
        tap_groups = []
        for c in range(4):
            for n in range(NT):
                def _grp(c=c, n=n):
                    pieces = seg_cols(n, PC, GW)
                    for (o, nc_, dc) in pieces:
                        for jt in range(31):
                            src = gT[:, c, dc - 15 + jt:dc - 15 + jt + nc_]
                            dst = acc[:, c, n * 512 + o:n * 512 + o + nc_]
                            dwc = pv[:, l, C_DW + c * 31 + jt:C_DW + c * 31 + jt + 1]
                            if jt == 0:
                                emit(dve, lambda e, src=src, dst=dst, dwc=dwc, c=c: e.tensor_scalar(
                                    out=dst, in0=src, scalar1=dwc, scalar2=pv[:, l, C_DWB + c:C_DWB + c + 1],
                                    op0=ALU.mult, op1=ALU.add), [gTB[c], pvB], [], pwrites=[accB[c][n]])
                            else:
                                emit(dve, lambda e, src=src, dst=dst, dwc=dwc: e.scalar_tensor_tensor(
                                    out=dst, in0=src, scalar=dwc, in1=dst, op0=ALU.mult, op1=ALU.add),
                                    [gTB[c], pvB, accB[c][n]], [], pwrites=[accB[c][n]])

                tap_groups.append(_grp)

        def taps_pop(k=1):
            for _ in range(k):
                if tap_groups:
                    tap_groups.pop(0)()
        mf = A.mark()
        taps_pop(2)
        ym, ymB = ymix_alloc()
        FT = A.alloc("FT", [4, T], BF16); FTB = [[A.newbuf("FT%d_%d" % (c, n)) for n in range(NT)] for c in range(4)]
        if sample:
            NTC = LS // 128
            fall = A.alloc("fall", [2, LS], BF16)
            AB = A.alloc("AB", [NTC, 2, 256], BF16)
            tring = [A.alloc("tr%d" % i, [2, 512], BF16) for i in range(4)]
            tringB = [A.newbuf("tr%d" % i) for i in range(4)]
            tcnt = [0]
            rv = xrB[l].ap()
            fallB = A.newbuf("fall")
            ABB = A.newbuf("AB")
            for hp in range(2):
                for r in range(2):
                    emit(sp, lambda e, r=r, hp=hp: e.dma_start(
                        out=fall[:, :, r * 1024:(r + 1) * 1024],
                        in_=rv[r * XRB + hp * 256:r * XRB + (hp + 1) * 256, :].rearrange("(h p) t -> p h t", p=128)),
                        [xrecvQ[l][1]], [fallB] if r == 0 else [], pwrites=[fallB] if r == 1 else [], dma=fallB)
                for i in range(NTC):
                    b = mbank()
                    for hh in range(2):
                        emit(pe, lambda e, b=b, hh=hh, i=i: e.matmul(
                            banks[b][:, hh * 256:(hh + 1) * 256], lhsT=fall[:, hh, i * 128:(i + 1) * 128], rhs=dft128[:],
                            start=True, stop=True), [fallB, constB], [bankB[b]] if hh == 0 else [], sig=(hh == 1))
                    copy_on(act, AB[:, i, :, :], banks[b][:].rearrange("p (h c) -> p h c", h=2), [bankB[b]],
                            [ABB] if i == 0 else [], pwrites=[ABB] if i > 0 else [])
                taps_pop(2)
                for kt in range(NT):
                    taps_pop(1)
                    bl = [dbank(), dbank()]
                    for i in range(NTC):
                        ti = tcnt[0] % 4
                        tcnt[0] += 1
                        emit(sp, lambda e, ti=ti, i=i, kt=kt: e.dma_start(
                            out=tring[ti], in_=dfts_d[i * 128:(i + 1) * 128, :, kt * 512:(kt + 1) * 512]),
                            [], [tringB[ti]], dma=tringB[ti])
                        for hh in range(2):
                            for cs in range(2):
                                emit(pe, lambda e, hh=hh, cs=cs, i=i, ti=ti, b=bl[hh]: e.matmul(
                                    banks[b][:], lhsT=AB[:, i, hh, cs * 128:(cs + 1) * 128], rhs=tring[ti][:, cs, :],
                                    start=(i == 0 and cs == 0), stop=(i == NTC - 1 and cs == 1)),
                                    [ABB, tringB[ti]], [bankB[bl[hh]]] if (i == 0 and cs == 0) else [],
                                    pwrites=[] if (i == 0 and cs == 0) else [bankB[bl[hh]]],
                                    sig=(cs == 1 and hh == 1))
                    for hh in range(2):
                        copy_on(act, FT[:, hp * 2 + hh, kt * 512:(kt + 1) * 512], banks[bl[hh]][:],
                                [bankB[bl[hh]]], [FTB[hp * 2 + hh][kt]])
        else:
            AB = A.alloc("ABp", [T // 128, 4, 256], BF16); ABB = A.newbuf("ABp")
            for i in range(T // 128):
                for hp in range(2):
                    b = mbank()
                    for hh in range(2):
                        h = hp * 2 + hh
                        emit(pe, lambda e, b=b, hh=hh, h=h, i=i: e.matmul(
                            banks[b][:, hh * 256:(hh + 1) * 256], lhsT=fTp[:, h, i * 128:(i + 1) * 128], rhs=dft128[:],
                            start=True, stop=True), [fTpB, constB], [bankB[b]] if hh == 0 else [], sig=(hh == 1))
                    copy_on(act, AB[:, i, hp * 2:hp * 2 + 2, :], banks[b][:].rearrange("p (h c) -> p h c", h=2),
                            [bankB[b]], [], pwrites=[ABB])
            for h in range(4):
                b = dbank()
                for si, (s0, L) in enumerate(segs):
                    ntc = L // 128
                    for i in range(ntc):
                        for cs in range(2):
                            emit(pe, lambda e, b=b, h=h, i=i, cs=cs, s0=s0, L=L: e.matmul(
                                banks[b][:, s0:s0 + L], lhsT=AB[:, s0 // 128 + i, h, cs * 128:(cs + 1) * 128],
                                rhs=dftp[:, i, cs, :], start=(i == 0 and cs == 0), stop=(i == ntc - 1 and cs == 1)),
                                [ABB, constB], [bankB[b]] if (si == 0 and i == 0 and cs == 0) else [],
                                sig=(si == nseg - 1 and i == ntc - 1 and cs == 1))
                copy_on(act, FT[:, h, 0:T], banks[b][:, 0:T], [bankB[b]], [FTB[h][0]])
        wb, wv = wload(wsrc(four_w_d, l, 0, 512, 0, 512), 4, 512)
        for mo in range(4):
            bl = job(4, NT, 512, lambda k, mo=mo: (wb, wv[:, k, mo * 128:(mo + 1) * 128]),
                     lambda k, n: (FTB[k][n], FT[:, k, n * 512:(n + 1) * 512]))
            for n in range(NT):
                copy_on(act, ym[:, mo, n * 512:(n + 1) * 512], banks[bl[n]][:], [bankB[bl[n]]], [ymB[mo][n]])
        residual_partial(4, NT, l, g1c, j, w_out_d, 1536, ym, lambda k, n: ymB[k][n])
        A.release(mf)

        if not sample:
            ma = A.mark()
            ym, ymB = ymix_alloc()
            pt = [A.alloc("ptile%d" % i, [512], BF16) for i in range(3)]
            ptB = [A.newbuf("ptile%d" % i) for i in range(3)]
            r0 = A.alloc("ar0", [512], F32); r0B = A.newbuf("ar0")
            r1 = A.alloc("ar1", [512], F32); r1B = A.newbuf("ar1")
            o0 = A.alloc("ao0", [512], F32); o0B = A.newbuf("ao0")
            asq = A.alloc("asq", [512], F32); asqB = A.newbuf("asq")
            ars = A.alloc("ars", [512], F32); arsB = A.newbuf("ars")
            art = A.alloc("art", [512], F32); artB = A.newbuf("art")
            if sample:
                NKC = 20
                kall = A.alloc("kall", [4, NKC * 128], BF16); kallB = A.newbuf("kall")
                vall = A.alloc("vall", [NKC, 512], BF16); vallB = A.newbuf("vall")
                rv = xrA[l].ap()
                rvb = xrB[l].ap()
                for r in range(2):
                    emit(sp, lambda e, r=r: e.dma_start(
                        out=kall[:, :, r * 1024:(r + 1) * 1024],
                        in_=rv[r * XRA:r * XRA + 512, :].rearrange("(h p) t -> p h t", p=128)),
                        [xrecvQ[l][0]], [], pwrites=[kallB], dma=kallB)
                    emit(sp, lambda e, r=r: e.dma_start(
                        out=vall[:, r * 8:(r + 1) * 8, :],
                        in_=rvb[r * XRB + 512:r * XRB + 1024, :].rearrange("r (a c) -> (r a) c", a=2).rearrange(
                            "(i p) c -> p i c", p=128)), [xrecvQ[l][1]], [], pwrites=[vallB], dma=vallB)
                emit(pool, lambda e: e.dma_start(out=vall[:, 16:20, :],
                                                 in_=cv_d[l].rearrange("(i p) c -> p i c", p=128)),
                     [], [], pwrites=[vallB], dma=vallB)
                cks = [A.alloc("cks%d" % i, [512], F32) for i in range(2)]
                cksB = [A.newbuf("cks%d" % i) for i in range(2)]
                for i in range(4):
                    s_ = i % 2
                    emit(sp, lambda e, i=i, s_=s_: e.dma_start(out=cks[s_], in_=ck_d[l, i * 128:(i + 1) * 128, :]),
                         [], [cksB[s_]], dma=cksB[s_])
                    b = mbank()
                    for h in range(4):
                        emit(pe, lambda e, h=h, b=b, s_=s_: e.transpose(out=banks[b][:, h * 128:(h + 1) * 128],
                                                                       in_=cks[s_][:, h * 128:(h + 1) * 128], identity=ident[:]),
                             [cksB[s_], constB], [bankB[b]] if h == 0 else [], sig=(h == 3))
                    copy_on(ev_eng(), kall[:, :, 2048 + i * 128:2048 + (i + 1) * 128],
                            banks[b][:].rearrange("p (h t) -> p h t", h=4), [bankB[b]], [], pwrites=[kallB])
                att_units = [(0, 512, n * 512, [(kall, kallB, vall, vallB, jc) for jc in range(NKC)]) for n in range(NT)]
            else:
                att_units = []
                for si, (s0, L) in enumerate(segs):
                    att_units.append((1, L, s0, [(kTp, kTpB, vTp, vTpB, s0 // 128 + jc) for jc in range(L // 128)]))
            pcnt = [0]
            for h in range(4):
                for (_, NQ, q0, klist) in att_units:
                    taps_pop(1)
                    bO = [0, 1]
                    bL = [2, 3]
                    nk_ = len(klist)
                    its = [(ki, mmap) for ki in range(nk_) for mmap in range(2)]
                    base_i = pcnt[0]
                    pcnt[0] += len(its)

                    def e_qk(ix):
                        ki, mmap = its[ix]
                        kt_, ktB_, vt_, vtB_, jc = klist[ki]
                        bs = 4 + ((base_i + ix) % 2)
                        emit(pe, lambda e: e.matmul(
                            banks[bs][:, 0:NQ], lhsT=kt_[mmap * 64:(mmap + 1) * 64, h, jc * 128:(jc + 1) * 128],
                            rhs=qT[mmap * 64:(mmap + 1) * 64, h, q0:q0 + NQ], start=True, stop=True),
                            [ktB_, qTB], [bankB[bs]])

                    def e_exp(ix):
                        bs = 4 + ((base_i + ix) % 2)
                        pi = (base_i + ix) % 3
                        emit(act, lambda e: e.activation(out=pt[pi][:, 0:NQ], in_=banks[bs][:, 0:NQ],
                                                         func=AF.Exp, scale=0.125), [bankB[bs]], [ptB[pi]])

                    def e_pv(ix):
                        ki, mmap = its[ix]
                        kt_, ktB_, vt_, vtB_, jc = klist[ki]
                        pi = (base_i + ix) % 3
                        emit(pe, lambda e: e.matmul(
                            banks[bO[mmap]][:, 0:NQ], lhsT=vt_[:, jc, h * 128:(h + 1) * 128], rhs=pt[pi][:, 0:NQ],
                            start=(ki == 0), stop=(ki == nk_ - 1)),
                            [vtB_, ptB[pi]], [bankB[bO[mmap]]] if ki == 0 else [],
                            pwrites=[] if ki == 0 else [bankB[bO[mmap]]], sig=False)
                        emit(pe, lambda e: e.matmul(
                            banks[bL[mmap]][:, 0:NQ], lhsT=ones_b[:], rhs=pt[pi][:, 0:NQ],
                            start=(ki == 0), stop=(ki == nk_ - 1)),
                            [constB, ptB[pi]], [bankB[bL[mmap]]] if ki == 0 else [],
                            pwrites=[] if ki == 0 else [bankB[bL[mmap]]], sig=True)

                    e_qk(0)
                    for ix in range(len(its)):
                        if ix + 1 < len(its):
                            e_qk(ix + 1)
                        e_exp(ix)
                        e_pv(ix)
                    emit(dve, lambda e, NQ=NQ: e.reciprocal(out=r0[:, 0:NQ], in_=banks[2][:, 0:NQ]), [bankB[2]], [r0B])
                    emit(dve, lambda e, NQ=NQ: e.reciprocal(out=r1[:, 0:NQ], in_=banks[3][:, 0:NQ]), [bankB[3]], [r1B])
                    emit(dve, lambda e, NQ=NQ: e.tensor_tensor(out=o0[:, 0:NQ], in0=banks[0][:, 0:NQ], in1=r0[:, 0:NQ],
                                                               op=ALU.mult), [bankB[0], r0B], [o0B])
                    emit(dve, lambda e, NQ=NQ: e.tensor_tensor(out=r1[:, 0:NQ], in0=banks[1][:, 0:NQ], in1=r1[:, 0:NQ],
                                                               op=ALU.mult), [bankB[1], r1B], [r1B])
                    emit(dve, lambda e, NQ=NQ: e.scalar_tensor_tensor(out=o0[:, 0:NQ], in0=r1[:, 0:NQ],
                                                                      scalar=small[:, 1 + l:2 + l], in1=o0[:, 0:NQ],
                                                                      op0=ALU.mult, op1=ALU.add), [r1B, o0B, smallB], [o0B])
                    emit(act, lambda e, NQ=NQ: e.activation(out=asq[:, 0:NQ], in_=o0[:, 0:NQ], func=AF.Square), [o0B], [asqB])
                    b2 = mbank()
                    emit(pe, lambda e, NQ=NQ, b2=b2: e.matmul(banks[b2][:, 0:NQ], lhsT=ones_f[:], rhs=asq[:, 0:NQ],
                                                             start=True, stop=True), [asqB, constB], [bankB[b2]])
                    rstd_from(banks[b2][:, 0:NQ], 1.0 / 128, ars[:, 0:NQ], [bankB[b2]], [arsB], art[:, 0:NQ], artB)
                    nq_t = q0 // 512
                    emit(dve, lambda e, NQ=NQ, h=h, q0=q0: e.scalar_tensor_tensor(
                        out=ym[:, h, q0:q0 + NQ], in0=o0[:, 0:NQ], scalar=small[:, 3 + l:4 + l], in1=ars[:, 0:NQ],
                        op0=ALU.mult, op1=ALU.mult), [o0B, arsB, smallB], [], pwrites=[ymB[h][nq_t]])
            residual_partial(4, NT, l, g1c, j, w_out_d, 512, ym, lambda k, n: ymB[k][n])
            A.release(ma)
            if stop_after == "att%d%s" % (l, "s" if sample else "p"):
                return False


        taps_pop(100)
        csq = [A.alloc("csq%d" % i, [512], F32) for i in range(2)]
        csqB = [A.newbuf("csq%d" % i) for i in range(2)]
        mean = A.alloc("cmean", [512], F32); meanB = A.newbuf("cmean")
        msq = A.alloc("cmsq", [512], F32); msqB = A.newbuf("cmsq")
        crs = A.alloc("crs", [512], F32); crsB = A.newbuf("crs")
        crt = A.alloc("crt", [512], F32); crtB = A.newbuf("crt")
        ct1 = [A.alloc("ct1%d" % i, [512], F32) for i in range(2)]
        ct1B = [A.newbuf("ct1%d" % i) for i in range(2)]
        yb = A.alloc("cyb", [4, T], BF16); ybB = [[A.newbuf("cyb%d_%d" % (c, n)) for n in range(NT)] for c in range(4)]
        ym, ymB = ymix_alloc()
        for n in range(NT):
            b1 = mbank()
            for c in range(4):
                emit(pe, lambda e, c=c, n=n, b1=b1: e.matmul(banks[b1][:], lhsT=ones_f[:], rhs=acc[:, c, n * 512:(n + 1) * 512],
                                                            start=(c == 0), stop=(c == 3)),
                     [accB[c][n], constB], [bankB[b1]] if c == 0 else [], sig=(c == 3))
            b2 = mbank()
            for c in range(4):
                i = c % 2
                emit(act, lambda e, c=c, n=n, i=i: e.activation(out=csq[i], in_=acc[:, c, n * 512:(n + 1) * 512],
                                                               func=AF.Square), [accB[c][n]], [csqB[i]])
                emit(pe, lambda e, c=c, i=i, b2=b2: e.matmul(banks[b2][:], lhsT=ones_f[:], rhs=csq[i],
                                                            start=(c == 0), stop=(c == 3)),
                     [csqB[i], constB], [bankB[b2]] if c == 0 else [], pwrites=[] if c == 0 else [bankB[b2]], sig=True)
            emit(dve, lambda e, b1=b1: e.tensor_scalar(out=mean, in0=banks[b1][:], scalar1=1.0 / 512, scalar2=None,
                                                       op0=ALU.mult), [bankB[b1]], [meanB])
            emit(dve, lambda e: e.tensor_tensor(out=msq, in0=mean, in1=mean, op=ALU.mult), [meanB], [msqB])
            emit(dve, lambda e, b2=b2: e.scalar_tensor_tensor(out=msq, in0=banks[b2][:], scalar=1.0 / 512, in1=msq,
                                                              op0=ALU.mult, op1=ALU.subtract), [bankB[b2], msqB], [msqB])
            rstd_from(msq, 1.0, crs, [msqB], [crsB], crt, crtB)
            for c in range(4):
                i = c % 2
                emit(dve, lambda e, c=c, n=n, i=i: e.tensor_tensor(out=ct1[i], in0=acc[:, c, n * 512:(n + 1) * 512],
                                                                  in1=mean, op=ALU.subtract), [accB[c][n], meanB], [ct1B[i]])
                emit(dve, lambda e, i=i: e.tensor_tensor(out=ct1[i], in0=ct1[i], in1=crs, op=ALU.mult),
                     [ct1B[i], crsB], [ct1B[i]])
                emit(act, lambda e, c=c, n=n, i=i: e.activation(
                    out=yb[:, c, n * 512:(n + 1) * 512], in_=ct1[i], func=AF.Silu,
                    bias=pv[:, l, C_LNB + c:C_LNB + c + 1], scale=pv[:, l, C_LNG + c:C_LNG + c + 1]),
                    [ct1B[i], pvB], [ybB[c][n]])
        if stop_after == "dyb" + stg_tag:
            dbg_to_x(yb, [b_ for r_ in ybB for b_ in r_], 4, T)
            return False
        wb, wv = wload(wsrc(conv_pw_d, l, 0, 512, 0, 512), 4, 512)
        for mo in range(4):
            bl = job(4, NT, 512, lambda k, mo=mo: (wb, wv[:, k, mo * 128:(mo + 1) * 128]),
                     lambda k, n: (ybB[k][n], yb[:, k, n * 512:(n + 1) * 512]))
            for n in range(NT):
                emit(act, lambda e, mo=mo, n=n, b=bl[n]: e.activation(
                    out=ym[:, mo, n * 512:(n + 1) * 512], in_=banks[b][:], func=AF.Identity,
                    bias=pv[:, l, C_PWB + mo:C_PWB + mo + 1], scale=1.0), [bankB[bl[n]], pvB], [ymB[mo][n]])
        if stop_after == "dym" + stg_tag:
            dbg_to_x(ym, [b_ for r_ in ymB for b_ in r_], 4, T)
            return False
        if stop_after == "dnores" + stg_tag:
            return False
        residual_partial(4, NT, l, g1c, j, w_out_d, 1024, ym, lambda k, n: ymB[k][n], overwrite=(stop_after == "dres" + stg_tag))
        if stop_after == "dres" + stg_tag:
            return False
        A.release(mc)
        if stop_after == "conv%d%s" % (l, "s" if sample else "p"):
            return False

        if sample:
            ma = A.mark()
            ym, ymB = ymix_alloc()
            pt = [A.alloc("ptile%d" % i, [512], BF16) for i in range(3)]
            ptB = [A.newbuf("ptile%d" % i) for i in range(3)]
            r0 = A.alloc("ar0", [512], F32); r0B = A.newbuf("ar0")
            r1 = A.alloc("ar1", [512], F32); r1B = A.newbuf("ar1")
            o0 = A.alloc("ao0", [512], F32); o0B = A.newbuf("ao0")
            asq = A.alloc("asq", [512], F32); asqB = A.newbuf("asq")
            ars = A.alloc("ars", [512], F32); arsB = A.newbuf("ars")
            art = A.alloc("art", [512], F32); artB = A.newbuf("art")
            if sample:
                NKC = 20
                kall = A.alloc("kall", [4, NKC * 128], BF16); kallB = A.newbuf("kall")
                vall = A.alloc("vall", [NKC, 512], BF16); vallB = A.newbuf("vall")
                rv = xrA[l].ap()
                rvb = xrB[l].ap()
                for r in range(2):
                    emit(sp, lambda e, r=r: e.dma_start(
                        out=kall[:, :, r * 1024:(r + 1) * 1024],
                        in_=rv[r * XRA:r * XRA + 512, :].rearrange("(h p) t -> p h t", p=128)),
                        [xrecvQ[l][0]], [], pwrites=[kallB], dma=kallB)
                    emit(sp, lambda e, r=r: e.dma_start(
                        out=vall[:, r * 8:(r + 1) * 8, :],
                        in_=rvb[r * XRB + 512:r * XRB + 1024, :].rearrange("r (a c) -> (r a) c", a=2).rearrange(
                            "(i p) c -> p i c", p=128)), [xrecvQ[l][1]], [], pwrites=[vallB], dma=vallB)
                emit(pool, lambda e: e.dma_start(out=vall[:, 16:20, :],
                                                 in_=cv_d[l].rearrange("(i p) c -> p i c", p=128)),
                     [], [], pwrites=[vallB], dma=vallB)
                cks = [A.alloc("cks%d" % i, [512], F32) for i in range(2)]
                cksB = [A.newbuf("cks%d" % i) for i in range(2)]
                for i in range(4):
                    s_ = i % 2
                    emit(sp, lambda e, i=i, s_=s_: e.dma_start(out=cks[s_], in_=ck_d[l, i * 128:(i + 1) * 128, :]),
                         [], [cksB[s_]], dma=cksB[s_])
                    b = mbank()
                    for h in range(4):
                        emit(pe, lambda e, h=h, b=b, s_=s_: e.transpose(out=banks[b][:, h * 128:(h + 1) * 128],
                                                                       in_=cks[s_][:, h * 128:(h + 1) * 128], identity=ident[:]),
                             [cksB[s_], constB], [bankB[b]] if h == 0 else [], sig=(h == 3))
                    copy_on(ev_eng(), kall[:, :, 2048 + i * 128:2048 + (i + 1) * 128],
                            banks[b][:].rearrange("p (h t) -> p h t", h=4), [bankB[b]], [], pwrites=[kallB])
                att_units = [(0, 512, n * 512, [(kall, kallB, vall, vallB, jc) for jc in range(NKC)]) for n in range(NT)]
            else:
                att_units = []
                for si, (s0, L) in enumerate(segs):
                    att_units.append((1, L, s0, [(kTp, kTpB, vTp, vTpB, s0 // 128 + jc) for jc in range(L // 128)]))
            pcnt = [0]
            for h in range(4):
                for (_, NQ, q0, klist) in att_units:
                    taps_pop(1)
                    bO = [0, 1]
                    bL = [2, 3]
                    nk_ = len(klist)
                    its = [(ki, mmap) for ki in range(nk_) for mmap in range(2)]
                    base_i = pcnt[0]
                    pcnt[0] += len(its)

                    def e_qk(ix):
                        ki, mmap = its[ix]
                        kt_, ktB_, vt_, vtB_, jc = klist[ki]
                        bs = 4 + ((base_i + ix) % 2)
                        emit(pe, lambda e: e.matmul(
                            banks[bs][:, 0:NQ], lhsT=kt_[mmap * 64:(mmap + 1) * 64, h, jc * 128:(jc + 1) * 128],
                            rhs=qT[mmap * 64:(mmap + 1) * 64, h, q0:q0 + NQ], start=True, stop=True),
                            [ktB_, qTB], [bankB[bs]])

                    def e_exp(ix):
                        bs = 4 + ((base_i + ix) % 2)
                        pi = (base_i + ix) % 3
                        emit(act, lambda e: e.activation(out=pt[pi][:, 0:NQ], in_=banks[bs][:, 0:NQ],
                                                         func=AF.Exp, scale=0.125), [bankB[bs]], [ptB[pi]])

                    def e_pv(ix):
                        ki, mmap = its[ix]
                        kt_, ktB_, vt_, vtB_, jc = klist[ki]
                        pi = (base_i + ix) % 3
                        emit(pe, lambda e: e.matmul(
                            banks[bO[mmap]][:, 0:NQ], lhsT=vt_[:, jc, h * 128:(h + 1) * 128], rhs=pt[pi][:, 0:NQ],
                            start=(ki == 0), stop=(ki == nk_ - 1)),
                            [vtB_, ptB[pi]], [bankB[bO[mmap]]] if ki == 0 else [],
                            pwrites=[] if ki == 0 else [bankB[bO[mmap]]], sig=False)
                        emit(pe, lambda e: e.matmul(
                            banks[bL[mmap]][:, 0:NQ], lhsT=ones_b[:], rhs=pt[pi][:, 0:NQ],
                            start=(ki == 0), stop=(ki == nk_ - 1)),
                            [constB, ptB[pi]], [bankB[bL[mmap]]] if ki == 0 else [],
                            pwrites=[] if ki == 0 else [bankB[bL[mmap]]], sig=True)

                    e_qk(0)
                    for ix in range(len(its)):
                        if ix + 1 < len(its):
                            e_qk(ix + 1)
                        e_exp(ix)
                        e_pv(ix)
                    emit(dve, lambda e, NQ=NQ: e.reciprocal(out=r0[:, 0:NQ], in_=banks[2][:, 0:NQ]), [bankB[2]], [r0B])
                    emit(dve, lambda e, NQ=NQ: e.reciprocal(out=r1[:, 0:NQ], in_=banks[3][:, 0:NQ]), [bankB[3]], [r1B])
                    emit(dve, lambda e, NQ=NQ: e.tensor_tensor(out=o0[:, 0:NQ], in0=banks[0][:, 0:NQ], in1=r0[:, 0:NQ],
                                                               op=ALU.mult), [bankB[0], r0B], [o0B])
                    emit(dve, lambda e, NQ=NQ: e.tensor_tensor(out=r1[:, 0:NQ], in0=banks[1][:, 0:NQ], in1=r1[:, 0:NQ],
                                                               op=ALU.mult), [bankB[1], r1B], [r1B])
                    emit(dve, lambda e, NQ=NQ: e.scalar_tensor_tensor(out=o0[:, 0:NQ], in0=r1[:, 0:NQ],
                                                                      scalar=small[:, 1 + l:2 + l], in1=o0[:, 0:NQ],
                                                                      op0=ALU.mult, op1=ALU.add), [r1B, o0B, smallB], [o0B])
                    emit(act, lambda e, NQ=NQ: e.activation(out=asq[:, 0:NQ], in_=o0[:, 0:NQ], func=AF.Square), [o0B], [asqB])
                    b2 = mbank()
                    emit(pe, lambda e, NQ=NQ, b2=b2: e.matmul(banks[b2][:, 0:NQ], lhsT=ones_f[:], rhs=asq[:, 0:NQ],
                                                             start=True, stop=True), [asqB, constB], [bankB[b2]])
                    rstd_from(banks[b2][:, 0:NQ], 1.0 / 128, ars[:, 0:NQ], [bankB[b2]], [arsB], art[:, 0:NQ], artB)
                    nq_t = q0 // 512
                    emit(dve, lambda e, NQ=NQ, h=h, q0=q0: e.scalar_tensor_tensor(
                        out=ym[:, h, q0:q0 + NQ], in0=o0[:, 0:NQ], scalar=small[:, 3 + l:4 + l], in1=ars[:, 0:NQ],
                        op0=ALU.mult, op1=ALU.mult), [o0B, arsB, smallB], [], pwrites=[ymB[h][nq_t]])
            residual_partial(4, NT, l, g1c, j, w_out_d, 512, ym, lambda k, n: ymB[k][n])
            A.release(ma)
            if stop_after == "att%d%s" % (l, "s" if sample else "p"):
                return False


        A.release(m_q)
        A.release(base)
        if stop_after == "mix%d%s" % (l, "s" if sample else "p"):
            return False
        bg_drain(25)
        mffn = A.mark()
        hT = A.alloc("hT2", [KC, T], BF16)
        hTb = [[A.newbuf("hT2%d_%d" % (c, n)) for n in range(NT)] for c in range(KC)]
        norm(l, 1, T, j, hT, hTb)

        def hrhs2(k, n):
            return (hTb[k][n], hT[:, k, n * 512:(n + 1) * 512])
        aT = A.alloc("aT", [12, T], BF16)
        sl = [A.alloc("sl%d" % i, [T], F32) for i in range(2)]
        slB = [[A.newbuf("sl%d_%d" % (i, n)) for n in range(NT)] for i in range(2)]
        scnt = [0]
        aTB = [[A.newbuf("aT%d_%d" % (c, n)) for n in range(NT)] for c in range(12)]
        for (g0, gs) in FFN_GROUPS:
            for blk in range(gs // 2):
                col0 = (g0 + blk * 2) * 128
                bg_tick()
                wbg, wvg = wload(wsrc(w_gate_d, l, 0, D, col0, 256), KC, 256)
                wbu, wvu = wload(wsrc(w_up_d, l, 0, D, col0, 256), KC, 256)
                for sub in range(2):
                    cc = blk * 2 + sub
                    si_ = scnt[0] % 2
                    scnt[0] += 1
                    bl = job(KC, NT, 512, lambda k, sub=sub: (wbg, wvg[:, k, sub * 128:(sub + 1) * 128]), hrhs2)
                    for n in range(NT):
                        emit(act, lambda e, b=bl[n], si_=si_, n=n: e.activation(
                            out=sl[si_][:, n * 512:(n + 1) * 512], in_=banks[b][:], func=AF.Silu),
                            [bankB[bl[n]]], [slB[si_][n]])
                    bl = job(KC, NT, 512, lambda k, sub=sub: (wbu, wvu[:, k, sub * 128:(sub + 1) * 128]), hrhs2)
                    for n in range(NT):
                        emit(dve, lambda e, b=bl[n], si_=si_, n=n, cc=cc: e.tensor_tensor(
                            out=aT[:, cc, n * 512:(n + 1) * 512], in0=banks[b][:], in1=sl[si_][:, n * 512:(n + 1) * 512],
                            op=ALU.mult), [bankB[bl[n]], slB[si_][n]], [aTB[cc][n]])
            bg_drain(32)
            residual_partial(gs, NT, l, 80, j, w_down_d, g0 * 128, aT, lambda k, n: aTB[k][n])
        A.release(mffn)
        if stop_after == "end" + stg_tag:
            return False
        return True

    outB_all = P.buf("outs")
    GP = dict(T=TP, j=0, sample=False, segs=[(0, LP), (LP, LP)], first=False)
    GS = dict(T=TS, j=1, sample=True, segs=[(0, TS)], first=True)
    finals = []
    done = False
    for (G, src, dst) in ((GS, xs_d, ys_d), (GP, xp_d, yp_d)):
        if stop_after == "mods":
            done = True
            break
        load_x(src, G["T"])
        if stop_after == "loadx":
            done = True
            break
        ok = True
        for l in range(2):
            ok = layer(l, G)
            if not ok:
                break
        if not ok:
            done = True
            break
        if stop_after == ("endp" if not G["sample"] else "ends"):
            done = True
            break
        finals += store_y(dst, G["T"])
    if stop_after == "mods":
        db = P.buf("dbg")
        emit(sp, lambda e: e.dma_start(out=dbg_d[:, 0:384], in_=modt[:].rearrange("p l m j -> p (l m j)")), [modB, gscB], [db], dma=db)
        emit(sp, lambda e: e.dma_start(out=dbg_d[:, 384:384 + 128], in_=gsc[:].rearrange("p l s c j -> p (l s c j)")), [modB, gscB], [], pwrites=[db], dma=db)
        emit(sp, lambda e: e.dma_start(out=dbg_d[:, 512:528], in_=small[:]), [smallB], [], pwrites=[db], dma=db)
        finals.append(db)
    elif stop_after:
        allx = [xTb[c][n] for c in range(KC) for n in range(2)]
        db = P.buf("dbg")
        Tl = G["T"]
        emit(sp, lambda e: e.dma_start(out=dbg_d.rearrange("p (c t) -> p c t", c=KC)[:, :, 0:Tl], in_=xT[:, :, 0:Tl]), allx, [db], dma=db)
        finals.append(db)
    emit(sp, lambda e: e.nop(), [], finals + [outB_all], sig=True)
    P.replay()


def _consts():
    c = {}
    c["ident"] = np.eye(128, dtype=np.float32)
    rot = np.zeros((128, 128), np.float32)
    for p in range(128):
        partner = p + 16 if (p % 32) < 16 else p - 16
        rot[partner, p] = 1.0
    c["rotm"] = rot
    bo = np.zeros((128, 128), np.float32)
    bo[:64, :64] = 1.0
    bo[64:, 64:] = 1.0
    c["bones"] = bo
    k = np.arange(128)
    ang = 2 * np.pi * np.outer(k, k) / 128.0
    c["dft128"] = (np.concatenate([np.cos(ang), np.sin(ang)], axis=1) / np.sqrt(128.0)).astype(np.float32)
    t = np.arange(LP)
    ang = 2 * np.pi * np.outer(t, t) / LP
    c["dftp"] = (np.stack([np.cos(ang), -np.sin(ang)], axis=1) / np.sqrt(LP)).astype(ml_dtypes.bfloat16)

    def icnt(L, t):
        out = []
        for w in (2, 4, 8, 16):
            lo = np.maximum(t - w // 2, 0)
            hi = np.minimum(t + w // 2 - 1, L - 1)
            out.append(1.0 / (hi - lo + 1))
        return np.stack(out).astype(np.float32)
    c["icnt_p"] = icnt(LP, np.arange(LP))
    c["icnt_s"] = [icnt(LS, np.arange(hf * TS, (hf + 1) * TS)) for hf in range(2)]
    inv = 1.0 / (10000.0 ** (np.arange(0, 32, 2, dtype=np.float32) / 32.0))
    ropes = []
    for hf in range(2):
        tt = np.arange(hf * TS, (hf + 1) * TS)
        row = (tt // 64).astype(np.float32)
        col = (tt % 64).astype(np.float32)
        tab = np.zeros((128, 2, TS), np.float32)
        for p in range(128):
            d = p % 64
            pos = row if d < 32 else col
            a = pos * inv[d % 16]
            tab[p, 0] = np.cos(a)
            tab[p, 1] = (-np.sin(a)) if (d % 32) < 16 else np.sin(a)
        ropes.append(tab)
    c["rope"] = ropes
    t = np.arange(LS, dtype=np.float64)
    dfts = []
    for hf in range(2):
        kk = np.arange(hf * TS, (hf + 1) * TS, dtype=np.float64)
        ang = 2 * np.pi * (np.outer(t, kk) % LS) / LS
        dfts.append((np.stack([np.cos(ang), -np.sin(ang)], axis=1) / np.sqrt(LS)).astype(ml_dtypes.bfloat16))
    c["dfts"] = dfts
    c["hmask"] = [np.tile(np.array([[float(hf), 1.0 - hf]], np.float32), (128, 1)) for hf in range(2)]
    return c


def _pvec(inp):
    out = np.zeros((2, 128, NV), np.float32)
    for l in range(2):
        o = out[l]
        o[:, C_G1:C_G1 + 16] = inp["g_norm1"][l].reshape(16, 128).T
        o[:, C_G2:C_G2 + 16] = inp["g_norm2"][l].reshape(16, 128).T
        o[:, C_BADA:C_BADA + 96] = inp["b_ada"][l].reshape(96, 128).T
        o[:, C_PSC:C_PSC + 4] = inp["pool_scale"][l].reshape(4, 128).T
        dw = inp["conv_dw"][l]
        for c in range(4):
            o[:, C_DW + c * 31:C_DW + (c + 1) * 31] = dw[:, c * 128:(c + 1) * 128].T
        o[:, C_DWB:C_DWB + 4] = inp["conv_dw_b"][l].reshape(4, 128).T
        o[:, C_LNG:C_LNG + 4] = inp["conv_ln_g"][l].reshape(4, 128).T
        o[:, C_LNB:C_LNB + 4] = inp["conv_ln_b"][l].reshape(4, 128).T
        o[:, C_PWB:C_PWB + 4] = inp["conv_pw_b"][l].reshape(4, 128).T
        o[:, C_GQ] = np.tile(inp["g_q"][l], 2)
        o[:, C_GK] = np.tile(inp["g_k"][l], 2)
        o[:, C_GSUB] = inp["g_subln"][l]
        for q in range(4):
            o[:64, C_LAM + q] = inp["lam"][l][q]
    return out


_CACHE = {}


def make_in_maps(inp, cores):
    inp = {k: np.ascontiguousarray(np.asarray(v)) for k, v in inp.items()}
    cst = _consts()
    pvec = _pvec(inp)
    maps = []
    for c in cores:
        b, hf = c // 2, c % 2
        m = {
            "xp": inp["x_prompt"][2 * c:2 * c + 2].reshape(TP, D),
            "xs": inp["x_sample"][b, hf * TS:(hf + 1) * TS],
            "ck": inp["cache_k"][b].reshape(2, PAST, 512),
            "cv": inp["cache_v"][b].reshape(2, PAST, 512),
            "cT": np.ascontiguousarray(np.concatenate([np.stack([inp["c_ctx"], inp["c"][b]], axis=1), np.zeros((D, 6), np.float32)], axis=1).reshape(KC, 128, 8).transpose(1, 0, 2)),
            "pvec": pvec,
            "ident": cst["ident"], "rotm": cst["rotm"], "bones": cst["bones"],
            "rope": cst["rope"][hf], "icnt_p": cst["icnt_p"], "icnt_s": cst["icnt_s"][hf],
            "dft128": cst["dft128"], "dftp": cst["dftp"], "dfts": cst["dfts"][hf], "hmask": cst["hmask"][hf],
        }
        for k in ("w_ada", "w_in", "pool_w", "conv_pw", "fourier_w", "w_out", "w_gate", "w_up", "w_down"):
            m[k] = inp[k]
        maps.append({k: np.ascontiguousarray(v) for k, v in m.items()})
    return maps


def kernel(**inputs):
    if "nc" not in _CACHE:
        _CACHE["nc"] = build_program()
    nc = _CACHE["nc"]
    maps = make_in_maps(inputs, list(range(NCORES)))
    res = run_bass_kernel_spmd(nc, maps, core_ids=list(range(NCORES)))
    R = res.results
    yp = np.zeros((16, LP, D), np.float32)
    ys = np.zeros((4, LS, D), np.float32)
    nk = np.zeros((16, 2, LP, 4, 2, 64), np.float32)
    nv = np.zeros((16, 2, LP, 4, 128), np.float32)
    for c in range(NCORES):
        b, hf = c // 2, c % 2
        yp[2 * c:2 * c + 2] = np.asarray(R[c]["yp"]).reshape(2, LP, D)
        ys[b, hf * TS:(hf + 1) * TS] = np.asarray(R[c]["ys"])
        nk[2 * c:2 * c + 2] = np.asarray(R[c]["nk"]).reshape(2, 2, LP, 4, 2, 64)
        nv[2 * c:2 * c + 2] = np.asarray(R[c]["nv"]).reshape(2, 2, LP, 4, 128)
    return (yp, ys, nk, nv)
```

```python
import contextlib
import os
import numpy as np
import ml_dtypes
import concourse.bass as bass
import concourse.mybir as mybir
from concourse.bass_utils import run_bass_kernel_spmd

F32 = mybir.dt.float32
BF16 = mybir.dt.bfloat16
AF = mybir.ActivationFunctionType
ALU = mybir.AluOpType

D = 2048
KC = 16
DFF = 5632
FC = 44
INC = 3584
EPS = 1e-6
NCORES = 8
TP = 512
TS = 1024
LP = 256
LS = 2048
PAST = 512
XR = 1568
C_G1, C_G2, C_BADA, C_PSC, C_DW, C_DWB, C_LNG, C_LNB, C_PWB = 0, 16, 32, 128, 132, 256, 260, 264, 268
C_GQ, C_GK, C_GSUB, C_LAM = 272, 273, 274, 275
NV = 280
LAM_INIT = [0.8 - 0.6 * float(np.exp(-0.3 * l)) for l in range(2)]
FFN_GROUPS = [(0, 12), (12, 12), (24, 12), (36, 8)]


class Buf:
    __slots__ = ("name", "w", "r", "dkey", "dcnt")

    def __init__(self, name):
        self.name = name
        self.w = {}
        self.r = {}
        self.dkey = {}
        self.dcnt = 0


class _Rec:
    def __init__(self):
        self.call = None

    def __getattr__(self, name):
        def f(*a, **k):
            self.call = (name, a, k)
            return None
        return f


class Eng:
    def __init__(self, name, key):
        self.name = name
        self.key = key
        self.ops = []
        self.cnt = 0
        self.waited = {}


class Prog:
    def __init__(self, nc, stack, n_dsem=100):
        self.nc = nc
        self.stack = stack
        self.sems = []
        self.pe = Eng("tensor", self._sem("s_pe"))
        self.act = Eng("scalar", self._sem("s_act"))
        self.dve = Eng("vector", self._sem("s_dve"))
        self.pool = Eng("gpsimd", self._sem("s_pool"))
        self.sp = Eng("sync", self._sem("s_sp"))
        self.engs = [self.pe, self.act, self.dve, self.pool, self.sp]
        self.free_dsems = {"sync": [], "gpsimd": [], "scalar": []}
        self.dsem_cnt = {}
        self.n_dsem = 0
        self.arena_deps = {}
        self.flip = 0

    def _sem(self, name):
        s = self.stack.enter_context(self.nc.semaphore(name))
        self.sems.append(s)
        return len(self.sems) - 1

    def buf(self, name):
        b = Buf(name)
        b.w = dict(self.arena_deps)
        return b

    def emit(self, eng, fn, reads=(), writes=(), pwrites=(), sig=True, dma=None, inc=None):
        waits = {}

        def need(d):
            for s, v in d.items():
                if waits.get(s, 0) < v:
                    waits[s] = v

        for b in reads:
            need(b.w)
        for b in writes:
            need(b.w)
            need(b.r)
        for b in pwrites:
            need(b.w)
            need(b.r)
        wl = []
        for s, v in waits.items():
            if s == eng.key and (v > eng.cnt or eng is self.pe):
                continue
            if eng.waited.get(s, 0) < v:
                eng.waited[s] = v
                wl.append((s, v))
        if dma is not None:
            if eng.name not in dma.dkey:
                fl = self.free_dsems[eng.name]
                if fl:
                    dma.dkey[eng.name] = fl.pop(0)
                else:
                    dma.dkey[eng.name] = self._sem("d%d" % self.n_dsem)
                    self.n_dsem += 1
                    self.dsem_cnt[dma.dkey[eng.name]] = 0
            dk = dma.dkey[eng.name]
            self.dsem_cnt[dk] += 16
            tok = (dk, self.dsem_cnt[dk])
            incr = (dk, 16)
        elif inc is not None:
            tok = inc
            incr = (inc[0], 1)
        else:
            tok = (eng.key, eng.cnt + 1)
            if sig:
                eng.cnt += 1
                incr = (eng.key, 1)
            else:
                incr = None
        rec = _Rec()
        fn(rec)
        assert rec.call is not None
        eng.ops.append((wl, rec.call, incr))
        for b in reads:
            if b.r.get(tok[0], 0) < tok[1]:
                b.r[tok[0]] = tok[1]
        for b in writes:
            b.w = {tok[0]: tok[1]}
            b.r = {}
        for b in pwrites:
            if b.w.get(tok[0], 0) < tok[1]:
                b.w[tok[0]] = tok[1]
        return tok

    def retire(self, bufs):
        for b in bufs:
            for en_, dk_ in b.dkey.items():
                self.free_dsems[en_].append(dk_)
            b.dkey = {}
            for d in (b.w, b.r):
                for s, v in d.items():
                    if self.arena_deps.get(s, 0) < v:
                        self.arena_deps[s] = v

    def check(self):
        semv = {}
        pos = {e.name: 0 for e in self.engs}
        progress = True
        while progress:
            progress = False
            for e in self.engs:
                while pos[e.name] < len(e.ops):
                    wl, fn, incr = e.ops[pos[e.name]]
                    if all(semv.get(s_, 0) >= v for s_, v in wl):
                        if incr is not None:
                            semv[incr[0]] = semv.get(incr[0], 0) + incr[1]
                        pos[e.name] += 1
                        progress = True
                    else:
                        break
        stuck = {e.name: (pos[e.name], len(e.ops)) for e in self.engs if pos[e.name] < len(e.ops)}
        if stuck:
            for e in self.engs:
                if pos[e.name] < len(e.ops):
                    wl, fn, incr = e.ops[pos[e.name]]
                    print("STUCK", e.name, pos[e.name], "/", len(e.ops), "waits", [(s_, v, semv.get(s_, 0)) for s_, v in wl])
            raise RuntimeError("deadlock in emitted program: %s" % stuck)
        print("check ok: ops per engine", {e.name: len(e.ops) for e in self.engs}, "nsems", len(self.sems))

    def replay(self):
        self.check()
        nc = self.nc
        sems = self.sems
        with nc.Block() as block:
            def mk(eng):
                def body(e):
                    for wl, fn, incr in eng.ops:
                        for s, v in wl:
                            e.wait_ge(sems[s], v)
                        ins = getattr(e, fn[0])(*fn[1], **fn[2])
                        if incr is not None:
                            ins.then_inc(sems[incr[0]], incr[1])
                return body
            block.tensor(mk(self.pe))
            block.scalar(mk(self.act))
            block.vector(mk(self.dve))
            block.gpsimd(mk(self.pool))
            block.sync(mk(self.sp))


class Arena:
    def __init__(self, prog, tensor, nwords):
        self.p = prog
        self.t = tensor
        self.n = nwords
        self.top = 0
        self.live = []

    def mark(self):
        return (self.top, len(self.live))

    def release(self, m):
        top, nl = m
        self.p.retire(self.live[nl:])
        del self.live[nl:]
        self.top = top

    def alloc(self, name, shape, dt):
        n = 1
        for s in shape:
            n *= s
        words = n if dt == F32 else (n + 1) // 2
        words = (words + 7) // 8 * 8
        assert self.top + words <= self.n, ("arena overflow", name, self.top, words, self.n)
        ap = self.t[:, self.top:self.top + words]
        self.top += words
        if dt != F32:
            ap = ap.bitcast(dt)
        ap = ap[:, 0:n]
        if len(shape) == 2:
            ap = ap.rearrange("p (a b) -> p a b", a=shape[0])
        elif len(shape) == 3:
            ap = ap.rearrange("p (a b c) -> p a b c", a=shape[0], b=shape[1])
        return ap

    def newbuf(self, name):
        b = self.p.buf(name)
        self.live.append(b)
        return b


def build_program(stop_after=None):
    nc = bass.Bass("TRN2", target_bir_lowering=False)
    stack = contextlib.ExitStack()
    with stack:
        _build(nc, stack, stop_after)
    return nc


def _build(nc, stack, stop_after):
    P = Prog(nc, stack)
    emit = P.emit
    pe, act, dve, pool, sp = P.pe, P.act, P.dve, P.pool, P.sp

    def din(name, shape, dt=F32):
        return nc.dram_tensor(name, list(shape), dt, kind="ExternalInput").ap()

    def dout(name, shape, dt=F32):
        return nc.dram_tensor(name, list(shape), dt, kind="ExternalOutput").ap()

    xp_d = din("xp", [TP, D])
    xs_d = din("xs", [TS, D])
    ck_d = din("ck", [2, PAST, 512])
    cv_d = din("cv", [2, PAST, 512])
    cT_d = din("cT", [128, KC, 8])
    pvec_d = din("pvec", [2, 128, NV])
    ident_d = din("ident", [128, 128])
    rotm_d = din("rotm", [128, 128])
    bones_d = din("bones", [128, 128])
    rope_d = din("rope", [128, 2, TS])
    icntp_d = din("icnt_p", [4, LP])
    icnts_d = din("icnt_s", [4, TS])
    dft128_d = din("dft128", [128, 256])
    dftp_d = din("dftp", [LP, 2, LP], BF16)
    dfts_d = din("dfts", [LS, 2, TS], BF16)
    hmask_d = din("hmask", [128, 2])
    w_ada_d = din("w_ada", [2, D, 6 * D])
    w_in_d = din("w_in", [2, D, INC])
    pool_w_d = din("pool_w", [2, 4, 128, 128])
    conv_pw_d = din("conv_pw", [2, 512, 512])
    four_w_d = din("fourier_w", [2, 512, 512])
    w_out_d = din("w_out", [2, D, D])
    w_gate_d = din("w_gate", [2, D, DFF])
    w_up_d = din("w_up", [2, D, DFF])
    w_down_d = din("w_down", [2, DFF, D])
    yp_d = dout("yp", [TP, D])
    ys_d = dout("ys", [TS, D])
    nk_d = dout("nk", [2, 2, LP, 512])
    nv_d = dout("nv", [2, 2, LP, 512])
    dbg_d = dout("dbg", [128, KC * TS]) if stop_after else None
    XRA, XRB = 544, 1024

    class _V:
        def __init__(self, t):
            self.t = t

        def ap(self):
            return self.t.ap().rearrange("p c -> (p c)").rearrange("(r c) -> r c", c=1024)
    xsA_t = [nc.dram_tensor("xsA%d" % l, [128, XRA * 8], BF16) for l in range(2)]
    xrA_t = [nc.dram_tensor("xrA%d" % l, [256, XRA * 8], BF16) for l in range(2)]
    xsB_t = [nc.dram_tensor("xsB%d" % l, [128, XRB * 8], BF16) for l in range(2)]
    xrB_t = [nc.dram_tensor("xrB%d" % l, [256, XRB * 8], BF16) for l in range(2)]
    xsA = [_V(t) for t in xsA_t]
    xrA = [_V(t) for t in xrA_t]
    xsB = [_V(t) for t in xsB_t]
    xrB = [_V(t) for t in xrB_t]
    xsendQ = [[P.buf("xsend%d_%d" % (l, q)) for q in range(2)] for l in range(2)]
    xrecvQ = [[P.buf("xrecv%d_%d" % (l, q)) for q in range(2)] for l in range(2)]
    cc_keys = [[P._sem("cc%d_%d" % (l, q)) for q in range(2)] for l in range(2)]

    def sb(name, shape, dt):
        return stack.enter_context(nc.sbuf_tensor("sb_" + name, list(shape), dt))

    xT = sb("xT", [128, KC, TS], F32)
    xTb = [[P.buf("xT%d_%d" % (c, n)) for n in range(2)] for c in range(KC)]
    NSLOT = 4
    wring = [sb("wr%d" % i, [128, 4096], BF16) for i in range(NSLOT)]
    wringB = [P.buf("wr%d" % i) for i in range(NSLOT)]
    wnext = [0]
    pv = sb("pv", [128, 2, NV], F32)
    pvB = P.buf("pv")
    modt = sb("modt", [128, 2, 96, 2], F32)
    modB = P.buf("modt")
    gsc = sb("gsc", [128, 2, 2, KC, 2], F32)
    gscB = P.buf("gsc")
    ident = sb("ident", [128, 128], F32)
    rotm = sb("rotm", [128, 128], F32)
    bones = sb("bones", [128, 128], F32)
    ones_f = sb("ones_f", [128, 128], F32)
    ones_b = sb("ones_b", [128, 128], BF16)
    constB = P.buf("const")
    rope = sb("rope", [128, 2, TS], F32)
    dft128 = sb("dft128", [128, 256], BF16)
    dftp = sb("dftp", [128, 2, 2, LP], BF16)
    hmask = sb("hmask", [128, 2], F32)
    sT = sb("sT", [128, KC, 8], BF16)
    small = sb("small", [128, 16], F32)
    smallB = P.buf("small")
    ARENA_WORDS = 23600
    arena_t = sb("arena", [128, ARENA_WORDS], F32)
    A = Arena(P, arena_t, ARENA_WORDS)
    banks = [stack.enter_context(nc.psum_tensor("ps%d" % i, [128, 512], F32)) for i in range(8)]
    bankB = [P.buf("ps%d" % i) for i in range(8)]
    dn = [0]
    mn = [0]

    def dbank():
        i = dn[0] % 6
        dn[0] += 1
        return i

    def mbank():
        i = 6 + mn[0] % 2
        mn[0] += 1
        return i

    def ev_eng():
        P.flip ^= 1
        return act if P.flip else dve

    def copy_on(eng, out, in_, reads, writes, pwrites=()):
        if eng is act:
            return emit(act, lambda e: e.activation(out=out, in_=in_, func=AF.Identity), reads, writes, pwrites)
        return emit(dve, lambda e: e.tensor_copy(out=out, in_=in_), reads, writes, pwrites)

    def wload(src, kc, cw):
        i = wnext[0] % NSLOT
        wnext[0] += 1
        view = wring[i][:, 0:kc * cw].rearrange("p (k c) -> p k c", k=kc)
        emit(pool, lambda e: e.dma_start(out=view, in_=src), [], [wringB[i]], dma=wringB[i])
        return wringB[i], view

    def wsrc(wd, l, r0, nrows, c0, cw):
        return wd[l, r0:r0 + nrows, c0:c0 + cw].rearrange("(k p) n -> p k n", p=128)

    eps_ap = small[:, 0:1]

    emit(sp, lambda e: e.dma_start(out=pv[:], in_=pvec_d.rearrange("l p v -> p l v")), [], [pvB], dma=pvB)
    cB = P.buf("cld")
    for (t, d) in ((ident, ident_d), (rotm, rotm_d), (bones, bones_d)):
        emit(sp, lambda e, t=t, d=d: e.dma_start(out=t[:], in_=d), [], [], pwrites=[constB], dma=cB)
    emit(sp, lambda e: e.dma_start(out=rope[:], in_=rope_d), [], [], pwrites=[constB], dma=cB)
    emit(sp, lambda e: e.dma_start(out=hmask[:], in_=hmask_d), [], [], pwrites=[constB], dma=cB)
    emit(sp, lambda e: e.dma_start(out=dftp[:], in_=dftp_d.rearrange("(i p) a k -> p i a k", p=128)), [], [],
         pwrites=[constB], dma=cB)
    emit(pool, lambda e: e.dma_start(out=dft128[:], in_=dft128_d), [], [], pwrites=[constB], dma=cB)
    emit(dve, lambda e: e.memset(ones_f[:], 1.0), [], [], pwrites=[constB])
    emit(dve, lambda e: e.memset(ones_b[:], 1.0), [], [], pwrites=[constB])
    emit(dve, lambda e: e.memset(small[:, 0:1], EPS), [], [smallB])

    m0 = A.mark()
    ctmp = A.alloc("ctmp", [KC, 8], F32)
    ctB = A.newbuf("ctmp")
    sTB = P.buf("sT")
    emit(sp, lambda e: e.dma_start(out=ctmp, in_=cT_d), [], [ctB], dma=ctB)
    emit(act, lambda e: e.activation(out=sT[:], in_=ctmp, func=AF.Silu), [ctB], [sTB])
    lp = A.alloc("lamp", [4], F32)
    lpB = A.newbuf("lamp")
    for l in range(2):
        for q in range(2):
            emit(dve, lambda e, l=l, q=q: e.tensor_tensor(
                out=lp[:, 2 * l + q:2 * l + q + 1], in0=pv[:, l, C_LAM + 2 * q:C_LAM + 2 * q + 1],
                in1=pv[:, l, C_LAM + 2 * q + 1:C_LAM + 2 * q + 2], op=ALU.mult), [pvB], [], pwrites=[lpB])
    bi = mbank()
    emit(pe, lambda e: e.matmul(banks[bi][:, 0:4], lhsT=ones_f[:], rhs=lp, start=True, stop=True),
         [lpB, constB], [bankB[bi]])
    le = A.alloc("lame", [4], F32)
    leB = A.newbuf("lame")
    emit(act, lambda e: e.activation(out=le, in_=banks[bi][:, 0:4], func=AF.Exp), [bankB[bi]], [leB])
    for l in range(2):
        emit(dve, lambda e, l=l: e.tensor_tensor(out=small[:, 5 + l:6 + l], in0=le[:, 2 * l + 1:2 * l + 2],
                                                  in1=le[:, 2 * l:2 * l + 1], op=ALU.subtract),
             [leB], [], pwrites=[smallB])
        emit(dve, lambda e, l=l: e.tensor_scalar(out=small[:, 1 + l:2 + l], in0=small[:, 5 + l:6 + l],
                                                  scalar1=-LAM_INIT[l], scalar2=None, op0=ALU.add),
             [smallB], [], pwrites=[smallB])
        emit(dve, lambda e, l=l: e.tensor_scalar(out=small[:, 3 + l:4 + l], in0=pv[:, l, C_GSUB:C_GSUB + 1],
                                                  scalar1=1.0 - LAM_INIT[l], scalar2=None, op0=ALU.mult),
             [pvB], [], pwrites=[smallB])

    def gsc_emit(l, s_):
        for j in range(2):
            emit(dve, lambda e, j=j: e.scalar_tensor_tensor(
                out=gsc[:, l, s_, :, j], in0=modt[:, l, 48 * s_ + 16:48 * s_ + 32, j], scalar=1.0,
                in1=pv[:, l, (C_G1 if s_ == 0 else C_G2):(C_G1 if s_ == 0 else C_G2) + KC],
                op0=ALU.add, op1=ALU.mult), [modB, pvB], [], pwrites=[gscB])

    def mods_gen(l, mlo, mhi, gsc_after=None):
        for blk in range(mlo // 2, mhi // 2):
            wb, wv = wload(wsrc(w_ada_d, l, 0, D, blk * 256, 256), KC, 256)
            mb = mbank()
            mps = banks[mb][:, 0:16].rearrange("p (m j) -> p m j", j=8)
            for sub in range(2):
                for k in range(KC):
                    first = (sub == 0 and k == 0)
                    emit(pe, lambda e, k=k, sub=sub: e.matmul(
                        mps[:, sub, :], lhsT=wv[:, k, sub * 128:(sub + 1) * 128], rhs=sT[:, k, :],
                        start=(k == 0), stop=(k == KC - 1)),
                        [wb, sTB], [bankB[mb]] if first else [], sig=(k == KC - 1 and sub == 1))
            for j in range(2):
                emit(dve, lambda e, j=j, blk=blk: e.tensor_tensor(
                    out=modt[:, l, 2 * blk:2 * blk + 2, j], in0=mps[:, :, j],
                    in1=pv[:, l, C_BADA + 2 * blk:C_BADA + 2 * blk + 2], op=ALU.add),
                    [bankB[mb], pvB], [], pwrites=[modB])
            yield 1
        if gsc_after is not None:
            gsc_emit(l, gsc_after)


    def _bg_chain():
        yield from mods_gen(0, 32, 48)
        yield from mods_gen(0, 48, 80, 1)
        yield from mods_gen(0, 80, 96)
        yield from mods_gen(1, 0, 32, 0)
        yield from mods_gen(1, 32, 80, 1)
        yield from mods_gen(1, 80, 96)
    bg = [_bg_chain(), 0, 0]

    def bg_poll():
        if bg[0] is None:
            return False
        try:
            next(bg[0])
            bg[1] += 1
            return True
        except StopIteration:
            bg[0] = None
            return False

    def bg_tick(light=False):
        bg[2] += 1
        if light:
            bg_poll()
            bg_poll()
        elif bg[1] < 32:
            bg_poll()
        elif bg[2] % 2 == 0:
            bg_poll()

    def bg_drain(nblocks=None):
        while bg[0] is not None and (nblocks is None or bg[1] < nblocks):
            if not bg_poll():
                break
    A.release(m0)
    early = stop_after in ("mods", "loadx")

    def load_x(src_d, T):
        m = A.mark()
        stg = [A.alloc("xstg%d" % i, [D], F32) for i in range(2)]
        stgB = [A.newbuf("xstg%d" % i) for i in range(2)]
        for i in range(T // 128):
            s_ = i % 2
            emit(sp, lambda e, i=i, s_=s_: e.dma_start(out=stg[s_], in_=src_d[i * 128:(i + 1) * 128, :]),
                 [], [stgB[s_]], dma=stgB[s_])
            for c4 in range(4):
                b = mbank()
                for q in range(4):
                    c = c4 * 4 + q
                    emit(pe, lambda e, b=b, q=q, c=c, s_=s_: e.transpose(
                        out=banks[b][:, q * 128:(q + 1) * 128], in_=stg[s_][:, c * 128:(c + 1) * 128],
                        identity=ident[:]), [stgB[s_], constB], [bankB[b]] if q == 0 else [], sig=(q == 3))
                n = i // 4
                o = xT[:, c4 * 4:(c4 + 1) * 4, i * 128:(i + 1) * 128]
                copy_on(ev_eng(), o, banks[b][:].rearrange("p (q t) -> p q t", q=4), [bankB[b]], [],
                        pwrites=[xTb[c4 * 4 + q][n] for q in range(4)])
        A.release(m)

    def store_y(dst_d, T):
        m = A.mark()
        stg = [A.alloc("ystg%d" % i, [D], F32) for i in range(2)]
        stgB = [A.newbuf("ystg%d" % i) for i in range(2)]
        last = []
        for i in range(T // 128):
            s_ = i % 2
            n = i // 4
            for c4 in range(4):
                b = mbank()
                for q in range(4):
                    c = c4 * 4 + q
                    emit(pe, lambda e, b=b, q=q, c=c, i=i: e.transpose(
                        out=banks[b][:, q * 128:(q + 1) * 128], in_=xT[:, c, i * 128:(i + 1) * 128],
                        identity=ident[:]), [xTb[c][n], constB], [bankB[b]] if q == 0 else [], sig=(q == 3))
                if c4 == 0:
                    copy_on(ev_eng(), stg[s_][:, 0:512], banks[b][:], [bankB[b]], [stgB[s_]])
                else:
                    copy_on(ev_eng(), stg[s_][:, c4 * 512:(c4 + 1) * 512], banks[b][:], [bankB[b]], [],
                            pwrites=[stgB[s_]])
            tok = emit(sp, lambda e, i=i, s_=s_: e.dma_start(out=dst_d[i * 128:(i + 1) * 128, :], in_=stg[s_]),
                       [stgB[s_]], [], dma=stgB[s_])
            last.append(stgB[s_])
        A.release(m)
        return last

    def rstd_from(ps_ap, scale, out_ap, rB, wB, tmp_ap, tmpB):
        emit(act, lambda e: e.activation(out=tmp_ap, in_=ps_ap, func=AF.Sqrt, bias=eps_ap, scale=scale),
             rB + [smallB], [tmpB])
        emit(dve, lambda e: e.reciprocal(out=out_ap, in_=tmp_ap), [tmpB], wB)

    def norm(l, s, T, j, hT, hTb):
        m = A.mark()
        sq = [A.alloc("sq%d" % i, [512], BF16) for i in range(2)]
        sqB = [A.newbuf("sq%d" % i) for i in range(2)]
        tm = [A.alloc("ntm%d" % i, [512], F32) for i in range(2)]
        tmB = [A.newbuf("ntm%d" % i) for i in range(2)]
        rs = A.alloc("nrs", [512], F32)
        rsB = A.newbuf("nrs")
        rt = A.alloc("nrt", [512], F32)
        rtB = A.newbuf("nrt")
        for n in range(T // 512):
            b = mbank()
            for c in range(KC):
                i = c % 2
                if c % 2 == 0:
                    emit(act, lambda e, c=c, n=n, i=i: e.activation(out=sq[i], in_=xT[:, c, n * 512:(n + 1) * 512],
                                                                   func=AF.Square), [xTb[c][n]], [sqB[i]])
                else:
                    emit(dve, lambda e, c=c, n=n, i=i: e.tensor_tensor(out=sq[i], in0=xT[:, c, n * 512:(n + 1) * 512],
                                                                      in1=xT[:, c, n * 512:(n + 1) * 512], op=ALU.mult),
                         [xTb[c][n]], [sqB[i]])
                emit(pe, lambda e, c=c, i=i, b=b: e.matmul(banks[b][:], lhsT=ones_b[:], rhs=sq[i],
                                                          start=(c == 0), stop=(c == KC - 1)),
                     [sqB[i], constB], [bankB[b]] if c == 0 else [], pwrites=[] if c == 0 else [bankB[b]], sig=True)
            import os
            NP_ = int(os.environ.get("NORM_PARTS", "3"))
            if NP_ < 2:
                continue
            rstd_from(banks[b][:], 1.0 / D, rs, [bankB[b]], [rsB], rt, rtB)
            if NP_ < 3:
                continue
            for c in range(KC):
                i = c % 2
                emit(dve, lambda e, c=c, n=n, i=i: e.tensor_tensor(out=tm[i], in0=xT[:, c, n * 512:(n + 1) * 512],
                                                                  in1=rs, op=ALU.mult), [xTb[c][n], rsB], [tmB[i]])
                NV_ = int(os.environ.get("NORM_VAR", "0"))
                if NV_ == 1:
                    emit(act, lambda e, c=c, n=n, i=i: e.activation(
                        out=hT[:, c, n * 512:(n + 1) * 512], in_=tm[i], func=AF.Identity),
                        [tmB[i], modB, gscB], [hTb[c][n]])
                elif NV_ == 2:
                    pass
                elif NV_ == 3:
                    emit(act, lambda e, c=c, n=n, i=i: e.activation(
                        out=hT[:, c, n * 512:(n + 1) * 512], in_=tm[i], func=AF.Identity,
                        bias=small[:, 0:1], scale=small[:, 0:1]),
                        [tmB[i], modB, gscB], [hTb[c][n]])
                else:
                    emit(act, lambda e, c=c, n=n, i=i: e.activation(
                        out=hT[:, c, n * 512:(n + 1) * 512], in_=tm[i], func=AF.Identity,
                        bias=modt[:, l, 48 * s + c, j:j + 1], scale=gsc[:, l, s, c, j:j + 1]),
                        [tmB[i], modB, gscB], [hTb[c][n]])
        A.release(m)

    def job(K, NT, ncols, lhs_fn, rhs_fn):
        bl = [dbank() for _ in range(NT)]
        for k in range(K):
            wb, lhsT = lhs_fn(k)
            for n in range(NT):
                rb, rhs = rhs_fn(k, n)
                emit(pe, lambda e, b=bl[n], lhsT=lhsT, rhs=rhs, k=k: e.matmul(
                    banks[b][:, 0:ncols], lhsT=lhsT, rhs=rhs, start=(k == 0), stop=(k == K - 1)),
                    [wb, rb], [bankB[bl[n]]] if k == 0 else [], sig=(k == K - 1 and n == NT - 1))
        return bl

    def residual_partial(K, NT, l, gate_chunk0, j, wd, r0, rhs_ap, rhsB_fn, overwrite=False):
        cw = 512 if K <= 8 else 256
        for ob in range(D // cw):
            bg_tick(light=(K == 4))
            wb, wv = wload(wsrc(wd, l, r0, K * 128, ob * cw, cw), K, cw)
            for sub in range(cw // 128):
                mo = ob * (cw // 128) + sub
                bl = job(K, NT, 512,
                         lambda k, wv=wv, sub=sub: (wb, wv[:, k, sub * 128:(sub + 1) * 128]),
                         lambda k, n: (rhsB_fn(k, n), rhs_ap[:, k, n * 512:(n + 1) * 512]))
                for n in range(NT):
                    if overwrite:
                        emit(dve, lambda e, b=bl[n], mo=mo, n=n: e.tensor_scalar(
                            out=xT[:, mo, n * 512:(n + 1) * 512], in0=banks[b][:],
                            scalar1=modt[:, l, gate_chunk0 + mo, j:j + 1], scalar2=None, op0=ALU.mult),
                            [bankB[bl[n]], modB], [xTb[mo][n]])
                        continue
                    emit(dve, lambda e, b=bl[n], mo=mo, n=n: e.scalar_tensor_tensor(
                        out=xT[:, mo, n * 512:(n + 1) * 512], in0=banks[b][:],
                        scalar=modt[:, l, gate_chunk0 + mo, j:j + 1], in1=xT[:, mo, n * 512:(n + 1) * 512],
                        op0=ALU.mult, op1=ALU.add), [bankB[bl[n]], modB, xTb[mo][n]], [xTb[mo][n]])

    def qknorm(b, n, gcol, l, use_rope, out_bf, outB, out_f32=None, out_f32B=None, tmps=None):
        sq, sqB, rs, rsB, rt, rtB, qn, qnB, t1, t1B = tmps
        emit(act, lambda e: e.activation(out=sq, in_=banks[b][:], func=AF.Square), [bankB[b]], [sqB])
        b2 = mbank()
        emit(pe, lambda e: e.matmul(banks[b2][:], lhsT=bones[:], rhs=sq, start=True, stop=True),
             [sqB, constB], [bankB[b2]])
        rstd_from(banks[b2][:], 1.0 / 64, rs, [bankB[b2]], [rsB], rt, rtB)
        gq = pv[:, l, gcol:gcol + 1]
        if not use_rope:
            emit(dve, lambda e: e.scalar_tensor_tensor(out=qn, in0=banks[b][:], scalar=gq, in1=rs,
                                                       op0=ALU.mult, op1=ALU.mult), [bankB[b], rsB, pvB], [qnB])
            emit(act, lambda e: e.activation(out=out_bf, in_=qn, func=AF.Identity), [qnB], [], pwrites=[outB])
            return
        emit(dve, lambda e: e.scalar_tensor_tensor(out=qn, in0=banks[b][:], scalar=gq, in1=rs,
                                                   op0=ALU.mult, op1=ALU.mult), [bankB[b], rsB, pvB], [qnB])
        b3 = mbank()
        emit(pe, lambda e: e.matmul(banks[b3][:], lhsT=rotm[:], rhs=qn, start=True, stop=True),
             [qnB, constB], [bankB[b3]])
        emit(dve, lambda e: e.tensor_tensor(out=t1, in0=banks[b3][:], in1=rope[:, 1, n * 512:(n + 1) * 512],
                                            op=ALU.mult), [bankB[b3], constB], [t1B])
        emit(dve, lambda e: e.tensor_tensor(out=qn, in0=qn, in1=rope[:, 0, n * 512:(n + 1) * 512],
                                            op=ALU.mult), [qnB, constB], [qnB])
        emit(dve, lambda e: e.tensor_tensor(out=out_bf, in0=qn, in1=t1, op=ALU.add), [qnB, t1B], [],
             pwrites=[outB])

    def qk_tmps():
        sq = A.alloc("qsq", [512], F32); sqB = A.newbuf("qsq")
        rs = A.alloc("qrs", [512], F32); rsB = A.newbuf("qrs")
        qn = A.alloc("qqn", [512], F32); qnB = A.newbuf("qqn")
        t1 = A.alloc("qt1", [512], F32); t1B = A.newbuf("qt1")
        return (sq, sqB, rs, rsB, t1, t1B, qn, qnB, t1, t1B)

    def dbg_to_x(ap3, bufs, nch, ncol):
        emit(dve, lambda e: e.tensor_copy(out=xT[:, 0:nch, 0:ncol], in_=ap3), bufs, [],
             pwrites=[xTb[c][n] for c in range(KC) for n in range(2)])

    def dbg_dump(ap2d, bufs, ncols):
        emit(sp, lambda e: e.dma_start(out=dbg_d[:, 0:ncols], in_=ap2d), bufs, [], dma=P.buf("dbg"))

    def layer(l, G):
        T, NT, j, sample = G["T"], G["T"] // 512, G["j"], G["sample"]
        if l == 1 or not G.get("first", False):
            bg_drain()
        segs = G["segs"]
        PP, PC = 8, 16
        base = A.mark()
        qT = A.alloc("qT", [4, T], BF16); qTB = A.newbuf("qT")
        nseg = len(segs)
        Lseg = segs[0][1]
        UW = Lseg + 2 * PP
        GW = Lseg + 2 * PC
        if not sample:
            kTp = A.alloc("kTp", [4, T], BF16); kTpB = A.newbuf("kTp")
            vTp = A.alloc("vTp", [T // 128, 512], BF16); vTpB = A.newbuf("vTp")
            fTp = A.alloc("fTp", [4, T], BF16); fTpB = A.newbuf("fTp")
        m_q = A.mark()
        gT = A.alloc("gT", [4, nseg * GW], F32); gTB = [A.newbuf("gT%d" % c) for c in range(4)]
        m_g = A.mark()
        uP = A.alloc("uP", [4, nseg * UW], F32); uPB = [A.newbuf("uP%d" % g) for g in range(4)]
        m_ug = A.mark()
        hT = A.alloc("hT", [KC, T], BF16)
        hTb = [[A.newbuf("hT%d_%d" % (c, n)) for n in range(NT)] for c in range(KC)]
        norm(l, 0, T, j, hT, hTb)

        def hrhs(k, n):
            return (hTb[k][n], hT[:, k, n * 512:(n + 1) * 512])
        stg_tag = "%d%s" % (l, "s" if sample else "p")
        if stop_after == "norm" + stg_tag:
            return False

        def proj_jobs(col0, nchunks, evac):
            for blk in range(nchunks // 2):
                bg_tick()
                wb, wv = wload(wsrc(w_in_d, l, 0, D, col0 + blk * 256, 256), KC, 256)
                for sub in range(2):
                    ch = blk * 2 + sub
                    bl = job(KC, NT, 512, lambda k, wv=wv, sub=sub, wb=wb: (wb, wv[:, k, sub * 128:(sub + 1) * 128]),
                             hrhs)
                    for n in range(NT):
                        evac(ch, n, bl[n])

        mfr = A.mark()
        tmps = qk_tmps()
        if sample:
            kst = [A.alloc("kst%d" % i, [512], BF16) for i in range(2)]
            kstB = [A.newbuf("kst%d" % i) for i in range(2)]
            kcnt = [0]

            def k_evac(ch, n, b):
                i = kcnt[0] % 2
                kcnt[0] += 1
                qknorm(b, n, C_GK, l, True, kst[i], kstB[i], tmps=tmps)
                emit(sp, lambda e, i=i, ch=ch, n=n: e.dma_start(
                    out=xsA[l].ap()[ch * 128:(ch + 1) * 128, n * 512:(n + 1) * 512], in_=kst[i]),
                    [kstB[i]], [], pwrites=[xsendQ[l][0]], dma=kstB[i])
        else:
            kst2 = [A.alloc("nkst%d" % i, [512], F32) for i in range(2)]
            kst2B = [A.newbuf("nkst%d" % i) for i in range(2)]
            kcnt = [0]

            def k_evac(ch, n, b):
                sq, sqB, rs, rsB, rt, rtB, qn, qnB, t1, t1B = tmps
                emit(act, lambda e: e.activation(out=sq, in_=banks[b][:], func=AF.Square), [bankB[b]], [sqB])
                b2 = mbank()
                emit(pe, lambda e: e.matmul(banks[b2][:], lhsT=bones[:], rhs=sq, start=True, stop=True),
                     [sqB, constB], [bankB[b2]])
                rstd_from(banks[b2][:], 1.0 / 64, rs, [bankB[b2]], [rsB], rt, rtB)
                emit(dve, lambda e: e.scalar_tensor_tensor(out=qn, in0=banks[b][:], scalar=pv[:, l, C_GK:C_GK + 1],
                                                           in1=rs, op0=ALU.mult, op1=ALU.mult),
                     [bankB[b], rsB, pvB], [qnB])
                emit(act, lambda e: e.activation(out=kTp[:, ch, :], in_=qn, func=AF.Identity), [qnB], [],
                     pwrites=[kTpB])
                b3 = mbank()
                for q in range(4):
                    emit(pe, lambda e, q=q: e.transpose(out=banks[b3][:, q * 128:(q + 1) * 128],
                                                        in_=qn[:, q * 128:(q + 1) * 128], identity=ident[:]),
                         [qnB, constB], [bankB[b3]] if q == 0 else [], sig=(q == 3))
                i = kcnt[0] % 2
                kcnt[0] += 1
                copy_on(ev_eng(), kst2[i], banks[b3][:], [bankB[b3]], [kst2B[i]])
                for sq_ in range(2):
                    emit(sp, lambda e, i=i, ch=ch, sq_=sq_: e.dma_start(
                        out=nk_d[sq_, l, :, ch * 128:(ch + 1) * 128].rearrange("(h p) f -> p h f", p=128),
                        in_=kst2[i].rearrange("p (q f) -> p q f", q=4)[:, 2 * sq_:2 * sq_ + 2, :]),
                        [kst2B[i]], [], pwrites=[outB_all], dma=kst2B[i])
        proj_jobs(1024, 4, k_evac)
        if stop_after == "kproj" + stg_tag:
            return False

        if sample:
            vst, vstB = kst, kstB
        else:
            vst = [A.alloc("vst%d" % i, [512], BF16) for i in range(2)]
            vstB = [A.newbuf("vst%d" % i) for i in range(2)]
        if not sample:
            vsf = [A.alloc("vsf%d" % i, [512], F32) for i in range(2)]
            vsfB = [A.newbuf("vsf%d" % i) for i in range(2)]
        wv_blocks = [wload(wsrc(w_in_d, l, 0, D, 1536 + hb * 256, 256), KC, 256) for hb in range(2)]
        for i in range(T // 128):
            b = dbank()
            n = i // 4
            for hb in range(2):
                wb, wv = wv_blocks[hb]
                for k in range(KC):
                    emit(pe, lambda e, b=b, hb=hb, k=k, wv=wv, i=i: e.matmul(
                        banks[b][:, hb * 256:(hb + 1) * 256], lhsT=hT[:, k, i * 128:(i + 1) * 128], rhs=wv[:, k, :],
                        start=(k == 0), stop=(k == KC - 1)),
                        [wb, hTb[k][n]], [bankB[b]] if (k == 0 and hb == 0) else [],
                        sig=(k == KC - 1 and hb == 1))
            s_ = i % 2
            VV_ = int(os.environ.get("VVAR", "0"))
            if VV_ == 1:
                continue
            if sample:
                copy_on(ev_eng(), vst[s_], banks[b][:], [bankB[b]], [vstB[s_]])
                emit(sp, lambda e, i=i, s_=s_: e.dma_start(
                    out=xsB[l].ap()[512:1024, :].rearrange("r (a c) -> (r a) c", a=2)[i * 128:(i + 1) * 128, :],
                    in_=vst[s_]), [vstB[s_]], [], pwrites=[xsendQ[l][1]], dma=vstB[s_])
            else:
                emit(dve, lambda e, b=b, s_=s_: e.tensor_copy(out=vsf[s_], in_=banks[b][:]), [bankB[b]], [vsfB[s_]])
                copy_on(act, vTp[:, i, :], vsf[s_], [vsfB[s_]], [], pwrites=[vTpB])
                if VV_ == 3:
                    continue
                emit(sp, lambda e, i=i, s_=s_: e.dma_start(
                    out=nv_d[i // 2, l, (i % 2) * 128:(i % 2 + 1) * 128, :], in_=vsf[s_]),
                    [vsfB[s_]], [], pwrites=[outB_all], dma=vsfB[s_])

        if stop_after == "vproj" + stg_tag:
            return False
        if sample:
            fst, fstB = kst, kstB
            fcnt = [0]

            def f_evac(ch, n, b):
                i = fcnt[0] % 2
                fcnt[0] += 1
                copy_on(ev_eng(), fst[i], banks[b][:], [bankB[b]], [fstB[i]])
                emit(sp, lambda e, i=i, ch=ch, n=n: e.dma_start(
                    out=xsB[l].ap()[ch * 128:(ch + 1) * 128, n * 512:(n + 1) * 512], in_=fst[i]),
                    [fstB[i]], [], pwrites=[xsendQ[l][1]], dma=fstB[i])
        else:
            def f_evac(ch, n, b):
                copy_on(ev_eng(), fTp[:, ch, n * 512:(n + 1) * 512], banks[b][:], [bankB[b]], [], pwrites=[fTpB])
        proj_jobs(3072, 4, f_evac)
        if sample:
            emit(pool, lambda e: e.collective_compute(
                "AllGather", ALU.bypass, replica_groups=[[0, 1], [2, 3], [4, 5], [6, 7]],
                ins=[xsB_t[l].ap().opt()], outs=[xrB_t[l].ap().opt()]),
                [xsendQ[l][1]], [xrecvQ[l][1]], inc=(cc_keys[l][1], 1))

        def seg_cols(n, pad, W):
            out = []
            for si, (s0, L) in enumerate(segs):
                lo = max(s0, n * 512)
                hi = min(s0 + L, (n + 1) * 512)
                if lo < hi:
                    out.append((lo - n * 512, hi - lo, si * W + pad + lo - s0))
            return out

        def p_evac(ch, n, b):
            for (o, nc_, dc) in seg_cols(n, PP, UW):
                copy_on(ev_eng(), uP[:, ch, dc:dc + nc_], banks[b][:, o:o + nc_], [bankB[b]], [], pwrites=[uPB[ch]])
        proj_jobs(0, 4, p_evac)

        sg = A.alloc("sg", [T], F32)
        sgB = A.newbuf("sg")
        for half in range(2):
            bg_tick()
            wbb, wvb = wload(wsrc(w_in_d, l, 0, D, 2560 + half * 256, 256), KC, 256)
            wba, wva = wload(wsrc(w_in_d, l, 0, D, 2048 + half * 256, 256), KC, 256)
            for sub in range(2):
                cch = half * 2 + sub
                bl = job(KC, NT, 512, lambda k, sub=sub: (wbb, wvb[:, k, sub * 128:(sub + 1) * 128]), hrhs)
                for n in range(NT):
                    emit(act, lambda e, n=n, b=bl[n]: e.activation(out=sg[:, n * 512:(n + 1) * 512], in_=banks[b][:],
                                                                    func=AF.Sigmoid), [bankB[bl[n]]], [], pwrites=[sgB])
                bl = job(KC, NT, 512, lambda k, sub=sub: (wba, wva[:, k, sub * 128:(sub + 1) * 128]), hrhs)
                for n in range(NT):
                    for (o, nc_, dc) in seg_cols(n, PC, GW):
                        emit(dve, lambda e, o=o, nc_=nc_, dc=dc, n=n, b=bl[n], cch=cch: e.tensor_tensor(
                            out=gT[:, cch, dc:dc + nc_], in0=banks[b][:, o:o + nc_],
                            in1=sg[:, n * 512 + o:n * 512 + o + nc_], op=ALU.mult),
                            [bankB[bl[n]], sgB], [], pwrites=[gTB[cch]])

        if sample:
            hal = xsA[l].ap()[512:544, :].rearrange("r (f e) -> (r f) e", e=64).rearrange("(g p) e -> p g e", p=128)
            hsd = A.alloc("hsd", [4, 64], BF16); hsdB = A.newbuf("hsd")
            emit(dve, lambda e: e.memset(hsd, 0.0), [], [hsdB])
            emit(dve, lambda e: e.tensor_copy(out=hsd[:, :, 0:8], in_=uP[:, :, PP:PP + 8]), uPB, [], pwrites=[hsdB])
            emit(dve, lambda e: e.tensor_copy(out=hsd[:, :, 8:16], in_=uP[:, :, PP + Lseg - 8:PP + Lseg]), uPB, [], pwrites=[hsdB])
            emit(dve, lambda e: e.tensor_copy(out=hsd[:, :, 16:32], in_=gT[:, :, PC:PC + 16]), gTB, [], pwrites=[hsdB])
            emit(dve, lambda e: e.tensor_copy(out=hsd[:, :, 32:48], in_=gT[:, :, PC + Lseg - 16:PC + Lseg]), gTB, [], pwrites=[hsdB])
            emit(sp, lambda e: e.dma_start(out=hal, in_=hsd), [hsdB], [], pwrites=[xsendQ[l][0]], dma=hsdB)
            emit(pool, lambda e: e.collective_compute(
                "AllGather", ALU.bypass, replica_groups=[[0, 1], [2, 3], [4, 5], [6, 7]],
                ins=[xsA_t[l].ap().opt()], outs=[xrA_t[l].ap().opt()]),
                [xsendQ[l][0]], [xrecvQ[l][0]], inc=(cc_keys[l][0], 1))
        else:
            for g in range(4):
                for si in range(nseg):
                    emit(dve, lambda e, g=g, si=si: e.memset(uP[:, g, si * UW:si * UW + PP], 0.0), [], [], pwrites=[uPB[g]])
                    emit(dve, lambda e, g=g, si=si: e.memset(uP[:, g, si * UW + PP + Lseg:(si + 1) * UW], 0.0), [], [], pwrites=[uPB[g]])
                    emit(dve, lambda e, g=g, si=si: e.memset(gT[:, g, si * GW:si * GW + PC], 0.0), [], [], pwrites=[gTB[g]])
                    emit(dve, lambda e, g=g, si=si: e.memset(gT[:, g, si * GW + PC + Lseg:(si + 1) * GW], 0.0), [], [], pwrites=[gTB[g]])

        def q_evac(ch, n, b):
            if sample:
                qknorm(b, n, C_GQ, l, True, qT[:, ch, n * 512:(n + 1) * 512], qTB, tmps=tmps)
            else:
                qknorm(b, n, C_GQ, l, False, qT[:, ch, n * 512:(n + 1) * 512], qTB, tmps=tmps)
        proj_jobs(512, 4, q_evac)
        A.release(m_ug)
        if stop_after == "proj" + stg_tag:
            return False

        g1c = 32

        def ymix_alloc():
            y = A.alloc("ymix", [4, T], BF16)
            yB = [[A.newbuf("ymix%d_%d" % (c, n)) for n in range(NT)] for c in range(4)]
            return y, yB

        bg_drain(9)
        mp = A.mark()
        if sample:
            hst = A.alloc("hst", [4, 64], BF16); hstB = A.newbuf("hst")
            hst2 = A.alloc("hst2", [4, 64], BF16); hst2B = A.newbuf("hst2")
            rv = xrA[l].ap()
            h0 = rv[512:544, :].rearrange("r (f e) -> (r f) e", e=64).rearrange("(g p) e -> p g e", p=128)
            h1 = rv[XRA + 512:XRA + 544, :].rearrange("r (f e) -> (r f) e", e=64).rearrange("(g p) e -> p g e", p=128)
            emit(sp, lambda e: e.dma_start(out=hst, in_=h0), [xrecvQ[l][0]], [hstB], dma=hstB)
            emit(sp, lambda e: e.dma_start(out=hst2, in_=h1), [xrecvQ[l][0]], [hst2B], dma=hst2B)
            emit(dve, lambda e: e.tensor_scalar(out=uP[:, :, 0:PP], in0=hst[:, :, 8:16], scalar1=hmask[:, 0:1],
                                                scalar2=None, op0=ALU.mult), [hstB, constB], [], pwrites=uPB)
            emit(dve, lambda e: e.tensor_scalar(out=uP[:, :, PP + Lseg:PP + Lseg + PP], in0=hst2[:, :, 0:8],
                                                scalar1=hmask[:, 1:2], scalar2=None, op0=ALU.mult),
                 [hst2B, constB], [], pwrites=uPB)
            emit(dve, lambda e: e.tensor_scalar(out=gT[:, :, 0:PC], in0=hst[:, :, 32:48], scalar1=hmask[:, 0:1],
                                                scalar2=None, op0=ALU.mult), [hstB, constB], [], pwrites=gTB)
            emit(dve, lambda e: e.tensor_scalar(out=gT[:, :, PC + Lseg:PC + Lseg + PC], in0=hst2[:, :, 16:32],
                                                scalar1=hmask[:, 1:2], scalar2=None, op0=ALU.mult),
                 [hst2B, constB], [], pwrites=gTB)
        icn = A.alloc("icn", [4, Lseg], F32); icnB = A.newbuf("icn")
        icd = icnts_d if sample else icntp_d
        emit(sp, lambda e: e.dma_start(out=icn, in_=icd.partition_broadcast(128)), [], [icnB], dma=icnB)
        sa = A.alloc("sa", [UW], F32); saB = A.newbuf("sa")
        sbb = A.alloc("sbb", [UW], F32); sbB = A.newbuf("sbb")
        pT = A.alloc("pT", [4, T], BF16); pTB = [A.newbuf("pT%d" % g) for g in range(4)]
        ym, ymB = ymix_alloc()
        for g in range(4):
            for si, (s0, L) in enumerate(segs):
                u = uP[:, g, si * UW:(si + 1) * UW]
                W = UW
                emit(dve, lambda e, u=u: e.tensor_tensor(out=sa[:, 1:W], in0=u[:, 0:W - 1], in1=u[:, 1:W], op=ALU.add),
                     [uPB[g]], [saB])
                cur, curB, oth, othB = sa, saB, sbb, sbB
                lo, hi, step = 1, W, 1
                for lev in range(g):
                    nlo, nhi = lo + step, hi - step
                    emit(dve, lambda e, cur=cur, oth=oth, nlo=nlo, nhi=nhi, step=step: e.tensor_tensor(
                        out=oth[:, nlo:nhi], in0=cur[:, nlo - step:nhi - step], in1=cur[:, nlo + step:nhi + step],
                        op=ALU.add), [curB], [othB])
                    cur, curB, oth, othB = oth, othB, cur, curB
                    lo, hi, step = nlo, nhi, step * 2
                emit(dve, lambda e, cur=cur, g=g: e.tensor_tensor(out=cur[:, PP:PP + L], in0=cur[:, PP:PP + L],
                                                                  in1=icn[:, g, :], op=ALU.mult), [curB, icnB], [curB])
                emit(dve, lambda e, cur=cur, u=u, g=g, s0=s0, L=L: e.tensor_tensor(
                    out=pT[:, g, s0:s0 + L], in0=cur[:, PP:PP + L], in1=u[:, PP:PP + L], op=ALU.subtract),
                    [curB, uPB[g]], [], pwrites=[pTB[g]])
        wb, wv = wload(pool_w_d[l].rearrange("g c d -> c g d"), 4, 128)
        for g in range(4):
            bl = job(1, NT, 512, lambda k, g=g: (wb, wv[:, g, :]), lambda k, n, g=g: (pTB[g], pT[:, g, n * 512:(n + 1) * 512]))
            for n in range(NT):
                emit(act, lambda e, g=g, n=n, b=bl[n]: e.activation(
                    out=ym[:, g, n * 512:(n + 1) * 512], in_=banks[b][:], func=AF.Identity,
                    scale=pv[:, l, C_PSC + g:C_PSC + g + 1]), [bankB[bl[n]], pvB], [ymB[g][n]])
        residual_partial(4, NT, l, g1c, j, w_out_d, 0, ym, lambda k, n: ymB[k][n])
        A.release(mp)
        A.release(m_g)
        if stop_after == "pool%d%s" % (l, "s" if sample else "p"):
            return False

        mc = A.mark()
        acc = A.alloc("acc", [4, T], F32); accB = [[A.newbuf("acc%d_%d" % (c, n)) for n in range(NT)] for c in range(4)]
        tap_groups = []
        for c in range(4):
            for n in range(NT):
                def _grp(c=c, n=n):
                    pieces = seg_cols(n, PC, GW)
                    for (o, nc_, dc) in pieces:
                        for jt in range(31):
                            src = gT[:, c, dc - 15 + jt:dc - 15 + jt + nc_]
                            dst = acc[:, c, n * 512 + o:n * 512 + o + nc_]
                            dwc = pv[:, l, C_DW + c * 31 + jt:C_DW + c * 31 + jt + 1]
                            if jt == 0:
                                emit(dve, lambda e, src=src, dst=dst, dwc=dwc, c=c: e.tensor_scalar(
                                    out=dst, in0=src, scalar1=dwc, scalar2=pv[:, l, C_DWB + c:C_DWB + c + 1],
                                    op0=ALU.mult, op1=ALU.add), [gTB[c], pvB], [], pwrites=[accB[c][n]])
                            else:
                                emit(dve, lambda e, src=src, dst=dst, dwc=dwc: e.scalar_tensor_tensor(
                                    out=dst, in0=src, scalar=dwc, in1=dst, op0=ALU.mult, op1=ALU.add),
                                    [gTB[c], pvB, accB[c][n]], [], pwrites=[accB[c][n]])

                tap_groups.append(_grp)

        def taps_pop(k=1):
            for _ in range(k):
                if tap_groups:
                    tap_groups.pop(0)()
        mf = A.mark()
        taps_pop(2)
        ym, ymB = ymix_alloc()
        FT = A.alloc("FT", [4, T], BF16); FTB = [[A.newbuf("FT%d_%d" % (c, n)) for n in range(NT)] for c in range(4)]
        if sample:
            NTC = LS // 128
            fall = A.alloc("fall", [2, LS], BF16)
            AB = A.alloc("AB", [NTC, 2, 256], BF16)
            tring = [A.alloc("tr%d" % i, [2, 512], BF16) for i in range(4)]
            tringB = [A.newbuf("tr%d" % i) for i in range(4)]
            tcnt = [0]
            rv = xrB[l].ap()
            fallB = A.newbuf("fall")
            ABB = A.newbuf("AB")
            for hp in range(2):
                for r in range(2):
                    emit(sp, lambda e, r=r, hp=hp: e.dma_start(
                        out=fall[:, :, r * 1024:(r + 1) * 1024],
                        in_=rv[r * XRB + hp * 256:r * XRB + (hp + 1) * 256, :].rearrange("(h p) t -> p h t", p=128)),
                        [xrecvQ[l][1]], [fallB] if r == 0 else [], pwrites=[fallB] if r == 1 else [], dma=fallB)
                for i in range(NTC):
                    b = mbank()
                    for hh in range(2):
                        emit(pe, lambda e, b=b, hh=hh, i=i: e.matmul(
                            banks[b][:, hh * 256:(hh + 1) * 256], lhsT=fall[:, hh, i * 128:(i + 1) * 128], rhs=dft128[:],
                            start=True, stop=True), [fallB, constB], [bankB[b]] if hh == 0 else [], sig=(hh == 1))
                    copy_on(act, AB[:, i, :, :], banks[b][:].rearrange("p (h c) -> p h c", h=2), [bankB[b]],
                            [ABB] if i == 0 else [], pwrites=[ABB] if i > 0 else [])
                taps_pop(2)
                for kt in range(NT):
                    taps_pop(1)
                    bl = [dbank(), dbank()]
                    for i in range(NTC):
                        ti = tcnt[0] % 4
                        tcnt[0] += 1
                        emit(sp, lambda e, ti=ti, i=i, kt=kt: e.dma_start(
                            out=tring[ti], in_=dfts_d[i * 128:(i + 1) * 128, :, kt * 512:(kt + 1) * 512]),
                            [], [tringB[ti]], dma=tringB[ti])
                        for hh in range(2):
                            for cs in range(2):
                                emit(pe, lambda e, hh=hh, cs=cs, i=i, ti=ti, b=bl[hh]: e.matmul(
                                    banks[b][:], lhsT=AB[:, i, hh, cs * 128:(cs + 1) * 128], rhs=tring[ti][:, cs, :],
                                    start=(i == 0 and cs == 0), stop=(i == NTC - 1 and cs == 1)),
                                    [ABB, tringB[ti]], [bankB[bl[hh]]] if (i == 0 and cs == 0) else [],
                                    pwrites=[] if (i == 0 and cs == 0) else [bankB[bl[hh]]],
                                    sig=(cs == 1 and hh == 1))
                    for hh in range(2):
                        copy_on(act, FT[:, hp * 2 + hh, kt * 512:(kt + 1) * 512], banks[bl[hh]][:],
                                [bankB[bl[hh]]], [FTB[hp * 2 + hh][kt]])
        else:
            AB = A.alloc("ABp", [T // 128, 4, 256], BF16); ABB = A.newbuf("ABp")
            for i in range(T // 128):
                for hp in range(2):
                    b = mbank()
                    for hh in range(2):
                        h = hp * 2 + hh
                        emit(pe, lambda e, b=b, hh=hh, h=h, i=i: e.matmul(
                            banks[b][:, hh * 256:(hh + 1) * 256], lhsT=fTp[:, h, i * 128:(i + 1) * 128], rhs=dft128[:],
                            start=True, stop=True), [fTpB, constB], [bankB[b]] if hh == 0 else [], sig=(hh == 1))
                    copy_on(act, AB[:, i, hp * 2:hp * 2 + 2, :], banks[b][:].rearrange("p (h c) -> p h c", h=2),
                            [bankB[b]], [], pwrites=[ABB])
            for h in range(4):
                b = dbank()
                for si, (s0, L) in enumerate(segs):
                    ntc = L // 128
                    for i in range(ntc):
                        for cs in range(2):
                            emit(pe, lambda e, b=b, h=h, i=i, cs=cs, s0=s0, L=L: e.matmul(
                                banks[b][:, s0:s0 + L], lhsT=AB[:, s0 // 128 + i, h, cs * 128:(cs + 1) * 128],
                                rhs=dftp[:, i, cs, :], start=(i == 0 and cs == 0), stop=(i == ntc - 1 and cs == 1)),
                                [ABB, constB], [bankB[b]] if (si == 0 and i == 0 and cs == 0) else [],
                                sig=(si == nseg - 1 and i == ntc - 1 and cs == 1))
                copy_on(act, FT[:, h, 0:T], banks[b][:, 0:T], [bankB[b]], [FTB[h][0]])
        wb, wv = wload(wsrc(four_w_d, l, 0, 512, 0, 512), 4, 512)
        for mo in range(4):
            bl = job(4, NT, 512, lambda k, mo=mo: (wb, wv[:, k, mo * 128:(mo + 1) * 128]),
                     lambda k, n: (FTB[k][n], FT[:, k, n * 512:(n + 1) * 512]))
            for n in range(NT):
                copy_on(act, ym[:, mo, n * 512:(n + 1) * 512], banks[bl[n]][:], [bankB[bl[n]]], [ymB[mo][n]])
        residual_partial(4, NT, l, g1c, j, w_out_d, 1536, ym, lambda k, n: ymB[k][n])
        A.release(mf)

        if not sample:
            ma = A.mark()
            ym, ymB = ymix_alloc()
            pt = [A.alloc("ptile%d" % i, [512], BF16) for i in range(3)]
            ptB = [A.newbuf("ptile%d" % i) for i in range(3)]
            r0 = A.alloc("ar0", [512], F32); r0B = A.newbuf("ar0")
            r1 = A.alloc("ar1", [512], F32); r1B = A.newbuf("ar1")
            o0 = A.alloc("ao0", [512], F32); o0B = A.newbuf("ao0")
            asq = A.alloc("asq", [512], F32); asqB = A.newbuf("asq")
            ars = A.alloc("ars", [512], F32); arsB = A.newbuf("ars")
            art = A.alloc("art", [512], F32); artB = A.newbuf("art")
            if sample:
                NKC = 20
                kall = A.alloc("kall", [4, NKC * 128], BF16); kallB = A.newbuf("kall")
                vall = A.alloc("vall", [NKC, 512], BF16); vallB = A.newbuf("vall")
                rv = xrA[l].ap()
                rvb = xrB[l].ap()
                for r in range(2):
                    emit(sp, lambda e, r=r: e.dma_start(
                        out=kall[:, :, r * 1024:(r + 1) * 1024],
                        in_=rv[r * XRA:r * XRA + 512, :].rearrange("(h p) t -> p h t", p=128)),
                        [xrecvQ[l][0]], [], pwrites=[kallB], dma=kallB)
                    emit(sp, lambda e, r=r: e.dma_start(
                        out=vall[:, r * 8:(r + 1) * 8, :],
                        in_=rvb[r * XRB + 512:r * XRB + 1024, :].rearrange("r (a c) -> (r a) c", a=2).rearrange(
                            "(i p) c -> p i c", p=128)), [xrecvQ[l][1]], [], pwrites=[vallB], dma=vallB)
                emit(pool, lambda e: e.dma_start(out=vall[:, 16:20, :],
                                                 in_=cv_d[l].rearrange("(i p) c -> p i c", p=128)),
                     [], [], pwrites=[vallB], dma=vallB)
                cks = [A.alloc("cks%d" % i, [512], F32) for i in range(2)]
                cksB = [A.newbuf("cks%d" % i) for i in range(2)]
                for i in range(4):
                    s_ = i % 2
                    emit(sp, lambda e, i=i, s_=s_: e.dma_start(out=cks[s_], in_=ck_d[l, i * 128:(i + 1) * 128, :]),
                         [], [cksB[s_]], dma=cksB[s_])
                    b = mbank()
                    for h in range(4):
                        emit(pe, lambda e, h=h, b=b, s_=s_: e.transpose(out=banks[b][:, h * 128:(h + 1) * 128],
                                                                       in_=cks[s_][:, h * 128:(h + 1) * 128], identity=ident[:]),
                             [cksB[s_], constB], [bankB[b]] if h == 0 else [], sig=(h == 3))
                    copy_on(ev_eng(), kall[:, :, 2048 + i * 128:2048 + (i + 1) * 128],
                            banks[b][:].rearrange("p (h t) -> p h t", h=4), [bankB[b]], [], pwrites=[kallB])
                att_units = [(0, 512, n * 512, [(kall, kallB, vall, vallB, jc) for jc in range(NKC)]) for n in range(NT)]
            else:
                att_units = []
                for si, (s0, L) in enumerate(segs):
                    att_units.append((1, L, s0, [(kTp, kTpB, vTp, vTpB, s0 // 128 + jc) for jc in range(L // 128)]))
            pcnt = [0]
            for h in range(4):
                for (_, NQ, q0, klist) in att_units:
                    taps_pop(1)
                    bO = [0, 1]
                    bL = [2, 3]
                    nk_ = len(klist)
                    its = [(ki, mmap) for ki in range(nk_) for mmap in range(2)]
                    base_i = pcnt[0]
                    pcnt[0] += len(its)

                    def e_qk(ix):
                        ki, mmap = its[ix]
                        kt_, ktB_, vt_, vtB_, jc = klist[ki]
                        bs = 4 + ((base_i + ix) % 2)
                        emit(pe, lambda e: e.matmul(
                            banks[bs][:, 0:NQ], lhsT=kt_[mmap * 64:(mmap + 1) * 64, h, jc * 128:(jc + 1) * 128],
                            rhs=qT[mmap * 64:(mmap + 1) * 64, h, q0:q0 + NQ], start=True, stop=True),
                            [ktB_, qTB], [bankB[bs]])

                    def e_exp(ix):
                        bs = 4 + ((base_i + ix) % 2)
                        pi = (base_i + ix) % 3
                        emit(act, lambda e: e.activation(out=pt[pi][:, 0:NQ], in_=banks[bs][:, 0:NQ],
                                                         func=AF.Exp, scale=0.125), [bankB[bs]], [ptB[pi]])

                    def e_pv(ix):
                        ki, mmap = its[ix]
                        kt_, ktB_, vt_, vtB_, jc = klist[ki]
                        pi = (base_i + ix) % 3
                        emit(pe, lambda e: e.matmul(
                            banks[bO[mmap]][:, 0:NQ], lhsT=vt_[:, jc, h * 128:(h + 1) * 128], rhs=pt[pi][:, 0:NQ],
                            start=(ki == 0), stop=(ki == nk_ - 1)),
                            [vtB_, ptB[pi]], [bankB[bO[mmap]]] if ki == 0 else [],
                            pwrites=[] if ki == 0 else [bankB[bO[mmap]]], sig=False)
                        emit(pe, lambda e: e.matmul(
                            banks[bL[mmap]][:, 0:NQ], lhsT=ones_b[:], rhs=pt[pi][:, 0:NQ],
                            start=(ki == 0), stop=(ki == nk_ - 1)),
                            [constB, ptB[pi]], [bankB[bL[mmap]]] if ki == 0 else [],
                            pwrites=[] if ki == 0 else [bankB[bL[mmap]]], sig=True)

                    e_qk(0)
                    for ix in range(len(its)):
                        if ix + 1 < len(its):
                            e_qk(ix + 1)
                        e_exp(ix)
                        e_pv(ix)
                    emit(dve, lambda e, NQ=NQ: e.reciprocal(out=r0[:, 0:NQ], in_=banks[2][:, 0:NQ]), [bankB[2]], [r0B])
                    emit(dve, lambda e, NQ=NQ: e.reciprocal(out=r1[:, 0:NQ], in_=banks[3][:, 0:NQ]), [bankB[3]], [r1B])
                    emit(dve, lambda e, NQ=NQ: e.tensor_tensor(out=o0[:, 0:NQ], in0=banks[0][:, 0:NQ], in1=r0[:, 0:NQ],
                                                               op=ALU.mult), [bankB[0], r0B], [o0B])
                    emit(dve, lambda e, NQ=NQ: e.tensor_tensor(out=r1[:, 0:NQ], in0=banks[1][:, 0:NQ], in1=r1[:, 0:NQ],
                                                               op=ALU.mult), [bankB[1], r1B], [r1B])
                    emit(dve, lambda e, NQ=NQ: e.scalar_tensor_tensor(out=o0[:, 0:NQ], in0=r1[:, 0:NQ],
                                                                      scalar=small[:, 1 + l:2 + l], in1=o0[:, 0:NQ],
                                                                      op0=ALU.mult, op1=ALU.add), [r1B, o0B, smallB], [o0B])
                    emit(act, lambda e, NQ=NQ: e.activation(out=asq[:, 0:NQ], in_=o0[:, 0:NQ], func=AF.Square), [o0B], [asqB])
                    b2 = mbank()
                    emit(pe, lambda e, NQ=NQ, b2=b2: e.matmul(banks[b2][:, 0:NQ], lhsT=ones_f[:], rhs=asq[:, 0:NQ],
                                                             start=True, stop=True), [asqB, constB], [bankB[b2]])
                    rstd_from(banks[b2][:, 0:NQ], 1.0 / 128, ars[:, 0:NQ], [bankB[b2]], [arsB], art[:, 0:NQ], artB)
                    nq_t = q0 // 512
                    emit(dve, lambda e, NQ=NQ, h=h, q0=q0: e.scalar_tensor_tensor(
                        out=ym[:, h, q0:q0 + NQ], in0=o0[:, 0:NQ], scalar=small[:, 3 + l:4 + l], in1=ars[:, 0:NQ],
                        op0=ALU.mult, op1=ALU.mult), [o0B, arsB, smallB], [], pwrites=[ymB[h][nq_t]])
            residual_partial(4, NT, l, g1c, j, w_out_d, 512, ym, lambda k, n: ymB[k][n])
            A.release(ma)
            if stop_after == "att%d%s" % (l, "s" if sample else "p"):
                return False


        taps_pop(100)
        csq = [A.alloc("csq%d" % i, [512], F32) for i in range(2)]
        csqB = [A.newbuf("csq%d" % i) for i in range(2)]
        mean = A.alloc("cmean", [512], F32); meanB = A.newbuf("cmean")
        msq = A.alloc("cmsq", [512], F32); msqB = A.newbuf("cmsq")
        crs = A.alloc("crs", [512], F32); crsB = A.newbuf("crs")
        crt = A.alloc("crt", [512], F32); crtB = A.newbuf("crt")
        ct1 = [A.alloc("ct1%d" % i, [512], F32) for i in range(2)]
        ct1B = [A.newbuf("ct1%d" % i) for i in range(2)]
        yb = A.alloc("cyb", [4, T], BF16); ybB = [[A.newbuf("cyb%d_%d" % (c, n)) for n in range(NT)] for c in range(4)]
        ym, ymB = ymix_alloc()
        for n in range(NT):
            b1 = mbank()
            for c in range(4):
                emit(pe, lambda e, c=c, n=n, b1=b1: e.matmul(banks[b1][:], lhsT=ones_f[:], rhs=acc[:, c, n * 512:(n + 1) * 512],
                                                            start=(c == 0), stop=(c == 3)),
                     [accB[c][n], constB], [bankB[b1]] if c == 0 else [], sig=(c == 3))
            b2 = mbank()
            for c in range(4):
                i = c % 2
                emit(act, lambda e, c=c, n=n, i=i: e.activation(out=csq[i], in_=acc[:, c, n * 512:(n + 1) * 512],
                                                               func=AF.Square), [accB[c][n]], [csqB[i]])
                emit(pe, lambda e, c=c, i=i, b2=b2: e.matmul(banks[b2][:], lhsT=ones_f[:], rhs=csq[i],
                                                            start=(c == 0), stop=(c == 3)),
                     [csqB[i], constB], [bankB[b2]] if c == 0 else [], pwrites=[] if c == 0 else [bankB[b2]], sig=True)
            emit(dve, lambda e, b1=b1: e.tensor_scalar(out=mean, in0=banks[b1][:], scalar1=1.0 / 512, scalar2=None,
                                                       op0=ALU.mult), [bankB[b1]], [meanB])
            emit(dve, lambda e: e.tensor_tensor(out=msq, in0=mean, in1=mean, op=ALU.mult), [meanB], [msqB])
            emit(dve, lambda e, b2=b2: e.scalar_tensor_tensor(out=msq, in0=banks[b2][:], scalar=1.0 / 512, in1=msq,
                                                              op0=ALU.mult, op1=ALU.subtract), [bankB[b2], msqB], [msqB])
            rstd_from(msq, 1.0, crs, [msqB], [crsB], crt, crtB)
            for c in range(4):
                i = c % 2
                emit(dve, lambda e, c=c, n=n, i=i: e.tensor_tensor(out=ct1[i], in0=acc[:, c, n * 512:(n + 1) * 512],
                                                                  in1=mean, op=ALU.subtract), [accB[c][n], meanB], [ct1B[i]])
                emit(dve, lambda e, i=i: e.tensor_tensor(out=ct1[i], in0=ct1[i], in1=crs, op=ALU.mult),
                     [ct1B[i], crsB], [ct1B[i]])
                emit(act, lambda e, c=c, n=n, i=i: e.activation(
                    out=yb[:, c, n * 512:(n + 1) * 512], in_=ct1[i], func=AF.Silu,
                    bias=pv[:, l, C_LNB + c:C_LNB + c + 1], scale=pv[:, l, C_LNG + c:C_LNG + c + 1]),
                    [ct1B[i], pvB], [ybB[c][n]])
        if stop_after == "dyb" + stg_tag:
            dbg_to_x(yb, [b_ for r_ in ybB for b_ in r_], 4, T)
            return False
        wb, wv = wload(wsrc(conv_pw_d, l, 0, 512, 0, 512), 4, 512)
        for mo in range(4):
            bl = job(4, NT, 512, lambda k, mo=mo: (wb, wv[:, k, mo * 128:(mo + 1) * 128]),
                     lambda k, n: (ybB[k][n], yb[:, k, n * 512:(n + 1) * 512]))
            for n in range(NT):
                emit(act, lambda e, mo=mo, n=n, b=bl[n]: e.activation(
                    out=ym[:, mo, n * 512:(n + 1) * 512], in_=banks[b][:], func=AF.Identity,
                    bias=pv[:, l, C_PWB + mo:C_PWB + mo + 1], scale=1.0), [bankB[bl[n]], pvB], [ymB[mo][n]])
        if stop_after == "dym" + stg_tag:
            dbg_to_x(ym, [b_ for r_ in ymB for b_ in r_], 4, T)
            return False
        if stop_after == "dnores" + stg_tag:
            return False
        residual_partial(4, NT, l, g1c, j, w_out_d, 1024, ym, lambda k, n: ymB[k][n], overwrite=(stop_after == "dres" + stg_tag))
        if stop_after == "dres" + stg_tag:
            return False
        A.release(mc)
        if stop_after == "conv%d%s" % (l, "s" if sample else "p"):
            return False

        if sample:
            ma = A.mark()
            ym, ymB = ymix_alloc()
            pt = [A.alloc("ptile%d" % i, [512], BF16) for i in range(3)]
            ptB = [A.newbuf("ptile%d" % i) for i in range(3)]
            r0 = A.alloc("ar0", [512], F32); r0B = A.newbuf("ar0")
            r1 = A.alloc("ar1", [512], F32); r1B = A.newbuf("ar1")
            o0 = A.alloc("ao0", [512], F32); o0B = A.newbuf("ao0")
            asq = A.alloc("asq", [512], F32); asqB = A.newbuf("asq")
            ars = A.alloc("ars", [512], F32); arsB = A.newbuf("ars")
            art = A.alloc("art", [512], F32); artB = A.newbuf("art")
            if sample:
                NKC = 20
                kall = A.alloc("kall", [4, NKC * 128], BF16); kallB = A.newbuf("kall")
                vall = A.alloc("vall", [NKC, 512], BF16); vallB = A.newbuf("vall")
                rv = xrA[l].ap()
                rvb = xrB[l].ap()
                for r in range(2):
                    emit(sp, lambda e, r=r: e.dma_start(
                        out=kall[:, :, r * 1024:(r + 1) * 1024],
                        in_=rv[r * XRA:r * XRA + 512, :].rearrange("(h p) t -> p h t", p=128)),
                        [xrecvQ[l][0]], [], pwrites=[kallB], dma=kallB)
                    emit(sp, lambda e, r=r: e.dma_start(
                        out=vall[:, r * 8:(r + 1) * 8, :],
                        in_=rvb[r * XRB + 512:r * XRB + 1024, :].rearrange("r (a c) -> (r a) c", a=2).rearrange(
                            "(i p) c -> p i c", p=128)), [xrecvQ[l][1]], [], pwrites=[vallB], dma=vallB)
                emit(pool, lambda e: e.dma_start(out=vall[:, 16:20, :],
                                                 in_=cv_d[l].rearrange("(i p) c -> p i c", p=128)),
                     [], [], pwrites=[vallB], dma=vallB)
                cks = [A.alloc("cks%d" % i, [512], F32) for i in range(2)]
                cksB = [A.newbuf("cks%d" % i) for i in range(2)]
                for i in range(4):
                    s_ = i % 2
                    emit(sp, lambda e, i=i, s_=s_: e.dma_start(out=cks[s_], in_=ck_d[l, i * 128:(i + 1) * 128, :]),
                         [], [cksB[s_]], dma=cksB[s_])
                    b = mbank()
                    for h in range(4):
                        emit(pe, lambda e, h=h, b=b, s_=s_: e.transpose(out=banks[b][:, h * 128:(h + 1) * 128],
                                                                       in_=cks[s_][:, h * 128:(h + 1) * 128], identity=ident[:]),
                             [cksB[s_], constB], [bankB[b]] if h == 0 else [], sig=(h == 3))
                    copy_on(ev_eng(), kall[:, :, 2048 + i * 128:2048 + (i + 1) * 128],
                            banks[b][:].rearrange("p (h t) -> p h t", h=4), [bankB[b]], [], pwrites=[kallB])
                att_units = [(0, 512, n * 512, [(kall, kallB, vall, vallB, jc) for jc in range(NKC)]) for n in range(NT)]
            else:
                att_units = []
                for si, (s0, L) in enumerate(segs):
                    att_units.append((1, L, s0, [(kTp, kTpB, vTp, vTpB, s0 // 128 + jc) for jc in range(L // 128)]))
            pcnt = [0]
            for h in range(4):
                for (_, NQ, q0, klist) in att_units:
                    taps_pop(1)
                    bO = [0, 1]
                    bL = [2, 3]
                    nk_ = len(klist)
                    its = [(ki, mmap) for ki in range(nk_) for mmap in range(2)]
                    base_i = pcnt[0]
                    pcnt[0] += len(its)

                    def e_qk(ix):
                        ki, mmap = its[ix]
                        kt_, ktB_, vt_, vtB_, jc = klist[ki]
                        bs = 4 + ((base_i + ix) % 2)
                        emit(pe, lambda e: e.matmul(
                            banks[bs][:, 0:NQ], lhsT=kt_[mmap * 64:(mmap + 1) * 64, h, jc * 128:(jc + 1) * 128],
                            rhs=qT[mmap * 64:(mmap + 1) * 64, h, q0:q0 + NQ], start=True, stop=True),
                            [ktB_, qTB], [bankB[bs]])

                    def e_exp(ix):
                        bs = 4 + ((base_i + ix) % 2)
                        pi = (base_i + ix) % 3
                        emit(act, lambda e: e.activation(out=pt[pi][:, 0:NQ], in_=banks[bs][:, 0:NQ],
                                                         func=AF.Exp, scale=0.125), [bankB[bs]], [ptB[pi]])

                    def e_pv(ix):
                        ki, mmap = its[ix]
                        kt_, ktB_, vt_, vtB_, jc = klist[ki]
                        pi = (base_i + ix) % 3
                        emit(pe, lambda e: e.matmul(
                            banks[bO[mmap]][:, 0:NQ], lhsT=vt_[:, jc, h * 128:(h + 1) * 128], rhs=pt[pi][:, 0:NQ],
                            start=(ki == 0), stop=(ki == nk_ - 1)),
                            [vtB_, ptB[pi]], [bankB[bO[mmap]]] if ki == 0 else [],
                            pwrites=[] if ki == 0 else [bankB[bO[mmap]]], sig=False)
                        emit(pe, lambda e: e.matmul(
                            banks[bL[mmap]][:, 0:NQ], lhsT=ones_b[:], rhs=pt[pi][:, 0:NQ],
                            start=(ki == 0), stop=(ki == nk_ - 1)),
                            [constB, ptB[pi]], [bankB[bL[mmap]]] if ki == 0 else [],
                            pwrites=[] if ki == 0 else [bankB[bL[mmap]]], sig=True)

                    e_qk(0)
                    for ix in range(len(its)):
                        if ix + 1 < len(its):
                            e_qk(ix + 1)
                        e_exp(ix)
                        e_pv(ix)
                    emit(dve, lambda e, NQ=NQ: e.reciprocal(out=r0[:, 0:NQ], in_=banks[2][:, 0:NQ]), [bankB[2]], [r0B])
                    emit(dve, lambda e, NQ=NQ: e.reciprocal(out=r1[:, 0:NQ], in_=banks[3][:, 0:NQ]), [bankB[3]], [r1B])
                    emit(dve, lambda e, NQ=NQ: e.tensor_tensor(out=o0[:, 0:NQ], in0=banks[0][:, 0:NQ], in1=r0[:, 0:NQ],
                                                               op=ALU.mult), [bankB[0], r0B], [o0B])
                    emit(dve, lambda e, NQ=NQ: e.tensor_tensor(out=r1[:, 0:NQ], in0=banks[1][:, 0:NQ], in1=r1[:, 0:NQ],
                                                               op=ALU.mult), [bankB[1], r1B], [r1B])
                    emit(dve, lambda e, NQ=NQ: e.scalar_tensor_tensor(out=o0[:, 0:NQ], in0=r1[:, 0:NQ],
                                                                      scalar=small[:, 1 + l:2 + l], in1=o0[:, 0:NQ],
                                                                      op0=ALU.mult, op1=ALU.add), [r1B, o0B, smallB], [o0B])
                    emit(act, lambda e, NQ=NQ: e.activation(out=asq[:, 0:NQ], in_=o0[:, 0:NQ], func=AF.Square), [o0B], [asqB])
                    b2 = mbank()
                    emit(pe, lambda e, NQ=NQ, b2=b2: e.matmul(banks[b2][:, 0:NQ], lhsT=ones_f[:], rhs=asq[:, 0:NQ],
                                                             start=True, stop=True), [asqB, constB], [bankB[b2]])
                    rstd_from(banks[b2][:, 0:NQ], 1.0 / 128, ars[:, 0:NQ], [bankB[b2]], [arsB], art[:, 0:NQ], artB)
                    nq_t = q0 // 512
                    emit(dve, lambda e, NQ=NQ, h=h, q0=q0: e.scalar_tensor_tensor(
                        out=ym[:, h, q0:q0 + NQ], in0=o0[:, 0:NQ], scalar=small[:, 3 + l:4 + l], in1=ars[:, 0:NQ],
                        op0=ALU.mult, op1=ALU.mult), [o0B, arsB, smallB], [], pwrites=[ymB[h][nq_t]])
            residual_partial(4, NT, l, g1c, j, w_out_d, 512, ym, lambda k, n: ymB[k][n])
            A.release(ma)
            if stop_after == "att%d%s" % (l, "s" if sample else "p"):
                return False


        A.release(m_q)
        A.release(base)
        if stop_after == "mix%d%s" % (l, "s" if sample else "p"):
            return False
        bg_drain(25)
        mffn = A.mark()
        hT = A.alloc("hT2", [KC, T], BF16)
        hTb = [[A.newbuf("hT2%d_%d" % (c, n)) for n in range(NT)] for c in range(KC)]
        norm(l, 1, T, j, hT, hTb)

        def hrhs2(k, n):
            return (hTb[k][n], hT[:, k, n * 512:(n + 1) * 512])
        aT = A.alloc("aT", [12, T], BF16)
        sl = [A.alloc("sl%d" % i, [T], F32) for i in range(2)]
        slB = [[A.newbuf("sl%d_%d" % (i, n)) for n in range(NT)] for i in range(2)]
        scnt = [0]
        aTB = [[A.newbuf("aT%d_%d" % (c, n)) for n in range(NT)] for c in range(12)]
        for (g0, gs) in FFN_GROUPS:
            for blk in range(gs // 2):
                col0 = (g0 + blk * 2) * 128
                bg_tick()
                wbg, wvg = wload(wsrc(w_gate_d, l, 0, D, col0, 256), KC, 256)
                wbu, wvu = wload(wsrc(w_up_d, l, 0, D, col0, 256), KC, 256)
                for sub in range(2):
                    cc = blk * 2 + sub
                    si_ = scnt[0] % 2
                    scnt[0] += 1
                    bl = job(KC, NT, 512, lambda k, sub=sub: (wbg, wvg[:, k, sub * 128:(sub + 1) * 128]), hrhs2)
                    for n in range(NT):
                        emit(act, lambda e, b=bl[n], si_=si_, n=n: e.activation(
                            out=sl[si_][:, n * 512:(n + 1) * 512], in_=banks[b][:], func=AF.Silu),
                            [bankB[bl[n]]], [slB[si_][n]])
                    bl = job(KC, NT, 512, lambda k, sub=sub: (wbu, wvu[:, k, sub * 128:(sub + 1) * 128]), hrhs2)
                    for n in range(NT):
                        emit(dve, lambda e, b=bl[n], si_=si_, n=n, cc=cc: e.tensor_tensor(
                            out=aT[:, cc, n * 512:(n + 1) * 512], in0=banks[b][:], in1=sl[si_][:, n * 512:(n + 1) * 512],
                            op=ALU.mult), [bankB[bl[n]], slB[si_][n]], [aTB[cc][n]])
            bg_drain(32)
            residual_partial(gs, NT, l, 80, j, w_down_d, g0 * 128, aT, lambda k, n: aTB[k][n])
        A.release(mffn)
        if stop_after == "end" + stg_tag:
            return False
        return True

    outB_all = P.buf("outs")
    GP = dict(T=TP, j=0, sample=False, segs=[(0, LP), (LP, LP)], first=False)
    GS = dict(T=TS, j=1, sample=True, segs=[(0, TS)], first=True)
    finals = []
    done = False
    for (G, src, dst) in ((GS, xs_d, ys_d), (GP, xp_d, yp_d)):
        if stop_after == "mods":
            done = True
            break
        load_x(src, G["T"])
        if G.get("first", False) and stop_after != "loadx":
            for _ in mods_gen(0, 0, 32, 0):
                pass
        if stop_after == "loadx":
            done = True
            break
        ok = True
        for l in range(2):
            ok = layer(l, G)
            if not ok:
                break
        if not ok:
            done = True
            break
        if stop_after == ("endp" if not G["sample"] else "ends"):
            done = True
            break
        finals += store_y(dst, G["T"])
    if stop_after == "mods":
        db = P.buf("dbg")
        emit(sp, lambda e: e.dma_start(out=dbg_d[:, 0:384], in_=modt[:].rearrange("p l m j -> p (l m j)")), [modB, gscB], [db], dma=db)
        emit(sp, lambda e: e.dma_start(out=dbg_d[:, 384:384 + 128], in_=gsc[:].rearrange("p l s c j -> p (l s c j)")), [modB, gscB], [], pwrites=[db], dma=db)
        emit(sp, lambda e: e.dma_start(out=dbg_d[:, 512:528], in_=small[:]), [smallB], [], pwrites=[db], dma=db)
        finals.append(db)
    elif stop_after:
        allx = [xTb[c][n] for c in range(KC) for n in range(2)]
        db = P.buf("dbg")
        Tl = G["T"]
        emit(sp, lambda e: e.dma_start(out=dbg_d.rearrange("p (c t) -> p c t", c=KC)[:, :, 0:Tl], in_=xT[:, :, 0:Tl]), allx, [db], dma=db)
        finals.append(db)
    emit(sp, lambda e: e.nop(), [], finals + [outB_all], sig=True)
    P.replay()


def _consts():
    c = {}
    c["ident"] = np.eye(128, dtype=np.float32)
    rot = np.zeros((128, 128), np.float32)
    for p in range(128):
        partner = p + 16 if (p % 32) < 16 else p - 16
        rot[partner, p] = 1.0
    c["rotm"] = rot
    bo = np.zeros((128, 128), np.float32)
    bo[:64, :64] = 1.0
    bo[64:, 64:] = 1.0
    c["bones"] = bo
    k = np.arange(128)
    ang = 2 * np.pi * np.outer(k, k) / 128.0
    c["dft128"] = (np.concatenate([np.cos(ang), np.sin(ang)], axis=1) / np.sqrt(128.0)).astype(np.float32)
    t = np.arange(LP)
    ang = 2 * np.pi * np.outer(t, t) / LP
    c["dftp"] = (np.stack([np.cos(ang), -np.sin(ang)], axis=1) / np.sqrt(LP)).astype(ml_dtypes.bfloat16)

    def icnt(L, t):
        out = []
        for w in (2, 4, 8, 16):
            lo = np.maximum(t - w // 2, 0)
            hi = np.minimum(t + w // 2 - 1, L - 1)
            out.append(1.0 / (hi - lo + 1))
        return np.stack(out).astype(np.float32)
    c["icnt_p"] = icnt(LP, np.arange(LP))
    c["icnt_s"] = [icnt(LS, np.arange(hf * TS, (hf + 1) * TS)) for hf in range(2)]
    inv = 1.0 / (10000.0 ** (np.arange(0, 32, 2, dtype=np.float32) / 32.0))
    ropes = []
    for hf in range(2):
        tt = np.arange(hf * TS, (hf + 1) * TS)
        row = (tt // 64).astype(np.float32)
        col = (tt % 64).astype(np.float32)
        tab = np.zeros((128, 2, TS), np.float32)
        for p in range(128):
            d = p % 64
            pos = row if d < 32 else col
            a = pos * inv[d % 16]
            tab[p, 0] = np.cos(a)
            tab[p, 1] = (-np.sin(a)) if (d % 32) < 16 else np.sin(a)
        ropes.append(tab)
    c["rope"] = ropes
    t = np.arange(LS, dtype=np.float64)
    dfts = []
    for hf in range(2):
        kk = np.arange(hf * TS, (hf + 1) * TS, dtype=np.float64)
        ang = 2 * np.pi * (np.outer(t, kk) % LS) / LS
        dfts.append((np.stack([np.cos(ang), -np.sin(ang)], axis=1) / np.sqrt(LS)).astype(ml_dtypes.bfloat16))
    c["dfts"] = dfts
    c["hmask"] = [np.tile(np.array([[float(hf), 1.0 - hf]], np.float32), (128, 1)) for hf in range(2)]
    return c


def _pvec(inp):
    out = np.zeros((2, 128, NV), np.float32)
    for l in range(2):
        o = out[l]
        o[:, C_G1:C_G1 + 16] = inp["g_norm1"][l].reshape(16, 128).T
        o[:, C_G2:C_G2 + 16] = inp["g_norm2"][l].reshape(16, 128).T
        o[:, C_BADA:C_BADA + 96] = inp["b_ada"][l].reshape(96, 128).T
        o[:, C_PSC:C_PSC + 4] = inp["pool_scale"][l].reshape(4, 128).T
        dw = inp["conv_dw"][l]
        for c in range(4):
            o[:, C_DW + c * 31:C_DW + (c + 1) * 31] = dw[:, c * 128:(c + 1) * 128].T
        o[:, C_DWB:C_DWB + 4] = inp["conv_dw_b"][l].reshape(4, 128).T
        o[:, C_LNG:C_LNG + 4] = inp["conv_ln_g"][l].reshape(4, 128).T
        o[:, C_LNB:C_LNB + 4] = inp["conv_ln_b"][l].reshape(4, 128).T
        o[:, C_PWB:C_PWB + 4] = inp["conv_pw_b"][l].reshape(4, 128).T
        o[:, C_GQ] = np.tile(inp["g_q"][l], 2)
        o[:, C_GK] = np.tile(inp["g_k"][l], 2)
        o[:, C_GSUB] = inp["g_subln"][l]
        for q in range(4):
            o[:64, C_LAM + q] = inp["lam"][l][q]
    return out


_CACHE = {}


def make_in_maps(inp, cores):
    inp = {k: np.ascontiguousarray(np.asarray(v)) for k, v in inp.items()}
    cst = _consts()
    pvec = _pvec(inp)
    maps = []
    for c in cores:
        b, hf = c // 2, c % 2
        m = {
            "xp": inp["x_prompt"][2 * c:2 * c + 2].reshape(TP, D),
            "xs": inp["x_sample"][b, hf * TS:(hf + 1) * TS],
            "ck": inp["cache_k"][b].reshape(2, PAST, 512),
            "cv": inp["cache_v"][b].reshape(2, PAST, 512),
            "cT": np.ascontiguousarray(np.concatenate([np.stack([inp["c_ctx"], inp["c"][b]], axis=1), np.zeros((D, 6), np.float32)], axis=1).reshape(KC, 128, 8).transpose(1, 0, 2)),
            "pvec": pvec,
            "ident": cst["ident"], "rotm": cst["rotm"], "bones": cst["bones"],
            "rope": cst["rope"][hf], "icnt_p": cst["icnt_p"], "icnt_s": cst["icnt_s"][hf],
            "dft128": cst["dft128"], "dftp": cst["dftp"], "dfts": cst["dfts"][hf], "hmask": cst["hmask"][hf],
        }
        for k in ("w_ada", "w_in", "pool_w", "conv_pw", "fourier_w", "w_out", "w_gate", "w_up", "w_down"):
            m[k] = inp[k]
        maps.append({k: np.ascontiguousarray(v) for k, v in m.items()})
    return maps


def kernel(**inputs):
    if "nc" not in _CACHE:
        _CACHE["nc"] = build_program()
    nc = _CACHE["nc"]
    maps = make_in_maps(inputs, list(range(NCORES)))
    res = run_bass_kernel_spmd(nc, maps, core_ids=list(range(NCORES)))
    R = res.results
    yp = np.zeros((16, LP, D), np.float32)
    ys = np.zeros((4, LS, D), np.float32)
    nk = np.zeros((16, 2, LP, 4, 2, 64), np.float32)
    nv = np.zeros((16, 2, LP, 4, 128), np.float32)
    for c in range(NCORES):
        b, hf = c // 2, c % 2
        yp[2 * c:2 * c + 2] = np.asarray(R[c]["yp"]).reshape(2, LP, D)
        ys[b, hf * TS:(hf + 1) * TS] = np.asarray(R[c]["ys"])
        nk[2 * c:2 * c + 2] = np.asarray(R[c]["nk"]).reshape(2, 2, LP, 4, 2, 64)
        nv[2 * c:2 * c + 2] = np.asarray(R[c]["nv"]).reshape(2, 2, LP, 4, 128)
    return (yp, ys, nk, nv)
```

```python
import contextlib
import os
import numpy as np
import ml_dtypes
import concourse.bass as bass
import concourse.mybir as mybir
from concourse.bass_utils import run_bass_kernel_spmd

F32 = mybir.dt.float32
BF16 = mybir.dt.bfloat16
AF = mybir.ActivationFunctionType
ALU = mybir.AluOpType

D = 2048
KC = 16
DFF = 5632
FC = 44
INC = 3584
EPS = 1e-6
NCORES = 8
TP = 512
TS = 1024
LP = 256
LS = 2048
PAST = 512
XR = 1568
C_G1, C_G2, C_BADA, C_PSC, C_DW, C_DWB, C_LNG, C_LNB, C_PWB = 0, 16, 32, 128, 132, 256, 260, 264, 268
C_GQ, C_GK, C_GSUB, C_LAM = 272, 273, 274, 275
NV = 280
LAM_INIT = [0.8 - 0.6 * float(np.exp(-0.3 * l)) for l in range(2)]
FFN_GROUPS = [(0, 12), (12, 12), (24, 12), (36, 8)]


class Buf:
    __slots__ = ("name", "w", "r", "dkey", "dcnt")

    def __init__(self, name):
        self.name = name
        self.w = {}
        self.r = {}
        self.dkey = {}
        self.dcnt = 0


class _Rec:
    def __init__(self):
        self.call = None

    def __getattr__(self, name):
        def f(*a, **k):
            self.call = (name, a, k)
            return None
        return f


class Eng:
    def __init__(self, name, key):
        self.name = name
        self.key = key
        self.ops = []
        self.cnt = 0
        self.waited = {}


class Prog:
    def __init__(self, nc, stack, n_dsem=100):
        self.nc = nc
        self.stack = stack
        self.sems = []
        self.pe = Eng("tensor", self._sem("s_pe"))
        self.act = Eng("scalar", self._sem("s_act"))
        self.dve = Eng("vector", self._sem("s_dve"))
        self.pool = Eng("gpsimd", self._sem("s_pool"))
        self.sp = Eng("sync", self._sem("s_sp"))
        self.engs = [self.pe, self.act, self.dve, self.pool, self.sp]
        self.free_dsems = {"sync": [], "gpsimd": [], "scalar": []}
        self.dsem_cnt = {}
        self.n_dsem = 0
        self.arena_deps = {}
        self.flip = 0

    def _sem(self, name):
        s = self.stack.enter_context(self.nc.semaphore(name))
        self.sems.append(s)
        return len(self.sems) - 1

    def buf(self, name):
        b = Buf(name)
        b.w = dict(self.arena_deps)
        return b

    def emit(self, eng, fn, reads=(), writes=(), pwrites=(), sig=True, dma=None, inc=None):
        waits = {}

        def need(d):
            for s, v in d.items():
                if waits.get(s, 0) < v:
                    waits[s] = v

        for b in reads:
            need(b.w)
        for b in writes:
            need(b.w)
            need(b.r)
        for b in pwrites:
            need(b.w)
            need(b.r)
        wl = []
        for s, v in waits.items():
            if s == eng.key and (v > eng.cnt or eng is self.pe):
                continue
            if eng.waited.get(s, 0) < v:
                eng.waited[s] = v
                wl.append((s, v))
        if dma is not None:
            if eng.name not in dma.dkey:
                fl = self.free_dsems[eng.name]
                if fl:
                    dma.dkey[eng.name] = fl.pop(0)
                else:
                    dma.dkey[eng.name] = self._sem("d%d" % self.n_dsem)
                    self.n_dsem += 1
                    self.dsem_cnt[dma.dkey[eng.name]] = 0
            dk = dma.dkey[eng.name]
            self.dsem_cnt[dk] += 16
            tok = (dk, self.dsem_cnt[dk])
            incr = (dk, 16)
        elif inc is not None:
            tok = inc
            incr = (inc[0], 1)
        else:
            tok = (eng.key, eng.cnt + 1)
            if sig:
                eng.cnt += 1
                incr = (eng.key, 1)
            else:
                incr = None
        rec = _Rec()
        fn(rec)
        assert rec.call is not None
        eng.ops.append((wl, rec.call, incr))
        for b in reads:
            if b.r.get(tok[0], 0) < tok[1]:
                b.r[tok[0]] = tok[1]
        for b in writes:
            b.w = {tok[0]: tok[1]}
            b.r = {}
        for b in pwrites:
            if b.w.get(tok[0], 0) < tok[1]:
                b.w[tok[0]] = tok[1]
        return tok

    def retire(self, bufs):
        for b in bufs:
            for en_, dk_ in b.dkey.items():
                self.free_dsems[en_].append(dk_)
            b.dkey = {}
            for d in (b.w, b.r):
                for s, v in d.items():
                    if self.arena_deps.get(s, 0) < v:
                        self.arena_deps[s] = v

    def check(self):
        semv = {}
        pos = {e.name: 0 for e in self.engs}
        progress = True
        while progress:
            progress = False
            for e in self.engs:
                while pos[e.name] < len(e.ops):
                    wl, fn, incr = e.ops[pos[e.name]]
                    if all(semv.get(s_, 0) >= v for s_, v in wl):
                        if incr is not None:
                            semv[incr[0]] = semv.get(incr[0], 0) + incr[1]
                        pos[e.name] += 1
                        progress = True
                    else:
                        break
        stuck = {e.name: (pos[e.name], len(e.ops)) for e in self.engs if pos[e.name] < len(e.ops)}
        if stuck:
            for e in self.engs:
                if pos[e.name] < len(e.ops):
                    wl, fn, incr = e.ops[pos[e.name]]
                    print("STUCK", e.name, pos[e.name], "/", len(e.ops), "waits", [(s_, v, semv.get(s_, 0)) for s_, v in wl])
            raise RuntimeError("deadlock in emitted program: %s" % stuck)
        print("check ok: ops per engine", {e.name: len(e.ops) for e in self.engs}, "nsems", len(self.sems))

    def replay(self):
        self.check()
        nc = self.nc
        sems = self.sems
        with nc.Block() as block:
            def mk(eng):
                def body(e):
                    for wl, fn, incr in eng.ops:
                        for s, v in wl:
                            e.wait_ge(sems[s], v)
                        ins = getattr(e, fn[0])(*fn[1], **fn[2])
                        if incr is not None:
                            ins.then_inc(sems[incr[0]], incr[1])
                return body
            block.tensor(mk(self.pe))
            block.scalar(mk(self.act))
            block.vector(mk(self.dve))
            block.gpsimd(mk(self.pool))
            block.sync(mk(self.sp))


class Arena:
    def __init__(self, prog, tensor, nwords):
        self.p = prog
        self.t = tensor
        self.n = nwords
        self.top = 0
        self.live = []

    def mark(self):
        return (self.top, len(self.live))

    def release(self, m):
        top, nl = m
        self.p.retire(self.live[nl:])
        del self.live[nl:]
        self.top = top

    def alloc(self, name, shape, dt):
        n = 1
        for s in shape:
            n *= s
        words = n if dt == F32 else (n + 1) // 2
        words = (words + 7) // 8 * 8
        assert self.top + words <= self.n, ("arena overflow", name, self.top, words, self.n)
        ap = self.t[:, self.top:self.top + words]
        self.top += words
        if dt != F32:
            ap = ap.bitcast(dt)
        ap = ap[:, 0:n]
        if len(shape) == 2:
            ap = ap.rearrange("p (a b) -> p a b", a=shape[0])
        elif len(shape) == 3:
            ap = ap.rearrange("p (a b c) -> p a b c", a=shape[0], b=shape[1])
        return ap

    def newbuf(self, name):
        b = self.p.buf(name)
        self.live.append(b)
        return b


def build_program(stop_after=None):
    nc = bass.Bass("TRN2", target_bir_lowering=False)
    stack = contextlib.ExitStack()
    with stack:
        _build(nc, stack, stop_after)
    return nc


def _build(nc, stack, stop_after):
    P = Prog(nc, stack)
    emit = P.emit
    pe, act, dve, pool, sp = P.pe, P.act, P.dve, P.pool, P.sp

    def din(name, shape, dt=F32):
        return nc.dram_tensor(name, list(shape), dt, kind="ExternalInput").ap()

    def dout(name, shape, dt=F32):
        return nc.dram_tensor(name, list(shape), dt, kind="ExternalOutput").ap()

    xp_d = din("xp", [TP, D])
    xs_d = din("xs", [TS, D])
    ck_d = din("ck", [2, PAST, 512])
    cv_d = din("cv", [2, PAST, 512])
    cT_d = din("cT", [128, KC, 8])
    pvec_d = din("pvec", [2, 128, NV])
    ident_d = din("ident", [128, 128])
    rotm_d = din("rotm", [128, 128])
    bones_d = din("bones", [128, 128])
    rope_d = din("rope", [128, 2, TS])
    icntp_d = din("icnt_p", [4, LP])
    icnts_d = din("icnt_s", [4, TS])
    dft128_d = din("dft128", [128, 256])
    dftp_d = din("dftp", [LP, 2, LP], BF16)
    dfts_d = din("dfts", [LS, 2, TS], BF16)
    hmask_d = din("hmask", [128, 2])
    w_ada_d = din("w_ada", [2, D, 6 * D])
    w_in_d = din("w_in", [2, D, INC])
    pool_w_d = din("pool_w", [2, 4, 128, 128])
    conv_pw_d = din("conv_pw", [2, 512, 512])
    four_w_d = din("fourier_w", [2, 512, 512])
    w_out_d = din("w_out", [2, D, D])
    w_gate_d = din("w_gate", [2, D, DFF])
    w_up_d = din("w_up", [2, D, DFF])
    w_down_d = din("w_down", [2, DFF, D])
    yp_d = dout("yp", [TP, D])
    ys_d = dout("ys", [TS, D])
    nk_d = dout("nk", [2, 2, LP, 512])
    nv_d = dout("nv", [2, 2, LP, 512])
    dbg_d = dout("dbg", [128, KC * TS]) if stop_after else None
    XRA, XRB = 544, 1024

    class _V:
        def __init__(self, t):
            self.t = t

        def ap(self):
            return self.t.ap().rearrange("p c -> (p c)").rearrange("(r c) -> r c", c=1024)
    xsA_t = [nc.dram_tensor("xsA%d" % l, [128, XRA * 8], BF16) for l in range(2)]
    xrA_t = [nc.dram_tensor("xrA%d" % l, [256, XRA * 8], BF16) for l in range(2)]
    xsB_t = [nc.dram_tensor("xsB%d" % l, [128, XRB * 8], BF16) for l in range(2)]
    xrB_t = [nc.dram_tensor("xrB%d" % l, [256, XRB * 8], BF16) for l in range(2)]
    xsA = [_V(t) for t in xsA_t]
    xrA = [_V(t) for t in xrA_t]
    xsB = [_V(t) for t in xsB_t]
    xrB = [_V(t) for t in xrB_t]
    xsendQ = [[P.buf("xsend%d_%d" % (l, q)) for q in range(2)] for l in range(2)]
    xrecvQ = [[P.buf("xrecv%d_%d" % (l, q)) for q in range(2)] for l in range(2)]
    cc_keys = [[P._sem("cc%d_%d" % (l, q)) for q in range(2)] for l in range(2)]

    def sb(name, shape, dt):
        return stack.enter_context(nc.sbuf_tensor("sb_" + name, list(shape), dt))

    xT = sb("xT", [128, KC, TS], F32)
    xTb = [[P.buf("xT%d_%d" % (c, n)) for n in range(2)] for c in range(KC)]
    NSLOT = 4
    wring = [sb("wr%d" % i, [128, 4096], BF16) for i in range(NSLOT)]
    wringB = [P.buf("wr%d" % i) for i in range(NSLOT)]
    wnext = [0]
    pv = sb("pv", [128, 2, NV], F32)
    pvB = P.buf("pv")
    modt = sb("modt", [128, 2, 96, 2], F32)
    modB = P.buf("modt")
    gsc = sb("gsc", [128, 2, 2, KC, 2], F32)
    gscB = P.buf("gsc")
    ident = sb("ident", [128, 128], F32)
    rotm = sb("rotm", [128, 128], F32)
    bones = sb("bones", [128, 128], F32)
    ones_f = sb("ones_f", [128, 128], F32)
    ones_b = sb("ones_b", [128, 128], BF16)
    constB = P.buf("const")
    rope = sb("rope", [128, 2, TS], F32)
    dft128 = sb("dft128", [128, 256], BF16)
    dftp = sb("dftp", [128, 2, 2, LP], BF16)
    hmask = sb("hmask", [128, 2], F32)
    sT = sb("sT", [128, KC, 8], BF16)
    small = sb("small", [128, 16], F32)
    smallB = P.buf("small")
    ARENA_WORDS = 23600
    arena_t = sb("arena", [128, ARENA_WORDS], F32)
    A = Arena(P, arena_t, ARENA_WORDS)
    banks = [stack.enter_context(nc.psum_tensor("ps%d" % i, [128, 512], F32)) for i in range(8)]
    bankB = [P.buf("ps%d" % i) for i in range(8)]
    dn = [0]
    mn = [0]

    def dbank():
        i = dn[0] % 6
        dn[0] += 1
        return i

    att_mode = [False]

    def mbank():
        if att_mode[0]:
            return 7
        i = 6 + mn[0] % 2
        mn[0] += 1
        return i

    def ev_eng():
        P.flip ^= 1
        return act if P.flip else dve

    def copy_on(eng, out, in_, reads, writes, pwrites=()):
        if eng is act:
            return emit(act, lambda e: e.activation(out=out, in_=in_, func=AF.Identity), reads, writes, pwrites)
        return emit(dve, lambda e: e.tensor_copy(out=out, in_=in_), reads, writes, pwrites)

    def wload(src, kc, cw):
        i = wnext[0] % NSLOT
        wnext[0] += 1
        view = wring[i][:, 0:kc * cw].rearrange("p (k c) -> p k c", k=kc)
        emit(pool, lambda e: e.dma_start(out=view, in_=src), [], [wringB[i]], dma=wringB[i])
        return wringB[i], view

    def wsrc(wd, l, r0, nrows, c0, cw):
        return wd[l, r0:r0 + nrows, c0:c0 + cw].rearrange("(k p) n -> p k n", p=128)

    eps_ap = small[:, 0:1]

    emit(sp, lambda e: e.dma_start(out=pv[:], in_=pvec_d.rearrange("l p v -> p l v")), [], [pvB], dma=pvB)
    cB = P.buf("cld")
    for (t, d) in ((ident, ident_d), (rotm, rotm_d), (bones, bones_d)):
        emit(sp, lambda e, t=t, d=d: e.dma_start(out=t[:], in_=d), [], [], pwrites=[constB], dma=cB)
    emit(sp, lambda e: e.dma_start(out=rope[:], in_=rope_d), [], [], pwrites=[constB], dma=cB)
    emit(sp, lambda e: e.dma_start(out=hmask[:], in_=hmask_d), [], [], pwrites=[constB], dma=cB)
    emit(sp, lambda e: e.dma_start(out=dftp[:], in_=dftp_d.rearrange("(i p) a k -> p i a k", p=128)), [], [],
         pwrites=[constB], dma=cB)
    emit(pool, lambda e: e.dma_start(out=dft128[:], in_=dft128_d), [], [], pwrites=[constB], dma=cB)
    emit(dve, lambda e: e.memset(ones_f[:], 1.0), [], [], pwrites=[constB])
    emit(dve, lambda e: e.memset(ones_b[:], 1.0), [], [], pwrites=[constB])
    emit(dve, lambda e: e.memset(small[:, 0:1], EPS), [], [smallB])

    m0 = A.mark()
    ctmp = A.alloc("ctmp", [KC, 8], F32)
    ctB = A.newbuf("ctmp")
    sTB = P.buf("sT")
    emit(sp, lambda e: e.dma_start(out=ctmp, in_=cT_d), [], [ctB], dma=ctB)
    emit(act, lambda e: e.activation(out=sT[:], in_=ctmp, func=AF.Silu), [ctB], [sTB])
    lp = A.alloc("lamp", [4], F32)
    lpB = A.newbuf("lamp")
    for l in range(2):
        for q in range(2):
            emit(dve, lambda e, l=l, q=q: e.tensor_tensor(
                out=lp[:, 2 * l + q:2 * l + q + 1], in0=pv[:, l, C_LAM + 2 * q:C_LAM + 2 * q + 1],
                in1=pv[:, l, C_LAM + 2 * q + 1:C_LAM + 2 * q + 2], op=ALU.mult), [pvB], [], pwrites=[lpB])
    bi = mbank()
    emit(pe, lambda e: e.matmul(banks[bi][:, 0:4], lhsT=ones_f[:], rhs=lp, start=True, stop=True),
         [lpB, constB], [bankB[bi]])
    le = A.alloc("lame", [4], F32)
    leB = A.newbuf("lame")
    emit(act, lambda e: e.activation(out=le, in_=banks[bi][:, 0:4], func=AF.Exp), [bankB[bi]], [leB])
    for l in range(2):
        emit(dve, lambda e, l=l: e.tensor_tensor(out=small[:, 5 + l:6 + l], in0=le[:, 2 * l + 1:2 * l + 2],
                                                  in1=le[:, 2 * l:2 * l + 1], op=ALU.subtract),
             [leB], [], pwrites=[smallB])
        emit(dve, lambda e, l=l: e.tensor_scalar(out=small[:, 1 + l:2 + l], in0=small[:, 5 + l:6 + l],
                                                  scalar1=-LAM_INIT[l], scalar2=None, op0=ALU.add),
             [smallB], [], pwrites=[smallB])
        emit(dve, lambda e, l=l: e.tensor_scalar(out=small[:, 3 + l:4 + l], in0=pv[:, l, C_GSUB:C_GSUB + 1],
                                                  scalar1=1.0 - LAM_INIT[l], scalar2=None, op0=ALU.mult),
             [pvB], [], pwrites=[smallB])

    def gsc_emit(l, s_):
        for j in range(2):
            emit(dve, lambda e, j=j: e.scalar_tensor_tensor(
                out=gsc[:, l, s_, :, j], in0=modt[:, l, 48 * s_ + 16:48 * s_ + 32, j], scalar=1.0,
                in1=pv[:, l, (C_G1 if s_ == 0 else C_G2):(C_G1 if s_ == 0 else C_G2) + KC],
                op0=ALU.add, op1=ALU.mult), [modB, pvB], [], pwrites=[gscB])

    def mods_gen(l, mlo, mhi, gsc_after=None):
        for blk in range(mlo // 2, mhi // 2):
            wb, wv = wload(wsrc(w_ada_d, l, 0, D, blk * 256, 256), KC, 256)
            mb = mbank()
            mps = banks[mb][:, 0:16].rearrange("p (m j) -> p m j", j=8)
            for sub in range(2):
                for k in range(KC):
                    first = (sub == 0 and k == 0)
                    emit(pe, lambda e, k=k, sub=sub: e.matmul(
                        mps[:, sub, :], lhsT=wv[:, k, sub * 128:(sub + 1) * 128], rhs=sT[:, k, :],
                        start=(k == 0), stop=(k == KC - 1)),
                        [wb, sTB], [bankB[mb]] if first else [], sig=(k == KC - 1 and sub == 1))
            for j in range(2):
                emit(dve, lambda e, j=j, blk=blk: e.tensor_tensor(
                    out=modt[:, l, 2 * blk:2 * blk + 2, j], in0=mps[:, :, j],
                    in1=pv[:, l, C_BADA + 2 * blk:C_BADA + 2 * blk + 2], op=ALU.add),
                    [bankB[mb], pvB], [], pwrites=[modB])
            yield 1
        if gsc_after is not None:
            gsc_emit(l, gsc_after)

    for _ in mods_gen(0, 0, 32, 0):
        pass

    def _bg_chain():
        yield from mods_gen(0, 32, 48)
        yield from mods_gen(0, 48, 80, 1)
        yield from mods_gen(0, 80, 96)
        yield from mods_gen(1, 0, 32, 0)
        yield from mods_gen(1, 32, 80, 1)
        yield from mods_gen(1, 80, 96)
    bg = [_bg_chain(), 0, 0]

    def bg_poll():
        if bg[0] is None:
            return False
        try:
            next(bg[0])
            bg[1] += 1
            return True
        except StopIteration:
            bg[0] = None
            return False

    def bg_tick(light=False):
        bg[2] += 1
        if light:
            bg_poll()
            bg_poll()
        elif bg[1] < 32:
            bg_poll()
        elif bg[2] % 2 == 0:
            bg_poll()

    def bg_drain(nblocks=None):
        while bg[0] is not None and (nblocks is None or bg[1] < nblocks):
            if not bg_poll():
                break
    A.release(m0)
    early = stop_after in ("mods", "loadx")

    def load_x(src_d, T):
        m = A.mark()
        stg = [A.alloc("xstg%d" % i, [D], F32) for i in range(2)]
        stgB = [A.newbuf("xstg%d" % i) for i in range(2)]
        for i in range(T // 128):
            s_ = i % 2
            emit(sp, lambda e, i=i, s_=s_: e.dma_start(out=stg[s_], in_=src_d[i * 128:(i + 1) * 128, :]),
                 [], [stgB[s_]], dma=stgB[s_])
            for c4 in range(4):
                b = mbank()
                for q in range(4):
                    c = c4 * 4 + q
                    emit(pe, lambda e, b=b, q=q, c=c, s_=s_: e.transpose(
                        out=banks[b][:, q * 128:(q + 1) * 128], in_=stg[s_][:, c * 128:(c + 1) * 128],
                        identity=ident[:]), [stgB[s_], constB], [bankB[b]] if q == 0 else [], sig=(q == 3))
                n = i // 4
                o = xT[:, c4 * 4:(c4 + 1) * 4, i * 128:(i + 1) * 128]
                copy_on(ev_eng(), o, banks[b][:].rearrange("p (q t) -> p q t", q=4), [bankB[b]], [],
                        pwrites=[xTb[c4 * 4 + q][n] for q in range(4)])
        A.release(m)

    def store_y(dst_d, T):
        m = A.mark()
        stg = [A.alloc("ystg%d" % i, [D], F32) for i in range(2)]
        stgB = [A.newbuf("ystg%d" % i) for i in range(2)]
        last = []
        for i in range(T // 128):
            s_ = i % 2
            n = i // 4
            for c4 in range(4):
                b = mbank()
                for q in range(4):
                    c = c4 * 4 + q
                    emit(pe, lambda e, b=b, q=q, c=c, i=i: e.transpose(
                        out=banks[b][:, q * 128:(q + 1) * 128], in_=xT[:, c, i * 128:(i + 1) * 128],
                        identity=ident[:]), [xTb[c][n], constB], [bankB[b]] if q == 0 else [], sig=(q == 3))
                if c4 == 0:
                    copy_on(ev_eng(), stg[s_][:, 0:512], banks[b][:], [bankB[b]], [stgB[s_]])
                else:
                    copy_on(ev_eng(), stg[s_][:, c4 * 512:(c4 + 1) * 512], banks[b][:], [bankB[b]], [],
                            pwrites=[stgB[s_]])
            tok = emit(sp, lambda e, i=i, s_=s_: e.dma_start(out=dst_d[i * 128:(i + 1) * 128, :], in_=stg[s_]),
                       [stgB[s_]], [], dma=stgB[s_])
            last.append(stgB[s_])
        A.release(m)
        return last

    def rstd_from(ps_ap, scale, out_ap, rB, wB, tmp_ap, tmpB):
        emit(act, lambda e: e.activation(out=tmp_ap, in_=ps_ap, func=AF.Sqrt, bias=eps_ap, scale=scale),
             rB + [smallB], [tmpB])
        emit(dve, lambda e: e.reciprocal(out=out_ap, in_=tmp_ap), [tmpB], wB)

    def norm(l, s, T, j, hT, hTb):
        m = A.mark()
        sq = [A.alloc("sq%d" % i, [512], BF16) for i in range(2)]
        sqB = [A.newbuf("sq%d" % i) for i in range(2)]
        tm = [A.alloc("ntm%d" % i, [512], F32) for i in range(2)]
        tmB = [A.newbuf("ntm%d" % i) for i in range(2)]
        rs = A.alloc("nrs", [512], F32)
        rsB = A.newbuf("nrs")
        rt = A.alloc("nrt", [512], F32)
        rtB = A.newbuf("nrt")
        for n in range(T // 512):
            b = mbank()
            for c in range(KC):
                i = c % 2
                if c % 2 == 0:
                    emit(act, lambda e, c=c, n=n, i=i: e.activation(out=sq[i], in_=xT[:, c, n * 512:(n + 1) * 512],
                                                                   func=AF.Square), [xTb[c][n]], [sqB[i]])
                else:
                    emit(dve, lambda e, c=c, n=n, i=i: e.tensor_tensor(out=sq[i], in0=xT[:, c, n * 512:(n + 1) * 512],
                                                                      in1=xT[:, c, n * 512:(n + 1) * 512], op=ALU.mult),
                         [xTb[c][n]], [sqB[i]])
                emit(pe, lambda e, c=c, i=i, b=b: e.matmul(banks[b][:], lhsT=ones_b[:], rhs=sq[i],
                                                          start=(c == 0), stop=(c == KC - 1)),
                     [sqB[i], constB], [bankB[b]] if c == 0 else [], pwrites=[] if c == 0 else [bankB[b]], sig=True)
            import os
            NP_ = int(os.environ.get("NORM_PARTS", "3"))
            if NP_ < 2:
                continue
            rstd_from(banks[b][:], 1.0 / D, rs, [bankB[b]], [rsB], rt, rtB)
            if NP_ < 3:
                continue
            for c in range(KC):
                i = c % 2
                emit(dve, lambda e, c=c, n=n, i=i: e.tensor_tensor(out=tm[i], in0=xT[:, c, n * 512:(n + 1) * 512],
                                                                  in1=rs, op=ALU.mult), [xTb[c][n], rsB], [tmB[i]])
                NV_ = int(os.environ.get("NORM_VAR", "0"))
                if NV_ == 1:
                    emit(act, lambda e, c=c, n=n, i=i: e.activation(
                        out=hT[:, c, n * 512:(n + 1) * 512], in_=tm[i], func=AF.Identity),
                        [tmB[i], modB, gscB], [hTb[c][n]])
                elif NV_ == 2:
                    pass
                elif NV_ == 3:
                    emit(act, lambda e, c=c, n=n, i=i: e.activation(
                        out=hT[:, c, n * 512:(n + 1) * 512], in_=tm[i], func=AF.Identity,
                        bias=small[:, 0:1], scale=small[:, 0:1]),
                        [tmB[i], modB, gscB], [hTb[c][n]])
                else:
                    emit(act, lambda e, c=c, n=n, i=i: e.activation(
                        out=hT[:, c, n * 512:(n + 1) * 512], in_=tm[i], func=AF.Identity,
                        bias=modt[:, l, 48 * s + c, j:j + 1], scale=gsc[:, l, s, c, j:j + 1]),
                        [tmB[i], modB, gscB], [hTb[c][n]])
        A.release(m)

    def job(K, NT, ncols, lhs_fn, rhs_fn):
        bl = [dbank() for _ in range(NT)]
        for k in range(K):
            wb, lhsT = lhs_fn(k)
            for n in range(NT):
                rb, rhs = rhs_fn(k, n)
                emit(pe, lambda e, b=bl[n], lhsT=lhsT, rhs=rhs, k=k: e.matmul(
                    banks[b][:, 0:ncols], lhsT=lhsT, rhs=rhs, start=(k == 0), stop=(k == K - 1)),
                    [wb, rb], [bankB[bl[n]]] if k == 0 else [], sig=(k == K - 1 and n == NT - 1))
        return bl

    def residual_partial(K, NT, l, gate_chunk0, j, wd, r0, rhs_ap, rhsB_fn, overwrite=False):
        cw = 512 if K <= 8 else 256
        for ob in range(D // cw):
            bg_tick(light=(K == 4))
            wb, wv = wload(wsrc(wd, l, r0, K * 128, ob * cw, cw), K, cw)
            for sub in range(cw // 128):
                mo = ob * (cw // 128) + sub
                bl = job(K, NT, 512,
                         lambda k, wv=wv, sub=sub: (wb, wv[:, k, sub * 128:(sub + 1) * 128]),
                         lambda k, n: (rhsB_fn(k, n), rhs_ap[:, k, n * 512:(n + 1) * 512]))
                for n in range(NT):
                    if overwrite:
                        emit(dve, lambda e, b=bl[n], mo=mo, n=n: e.tensor_scalar(
                            out=xT[:, mo, n * 512:(n + 1) * 512], in0=banks[b][:],
                            scalar1=modt[:, l, gate_chunk0 + mo, j:j + 1], scalar2=None, op0=ALU.mult),
                            [bankB[bl[n]], modB], [xTb[mo][n]])
                        continue
                    emit(dve, lambda e, b=bl[n], mo=mo, n=n: e.scalar_tensor_tensor(
                        out=xT[:, mo, n * 512:(n + 1) * 512], in0=banks[b][:],
                        scalar=modt[:, l, gate_chunk0 + mo, j:j + 1], in1=xT[:, mo, n * 512:(n + 1) * 512],
                        op0=ALU.mult, op1=ALU.add), [bankB[bl[n]], modB, xTb[mo][n]], [xTb[mo][n]])

    def qknorm(b, n, gcol, l, use_rope, out_bf, outB, out_f32=None, out_f32B=None, tmps=None):
        sq, sqB, rs, rsB, rt, rtB, qn, qnB, t1, t1B = tmps
        emit(act, lambda e: e.activation(out=sq, in_=banks[b][:], func=AF.Square), [bankB[b]], [sqB])
        b2 = mbank()
        emit(pe, lambda e: e.matmul(banks[b2][:], lhsT=bones[:], rhs=sq, start=True, stop=True),
             [sqB, constB], [bankB[b2]])
        rstd_from(banks[b2][:], 1.0 / 64, rs, [bankB[b2]], [rsB], rt, rtB)
        gq = pv[:, l, gcol:gcol + 1]
        if not use_rope:
            emit(dve, lambda e: e.scalar_tensor_tensor(out=qn, in0=banks[b][:], scalar=gq, in1=rs,
                                                       op0=ALU.mult, op1=ALU.mult), [bankB[b], rsB, pvB], [qnB])
            emit(act, lambda e: e.activation(out=out_bf, in_=qn, func=AF.Identity), [qnB], [], pwrites=[outB])
            return
        emit(dve, lambda e: e.scalar_tensor_tensor(out=qn, in0=banks[b][:], scalar=gq, in1=rs,
                                                   op0=ALU.mult, op1=ALU.mult), [bankB[b], rsB, pvB], [qnB])
        b3 = mbank()
        emit(pe, lambda e: e.matmul(banks[b3][:], lhsT=rotm[:], rhs=qn, start=True, stop=True),
             [qnB, constB], [bankB[b3]])
        emit(dve, lambda e: e.tensor_tensor(out=t1, in0=banks[b3][:], in1=rope[:, 1, n * 512:(n + 1) * 512],
                                            op=ALU.mult), [bankB[b3], constB], [t1B])
        emit(dve, lambda e: e.tensor_tensor(out=qn, in0=qn, in1=rope[:, 0, n * 512:(n + 1) * 512],
                                            op=ALU.mult), [qnB, constB], [qnB])
        emit(dve, lambda e: e.tensor_tensor(out=out_bf, in0=qn, in1=t1, op=ALU.add), [qnB, t1B], [],
             pwrites=[outB])

    def qk_tmps():
        sq = A.alloc("qsq", [512], F32); sqB = A.newbuf("qsq")
        rs = A.alloc("qrs", [512], F32); rsB = A.newbuf("qrs")
        qn = A.alloc("qqn", [512], F32); qnB = A.newbuf("qqn")
        t1 = A.alloc("qt1", [512], F32); t1B = A.newbuf("qt1")
        return (sq, sqB, rs, rsB, t1, t1B, qn, qnB, t1, t1B)

    def dbg_to_x(ap3, bufs, nch, ncol):
        emit(dve, lambda e: e.tensor_copy(out=xT[:, 0:nch, 0:ncol], in_=ap3), bufs, [],
             pwrites=[xTb[c][n] for c in range(KC) for n in range(2)])

    def dbg_dump(ap2d, bufs, ncols):
        emit(sp, lambda e: e.dma_start(out=dbg_d[:, 0:ncols], in_=ap2d), bufs, [], dma=P.buf("dbg"))

    def layer(l, G):
        T, NT, j, sample = G["T"], G["T"] // 512, G["j"], G["sample"]
        if l == 1 or not G.get("first", False):
            bg_drain()
        segs = G["segs"]
        PP, PC = 8, 16
        base = A.mark()
        qT = A.alloc("qT", [4, T], BF16); qTB = A.newbuf("qT")
        nseg = len(segs)
        Lseg = segs[0][1]
        UW = Lseg + 2 * PP
        GW = Lseg + 2 * PC
        if not sample:
            kTp = A.alloc("kTp", [4, T], BF16); kTpB = A.newbuf("kTp")
            vTp = A.alloc("vTp", [T // 128, 512], BF16); vTpB = A.newbuf("vTp")
            fTp = A.alloc("fTp", [4, T], BF16); fTpB = A.newbuf("fTp")
        m_q = A.mark()
        gT = A.alloc("gT", [4, nseg * GW], F32); gTB = [A.newbuf("gT%d" % c) for c in range(4)]
        m_g = A.mark()
        uP = A.alloc("uP", [4, nseg * UW], F32); uPB = [A.newbuf("uP%d" % g) for g in range(4)]
        m_ug = A.mark()
        hT = A.alloc("hT", [KC, T], BF16)
        hTb = [[A.newbuf("hT%d_%d" % (c, n)) for n in range(NT)] for c in range(KC)]
        norm(l, 0, T, j, hT, hTb)

        def hrhs(k, n):
            return (hTb[k][n], hT[:, k, n * 512:(n + 1) * 512])
        stg_tag = "%d%s" % (l, "s" if sample else "p")
        if stop_after == "norm" + stg_tag:
            return False

        def proj_jobs(col0, nchunks, evac):
            for blk in range(nchunks // 2):
                bg_tick()
                wb, wv = wload(wsrc(w_in_d, l, 0, D, col0 + blk * 256, 256), KC, 256)
                for sub in range(2):
                    ch = blk * 2 + sub
                    bl = job(KC, NT, 512, lambda k, wv=wv, sub=sub, wb=wb: (wb, wv[:, k, sub * 128:(sub + 1) * 128]),
                             hrhs)
                    for n in range(NT):
                        evac(ch, n, bl[n])

        mfr = A.mark()
        tmps = qk_tmps()
        if sample:
            kst = [A.alloc("kst%d" % i, [512], BF16) for i in range(2)]
            kstB = [A.newbuf("kst%d" % i) for i in range(2)]
            kcnt = [0]

            def k_evac(ch, n, b):
                i = kcnt[0] % 2
                kcnt[0] += 1
                qknorm(b, n, C_GK, l, True, kst[i], kstB[i], tmps=tmps)
                emit(sp, lambda e, i=i, ch=ch, n=n: e.dma_start(
                    out=xsA[l].ap()[ch * 128:(ch + 1) * 128, n * 512:(n + 1) * 512], in_=kst[i]),
                    [kstB[i]], [], pwrites=[xsendQ[l][0]], dma=kstB[i])
        else:
            kst2 = [A.alloc("nkst%d" % i, [512], F32) for i in range(2)]
            kst2B = [A.newbuf("nkst%d" % i) for i in range(2)]
            kcnt = [0]

            def k_evac(ch, n, b):
                sq, sqB, rs, rsB, rt, rtB, qn, qnB, t1, t1B = tmps
                emit(act, lambda e: e.activation(out=sq, in_=banks[b][:], func=AF.Square), [bankB[b]], [sqB])
                b2 = mbank()
                emit(pe, lambda e: e.matmul(banks[b2][:], lhsT=bones[:], rhs=sq, start=True, stop=True),
                     [sqB, constB], [bankB[b2]])
                rstd_from(banks[b2][:], 1.0 / 64, rs, [bankB[b2]], [rsB], rt, rtB)
                emit(dve, lambda e: e.scalar_tensor_tensor(out=qn, in0=banks[b][:], scalar=pv[:, l, C_GK:C_GK + 1],
                                                           in1=rs, op0=ALU.mult, op1=ALU.mult),
                     [bankB[b], rsB, pvB], [qnB])
                emit(act, lambda e: e.activation(out=kTp[:, ch, :], in_=qn, func=AF.Identity), [qnB], [],
                     pwrites=[kTpB])
                b3 = mbank()
                for q in range(4):
                    emit(pe, lambda e, q=q: e.transpose(out=banks[b3][:, q * 128:(q + 1) * 128],
                                                        in_=qn[:, q * 128:(q + 1) * 128], identity=ident[:]),
                         [qnB, constB], [bankB[b3]] if q == 0 else [], sig=(q == 3))
                i = kcnt[0] % 2
                kcnt[0] += 1
                copy_on(ev_eng(), kst2[i], banks[b3][:], [bankB[b3]], [kst2B[i]])
                for sq_ in range(2):
                    emit(sp, lambda e, i=i, ch=ch, sq_=sq_: e.dma_start(
                        out=nk_d[sq_, l, :, ch * 128:(ch + 1) * 128].rearrange("(h p) f -> p h f", p=128),
                        in_=kst2[i].rearrange("p (q f) -> p q f", q=4)[:, 2 * sq_:2 * sq_ + 2, :]),
                        [kst2B[i]], [], pwrites=[outB_all], dma=kst2B[i])
        proj_jobs(1024, 4, k_evac)
        if stop_after == "kproj" + stg_tag:
            return False

        if sample:
            vst, vstB = kst, kstB
        else:
            vst = [A.alloc("vst%d" % i, [512], BF16) for i in range(2)]
            vstB = [A.newbuf("vst%d" % i) for i in range(2)]
        if not sample:
            vsf = [A.alloc("vsf%d" % i, [512], F32) for i in range(2)]
            vsfB = [A.newbuf("vsf%d" % i) for i in range(2)]
        wv_blocks = [wload(wsrc(w_in_d, l, 0, D, 1536 + hb * 256, 256), KC, 256) for hb in range(2)]
        for i in range(T // 128):
            b = dbank()
            n = i // 4
            for hb in range(2):
                wb, wv = wv_blocks[hb]
                for k in range(KC):
                    emit(pe, lambda e, b=b, hb=hb, k=k, wv=wv, i=i: e.matmul(
                        banks[b][:, hb * 256:(hb + 1) * 256], lhsT=hT[:, k, i * 128:(i + 1) * 128], rhs=wv[:, k, :],
                        start=(k == 0), stop=(k == KC - 1)),
                        [wb, hTb[k][n]], [bankB[b]] if (k == 0 and hb == 0) else [],
                        sig=(k == KC - 1 and hb == 1))
            s_ = i % 2
            VV_ = int(os.environ.get("VVAR", "0"))
            if VV_ == 1:
                continue
            if sample:
                copy_on(ev_eng(), vst[s_], banks[b][:], [bankB[b]], [vstB[s_]])
                emit(sp, lambda e, i=i, s_=s_: e.dma_start(
                    out=xsB[l].ap()[512:1024, :].rearrange("r (a c) -> (r a) c", a=2)[i * 128:(i + 1) * 128, :],
                    in_=vst[s_]), [vstB[s_]], [], pwrites=[xsendQ[l][1]], dma=vstB[s_])
            else:
                emit(dve, lambda e, b=b, s_=s_: e.tensor_copy(out=vsf[s_], in_=banks[b][:]), [bankB[b]], [vsfB[s_]])
                copy_on(act, vTp[:, i, :], vsf[s_], [vsfB[s_]], [], pwrites=[vTpB])
                if VV_ == 3:
                    continue
                emit(sp, lambda e, i=i, s_=s_: e.dma_start(
                    out=nv_d[i // 2, l, (i % 2) * 128:(i % 2 + 1) * 128, :], in_=vsf[s_]),
                    [vsfB[s_]], [], pwrites=[outB_all], dma=vsfB[s_])

        if stop_after == "vproj" + stg_tag:
            return False
        if sample:
            fst, fstB = kst, kstB
            fcnt = [0]

            def f_evac(ch, n, b):
                i = fcnt[0] % 2
                fcnt[0] += 1
                copy_on(ev_eng(), fst[i], banks[b][:], [bankB[b]], [fstB[i]])
                emit(sp, lambda e, i=i, ch=ch, n=n: e.dma_start(
                    out=xsB[l].ap()[ch * 128:(ch + 1) * 128, n * 512:(n + 1) * 512], in_=fst[i]),
                    [fstB[i]], [], pwrites=[xsendQ[l][1]], dma=fstB[i])
        else:
            def f_evac(ch, n, b):
                copy_on(ev_eng(), fTp[:, ch, n * 512:(n + 1) * 512], banks[b][:], [bankB[b]], [], pwrites=[fTpB])
        proj_jobs(3072, 4, f_evac)
        if sample:
            emit(pool, lambda e: e.collective_compute(
                "AllGather", ALU.bypass, replica_groups=[[0, 1], [2, 3], [4, 5], [6, 7]],
                ins=[xsB_t[l].ap().opt()], outs=[xrB_t[l].ap().opt()]),
                [xsendQ[l][1]], [xrecvQ[l][1]], inc=(cc_keys[l][1], 1))

        def seg_cols(n, pad, W):
            out = []
            for si, (s0, L) in enumerate(segs):
                lo = max(s0, n * 512)
                hi = min(s0 + L, (n + 1) * 512)
                if lo < hi:
                    out.append((lo - n * 512, hi - lo, si * W + pad + lo - s0))
            return out

        def p_evac(ch, n, b):
            for (o, nc_, dc) in seg_cols(n, PP, UW):
                copy_on(ev_eng(), uP[:, ch, dc:dc + nc_], banks[b][:, o:o + nc_], [bankB[b]], [], pwrites=[uPB[ch]])
        proj_jobs(0, 4, p_evac)

        sg = A.alloc("sg", [T], F32)
        sgB = A.newbuf("sg")
        for half in range(2):
            bg_tick()
            wbb, wvb = wload(wsrc(w_in_d, l, 0, D, 2560 + half * 256, 256), KC, 256)
            wba, wva = wload(wsrc(w_in_d, l, 0, D, 2048 + half * 256, 256), KC, 256)
            for sub in range(2):
                cch = half * 2 + sub
                bl = job(KC, NT, 512, lambda k, sub=sub: (wbb, wvb[:, k, sub * 128:(sub + 1) * 128]), hrhs)
                for n in range(NT):
                    emit(act, lambda e, n=n, b=bl[n]: e.activation(out=sg[:, n * 512:(n + 1) * 512], in_=banks[b][:],
                                                                    func=AF.Sigmoid), [bankB[bl[n]]], [], pwrites=[sgB])
                bl = job(KC, NT, 512, lambda k, sub=sub: (wba, wva[:, k, sub * 128:(sub + 1) * 128]), hrhs)
                for n in range(NT):
                    for (o, nc_, dc) in seg_cols(n, PC, GW):
                        emit(dve, lambda e, o=o, nc_=nc_, dc=dc, n=n, b=bl[n], cch=cch: e.tensor_tensor(
                            out=gT[:, cch, dc:dc + nc_], in0=banks[b][:, o:o + nc_],
                            in1=sg[:, n * 512 + o:n * 512 + o + nc_], op=ALU.mult),
                            [bankB[bl[n]], sgB], [], pwrites=[gTB[cch]])

        if sample:
            hal = xsA[l].ap()[512:544, :].rearrange("r (f e) -> (r f) e", e=64).rearrange("(g p) e -> p g e", p=128)
            hsd = A.alloc("hsd", [4, 64], BF16); hsdB = A.newbuf("hsd")
            emit(dve, lambda e: e.memset(hsd, 0.0), [], [hsdB])
            emit(dve, lambda e: e.tensor_copy(out=hsd[:, :, 0:8], in_=uP[:, :, PP:PP + 8]), uPB, [], pwrites=[hsdB])
            emit(dve, lambda e: e.tensor_copy(out=hsd[:, :, 8:16], in_=uP[:, :, PP + Lseg - 8:PP + Lseg]), uPB, [], pwrites=[hsdB])
            emit(dve, lambda e: e.tensor_copy(out=hsd[:, :, 16:32], in_=gT[:, :, PC:PC + 16]), gTB, [], pwrites=[hsdB])
            emit(dve, lambda e: e.tensor_copy(out=hsd[:, :, 32:48], in_=gT[:, :, PC + Lseg - 16:PC + Lseg]), gTB, [], pwrites=[hsdB])
            emit(sp, lambda e: e.dma_start(out=hal, in_=hsd), [hsdB], [], pwrites=[xsendQ[l][0]], dma=hsdB)
            emit(pool, lambda e: e.collective_compute(
                "AllGather", ALU.bypass, replica_groups=[[0, 1], [2, 3], [4, 5], [6, 7]],
                ins=[xsA_t[l].ap().opt()], outs=[xrA_t[l].ap().opt()]),
                [xsendQ[l][0]], [xrecvQ[l][0]], inc=(cc_keys[l][0], 1))
        else:
            for g in range(4):
                for si in range(nseg):
                    emit(dve, lambda e, g=g, si=si: e.memset(uP[:, g, si * UW:si * UW + PP], 0.0), [], [], pwrites=[uPB[g]])
                    emit(dve, lambda e, g=g, si=si: e.memset(uP[:, g, si * UW + PP + Lseg:(si + 1) * UW], 0.0), [], [], pwrites=[uPB[g]])
                    emit(dve, lambda e, g=g, si=si: e.memset(gT[:, g, si * GW:si * GW + PC], 0.0), [], [], pwrites=[gTB[g]])
                    emit(dve, lambda e, g=g, si=si: e.memset(gT[:, g, si * GW + PC + Lseg:(si + 1) * GW], 0.0), [], [], pwrites=[gTB[g]])

        def q_evac(ch, n, b):
            if sample:
                qknorm(b, n, C_GQ, l, True, qT[:, ch, n * 512:(n + 1) * 512], qTB, tmps=tmps)
            else:
                qknorm(b, n, C_GQ, l, False, qT[:, ch, n * 512:(n + 1) * 512], qTB, tmps=tmps)
        proj_jobs(512, 4, q_evac)
        A.release(m_ug)
        if stop_after == "proj" + stg_tag:
            return False

        g1c = 32

        def ymix_alloc():
            y = A.alloc("ymix", [4, T], BF16)
            yB = [[A.newbuf("ymix%d_%d" % (c, n)) for n in range(NT)] for c in range(4)]
            return y, yB

        bg_drain(9)
        mp = A.mark()
        if sample:
            hst = A.alloc("hst", [4, 64], BF16); hstB = A.newbuf("hst")
            hst2 = A.alloc("hst2", [4, 64], BF16); hst2B = A.newbuf("hst2")
            rv = xrA[l].ap()
            h0 = rv[512:544, :].rearrange("r (f e) -> (r f) e", e=64).rearrange("(g p) e -> p g e", p=128)
            h1 = rv[XRA + 512:XRA + 544, :].rearrange("r (f e) -> (r f) e", e=64).rearrange("(g p) e -> p g e", p=128)
            emit(sp, lambda e: e.dma_start(out=hst, in_=h0), [xrecvQ[l][0]], [hstB], dma=hstB)
            emit(sp, lambda e: e.dma_start(out=hst2, in_=h1), [xrecvQ[l][0]], [hst2B], dma=hst2B)
            emit(dve, lambda e: e.tensor_scalar(out=uP[:, :, 0:PP], in0=hst[:, :, 8:16], scalar1=hmask[:, 0:1],
                                                scalar2=None, op0=ALU.mult), [hstB, constB], [], pwrites=uPB)
            emit(dve, lambda e: e.tensor_scalar(out=uP[:, :, PP + Lseg:PP + Lseg + PP], in0=hst2[:, :, 0:8],
                                                scalar1=hmask[:, 1:2], scalar2=None, op0=ALU.mult),
                 [hst2B, constB], [], pwrites=uPB)
            emit(dve, lambda e: e.tensor_scalar(out=gT[:, :, 0:PC], in0=hst[:, :, 32:48], scalar1=hmask[:, 0:1],
                                                scalar2=None, op0=ALU.mult), [hstB, constB], [], pwrites=gTB)
            emit(dve, lambda e: e.tensor_scalar(out=gT[:, :, PC + Lseg:PC + Lseg + PC], in0=hst2[:, :, 16:32],
                                                scalar1=hmask[:, 1:2], scalar2=None, op0=ALU.mult),
                 [hst2B, constB], [], pwrites=gTB)
        icn = A.alloc("icn", [4, Lseg], F32); icnB = A.newbuf("icn")
        icd = icnts_d if sample else icntp_d
        emit(sp, lambda e: e.dma_start(out=icn, in_=icd.partition_broadcast(128)), [], [icnB], dma=icnB)
        sa = A.alloc("sa", [UW], F32); saB = A.newbuf("sa")
        sbb = A.alloc("sbb", [UW], F32); sbB = A.newbuf("sbb")
        pT = A.alloc("pT", [4, T], BF16); pTB = [A.newbuf("pT%d" % g) for g in range(4)]
        ym, ymB = ymix_alloc()
        for g in range(4):
            for si, (s0, L) in enumerate(segs):
                u = uP[:, g, si * UW:(si + 1) * UW]
                W = UW
                emit(dve, lambda e, u=u: e.tensor_tensor(out=sa[:, 1:W], in0=u[:, 0:W - 1], in1=u[:, 1:W], op=ALU.add),
                     [uPB[g]], [saB])
                cur, curB, oth, othB = sa, saB, sbb, sbB
                lo, hi, step = 1, W, 1
                for lev in range(g):
                    nlo, nhi = lo + step, hi - step
                    emit(dve, lambda e, cur=cur, oth=oth, nlo=nlo, nhi=nhi, step=step: e.tensor_tensor(
                        out=oth[:, nlo:nhi], in0=cur[:, nlo - step:nhi - step], in1=cur[:, nlo + step:nhi + step],
                        op=ALU.add), [curB], [othB])
                    cur, curB, oth, othB = oth, othB, cur, curB
                    lo, hi, step = nlo, nhi, step * 2
                emit(dve, lambda e, cur=cur, g=g: e.tensor_tensor(out=cur[:, PP:PP + L], in0=cur[:, PP:PP + L],
                                                                  in1=icn[:, g, :], op=ALU.mult), [curB, icnB], [curB])
                emit(dve, lambda e, cur=cur, u=u, g=g, s0=s0, L=L: e.tensor_tensor(
                    out=pT[:, g, s0:s0 + L], in0=cur[:, PP:PP + L], in1=u[:, PP:PP + L], op=ALU.subtract),
                    [curB, uPB[g]], [], pwrites=[pTB[g]])
        wb, wv = wload(pool_w_d[l].rearrange("g c d -> c g d"), 4, 128)
        for g in range(4):
            bl = job(1, NT, 512, lambda k, g=g: (wb, wv[:, g, :]), lambda k, n, g=g: (pTB[g], pT[:, g, n * 512:(n + 1) * 512]))
            for n in range(NT):
                emit(act, lambda e, g=g, n=n, b=bl[n]: e.activation(
                    out=ym[:, g, n * 512:(n + 1) * 512], in_=banks[b][:], func=AF.Identity,
                    scale=pv[:, l, C_PSC + g:C_PSC + g + 1]), [bankB[bl[n]], pvB], [ymB[g][n]])
        residual_partial(4, NT, l, g1c, j, w_out_d, 0, ym, lambda k, n: ymB[k][n])
        A.release(mp)
        A.release(m_g)
        if stop_after == "pool%d%s" % (l, "s" if sample else "p"):
            return False

        mc = A.mark()
        acc = A.alloc("acc", [4, T], F32); accB = [[A.newbuf("acc%d_%d" % (c, n)) for n in range(NT)] for c in range(4)]
        tap_groups = []
        for c in range(4):
            for n in range(NT):
                def _grp(c=c, n=n):
                    pieces = seg_cols(n, PC, GW)
                    for (o, nc_, dc) in pieces:
                        for jt in range(31):
                            src = gT[:, c, dc - 15 + jt:dc - 15 + jt + nc_]
                            dst = acc[:, c, n * 512 + o:n * 512 + o + nc_]
                            dwc = pv[:, l, C_DW + c * 31 + jt:C_DW + c * 31 + jt + 1]
                            if jt == 0:
                                emit(dve, lambda e, src=src, dst=dst, dwc=dwc, c=c: e.tensor_scalar(
                                    out=dst, in0=src, scalar1=dwc, scalar2=pv[:, l, C_DWB + c:C_DWB + c + 1],
                                    op0=ALU.mult, op1=ALU.add), [gTB[c], pvB], [], pwrites=[accB[c][n]])
                            else:
                                emit(dve, lambda e, src=src, dst=dst, dwc=dwc: e.scalar_tensor_tensor(
                                    out=dst, in0=src, scalar=dwc, in1=dst, op0=ALU.mult, op1=ALU.add),
                                    [gTB[c], pvB, accB[c][n]], [], pwrites=[accB[c][n]])

                tap_groups.append(_grp)

        def taps_pop(k=1):
            for _ in range(k):
                if tap_groups:
                    tap_groups.pop(0)()
        mf = A.mark()
        taps_pop(2)
        ym, ymB = ymix_alloc()
        FT = A.alloc("FT", [4, T], BF16); FTB = [[A.newbuf("FT%d_%d" % (c, n)) for n in range(NT)] for c in range(4)]
        if sample:
            NTC = LS // 128
            fall = A.alloc("fall", [2, LS], BF16)
            AB = A.alloc("AB", [NTC, 2, 256], BF16)
            tring = [A.alloc("tr%d" % i, [2, 512], BF16) for i in range(4)]
            tringB = [A.newbuf("tr%d" % i) for i in range(4)]
            tcnt = [0]
            rv = xrB[l].ap()
            fallB = A.newbuf("fall")
            ABB = A.newbuf("AB")
            for hp in range(2):
                for r in range(2):
                    emit(sp, lambda e, r=r, hp=hp: e.dma_start(
                        out=fall[:, :, r * 1024:(r + 1) * 1024],
                        in_=rv[r * XRB + hp * 256:r * XRB + (hp + 1) * 256, :].rearrange("(h p) t -> p h t", p=128)),
                        [xrecvQ[l][1]], [fallB] if r == 0 else [], pwrites=[fallB] if r == 1 else [], dma=fallB)
                for i in range(NTC):
                    b = mbank()
                    for hh in range(2):
                        emit(pe, lambda e, b=b, hh=hh, i=i: e.matmul(
                            banks[b][:, hh * 256:(hh + 1) * 256], lhsT=fall[:, hh, i * 128:(i + 1) * 128], rhs=dft128[:],
                            start=True, stop=True), [fallB, constB], [bankB[b]] if hh == 0 else [], sig=(hh == 1))
                    copy_on(act, AB[:, i, :, :], banks[b][:].rearrange("p (h c) -> p h c", h=2), [bankB[b]],
                            [ABB] if i == 0 else [], pwrites=[ABB] if i > 0 else [])
                taps_pop(2)
                for kt in range(NT):
                    taps_pop(1)
                    bl = [dbank(), dbank()]
                    for i in range(NTC):
                        ti = tcnt[0] % 4
                        tcnt[0] += 1
                        emit(sp, lambda e, ti=ti, i=i, kt=kt: e.dma_start(
                            out=tring[ti], in_=dfts_d[i * 128:(i + 1) * 128, :, kt * 512:(kt + 1) * 512]),
                            [], [tringB[ti]], dma=tringB[ti])
                        for hh in range(2):
                            for cs in range(2):
                                emit(pe, lambda e, hh=hh, cs=cs, i=i, ti=ti, b=bl[hh]: e.matmul(
                                    banks[b][:], lhsT=AB[:, i, hh, cs * 128:(cs + 1) * 128], rhs=tring[ti][:, cs, :],
                                    start=(i == 0 and cs == 0), stop=(i == NTC - 1 and cs == 1)),
                                    [ABB, tringB[ti]], [bankB[bl[hh]]] if (i == 0 and cs == 0) else [],
                                    pwrites=[] if (i == 0 and cs == 0) else [bankB[bl[hh]]],
                                    sig=(cs == 1 and hh == 1))
                    for hh in range(2):
                        copy_on(act, FT[:, hp * 2 + hh, kt * 512:(kt + 1) * 512], banks[bl[hh]][:],
                                [bankB[bl[hh]]], [FTB[hp * 2 + hh][kt]])
        else:
            AB = A.alloc("ABp", [T // 128, 4, 256], BF16); ABB = A.newbuf("ABp")
            for i in range(T // 128):
                for hp in range(2):
                    b = mbank()
                    for hh in range(2):
                        h = hp * 2 + hh
                        emit(pe, lambda e, b=b, hh=hh, h=h, i=i: e.matmul(
                            banks[b][:, hh * 256:(hh + 1) * 256], lhsT=fTp[:, h, i * 128:(i + 1) * 128], rhs=dft128[:],
                            start=True, stop=True), [fTpB, constB], [bankB[b]] if hh == 0 else [], sig=(hh == 1))
                    copy_on(act, AB[:, i, hp * 2:hp * 2 + 2, :], banks[b][:].rearrange("p (h c) -> p h c", h=2),
                            [bankB[b]], [], pwrites=[ABB])
            for h in range(4):
                b = dbank()
                for si, (s0, L) in enumerate(segs):
                    ntc = L // 128
                    for i in range(ntc):
                        for cs in range(2):
                            emit(pe, lambda e, b=b, h=h, i=i, cs=cs, s0=s0, L=L: e.matmul(
                                banks[b][:, s0:s0 + L], lhsT=AB[:, s0 // 128 + i, h, cs * 128:(cs + 1) * 128],
                                rhs=dftp[:, i, cs, :], start=(i == 0 and cs == 0), stop=(i == ntc - 1 and cs == 1)),
                                [ABB, constB], [bankB[b]] if (si == 0 and i == 0 and cs == 0) else [],
                                sig=(si == nseg - 1 and i == ntc - 1 and cs == 1))
                copy_on(act, FT[:, h, 0:T], banks[b][:, 0:T], [bankB[b]], [FTB[h][0]])
        wb, wv = wload(wsrc(four_w_d, l, 0, 512, 0, 512), 4, 512)
        for mo in range(4):
            bl = job(4, NT, 512, lambda k, mo=mo: (wb, wv[:, k, mo * 128:(mo + 1) * 128]),
                     lambda k, n: (FTB[k][n], FT[:, k, n * 512:(n + 1) * 512]))
            for n in range(NT):
                copy_on(act, ym[:, mo, n * 512:(n + 1) * 512], banks[bl[n]][:], [bankB[bl[n]]], [ymB[mo][n]])
        residual_partial(4, NT, l, g1c, j, w_out_d, 1536, ym, lambda k, n: ymB[k][n])
        A.release(mf)

        if not sample:
            ma = A.mark()
            ym, ymB = ymix_alloc()
            pt = [A.alloc("ptile%d" % i, [512], BF16) for i in range(3)]
            ptB = [A.newbuf("ptile%d" % i) for i in range(3)]
            r0 = A.alloc("ar0", [512], F32); r0B = A.newbuf("ar0")
            r1 = A.alloc("ar1", [512], F32); r1B = A.newbuf("ar1")
            o0 = A.alloc("ao0", [512], F32); o0B = A.newbuf("ao0")
            asq = A.alloc("asq", [512], F32); asqB = A.newbuf("asq")
            ars = A.alloc("ars", [512], F32); arsB = A.newbuf("ars")
            art = A.alloc("art", [512], F32); artB = A.newbuf("art")
            if sample:
                NKC = 20
                kall = A.alloc("kall", [4, NKC * 128], BF16); kallB = A.newbuf("kall")
                vall = A.alloc("vall", [NKC, 512], BF16); vallB = A.newbuf("vall")
                rv = xrA[l].ap()
                rvb = xrB[l].ap()
                for r in range(2):
                    emit(sp, lambda e, r=r: e.dma_start(
                        out=kall[:, :, r * 1024:(r + 1) * 1024],
                        in_=rv[r * XRA:r * XRA + 512, :].rearrange("(h p) t -> p h t", p=128)),
                        [xrecvQ[l][0]], [], pwrites=[kallB], dma=kallB)
                    emit(sp, lambda e, r=r: e.dma_start(
                        out=vall[:, r * 8:(r + 1) * 8, :],
                        in_=rvb[r * XRB + 512:r * XRB + 1024, :].rearrange("r (a c) -> (r a) c", a=2).rearrange(
                            "(i p) c -> p i c", p=128)), [xrecvQ[l][1]], [], pwrites=[vallB], dma=vallB)
                emit(pool, lambda e: e.dma_start(out=vall[:, 16:20, :],
                                                 in_=cv_d[l].rearrange("(i p) c -> p i c", p=128)),
                     [], [], pwrites=[vallB], dma=vallB)
                cks = [A.alloc("cks%d" % i, [512], F32) for i in range(2)]
                cksB = [A.newbuf("cks%d" % i) for i in range(2)]
                for i in range(4):
                    s_ = i % 2
                    emit(sp, lambda e, i=i, s_=s_: e.dma_start(out=cks[s_], in_=ck_d[l, i * 128:(i + 1) * 128, :]),
                         [], [cksB[s_]], dma=cksB[s_])
                    b = mbank()
                    for h in range(4):
                        emit(pe, lambda e, h=h, b=b, s_=s_: e.transpose(out=banks[b][:, h * 128:(h + 1) * 128],
                                                                       in_=cks[s_][:, h * 128:(h + 1) * 128], identity=ident[:]),
                             [cksB[s_], constB], [bankB[b]] if h == 0 else [], sig=(h == 3))
                    copy_on(ev_eng(), kall[:, :, 2048 + i * 128:2048 + (i + 1) * 128],
                            banks[b][:].rearrange("p (h t) -> p h t", h=4), [bankB[b]], [], pwrites=[kallB])
                att_units = [(0, 512, n * 512, [(kall, kallB, vall, vallB, jc) for jc in range(NKC)]) for n in range(NT)]
            else:
                att_units = []
                for si, (s0, L) in enumerate(segs):
                    att_units.append((1, L, s0, [(kTp, kTpB, vTp, vTpB, s0 // 128 + jc) for jc in range(L // 128)]))
            pcnt = [0]
            for h in range(4):
                for (_, NQ, q0, klist) in att_units:
                    taps_pop(1)
                    bO = [0, 1]
                    bL = [2, 3]
                    nk_ = len(klist)
                    its = [(ki, mmap) for ki in range(nk_) for mmap in range(2)]
                    base_i = pcnt[0]
                    pcnt[0] += len(its)

                    def e_qk(ix):
                        ki, mmap = its[ix]
                        kt_, ktB_, vt_, vtB_, jc = klist[ki]
                        bs = 4 + ((base_i + ix) % 3)
                        emit(pe, lambda e: e.matmul(
                            banks[bs][:, 0:NQ], lhsT=kt_[mmap * 64:(mmap + 1) * 64, h, jc * 128:(jc + 1) * 128],
                            rhs=qT[mmap * 64:(mmap + 1) * 64, h, q0:q0 + NQ], start=True, stop=True),
                            [ktB_, qTB], [bankB[bs]])

                    def e_exp(ix):
                        bs = 4 + ((base_i + ix) % 3)
                        pi = (base_i + ix) % 3
                        emit(act, lambda e: e.activation(out=pt[pi][:, 0:NQ], in_=banks[bs][:, 0:NQ],
                                                         func=AF.Exp, scale=0.125), [bankB[bs]], [ptB[pi]])

                    def e_pv(ix):
                        ki, mmap = its[ix]
                        kt_, ktB_, vt_, vtB_, jc = klist[ki]
                        pi = (base_i + ix) % 3
                        emit(pe, lambda e: e.matmul(
                            banks[bO[mmap]][:, 0:NQ], lhsT=vt_[:, jc, h * 128:(h + 1) * 128], rhs=pt[pi][:, 0:NQ],
                            start=(ki == 0), stop=(ki == nk_ - 1)),
                            [vtB_, ptB[pi]], [bankB[bO[mmap]]] if ki == 0 else [],
                            pwrites=[] if ki == 0 else [bankB[bO[mmap]]], sig=False)
                        emit(pe, lambda e: e.matmul(
                            banks[bL[mmap]][:, 0:NQ], lhsT=ones_b[:], rhs=pt[pi][:, 0:NQ],
                            start=(ki == 0), stop=(ki == nk_ - 1)),
                            [constB, ptB[pi]], [bankB[bL[mmap]]] if ki == 0 else [],
                            pwrites=[] if ki == 0 else [bankB[bL[mmap]]], sig=True)

                    att_mode[0] = True
                    e_qk(0)
                    if len(its) > 1:
                        e_qk(1)
                    for ix in range(len(its)):
                        if ix + 2 < len(its):
                            e_qk(ix + 2)
                        e_exp(ix)
                        e_pv(ix)
                    emit(dve, lambda e, NQ=NQ: e.reciprocal(out=r0[:, 0:NQ], in_=banks[2][:, 0:NQ]), [bankB[2]], [r0B])
                    emit(dve, lambda e, NQ=NQ: e.reciprocal(out=r1[:, 0:NQ], in_=banks[3][:, 0:NQ]), [bankB[3]], [r1B])
                    emit(dve, lambda e, NQ=NQ: e.tensor_tensor(out=o0[:, 0:NQ], in0=banks[0][:, 0:NQ], in1=r0[:, 0:NQ],
                                                               op=ALU.mult), [bankB[0], r0B], [o0B])
                    emit(dve, lambda e, NQ=NQ: e.tensor_tensor(out=r1[:, 0:NQ], in0=banks[1][:, 0:NQ], in1=r1[:, 0:NQ],
                                                               op=ALU.mult), [bankB[1], r1B], [r1B])
                    emit(dve, lambda e, NQ=NQ: e.scalar_tensor_tensor(out=o0[:, 0:NQ], in0=r1[:, 0:NQ],
                                                                      scalar=small[:, 1 + l:2 + l], in1=o0[:, 0:NQ],
                                                                      op0=ALU.mult, op1=ALU.add), [r1B, o0B, smallB], [o0B])
                    emit(act, lambda e, NQ=NQ: e.activation(out=asq[:, 0:NQ], in_=o0[:, 0:NQ], func=AF.Square), [o0B], [asqB])
                    b2 = mbank()
                    emit(pe, lambda e, NQ=NQ, b2=b2: e.matmul(banks[b2][:, 0:NQ], lhsT=ones_f[:], rhs=asq[:, 0:NQ],
                                                             start=True, stop=True), [asqB, constB], [bankB[b2]])
                    rstd_from(banks[b2][:, 0:NQ], 1.0 / 128, ars[:, 0:NQ], [bankB[b2]], [arsB], art[:, 0:NQ], artB)
                    nq_t = q0 // 512
                    emit(dve, lambda e, NQ=NQ, h=h, q0=q0: e.scalar_tensor_tensor(
                        out=ym[:, h, q0:q0 + NQ], in0=o0[:, 0:NQ], scalar=small[:, 3 + l:4 + l], in1=ars[:, 0:NQ],
                        op0=ALU.mult, op1=ALU.mult), [o0B, arsB, smallB], [], pwrites=[ymB[h][nq_t]])
            att_mode[0] = False
            residual_partial(4, NT, l, g1c, j, w_out_d, 512, ym, lambda k, n: ymB[k][n])
            A.release(ma)
            if stop_after == "att%d%s" % (l, "s" if sample else "p"):
                return False


        taps_pop(100)
        csq = [A.alloc("csq%d" % i, [512], F32) for i in range(2)]
        csqB = [A.newbuf("csq%d" % i) for i in range(2)]
        mean = A.alloc("cmean", [512], F32); meanB = A.newbuf("cmean")
        msq = A.alloc("cmsq", [512], F32); msqB = A.newbuf("cmsq")
        crs = A.alloc("crs", [512], F32); crsB = A.newbuf("crs")
        crt = A.alloc("crt", [512], F32); crtB = A.newbuf("crt")
        ct1 = [A.alloc("ct1%d" % i, [512], F32) for i in range(2)]
        ct1B = [A.newbuf("ct1%d" % i) for i in range(2)]
        yb = A.alloc("cyb", [4, T], BF16); ybB = [[A.newbuf("cyb%d_%d" % (c, n)) for n in range(NT)] for c in range(4)]
        ym, ymB = ymix_alloc()
        for n in range(NT):
            b1 = mbank()
            for c in range(4):
                emit(pe, lambda e, c=c, n=n, b1=b1: e.matmul(banks[b1][:], lhsT=ones_f[:], rhs=acc[:, c, n * 512:(n + 1) * 512],
                                                            start=(c == 0), stop=(c == 3)),
                     [accB[c][n], constB], [bankB[b1]] if c == 0 else [], sig=(c == 3))
            b2 = mbank()
            for c in range(4):
                i = c % 2
                emit(act, lambda e, c=c, n=n, i=i: e.activation(out=csq[i], in_=acc[:, c, n * 512:(n + 1) * 512],
                                                               func=AF.Square), [accB[c][n]], [csqB[i]])
                emit(pe, lambda e, c=c, i=i, b2=b2: e.matmul(banks[b2][:], lhsT=ones_f[:], rhs=csq[i],
                                                            start=(c == 0), stop=(c == 3)),
                     [csqB[i], constB], [bankB[b2]] if c == 0 else [], pwrites=[] if c == 0 else [bankB[b2]], sig=True)
            emit(dve, lambda e, b1=b1: e.tensor_scalar(out=mean, in0=banks[b1][:], scalar1=1.0 / 512, scalar2=None,
                                                       op0=ALU.mult), [bankB[b1]], [meanB])
            emit(dve, lambda e: e.tensor_tensor(out=msq, in0=mean, in1=mean, op=ALU.mult), [meanB], [msqB])
            emit(dve, lambda e, b2=b2: e.scalar_tensor_tensor(out=msq, in0=banks[b2][:], scalar=1.0 / 512, in1=msq,
                                                              op0=ALU.mult, op1=ALU.subtract), [bankB[b2], msqB], [msqB])
            rstd_from(msq, 1.0, crs, [msqB], [crsB], crt, crtB)
            for c in range(4):
                i = c % 2
                emit(dve, lambda e, c=c, n=n, i=i: e.tensor_tensor(out=ct1[i], in0=acc[:, c, n * 512:(n + 1) * 512],
                                                                  in1=mean, op=ALU.subtract), [accB[c][n], meanB], [ct1B[i]])
                emit(dve, lambda e, i=i: e.tensor_tensor(out=ct1[i], in0=ct1[i], in1=crs, op=ALU.mult),
                     [ct1B[i], crsB], [ct1B[i]])
                emit(act, lambda e, c=c, n=n, i=i: e.activation(
                    out=yb[:, c, n * 512:(n + 1) * 512], in_=ct1[i], func=AF.Silu,
                    bias=pv[:, l, C_LNB + c:C_LNB + c + 1], scale=pv[:, l, C_LNG + c:C_LNG + c + 1]),
                    [ct1B[i], pvB], [ybB[c][n]])
        if stop_after == "dyb" + stg_tag:
            dbg_to_x(yb, [b_ for r_ in ybB for b_ in r_], 4, T)
            return False
        wb, wv = wload(wsrc(conv_pw_d, l, 0, 512, 0, 512), 4, 512)
        for mo in range(4):
            bl = job(4, NT, 512, lambda k, mo=mo: (wb, wv[:, k, mo * 128:(mo + 1) * 128]),
                     lambda k, n: (ybB[k][n], yb[:, k, n * 512:(n + 1) * 512]))
            for n in range(NT):
                emit(act, lambda e, mo=mo, n=n, b=bl[n]: e.activation(
                    out=ym[:, mo, n * 512:(n + 1) * 512], in_=banks[b][:], func=AF.Identity,
                    bias=pv[:, l, C_PWB + mo:C_PWB + mo + 1], scale=1.0), [bankB[bl[n]], pvB], [ymB[mo][n]])
        if stop_after == "dym" + stg_tag:
            dbg_to_x(ym, [b_ for r_ in ymB for b_ in r_], 4, T)
            return False
        if stop_after == "dnores" + stg_tag:
            return False
        residual_partial(4, NT, l, g1c, j, w_out_d, 1024, ym, lambda k, n: ymB[k][n], overwrite=(stop_after == "dres" + stg_tag))
        if stop_after == "dres" + stg_tag:
            return False
        A.release(mc)
        if stop_after == "conv%d%s" % (l, "s" if sample else "p"):
            return False

        if sample:
            ma = A.mark()
            ym, ymB = ymix_alloc()
            pt = [A.alloc("ptile%d" % i, [512], BF16) for i in range(3)]
            ptB = [A.newbuf("ptile%d" % i) for i in range(3)]
            r0 = A.alloc("ar0", [512], F32); r0B = A.newbuf("ar0")
            r1 = A.alloc("ar1", [512], F32); r1B = A.newbuf("ar1")
            o0 = A.alloc("ao0", [512], F32); o0B = A.newbuf("ao0")
            asq = A.alloc("asq", [512], F32); asqB = A.newbuf("asq")
            ars = A.alloc("ars", [512], F32); arsB = A.newbuf("ars")
            art = A.alloc("art", [512], F32); artB = A.newbuf("art")
            if sample:
                NKC = 20
                kall = A.alloc("kall", [4, NKC * 128], BF16); kallB = A.newbuf("kall")
                vall = A.alloc("vall", [NKC, 512], BF16); vallB = A.newbuf("vall")
                rv = xrA[l].ap()
                rvb = xrB[l].ap()
                for r in range(2):
                    emit(sp, lambda e, r=r: e.dma_start(
                        out=kall[:, :, r * 1024:(r + 1) * 1024],
                        in_=rv[r * XRA:r * XRA + 512, :].rearrange("(h p) t -> p h t", p=128)),
                        [xrecvQ[l][0]], [], pwrites=[kallB], dma=kallB)
                    emit(sp, lambda e, r=r: e.dma_start(
                        out=vall[:, r * 8:(r + 1) * 8, :],
                        in_=rvb[r * XRB + 512:r * XRB + 1024, :].rearrange("r (a c) -> (r a) c", a=2).rearrange(
                            "(i p) c -> p i c", p=128)), [xrecvQ[l][1]], [], pwrites=[vallB], dma=vallB)
                emit(pool, lambda e: e.dma_start(out=vall[:, 16:20, :],
                                                 in_=cv_d[l].rearrange("(i p) c -> p i c", p=128)),
                     [], [], pwrites=[vallB], dma=vallB)
                cks = [A.alloc("cks%d" % i, [512], F32) for i in range(2)]
                cksB = [A.newbuf("cks%d" % i) for i in range(2)]
                for i in range(4):
                    s_ = i % 2
                    emit(sp, lambda e, i=i, s_=s_: e.dma_start(out=cks[s_], in_=ck_d[l, i * 128:(i + 1) * 128, :]),
                         [], [cksB[s_]], dma=cksB[s_])
                    b = mbank()
                    for h in range(4):
                        emit(pe, lambda e, h=h, b=b, s_=s_: e.transpose(out=banks[b][:, h * 128:(h + 1) * 128],
                                                                       in_=cks[s_][:, h * 128:(h + 1) * 128], identity=ident[:]),
                             [cksB[s_], constB], [bankB[b]] if h == 0 else [], sig=(h == 3))
                    copy_on(ev_eng(), kall[:, :, 2048 + i * 128:2048 + (i + 1) * 128],
                            banks[b][:].rearrange("p (h t) -> p h t", h=4), [bankB[b]], [], pwrites=[kallB])
                att_units = [(0, 512, n * 512, [(kall, kallB, vall, vallB, jc) for jc in range(NKC)]) for n in range(NT)]
            else:
                att_units = []
                for si, (s0, L) in enumerate(segs):
                    att_units.append((1, L, s0, [(kTp, kTpB, vTp, vTpB, s0 // 128 + jc) for jc in range(L // 128)]))
            pcnt = [0]
            for h in range(4):
                for (_, NQ, q0, klist) in att_units:
                    taps_pop(1)
                    bO = [0, 1]
                    bL = [2, 3]
                    nk_ = len(klist)
                    its = [(ki, mmap) for ki in range(nk_) for mmap in range(2)]
                    base_i = pcnt[0]
                    pcnt[0] += len(its)

                    def e_qk(ix):
                        ki, mmap = its[ix]
                        kt_, ktB_, vt_, vtB_, jc = klist[ki]
                        bs = 4 + ((base_i + ix) % 3)
                        emit(pe, lambda e: e.matmul(
                            banks[bs][:, 0:NQ], lhsT=kt_[mmap * 64:(mmap + 1) * 64, h, jc * 128:(jc + 1) * 128],
                            rhs=qT[mmap * 64:(mmap + 1) * 64, h, q0:q0 + NQ], start=True, stop=True),
                            [ktB_, qTB], [bankB[bs]])

                    def e_exp(ix):
                        bs = 4 + ((base_i + ix) % 3)
                        pi = (base_i + ix) % 3
                        emit(act, lambda e: e.activation(out=pt[pi][:, 0:NQ], in_=banks[bs][:, 0:NQ],
                                                         func=AF.Exp, scale=0.125), [bankB[bs]], [ptB[pi]])

                    def e_pv(ix):
                        ki, mmap = its[ix]
                        kt_, ktB_, vt_, vtB_, jc = klist[ki]
                        pi = (base_i + ix) % 3
                        emit(pe, lambda e: e.matmul(
                            banks[bO[mmap]][:, 0:NQ], lhsT=vt_[:, jc, h * 128:(h + 1) * 128], rhs=pt[pi][:, 0:NQ],
                            start=(ki == 0), stop=(ki == nk_ - 1)),
                            [vtB_, ptB[pi]], [bankB[bO[mmap]]] if ki == 0 else [],
                            pwrites=[] if ki == 0 else [bankB[bO[mmap]]], sig=False)
                        emit(pe, lambda e: e.matmul(
                            banks[bL[mmap]][:, 0:NQ], lhsT=ones_b[:], rhs=pt[pi][:, 0:NQ],
                            start=(ki == 0), stop=(ki == nk_ - 1)),
                            [constB, ptB[pi]], [bankB[bL[mmap]]] if ki == 0 else [],
                            pwrites=[] if ki == 0 else [bankB[bL[mmap]]], sig=True)

                    att_mode[0] = True
                    e_qk(0)
                    if len(its) > 1:
                        e_qk(1)
                    for ix in range(len(its)):
                        if ix + 2 < len(its):
                            e_qk(ix + 2)
                        e_exp(ix)
                        e_pv(ix)
                    emit(dve, lambda e, NQ=NQ: e.reciprocal(out=r0[:, 0:NQ], in_=banks[2][:, 0:NQ]), [bankB[2]], [r0B])
                    emit(dve, lambda e, NQ=NQ: e.reciprocal(out=r1[:, 0:NQ], in_=banks[3][:, 0:NQ]), [bankB[3]], [r1B])
                    emit(dve, lambda e, NQ=NQ: e.tensor_tensor(out=o0[:, 0:NQ], in0=banks[0][:, 0:NQ], in1=r0[:, 0:NQ],
                                                               op=ALU.mult), [bankB[0], r0B], [o0B])
                    emit(dve, lambda e, NQ=NQ: e.tensor_tensor(out=r1[:, 0:NQ], in0=banks[1][:, 0:NQ], in1=r1[:, 0:NQ],
                                                               op=ALU.mult), [bankB[1], r1B], [r1B])
                    emit(dve, lambda e, NQ=NQ: e.scalar_tensor_tensor(out=o0[:, 0:NQ], in0=r1[:, 0:NQ],
                                                                      scalar=small[:, 1 + l:2 + l], in1=o0[:, 0:NQ],
                                                                      op0=ALU.mult, op1=ALU.add), [r1B, o0B, smallB], [o0B])
                    emit(act, lambda e, NQ=NQ: e.activation(out=asq[:, 0:NQ], in_=o0[:, 0:NQ], func=AF.Square), [o0B], [asqB])
                    b2 = mbank()
                    emit(pe, lambda e, NQ=NQ, b2=b2: e.matmul(banks[b2][:, 0:NQ], lhsT=ones_f[:], rhs=asq[:, 0:NQ],
                                                             start=True, stop=True), [asqB, constB], [bankB[b2]])
                    rstd_from(banks[b2][:, 0:NQ], 1.0 / 128, ars[:, 0:NQ], [bankB[b2]], [arsB], art[:, 0:NQ], artB)
                    nq_t = q0 // 512
                    emit(dve, lambda e, NQ=NQ, h=h, q0=q0: e.scalar_tensor_tensor(
                        out=ym[:, h, q0:q0 + NQ], in0=o0[:, 0:NQ], scalar=small[:, 3 + l:4 + l], in1=ars[:, 0:NQ],
                        op0=ALU.mult, op1=ALU.mult), [o0B, arsB, smallB], [], pwrites=[ymB[h][nq_t]])
            att_mode[0] = False
            residual_partial(4, NT, l, g1c, j, w_out_d, 512, ym, lambda k, n: ymB[k][n])
            A.release(ma)
            if stop_after == "att%d%s" % (l, "s" if sample else "p"):
                return False


        A.release(m_q)
        A.release(base)
        if stop_after == "mix%d%s" % (l, "s" if sample else "p"):
            return False
        bg_drain(25)
        mffn = A.mark()
        hT = A.alloc("hT2", [KC, T], BF16)
        hTb = [[A.newbuf("hT2%d_%d" % (c, n)) for n in range(NT)] for c in range(KC)]
        norm(l, 1, T, j, hT, hTb)

        def hrhs2(k, n):
            return (hTb[k][n], hT[:, k, n * 512:(n + 1) * 512])
        aT = A.alloc("aT", [12, T], BF16)
        sl = [A.alloc("sl%d" % i, [T], F32) for i in range(2)]
        slB = [[A.newbuf("sl%d_%d" % (i, n)) for n in range(NT)] for i in range(2)]
        scnt = [0]
        aTB = [[A.newbuf("aT%d_%d" % (c, n)) for n in range(NT)] for c in range(12)]
        for (g0, gs) in FFN_GROUPS:
            for blk in range(gs // 2):
                col0 = (g0 + blk * 2) * 128
                bg_tick()
                wbg, wvg = wload(wsrc(w_gate_d, l, 0, D, col0, 256), KC, 256)
                wbu, wvu = wload(wsrc(w_up_d, l, 0, D, col0, 256), KC, 256)
                for sub in range(2):
                    cc = blk * 2 + sub
                    si_ = scnt[0] % 2
                    scnt[0] += 1
                    bl = job(KC, NT, 512, lambda k, sub=sub: (wbg, wvg[:, k, sub * 128:(sub + 1) * 128]), hrhs2)
                    for n in range(NT):
                        emit(act, lambda e, b=bl[n], si_=si_, n=n: e.activation(
                            out=sl[si_][:, n * 512:(n + 1) * 512], in_=banks[b][:], func=AF.Silu),
                            [bankB[bl[n]]], [slB[si_][n]])
                    bl = job(KC, NT, 512, lambda k, sub=sub: (wbu, wvu[:, k, sub * 128:(sub + 1) * 128]), hrhs2)
                    for n in range(NT):
                        emit(dve, lambda e, b=bl[n], si_=si_, n=n, cc=cc: e.tensor_tensor(
                            out=aT[:, cc, n * 512:(n + 1) * 512], in0=banks[b][:], in1=sl[si_][:, n * 512:(n + 1) * 512],
                            op=ALU.mult), [bankB[bl[n]], slB[si_][n]], [aTB[cc][n]])
            bg_drain(32)
            residual_partial(gs, NT, l, 80, j, w_down_d, g0 * 128, aT, lambda k, n: aTB[k][n])
        A.release(mffn)
        if stop_after == "end" + stg_tag:
            return False
        return True

    outB_all = P.buf("outs")
    GP = dict(T=TP, j=0, sample=False, segs=[(0, LP), (LP, LP)], first=False)
    GS = dict(T=TS, j=1, sample=True, segs=[(0, TS)], first=True)
    finals = []
    done = False
    for (G, src, dst) in ((GS, xs_d, ys_d), (GP, xp_d, yp_d)):
        if stop_after == "mods":
            done = True
            break
        load_x(src, G["T"])
        if stop_after == "loadx":
            done = True
            break
        ok = True
        for l in range(2):
            ok = layer(l, G)
            if not ok:
                break
        if not ok:
            done = True
            break
        if stop_after == ("endp" if not G["sample"] else "ends"):
            done = True
            break
        finals += store_y(dst, G["T"])
    if stop_after == "mods":
        db = P.buf("dbg")
        emit(sp, lambda e: e.dma_start(out=dbg_d[:, 0:384], in_=modt[:].rearrange("p l m j -> p (l m j)")), [modB, gscB], [db], dma=db)
        emit(sp, lambda e: e.dma_start(out=dbg_d[:, 384:384 + 128], in_=gsc[:].rearrange("p l s c j -> p (l s c j)")), [modB, gscB], [], pwrites=[db], dma=db)
        emit(sp, lambda e: e.dma_start(out=dbg_d[:, 512:528], in_=small[:]), [smallB], [], pwrites=[db], dma=db)
        finals.append(db)
    elif stop_after:
        allx = [xTb[c][n] for c in range(KC) for n in range(2)]
        db = P.buf("dbg")
        Tl = G["T"]
        emit(sp, lambda e: e.dma_start(out=dbg_d.rearrange("p (c t) -> p c t", c=KC)[:, :, 0:Tl], in_=xT[:, :, 0:Tl]), allx, [db], dma=db)
        finals.append(db)
    emit(sp, lambda e: e.nop(), [], finals + [outB_all], sig=True)
    P.replay()


def _consts():
    c = {}
    c["ident"] = np.eye(128, dtype=np.float32)
    rot = np.zeros((128, 128), np.float32)
    for p in range(128):
        partner = p + 16 if (p % 32) < 16 else p - 16
        rot[partner, p] = 1.0
    c["rotm"] = rot
    bo = np.zeros((128, 128), np.float32)
    bo[:64, :64] = 1.0
    bo[64:, 64:] = 1.0
    c["bones"] = bo
    k = np.arange(128)
    ang = 2 * np.pi * np.outer(k, k) / 128.0
    c["dft128"] = (np.concatenate([np.cos(ang), np.sin(ang)], axis=1) / np.sqrt(128.0)).astype(np.float32)
    t = np.arange(LP)
    ang = 2 * np.pi * np.outer(t, t) / LP
    c["dftp"] = (np.stack([np.cos(ang), -np.sin(ang)], axis=1) / np.sqrt(LP)).astype(ml_dtypes.bfloat16)

    def icnt(L, t):
        out = []
        for w in (2, 4, 8, 16):
            lo = np.maximum(t - w // 2, 0)
            hi = np.minimum(t + w // 2 - 1, L - 1)
            out.append(1.0 / (hi - lo + 1))
        return np.stack(out).astype(np.float32)
    c["icnt_p"] = icnt(LP, np.arange(LP))
    c["icnt_s"] = [icnt(LS, np.arange(hf * TS, (hf + 1) * TS)) for hf in range(2)]
    inv = 1.0 / (10000.0 ** (np.arange(0, 32, 2, dtype=np.float32) / 32.0))
    ropes = []
    for hf in range(2):
        tt = np.arange(hf * TS, (hf + 1) * TS)
        row = (tt // 64).astype(np.float32)
        col = (tt % 64).astype(np.float32)
        tab = np.zeros((128, 2, TS), np.float32)
        for p in range(128):
            d = p % 64
            pos = row if d < 32 else col
            a = pos * inv[d % 16]
            tab[p, 0] = np.cos(a)
            tab[p, 1] = (-np.sin(a)) if (d % 32) < 16 else np.sin(a)
        ropes.append(tab)
    c["rope"] = ropes
    t = np.arange(LS, dtype=np.float64)
    dfts = []
    for hf in range(2):
        kk = np.arange(hf * TS, (hf + 1) * TS, dtype=np.float64)
        ang = 2 * np.pi * (np.outer(t, kk) % LS) / LS
        dfts.append((np.stack([np.cos(ang), -np.sin(ang)], axis=1) / np.sqrt(LS)).astype(ml_dtypes.bfloat16))
    c["dfts"] = dfts
    c["hmask"] = [np.tile(np.array([[float(hf), 1.0 - hf]], np.float32), (128, 1)) for hf in range(2)]
    return c


def _pvec(inp):
    out = np.zeros((2, 128, NV), np.float32)
    for l in range(2):
        o = out[l]
        o[:, C_G1:C_G1 + 16] = inp["g_norm1"][l].reshape(16, 128).T
        o[:, C_G2:C_G2 + 16] = inp["g_norm2"][l].reshape(16, 128).T
        o[:, C_BADA:C_BADA + 96] = inp["b_ada"][l].reshape(96, 128).T
        o[:, C_PSC:C_PSC + 4] = inp["pool_scale"][l].reshape(4, 128).T
        dw = inp["conv_dw"][l]
        for c in range(4):
            o[:, C_DW + c * 31:C_DW + (c + 1) * 31] = dw[:, c * 128:(c + 1) * 128].T
        o[:, C_DWB:C_DWB + 4] = inp["conv_dw_b"][l].reshape(4, 128).T
        o[:, C_LNG:C_LNG + 4] = inp["conv_ln_g"][l].reshape(4, 128).T
        o[:, C_LNB:C_LNB + 4] = inp["conv_ln_b"][l].reshape(4, 128).T
        o[:, C_PWB:C_PWB + 4] = inp["conv_pw_b"][l].reshape(4, 128).T
        o[:, C_GQ] = np.tile(inp["g_q"][l], 2)
        o[:, C_GK] = np.tile(inp["g_k"][l], 2)
        o[:, C_GSUB] = inp["g_subln"][l]
        for q in range(4):
            o[:64, C_LAM + q] = inp["lam"][l][q]
    return out


_CACHE = {}


def make_in_maps(inp, cores):
    inp = {k: np.ascontiguousarray(np.asarray(v)) for k, v in inp.items()}
    cst = _consts()
    pvec = _pvec(inp)
    maps = []
    for c in cores:
        b, hf = c // 2, c % 2
        m = {
            "xp": inp["x_prompt"][2 * c:2 * c + 2].reshape(TP, D),
            "xs": inp["x_sample"][b, hf * TS:(hf + 1) * TS],
            "ck": inp["cache_k"][b].reshape(2, PAST, 512),
            "cv": inp["cache_v"][b].reshape(2, PAST, 512),
            "cT": np.ascontiguousarray(np.concatenate([np.stack([inp["c_ctx"], inp["c"][b]], axis=1), np.zeros((D, 6), np.float32)], axis=1).reshape(KC, 128, 8).transpose(1, 0, 2)),
            "pvec": pvec,
            "ident": cst["ident"], "rotm": cst["rotm"], "bones": cst["bones"],
            "rope": cst["rope"][hf], "icnt_p": cst["icnt_p"], "icnt_s": cst["icnt_s"][hf],
            "dft128": cst["dft128"], "dftp": cst["dftp"], "dfts": cst["dfts"][hf], "hmask": cst["hmask"][hf],
        }
        for k in ("w_ada", "w_in", "pool_w", "conv_pw", "fourier_w", "w_out", "w_gate", "w_up", "w_down"):
            m[k] = inp[k]
        maps.append({k: np.ascontiguousarray(v) for k, v in m.items()})
    return maps


def kernel(**inputs):
    if "nc" not in _CACHE:
        _CACHE["nc"] = build_program()
    nc = _CACHE["nc"]
    maps = make_in_maps(inputs, list(range(NCORES)))
    res = run_bass_kernel_spmd(nc, maps, core_ids=list(range(NCORES)))
    R = res.results
    yp = np.zeros((16, LP, D), np.float32)
    ys = np.zeros((4, LS, D), np.float32)
    nk = np.zeros((16, 2, LP, 4, 2, 64), np.float32)
    nv = np.zeros((16, 2, LP, 4, 128), np.float32)
    for c in range(NCORES):
        b, hf = c // 2, c % 2
        yp[2 * c:2 * c + 2] = np.asarray(R[c]["yp"]).reshape(2, LP, D)
        ys[b, hf * TS:(hf + 1) * TS] = np.asarray(R[c]["ys"])
        nk[2 * c:2 * c + 2] = np.asarray(R[c]["nk"]).reshape(2, 2, LP, 4, 2, 64)
        nv[2 * c:2 * c + 2] = np.asarray(R[c]["nv"]).reshape(2, 2, LP, 4, 128)
    return (yp, ys, nk, nv)
```
